# Optimizing a Trainium2 kernel written in Bass

```python
import math
import jax, jax.numpy as jnp
from jax import lax
import numpy as np


D_MODEL = 2048
BATCH = 4
SEQ = 2048
DEPTH = 4

N_MIXERS = 3
NORM_EPS = 1e-6

RWKV_WIDTH = D_MODEL
RWKV_HEAD = 64
RWKV_HEADS = RWKV_WIDTH // RWKV_HEAD
RWKV_DECAY_LORA = max(32, int(round(1.8 * D_MODEL ** 0.5 / 32)) * 32)
RWKV_ICLR_LORA = max(32, int(round(1.8 * D_MODEL ** 0.5 / 32)) * 32)
RWKV_GN_EPS = 64e-5
RWKV_N_SHIFT = 6

GLA_HEADS = 4
GLA_KEY_WIDTH = D_MODEL // 2
GLA_VALUE_WIDTH = D_MODEL
GLA_HEAD_K = GLA_KEY_WIDTH // GLA_HEADS
GLA_HEAD_V = GLA_VALUE_WIDTH // GLA_HEADS
GLA_GATE_RANK = 16
GLA_GATE_TAU = 16.0
GLA_CHUNK = 64
GLA_IN_WIDTH = 2 * GLA_KEY_WIDTH + 2 * GLA_VALUE_WIDTH + GLA_GATE_RANK

SSM_WIDTH = 2 * D_MODEL
SSM_HEADDIM = 64
SSM_HEADS = SSM_WIDTH // SSM_HEADDIM
SSM_STATE = 128
SSM_GROUPS = 8
SSM_CONV = 4
SSM_CHUNK = 128
SSM_NORM_EPS = 1e-5
SSM_CONV_WIDTH = SSM_WIDTH + 2 * SSM_GROUPS * SSM_STATE
SSM_IN_WIDTH = SSM_WIDTH + SSM_CONV_WIDTH + SSM_HEADS

N_RWKV_LAYERS = (DEPTH + 2) // 3
N_GLA_LAYERS = (DEPTH + 1) // 3
N_SSD_LAYERS = DEPTH // 3

kernel_name = 'hybrid_rwkv7_gla_ssd_adaln'


def rms_normalize(x, gain, eps):
    xf = x.astype(jnp.float32)
    y = xf * lax.rsqrt(jnp.mean(xf * xf, axis=-1, keepdims=True) + eps) * gain.astype(jnp.float32)
    return y.astype(x.dtype)


def causal_depthwise_conv(x, w, b):
    k_width, ch = w.shape
    y = lax.conv_general_dilated(x, w[:, None, :], window_strides=(1,), padding=((k_width - 1, 0),),
                                 dimension_numbers=('NWC', 'WIO', 'NWC'), feature_group_count=ch)
    return y + b


def rwkv7_mixer(h, mu, w_in, dec_w1, dec_w2, dec_w0, iclr_w1, iclr_w2, iclr_w0,
                k_k, k_a, r_k, gn_w, gn_b, w_out):
    bsz, s, _ = h.shape
    H, N, W = RWKV_HEADS, RWKV_HEAD, RWKV_WIDTH
    h_prev = jnp.pad(h, ((0, 0), (1, 0), (0, 0)))[:, :-1]
    xs = h[None] + (h_prev - h)[None] * mu[:, None, None, :]
    r, k, v, g = jnp.einsum('cbsd,dcw->cbsw', xs[:4], w_in.reshape(D_MODEL, 4, W))
    w_log = -jax.nn.softplus(-(dec_w0 + jnp.tanh(xs[4] @ dec_w1) @ dec_w2)) - 0.5
    decay = jnp.exp(-jnp.exp(w_log.astype(jnp.float32)))
    a = jax.nn.sigmoid(iclr_w0 + (xs[5] @ iclr_w1) @ iclr_w2)
    heads = lambda t: t.reshape(bsz, s, H, N)
    kk = heads(k * k_k)
    kk = kk / jnp.maximum(jnp.sqrt(jnp.sum(kk * kk, axis=-1, keepdims=True)), 1e-12)
    k = k * (1 + (a - 1) * k_a)
    r, k, v, a, decay = heads(r), heads(k), heads(v), heads(a), heads(decay)

    def step(state, inp):
        r_t, k_t, v_t, w_t, kk_t, a_t = inp
        removal = jnp.einsum('bhvk,bhk->bhv', state, kk_t)
        state = (state * w_t[:, :, None, :]
                 - jnp.einsum('bhv,bhk->bhvk', removal, kk_t * a_t)
                 + jnp.einsum('bhv,bhk->bhvk', v_t, k_t))
        return state, jnp.einsum('bhvk,bhk->bhv', state, r_t)

    tm = lambda t: jnp.moveaxis(t, 1, 0)
    state0 = jnp.zeros((bsz, H, N, N), jnp.float32)
    _, y = lax.scan(step, state0, (tm(r), tm(k), tm(v), tm(decay), tm(kk), tm(a)))
    y = jnp.moveaxis(y, 0, 1).astype(jnp.float32)
    mean = jnp.mean(y, axis=-1, keepdims=True)
    var = jnp.mean(jnp.square(y - mean), axis=-1, keepdims=True)
    y = ((y - mean) * lax.rsqrt(var + RWKV_GN_EPS)).reshape(bsz, s, W) * gn_w + gn_b
    bonus = jnp.sum(r * k * r_k, axis=-1, keepdims=True) * v
    y = (y + bonus.reshape(bsz, s, W)).astype(h.dtype)
    return (y * jax.nn.silu(g)) @ w_out


def gla_mixer(h, w_in, gate_w2, gate_b, head_g, w_out):
    bsz, s, _ = h.shape
    H, DK, DV, C = GLA_HEADS, GLA_HEAD_K, GLA_HEAD_V, GLA_CHUNK
    nc = s // C
    kw, vw = GLA_KEY_WIDTH, GLA_VALUE_WIDTH
    proj = h @ w_in
    q, k, v, g, low = jnp.split(proj, [kw, 2 * kw, 2 * kw + vw, 2 * kw + 2 * vw], axis=-1)
    log_alpha = jax.nn.log_sigmoid((low @ gate_w2 + gate_b).astype(jnp.float32)) / GLA_GATE_TAU

    def chunks(t, d):
        return t.reshape(bsz, nc, C, H, d).transpose(1, 0, 3, 2, 4)

    qc = chunks(q * DK ** -0.5, DK)
    kc = chunks(k, DK)
    vc = chunks(v, DV)
    bcum = jnp.cumsum(chunks(log_alpha, DK), axis=-2)
    b_ref = bcum[..., C // 2 - 1:C // 2, :]
    causal = jnp.tril(jnp.ones((C, C), dtype=bool))
    scores = jnp.einsum('nbhik,nbhjk->nbhij', qc * jnp.exp(bcum - b_ref), kc * jnp.exp(b_ref - bcum))
    o_intra = jnp.einsum('nbhij,nbhjv->nbhiv', jnp.where(causal, scores, 0.0), vc)
    b_last = bcum[..., -1:, :]
    q_from_start = qc * jnp.exp(bcum)
    k_to_end = kc * jnp.exp(b_last - bcum)

    def step(state, inp):
        q_n, k_n, v_n, dec_n = inp
        o_n = jnp.einsum('bhik,bhkv->bhiv', q_n, state)
        state = state * dec_n[:, :, 0, :, None] + jnp.einsum('bhjk,bhjv->bhkv', k_n, v_n)
        return state, o_n

    state0 = jnp.zeros((bsz, H, DK, DV), jnp.float32)
    _, o_inter = lax.scan(step, state0, (q_from_start, k_to_end, vc, jnp.exp(b_last)))
    o = (o_intra + o_inter).transpose(1, 0, 3, 2, 4).reshape(bsz, s, H, DV)
    o = rms_normalize(o.astype(h.dtype), head_g, NORM_EPS).reshape(bsz, s, vw)
    return (o * jax.nn.silu(g)) @ w_out


def ssd_mixer(h, w_in, conv_w, conv_b, dt_bias, a_log, d_skip, norm_g, w_out):
    bsz, s, _ = h.shape
    H, P, N, G, C = SSM_HEADS, SSM_HEADDIM, SSM_STATE, SSM_GROUPS, SSM_CHUNK
    R = H // G
    nc = s // C
    proj = h @ w_in
    z, xbc, dt = jnp.split(proj, [SSM_WIDTH, SSM_WIDTH + SSM_CONV_WIDTH], axis=-1)
    xbc = jax.nn.silu(causal_depthwise_conv(xbc, conv_w, conv_b))
    xs, bm, cm = jnp.split(xbc, [SSM_WIDTH, SSM_WIDTH + G * N], axis=-1)
    dt = jax.nn.softplus((dt + dt_bias).astype(jnp.float32))
    a = -jnp.exp(a_log.astype(jnp.float32))
    xh = xs.reshape(bsz, s, H, P)
    xc = (xh * dt[..., None]).reshape(bsz, nc, C, G, R, P)
    bc = bm.reshape(bsz, nc, C, G, N)
    cc = cm.reshape(bsz, nc, C, G, N)
    acum = jnp.cumsum((dt * a).reshape(bsz, nc, C, G, R), axis=2)
    causal = jnp.tril(jnp.ones((C, C), dtype=bool))[:, :, None, None]
    seg = acum[:, :, :, None] - acum[:, :, None, :]
    decay_ij = jnp.exp(jnp.where(causal, seg, -jnp.inf))
    cb = jnp.einsum('bnigs,bnjgs->bnijg', cc, bc)
    y_diag = jnp.einsum('bnijgr,bnjgrp->bnigrp', cb[..., None] * decay_ij, xc)
    decay_to_end = jnp.exp(acum[:, :, -1:] - acum)
    chunk_states = jnp.einsum('bnjgs,bnjgrp->bngrsp', bc, xc * decay_to_end[..., None])
    chunk_decay = jnp.exp(acum[:, :, -1])

    def step(state, inp):
        st_n, dec_n = inp
        return state * dec_n[..., None, None] + st_n, state

    state0 = jnp.zeros((bsz, G, R, N, P), jnp.float32)
    _, prev = lax.scan(step, state0, (jnp.moveaxis(chunk_states, 1, 0), jnp.moveaxis(chunk_decay, 1, 0)))
    prev = jnp.moveaxis(prev, 0, 1)
    y_off = jnp.einsum('bnigs,bngrsp->bnigrp', cc, prev) * jnp.exp(acum)[..., None]
    y = (y_diag + y_off).reshape(bsz, s, H, P) + d_skip[:, None] * xh
    y = (y.reshape(bsz, s, SSM_WIDTH) * jax.nn.silu(z)).astype(h.dtype)
    y = rms_normalize(y.reshape(bsz, s, G, SSM_WIDTH // G), norm_g.reshape(G, SSM_WIDTH // G), SSM_NORM_EPS)
    return y.reshape(bsz, s, SSM_WIDTH) @ w_out


def setup_inputs(seed: int = 0) -> dict:
    key = jax.random.key(seed)
    ks = iter(jax.random.split(key, 40))

    def nrm(shape, scale):
        return scale * jax.random.normal(next(ks), shape, jnp.float32)

    def unif(shape, lo, hi):
        return jax.random.uniform(next(ks), shape, jnp.float32, lo, hi)

    D = D_MODEL
    LA, LB, LC = N_RWKV_LAYERS, N_GLA_LAYERS, N_SSD_LAYERS
    dt0 = jnp.exp(unif((LC, SSM_HEADS), math.log(1e-3), math.log(1e-1)))
    return {
        'x': nrm((BATCH, SEQ, D), 1.0),
        'c': nrm((BATCH, D), 1.0),
        'ada_w': nrm((DEPTH, D, 3 * D), 0.5 * D ** -0.5),
        'ada_b': nrm((DEPTH, 3 * D), 0.02),
        'norm_g': 1.0 + nrm((DEPTH, D), 0.02),
        'final_g': 1.0 + nrm((D,), 0.02),
        'rwkv_mu': unif((LA, RWKV_N_SHIFT, D), 0.0, 1.0),
        'rwkv_w_in': nrm((LA, D, 4 * RWKV_WIDTH), D ** -0.5),
        'rwkv_dec_w1': nrm((LA, D, RWKV_DECAY_LORA), D ** -0.5),
        'rwkv_dec_w2': nrm((LA, RWKV_DECAY_LORA, RWKV_WIDTH), 0.5 * RWKV_DECAY_LORA ** -0.5),
        'rwkv_dec_w0': unif((LA, RWKV_WIDTH), -6.5, -1.5),
        'rwkv_iclr_w1': nrm((LA, D, RWKV_ICLR_LORA), D ** -0.5),
        'rwkv_iclr_w2': nrm((LA, RWKV_ICLR_LORA, RWKV_WIDTH), 0.5 * RWKV_ICLR_LORA ** -0.5),
        'rwkv_iclr_w0': nrm((LA, RWKV_WIDTH), 0.1),
        'rwkv_k_k': 0.85 + nrm((LA, RWKV_WIDTH), 0.05),
        'rwkv_k_a': 1.0 + nrm((LA, RWKV_WIDTH), 0.05),
        'rwkv_r_k': nrm((LA, RWKV_HEADS, RWKV_HEAD), 0.1),
        'rwkv_gn_w': 1.0 + nrm((LA, RWKV_WIDTH), 0.02),
        'rwkv_gn_b': nrm((LA, RWKV_WIDTH), 0.02),
        'rwkv_w_out': nrm((LA, RWKV_WIDTH, D), RWKV_WIDTH ** -0.5),
        'gla_w_in': nrm((LB, D, GLA_IN_WIDTH), D ** -0.5),
        'gla_gate_w2': nrm((LB, GLA_GATE_RANK, GLA_KEY_WIDTH), GLA_GATE_RANK ** -0.5),
        'gla_gate_b': nrm((LB, GLA_KEY_WIDTH), 0.1),
        'gla_head_g': 1.0 + nrm((LB, GLA_HEAD_V), 0.02),
        'gla_w_out': nrm((LB, GLA_VALUE_WIDTH, D), GLA_VALUE_WIDTH ** -0.5),
        'ssd_w_in': nrm((LC, D, SSM_IN_WIDTH), D ** -0.5),
        'ssd_conv_w': nrm((LC, SSM_CONV, SSM_CONV_WIDTH), SSM_CONV ** -0.5),
        'ssd_conv_b': nrm((LC, SSM_CONV_WIDTH), 0.02),
        'ssd_dt_bias': dt0 + jnp.log(-jnp.expm1(-dt0)),
        'ssd_a_log': jnp.log(unif((LC, SSM_HEADS), 1.0, 16.0)),
        'ssd_d': 1.0 + nrm((LC, SSM_HEADS), 0.1),
        'ssd_norm_g': 1.0 + nrm((LC, SSM_WIDTH), 0.02),
        'ssd_w_out': nrm((LC, SSM_WIDTH, D), SSM_WIDTH ** -0.5),
    }


def reference(x, c, ada_w, ada_b, norm_g, final_g,
              rwkv_mu, rwkv_w_in, rwkv_dec_w1, rwkv_dec_w2, rwkv_dec_w0, rwkv_iclr_w1, rwkv_iclr_w2,
              rwkv_iclr_w0, rwkv_k_k, rwkv_k_a, rwkv_r_k, rwkv_gn_w, rwkv_gn_b, rwkv_w_out,
              gla_w_in, gla_gate_w2, gla_gate_b, gla_head_g, gla_w_out,
              ssd_w_in, ssd_conv_w, ssd_conv_b, ssd_dt_bias, ssd_a_log, ssd_d, ssd_norm_g, ssd_w_out):
    c_act = jax.nn.silu(c)
    for i in range(DEPTH):
        mod = jnp.einsum('bd,de->be', c_act, ada_w[i]) + ada_b[i]
        shift, scale, gate = jnp.split(mod[:, None, :], 3, axis=-1)
        h = rms_normalize(x, norm_g[i], NORM_EPS) * (1 + scale) + shift
        kind, j = i % N_MIXERS, i // N_MIXERS
        if kind == 0:
            out = rwkv7_mixer(h, rwkv_mu[j], rwkv_w_in[j], rwkv_dec_w1[j], rwkv_dec_w2[j], rwkv_dec_w0[j],
                              rwkv_iclr_w1[j], rwkv_iclr_w2[j], rwkv_iclr_w0[j], rwkv_k_k[j], rwkv_k_a[j],
                              rwkv_r_k[j], rwkv_gn_w[j], rwkv_gn_b[j], rwkv_w_out[j])
        elif kind == 1:
            out = gla_mixer(h, gla_w_in[j], gla_gate_w2[j], gla_gate_b[j], gla_head_g[j], gla_w_out[j])
        else:
            out = ssd_mixer(h, ssd_w_in[j], ssd_conv_w[j], ssd_conv_b[j], ssd_dt_bias[j], ssd_a_log[j],
                            ssd_d[j], ssd_norm_g[j], ssd_w_out[j])
        x = x + (gate * out).astype(x.dtype)
    return rms_normalize(x, final_g, NORM_EPS)
```

```python
from contextlib import ExitStack
import math
import numpy as np
import concourse.bass as bass
import concourse.mybir as mybir
from concourse.bass_utils import run_bass_kernel_spmd

F32 = mybir.dt.float32
BF16 = mybir.dt.bfloat16
AF = mybir.ActivationFunctionType
ALU = mybir.AluOpType
AX = mybir.AxisListType


class V:
    __slots__ = ("ap", "tl")

    def __init__(self, ap, tl):
        self.ap = ap
        self.tl = tl

    def __getitem__(self, idx):
        return V(self.ap[idx], self.tl)

    def re(self, pat, **kw):
        return V(self.ap.rearrange(pat, **kw), self.tl)

    def bc(self, axes, shape):
        a = self.ap
        for ax in axes:
            a = a.unsqueeze(ax)
        return V(a.broadcast_to(list(shape)), self.tl)


class Tl:
    __slots__ = ("t", "lw", "rd", "name", "excl")

    def __init__(self, t, name="", excl=False):
        self.t = t
        self.lw = []
        self.rd = []
        self.name = name
        self.excl = excl

    def __getitem__(self, idx):
        return V(self.t[idx], self)

    def v(self):
        return V(self.t[:], self)


ENGS = ("pe", "act", "dve", "pool", "sp")
DMA_ENGS = ("sp", "pool", "act")
NDMA_SLOTS = 12
WRITE_KW = ("out", "accum_out", "ap")


def _compress(toks):
    best = {}
    for s, v, src in toks:
        k = id(s)
        if k not in best or best[k][1] < v:
            best[k] = (s, v, src)
    return list(best.values())


class Prog:
    def __init__(self, nc):
        self.nc = nc
        self.stacks = [ExitStack()]
        self.q = {e: [] for e in ENGS}
        self.cnt = {e: 0 for e in ENGS}
        self.sem = {e: self.stacks[0].enter_context(nc.semaphore("s_" + e)) for e in ENGS}
        self.seen = {e: {} for e in ENGS}
        self.dsem, self.dval, self.dnext = {}, {}, {}
        for e in DMA_ENGS:
            self.dsem[e] = [self.stacks[0].enter_context(nc.semaphore("d_%s%d" % (e, i))) for i in range(NDMA_SLOTS)]
            self.dval[e] = [0] * NDMA_SLOTS
            self.dnext[e] = 0
        self.n_inst = 0
        self.uid = 0

    def _nm(self, name):
        self.uid += 1
        return "%s_%d" % (name, self.uid)

    def sb(self, name, shape, dt=F32):
        t = self.stacks[-1].enter_context(self.nc.sbuf_tensor(self._nm(name), list(shape), dt))
        return Tl(t, name)

    def ps(self, name, shape, dt=F32):
        nbytes = int(np.prod(shape[1:])) * (4 if dt == F32 else 2)
        assert nbytes % 2048 == 0, "PSUM tiles must cover whole banks"
        t = self.stacks[-1].enter_context(self.nc.psum_tensor(self._nm(name), list(shape), dt))
        return Tl(t, name, excl=True)

    def dram(self, name, shape, dt=F32, kind="Internal"):
        t = self.nc.dram_tensor(name, list(shape), dt, kind=kind)
        return Tl(t.ap(), name)

    class _Scope:
        def __init__(self, p):
            self.p = p

        def __enter__(self):
            self.p.stacks.append(ExitStack())

        def __exit__(self, *a):
            self.p.barrier()
            self.p.stacks.pop().close()
            return False

    def scope(self):
        return Prog._Scope(self)

    def _deps(self, eng, reads, writes, acc_w=False):
        waits = {}

        def need(tok):
            sem, val, src = tok
            if src == "pe" and eng == "pe":
                return
            k = id(sem)
            if self.seen[eng].get(k, 0) >= val:
                return
            if k not in waits or waits[k][1] < val:
                waits[k] = (sem, val)

        for tl in reads:
            for tok in tl.lw:
                need(tok)
        for tl in writes:
            if not acc_w:
                for tok in tl.lw:
                    need(tok)
            for tok in tl.rd:
                need(tok)
        for k, (sem, val) in waits.items():
            self.seen[eng][k] = val
        return list(waits.values())

    def _commit(self, tok, reads, writes, acc_w=False):
        for tl in writes:
            if acc_w:
                tl.lw.append(tok)
                if len(tl.lw) > 48:
                    tl.lw = _compress(tl.lw)
            else:
                tl.lw = [tok]
            tl.rd = []
        for tl in reads:
            if tl not in writes:
                tl.rd.append(tok)
                if len(tl.rd) > 48:
                    tl.rd = _compress(tl.rd)

    def I(self, eng, fn, *, acc_w=False, **kw):
        reads, writes, args = [], [], {}
        for k, a in kw.items():
            if isinstance(a, V):
                args[k] = a.ap
                (writes if (k in WRITE_KW or a.tl.excl) else reads).append(a.tl)
            else:
                args[k] = a
        waits = self._deps(eng, reads, writes, acc_w)
        self.cnt[eng] += 1
        tok = (self.sem[eng], self.cnt[eng], eng)
        self._commit(tok, reads, writes, acc_w)
        self.q[eng].append((waits, fn, args, (self.sem[eng], 1)))
        self.n_inst += 1

    def dma(self, eng, out, in_, acc_w=False, **kw):
        reads, writes = [in_.tl], [out.tl]
        waits = self._deps(eng, reads, writes, acc_w)
        s = self.dnext[eng]
        self.dnext[eng] = (s + 1) % NDMA_SLOTS
        sem = self.dsem[eng][s]
        prev = self.dval[eng][s]
        if prev > 0 and self.seen[eng].get(id(sem), 0) < prev:
            waits.append((sem, prev))
            self.seen[eng][id(sem)] = prev
        self.dval[eng][s] = prev + 16
        tok = (sem, prev + 16, "dma")
        self._commit(tok, reads, writes, acc_w)
        args = dict(out=out.ap, in_=in_.ap)
        args.update(kw)
        self.q[eng].append((waits, "dma_start", args, (sem, 16)))
        self.n_inst += 1

    def barrier(self):
        for e in ENGS:
            waits = []
            for e2 in ENGS:
                if self.cnt[e2] > 0 and self.seen[e].get(id(self.sem[e2]), 0) < self.cnt[e2] and e2 != e:
                    waits.append((self.sem[e2], self.cnt[e2]))
                    self.seen[e][id(self.sem[e2])] = self.cnt[e2]
            for de in DMA_ENGS:
                for s in range(NDMA_SLOTS):
                    v = self.dval[de][s]
                    sem = self.dsem[de][s]
                    if v > 0 and self.seen[e].get(id(sem), 0) < v:
                        waits.append((sem, v))
                        self.seen[e][id(sem)] = v
            if waits:
                self.q[e].append((waits, None, None, None))

    def emit(self):
        nc = self.nc
        self.barrier()
        with nc.Block() as block:
            def run(engname):
                def f(e):
                    for waits, fn, args, inc in self.q[engname]:
                        for sem, val in waits:
                            e.wait_ge(sem, val)
                        if fn is not None:
                            getattr(e, fn)(**args).then_inc(inc[0], inc[1])
                return f
            block.tensor(run("pe"))
            block.scalar(run("act"))
            block.vector(run("dve"))
            block.gpsimd(run("pool"))
            block.sync(run("sp"))
        while self.stacks:
            self.stacks.pop().close()


class Cfg:
    def __init__(self, D=2048, S=2048, kinds=(0, 1, 2, 0), lora=96,
                 gla_heads=4, gla_rank=16, ssm_groups=8):
        self.D, self.S, self.kinds, self.lora = D, S, tuple(kinds), lora
        self.gla_heads, self.gla_rank, self.ssm_groups = gla_heads, gla_rank, ssm_groups
        self.DC = D // 128
        self.TT = min(512, S)
        self.NT = S // self.TT
        self.TA = min(256, S)
        self.L = len(kinds)
        self.nR = sum(1 for k in kinds if k == 0)
        self.nG = sum(1 for k in kinds if k == 1)
        self.nS = sum(1 for k in kinds if k == 2)
        self.TB = min(1024, S)
        self.CB = 4
        self.stop = 99


NEG_EXP_HALF = -math.exp(-0.5)
NORM_EPS = 1e-6
RWKV_GN_EPS = 64e-5


def make_consts(cfg):
    c = np.zeros((128, 8, 128), np.float32)
    c[:, 0, :] = np.eye(128)
    c[:, 1, :] = 1.0
    c[0:64, 2, 0:64] = 1.0
    c[64:128, 2, 64:128] = 1.0
    su = np.triu(np.ones((64, 64), np.float32), 1)
    iu = np.triu(np.ones((64, 64), np.float32), 0)
    c[0:64, 3, 0:64] = su
    c[64:128, 3, 0:64] = su
    c[0:64, 3, 64:128] = iu
    c[64:128, 3, 64:128] = iu
    c[0:64, 4, 0:64] = -su
    c[0:64, 4, 64:128] = -su.T
    c[0:64, 5, 0:64] = np.eye(64)
    c[64:128, 5, 0:64] = np.eye(64)
    c[:, 6, :] = np.triu(np.ones((128, 128), np.float32), 0)
    c[:, 7, :] = np.where(np.triu(np.ones((128, 128)), 0) > 0, 0.0, -30000.0)
    rmask = np.ones((128, cfg.S), np.float32)
    rmask[:, 0::64] = 0.0
    rmask128 = np.ones((128, cfg.S), np.float32)
    rmask128[:, 0::128] = 0.0
    return c.reshape(128, 8 * 128), rmask, rmask128


def build(cfg):
    nc = bass.Bass("TRN2", target_bir_lowering=False)
    p = Prog(nc)
    D, S, DC, TT, NT, L = cfg.D, cfg.S, cfg.DC, cfg.TT, cfg.NT, cfg.L
    EI = "ExternalInput"
    xT = p.dram("xT", [D, S], F32, EI)
    cT = p.dram("cT", [128, DC], F32, EI)
    ada_w = p.dram("ada_w", [L, D, 3 * D], F32, EI)
    ada_bT = p.dram("ada_bT", [128, L, 3 * DC], F32, EI)
    norm_gT = p.dram("norm_gT", [128, L, DC], F32, EI)
    final_gT = p.dram("final_gT", [128, DC], F32, EI)
    consts_d = p.dram("consts", [128, 8 * 128], F32, EI)
    rmask_d = p.dram("rmask", [128, S], F32, EI)
    rmask128_d = p.dram("rmask128", [128, S], F32, EI)
    outT = p.dram("outT", [D, S], F32, "ExternalOutput")
    xres = [p.dram("xres%d" % i, [D, S], F32) for i in range(3)]
    W = D
    HP = W // 128
    R = cfg.lora
    if cfg.nR:
        nR = cfg.nR
        rw_in = p.dram("rwkv_w_in", [nR, D, 4 * W], F32, EI)
        rw_out = p.dram("rwkv_w_out", [nR, W, D], F32, EI)
        rw_dw1 = p.dram("rwkv_dec_w1", [nR, D, R], F32, EI)
        rw_dw2 = p.dram("rwkv_dec_w2", [nR, R, W], F32, EI)
        rw_aw1 = p.dram("rwkv_iclr_w1", [nR, D, R], F32, EI)
        rw_aw2 = p.dram("rwkv_iclr_w2", [nR, R, W], F32, EI)
        rw_muT = p.dram("rwkv_muT", [128, nR, 6, DC], F32, EI)
        rw_vecT = p.dram("rwkv_vecT", [128, nR, 7, HP], F32, EI)
        projT = [[Tl(t.t[f * 128:(f + 1) * 128, :], "projT") for f in range(HP)]
                 for t in [p.dram("projT%d" % c, [W, S], F32) for c in range(4)]]

    GH = cfg.gla_heads
    KW, VW = D // 2, D
    DK, DV = KW // GH, VW // GH
    KC, VC = max(DK // 128, 1), DV // 128
    GR = cfg.gla_rank
    if cfg.nG:
        nG = cfg.nG
        assert DK % 128 == 0 and DV % 128 == 0 and DV <= 512
        gl_in = p.dram("gla_w_in", [nG, D, 2 * KW + 2 * VW + GR], F32, EI)
        gl_out = p.dram("gla_w_out", [nG, VW, D], F32, EI)
        gl_w2 = p.dram("gla_gate_w2", [nG, GR, KW], F32, EI)
        gl_nbT = p.dram("gla_nbT", [128, nG, KW // 128], F32, EI)
        gl_hgb = p.dram("gla_hgb", [128, nG, DV], F32, EI)
        gqk = [[Tl(t.t[f * 128:(f + 1) * 128, :], "gqk") for f in range(KW // 128)]
               for t in [p.dram("gqk%d" % c, [KW, S], F32) for c in range(2)]]
        gvg_t = [p.dram("gvg%d" % c, [S, VW], F32) for c in range(2)]
        gvg = [[Tl(t.t[:, hh * DV:(hh + 1) * DV], "gvg") for hh in range(GH)] for t in gvg_t]

    SW = 2 * D
    SH = SW // 64
    SG = SW // 512
    SN = 128
    CW = SW + 2 * SG * SN
    SIN = SW + CW + SH
    if cfg.nS:
        nS = cfg.nS
        sd_in = p.dram("ssd_w_in", [nS, D, SIN], F32, EI)
        sd_out = p.dram("ssd_w_out", [nS, SW, D], F32, EI)
        sd_cwT = p.dram("ssd_cwT", [128, nS, CW // 128, 4], F32, EI)
        sd_cbT = p.dram("ssd_cbT", [128, nS, CW // 128], F32, EI)
        sd_hv = p.dram("ssd_hv", [64, nS, 2], F32, EI)
        sd_dsb = p.dram("ssd_dsb", [128, nS, SH], F32, EI)
        sd_ngb = p.dram("ssd_ngb", [128, nS, SW], F32, EI)
        sxbc_t = p.dram("sxbc", [CW, S], F32)
        sxbc = [Tl(sxbc_t.t[f * 128:(f + 1) * 128, :], "sxbc") for f in range(CW // 128)]
        sz_t = p.dram("sz", [S, SW], F32)
        sz = [Tl(sz_t.t[:, g * 512:(g + 1) * 512], "sz") for g in range(SG)]

    cst = p.sb("cst", [128, 8, 128], F32)
    p.dma("sp", cst.v().re("p a b -> p (a b)"), consts_d.v())
    ident_bf = p.sb("ident_bf", [128, 128], BF16)
    p.I("dve", "tensor_copy", out=ident_bf.v(), in_=cst[:, 0, :])
    ones32 = cst[:, 1, :]
    bones32 = cst[:, 2, :]
    maskA = cst[:, 3, :]
    negSU = cst[0:64, 4, 0:64]
    negSL = cst[0:64, 4, 64:128]
    ident2 = cst[:, 5, 0:64]
    rmask = p.sb("rmask", [128, S], BF16)
    p.dma("pool", rmask.v(), rmask_d.v())
    rmask128 = p.sb("rmask128", [128, S], BF16)
    p.dma("pool", rmask128.v(), rmask128_d.v())
    iu128 = cst[:, 6, :]

    modT = p.sb("modT", [128, L, 3 * DC], F32)
    gsT = p.sb("gsT", [128, L, DC], F32)
    with p.scope():
        cact = p.sb("cact", [128, DC], F32)
        abT = p.sb("abT", [128, L, 3 * DC], F32)
        ngT = p.sb("ngT", [128, L, DC], F32)
        p.dma("sp", cact.v(), cT.v())
        p.dma("sp", abT.v(), ada_bT.v())
        p.dma("sp", ngT.v(), norm_gT.v())
        p.I("act", "activation", out=cact.v(), in_=cact.v(), func=AF.Silu)
        EG = 4 if (3 * DC) % 4 == 0 else 2
        cact_bf = p.sb("cact_bf", [128, DC], BF16)
        p.I("dve", "tensor_copy", out=cact_bf.v(), in_=cact.v())
        wst = [p.sb("adaw", [128, DC, EG * 128], BF16) for _ in range(3)]
        psm = p.ps("psmod", [128, 512], F32)
        gi = 0
        for l in range(L):
            wv = ada_w.v()[l].re("(dc p) e -> p dc e", p=128)
            for eg in range(3 * DC // EG):
                wt = wst[gi % 3]
                p.dma("pool", wt.v(), wv[:, :, eg * EG * 128:(eg + 1) * EG * 128])
                gi += 1
                for j in range(EG):
                    col = l * 3 * DC + eg * EG + j
                    for dc in range(DC):
                        p.I("pe", "matmul", out=psm[:, col:col + 1], lhsT=wt[:, dc, j * 128:(j + 1) * 128],
                            rhs=cact_bf[:, dc:dc + 1], start=(dc == 0), stop=(dc == DC - 1))
        p.I("dve", "tensor_tensor", out=modT.v().re("p l e -> p (l e)"), in0=psm[:, 0:L * 3 * DC],
            in1=abT.v().re("p l e -> p (l e)"), op=ALU.add)
        p.I("dve", "scalar_tensor_tensor", out=gsT.v(), in0=modT[:, :, DC:2 * DC], scalar=1.0, in1=ngT.v(),
            op0=ALU.add, op1=ALU.mult)

    def norm_phase(src, dst_tiles, g_of_dc, sh_of_dc, out_dram=None):
        TA = cfg.TA
        with p.scope():
            xt = [p.sb("xt", [128, DC, TA], F32) for _ in range(2)]
            sq = [p.sb("sq", [128, TA], F32) for _ in range(2)]
            rstd = [p.sb("rstd", [128, TA], F32) for _ in range(2)]
            tmp = [p.sb("ntmp", [128, TA], F32) for _ in range(4)]
            pss = [p.ps("psn", [128, 512], F32) for _ in range(2)]
            k = 0
            for ta in range(S // TA):
                x_ = xt[ta % 2]
                ts = slice(ta * TA, (ta + 1) * TA)
                p.dma("sp", x_.v(), src.re("(dc p) s -> p dc s", p=128)[:, :, ts])
                ps_ = pss[ta % 2]
                for dc in range(DC):
                    s_ = sq[dc % 2]
                    if dc % 2 == 0:
                        p.I("act", "activation", out=s_.v(), in_=x_[:, dc, :], func=AF.Square)
                    else:
                        p.I("dve", "tensor_tensor", out=s_.v(), in0=x_[:, dc, :], in1=x_[:, dc, :], op=ALU.mult)
                    p.I("pe", "matmul", out=ps_[:, 0:TA], lhsT=ones32, rhs=s_.v(), start=(dc == 0), stop=(dc == DC - 1))
                r_ = rstd[ta % 2]
                p.I("act", "activation", out=r_.v(), in_=ps_[:, 0:TA], func=AF.Sqrt, bias=NORM_EPS, scale=1.0 / D)
                p.I("dve", "reciprocal", out=r_.v(), in_=r_.v())
                for dc in range(DC):
                    t_ = tmp[k % 4]
                    k += 1
                    p.I("dve", "scalar_tensor_tensor", out=t_.v(), in0=x_[:, dc, :],
                        scalar=g_of_dc(dc), in1=r_.v(), op0=ALU.mult, op1=ALU.mult)
                    if out_dram is None:
                        p.I("act", "activation", out=dst_tiles[dc][:, ts], in_=t_.v(), func=AF.Identity,
                            bias=sh_of_dc(dc), scale=1.0)
                    else:
                        p.dma("sp", out_dram[dc * 128:(dc + 1) * 128, ts], t_.v(), acc_w=True)

    wring = {}

    def out_proj(yg_tiles, w_dram, nci, gate_of_ft, src, dst):
        with p.scope():
            wts = [p.sb("wo", [128, nci, 512], BF16) for _ in range(2)]
            pso = [p.ps("pso", [128, 512], F32) for _ in range(4)]
            xin = [p.sb("xin", [128, TT], F32) for _ in range(4)]
            wv = w_dram.re("(ci p) f -> p ci f", p=128)
            k = 0
            G = 4 if (D // 128) % 4 == 0 else 2
            for fg in range(D // (128 * G)):
                wt = wts[fg % 2]
                p.dma("pool", wt[:, :, 0:G * 128], wv[:, :, fg * G * 128:(fg + 1) * G * 128])
                for j in range(G):
                    ft = fg * G + j
                    for tt in range(NT):
                        ts = slice(tt * TT, (tt + 1) * TT)
                        ps_ = pso[k % 4]
                        x_ = xin[k % 4]
                        k += 1
                        p.dma("sp", x_.v(), src[ft * 128:(ft + 1) * 128, ts])
                        for ci in range(nci):
                            p.I("pe", "matmul", out=ps_[:, 0:TT], lhsT=wt[:, ci, j * 128:(j + 1) * 128],
                                rhs=yg_tiles[ci][:, ts], start=(ci == 0), stop=(ci == nci - 1))
                        p.I("dve", "scalar_tensor_tensor", out=x_.v(), in0=ps_[:, 0:TT], scalar=gate_of_ft(ft),
                            in1=x_.v(), op0=ALU.mult, op1=ALU.add)
                        p.dma("sp", dst[ft * 128:(ft + 1) * 128, ts], x_.v(), acc_w=True)

    def proj_fm(xs_tiles, wv, f0, nft, sink, wts, pss, kdim=DC):
        k = 0
        G = 4 if nft % 4 == 0 else (2 if nft % 2 == 0 else 1)
        gi = 0
        for fg in range(nft // G):
            wt = wts[gi % 2]
            gi += 1
            p.dma("pool", wt[:, :, 0:G * 128], wv[:, :, f0 + fg * G * 128:f0 + (fg + 1) * G * 128])
            for j in range(G):
                ft = fg * G + j
                for tt in range(NT):
                    ts = slice(tt * TT, (tt + 1) * TT)
                    ps_ = pss[k % len(pss)]
                    k += 1
                    for dc in range(kdim):
                        p.I("pe", "matmul", out=ps_[:, 0:TT], lhsT=wt[:, dc, j * 128:(j + 1) * 128],
                            rhs=xs_tiles[dc][:, ts], start=(dc == 0), stop=(dc == kdim - 1))
                    sink(ft, tt, ps_)


    def proj_tm(hT, wv, f0, ngroups, gw, sink, wts, pss):
        k = 0
        for gi in range(ngroups):
            wt = wts[gi % 2]
            p.dma("pool", wt[:, :, 0:gw], wv[:, :, f0 + gi * gw:f0 + (gi + 1) * gw])
            for tk in range(S // 128):
                ps_ = pss[k % len(pss)]
                k += 1
                for dc in range(DC):
                    p.I("pe", "matmul", out=ps_[:, 0:gw], lhsT=hT[dc][:, tk * 128:(tk + 1) * 128], rhs=wt[:, dc, 0:gw],
                        start=(dc == 0), stop=(dc == DC - 1))
                sink(gi, tk, ps_)

    def gla_layer(l, j, src, dst):
        NCH = S // 128
        with p.scope():
            yg = [p.sb("ygg", [128, S], BF16) for _ in range(VW // 128)]
            lowT = p.sb("lowT", [GR, S], BF16)
            gw2 = p.sb("gw2", [GR, KW], BF16)
            nb = p.sb("gnb", [128, KW // 128], F32)
            hgb = p.sb("hgb", [128, DV], F32)
            p.dma("pool", gw2.v(), gl_w2.v()[j])
            p.dma("sp", nb.v(), gl_nbT.v()[:, j])
            p.dma("sp", hgb.v(), gl_hgb.v()[:, j])
            p.I("dve", "tensor_scalar", out=nb.v(), in0=nb.v(), scalar1=-1.0, scalar2=None, op0=ALU.mult)
            wv = gl_in.v()[j].re("(dc p) f -> p dc f", p=128)
            with p.scope():
                hT = [p.sb("hT", [128, S], BF16) for _ in range(DC)]
                norm_phase(src, hT, lambda dc: gsT[:, l, dc:dc + 1], lambda dc: modT[:, l, dc:dc + 1])
                wts = [p.sb("wi", [128, DC, 512], BF16) for _ in range(2)]
                wl = p.sb("wl", [128, DC, GR], BF16)
                pss = [p.ps("psp", [128, 512], F32) for _ in range(4)]
                stg = [p.sb("stg", [128, 512], F32) for _ in range(4)]
                sk = [0]

                def evac(ps_ap, dst_ap, width):
                    s_ = stg[sk[0] % 4]
                    e = "act" if sk[0] % 2 == 0 else "dve"
                    sk[0] += 1
                    if e == "act":
                        p.I("act", "copy", out=s_[:, 0:width], in_=ps_ap)
                    else:
                        p.I("dve", "tensor_copy", out=s_[:, 0:width], in_=ps_ap)
                    p.dma("sp", dst_ap, s_[:, 0:width], acc_w=True)

                for c in range(2):
                    proj_fm(hT, wv, c * KW, KW // 128,
                            lambda ft, tt, ps_, c=c: evac(ps_[:, 0:TT], gqk[c][ft][:, tt * TT:(tt + 1) * TT], TT), wts, pss)
                for c in range(2):
                    proj_tm(hT, wv, 2 * KW + c * VW, GH, DV,
                            lambda gi, tk, ps_, c=c: evac(ps_[:, 0:DV], gvg[c][gi][tk * 128:(tk + 1) * 128, :], DV), wts, pss)
                p.dma("pool", wl.v(), wv[:, :, 2 * KW + 2 * VW:2 * KW + 2 * VW + GR])
                for tt in range(NT):
                    ts = slice(tt * TT, (tt + 1) * TT)
                    ps_ = pss[tt % 4]
                    for dc in range(DC):
                        p.I("pe", "matmul", out=ps_[0:GR, 0:TT], lhsT=wl[:, dc, :], rhs=hT[dc][:, ts],
                            start=(dc == 0), stop=(dc == DC - 1))
                    p.I("act", "copy", out=lowT[:, ts], in_=ps_[0:GR, 0:TT])
            if cfg.stop <= 2:
                return False
            with p.scope():
                ldq = [p.sb("ldq", [128, S], F32) for _ in range(KC)]
                ldk = [p.sb("ldk", [128, S], F32) for _ in range(KC)]
                QT = [p.sb("QT", [128, S], BF16) for _ in range(KC)]
                KT = [p.sb("KT", [128, S], BF16) for _ in range(KC)]
                eb = [p.sb("eb", [128, S], F32) for _ in range(KC)]
                t1 = p.sb("gt1", [128, S], F32)
                t2 = p.sb("gt2", [128, S], F32)
                S32 = [p.sb("S32", [128, DV], F32) for _ in range(KC)]
                Sb = [p.sb("Sb", [128, DV], BF16) for _ in range(KC)]
                v32 = [p.sb("v32", [128, DV], F32) for _ in range(2)]
                g32 = [p.sb("g32", [128, DV], F32) for _ in range(2)]
                Vb = [p.sb("Vb", [128, DV], BF16) for _ in range(2)]
                SG = [p.sb("SG", [128, DV], F32) for _ in range(2)]
                KTM = [p.sb("KTM", [128, KC * 128], BF16) for _ in range(2)]
                ST = [p.sb("ST", [128, 128], BF16) for _ in range(2)]
                junk = p.sb("junk", [128, DV], F32)
                ssq = [p.sb("ssq", [128, 1], F32) for _ in range(2)]
                y32 = [p.sb("y32", [128, DV], F32) for _ in range(2)]
                yb = [p.sb("yb", [128, DV], BF16) for _ in range(2)]
                psP = p.ps("gpsP", [128, 512], F32)
                psTr = p.ps("gpsTr", [128, 8, 128], BF16)
                psS = p.ps("gpsS", [128, 512], F32)
                psO = [p.ps("gpsO", [128, 512], F32) for _ in range(2)]
                psSt = [p.ps("gpsSt", [128, 512], F32) for _ in range(2)]
                psTr2 = p.ps("gpsTr2", [128, 8, 128], BF16)
                for hh in range(GH):
                    for kc in range(KC):
                        ft = hh * KC + kc
                        p.dma("sp", ldq[kc].v(), gqk[0][ft].v())
                        p.dma("sp", ldk[kc].v(), gqk[1][ft].v())
                        for tt in range(NT):
                            ts = slice(tt * TT, (tt + 1) * TT)
                            p.I("pe", "matmul", out=psP[:, 0:TT], lhsT=gw2[:, ft * 128:(ft + 1) * 128], rhs=lowT[:, ts], start=True, stop=True)
                            p.I("act", "activation", out=t1[:, ts], in_=psP[:, 0:TT], func=AF.Exp, bias=nb[:, ft:ft + 1], scale=-1.0)
                        p.I("act", "activation", out=t1.v(), in_=t1.v(), func=AF.Ln, bias=1.0, scale=1.0)
                        p.I("dve", "tensor_scalar", out=t1.v(), in0=t1.v(), scalar1=-1.0 / 16.0, scalar2=None, op0=ALU.mult)
                        p.I("dve", "tensor_tensor_scan", out=t2.v(), data0=rmask128.v(), data1=t1.v(), initial=0.0,
                            op0=ALU.mult, op1=ALU.add)
                        p.I("act", "activation", out=eb[kc].v(), in_=t2.v(), func=AF.Exp)
                        p.I("act", "activation", out=t1.v(), in_=t2.v(), func=AF.Exp, scale=-1.0)
                        p.I("dve", "scalar_tensor_tensor", out=QT[kc].v(), in0=ldq[kc].v(), scalar=float(DK) ** -0.5, in1=eb[kc].v(),
                            op0=ALU.mult, op1=ALU.mult)
                        p.I("dve", "tensor_tensor", out=KT[kc].v(), in0=ldk[kc].v(), in1=t1.v(), op=ALU.mult)
                        p.I("dve", "memset", ap=S32[kc].v(), constant=0.0)
                        p.I("dve", "memset", ap=Sb[kc].v(), constant=0.0)
                    for n in range(NCH):
                        ns = slice(n * 128, (n + 1) * 128)
                        b2 = n % 2
                        p.dma("sp", v32[b2].v(), gvg[0][hh][ns, :])
                        p.dma("sp", g32[b2].v(), gvg[1][hh][ns, :])
                        p.I("act", "copy", out=Vb[b2].v(), in_=v32[b2].v())
                        p.I("act", "activation", out=SG[b2].v(), in_=g32[b2].v(), func=AF.Silu)
                        for kc in range(KC):
                            p.I("pe", "transpose", out=psTr[:, kc, :], in_=KT[kc][:, ns], identity=ident_bf.v())
                        p.I("dve", "tensor_copy", out=KTM[b2].v().re("p (k x) -> p k x", x=128), in_=psTr[:, 0:KC, :])
                        for kc in range(KC):
                            p.I("pe", "matmul", out=psS[:, 0:128], lhsT=KT[kc][:, ns], rhs=QT[kc][:, ns], start=(kc == 0), stop=(kc == KC - 1))
                        p.I("dve", "tensor_tensor", out=ST[b2].v(), in0=psS[:, 0:128], in1=iu128, op=ALU.mult)
                        po = psO[b2]
                        p.I("pe", "matmul", out=po[:, 0:DV], lhsT=ST[b2].v(), rhs=Vb[b2].v(), start=True, stop=False)
                        for kc in range(KC):
                            p.I("pe", "matmul", out=po[:, 0:DV], lhsT=QT[kc][:, ns], rhs=Sb[kc].v(), start=False, stop=(kc == KC - 1))
                        for kc in range(KC):
                            pst = psSt[kc % 2]
                            p.I("pe", "matmul", out=pst[:, 0:DV], lhsT=KTM[b2][:, kc * 128:(kc + 1) * 128], rhs=Vb[b2].v(), start=True, stop=True)
                            dcol = eb[kc][:, n * 128 + 127:n * 128 + 128]
                            p.I("dve", "tensor_scalar", out=S32[kc].v(), in0=S32[kc].v(), scalar1=dcol, scalar2=None, op0=ALU.mult)
                            p.I("dve", "scalar_tensor_tensor", out=S32[kc].v(), in0=pst[:, 0:DV], scalar=dcol, in1=S32[kc].v(),
                                op0=ALU.mult, op1=ALU.add)
                            p.I("act", "copy", out=Sb[kc].v(), in_=S32[kc].v())
                        p.I("act", "activation", out=junk.v(), in_=po[:, 0:DV], func=AF.Square, accum_out=ssq[b2].v())
                        p.I("act", "activation", out=ssq[b2].v(), in_=ssq[b2].v(), func=AF.Sqrt, bias=NORM_EPS, scale=1.0 / DV)
                        p.I("dve", "reciprocal", out=ssq[b2].v(), in_=ssq[b2].v())
                        p.I("dve", "scalar_tensor_tensor", out=y32[b2].v(), in0=po[:, 0:DV], scalar=ssq[b2].v(), in1=hgb.v(),
                            op0=ALU.mult, op1=ALU.mult)
                        p.I("dve", "tensor_tensor", out=yb[b2].v(), in0=y32[b2].v(), in1=SG[b2].v(), op=ALU.mult)
                        for vc in range(VC):
                            p.I("pe", "transpose", out=psTr2[:, vc, :], in_=yb[b2][:, vc * 128:(vc + 1) * 128], identity=ident_bf.v())
                        for vc in range(VC):
                            p.I("act" if vc % 2 == 0 else "dve", "copy" if vc % 2 == 0 else "tensor_copy",
                                out=yg[hh * VC + vc][:, ns], in_=psTr2[:, vc, :])
            if cfg.stop <= 9:
                return False
            out_proj(yg, gl_out.v()[j], VW // 128, lambda ft: modT[:, l, 2 * DC + ft:2 * DC + ft + 1], src, dst)
            return True


    def ssd_layer(l, j, src, nextbuf):
        NCH = S // 128
        srcv = [src]
        HN = SH
        maskb = cst[:, 7, :]
        ident32 = cst[:, 0, :]
        with p.scope():
            dtT = p.sb("dtT", [128, S], F32)
            acT = p.sb("acT", [128, S], F32)
            nacT = p.sb("nacT", [128, S], F32)
            hv = p.sb("hv", [64, 2], F32)
            dsb = p.sb("dsb", [128, SH], F32)
            cw = p.sb("cw", [128, CW // 128, 4], F32)
            cbv = p.sb("cbv", [128, CW // 128], F32)
            wtm = p.sb("wtm", [128, NCH, 128], F32)
            eatm = p.sb("eatm", [128, NCH, 64], F32)
            decbc = p.sb("decbc", [128, NCH, 64], F32)
            p.dma("sp", hv.v(), sd_hv.v()[:, j])
            p.dma("sp", dsb.v(), sd_dsb.v()[:, j])
            p.dma("sp", cw.v(), sd_cwT.v()[:, j])
            p.dma("sp", cbv.v(), sd_cbT.v()[:, j])
            wv = sd_in.v()[j].re("(dc p) f -> p dc f", p=128)
            with p.scope():
                hT = [p.sb("hT", [128, S], BF16) for _ in range(DC)]
                norm_phase(src, hT, lambda dc: gsT[:, l, dc:dc + 1], lambda dc: modT[:, l, dc:dc + 1])
                wts = [p.sb("wi", [128, DC, 512], BF16) for _ in range(2)]
                wdt = p.sb("wdt", [128, DC, SH], BF16)
                pss = [p.ps("psp", [128, 512], F32) for _ in range(4)]
                stg = [p.sb("stg", [128, 512], F32) for _ in range(4)]
                xst = [p.sb("xst", [128, S + 3], F32) for _ in range(2)]
                acc = [p.sb("cacc", [128, S], F32) for _ in range(2)]
                sk = [0]

                def zsink(gi, tk, ps_):
                    s_ = stg[sk[0] % 4]
                    e = "act" if sk[0] % 2 == 0 else "dve"
                    sk[0] += 1
                    if e == "act":
                        p.I("act", "copy", out=s_.v(), in_=ps_.v())
                    else:
                        p.I("dve", "tensor_copy", out=s_.v(), in_=ps_.v())
                    p.dma("sp", sz[gi][tk * 128:(tk + 1) * 128, :], s_.v(), acc_w=True)

                proj_tm(hT, wv, 0, SG, 512, zsink, wts, pss)
                for b in range(2):
                    p.I("dve", "memset", ap=xst[b][:, 0:3], constant=0.0)

                def csink(ft, tt, ps_):
                    x_ = xst[ft % 2]
                    e = "act" if (ft + tt) % 2 == 0 else "dve"
                    if e == "act":
                        p.I("act", "copy", out=x_[:, 3 + tt * TT:3 + (tt + 1) * TT], in_=ps_[:, 0:TT])
                    else:
                        p.I("dve", "tensor_copy", out=x_[:, 3 + tt * TT:3 + (tt + 1) * TT], in_=ps_[:, 0:TT])
                    if tt == NT - 1:
                        a_ = acc[ft % 2]
                        p.I("act", "mul", out=a_.v(), in_=x_[:, 3:S + 3], mul=cw[:, ft, 3:4])
                        for kk_ in range(3):
                            p.I("dve", "scalar_tensor_tensor", out=a_.v(), in0=x_[:, kk_:S + kk_], scalar=cw[:, ft, kk_:kk_ + 1],
                                in1=a_.v(), op0=ALU.mult, op1=ALU.add)
                        p.I("act", "activation", out=a_.v(), in_=a_.v(), func=AF.Silu, bias=cbv[:, ft:ft + 1], scale=1.0)
                        p.dma("sp", sxbc[ft].v(), a_.v())

                proj_fm(hT, wv, SW, CW // 128, csink, wts, pss)
                p.dma("pool", wdt.v(), wv[:, :, SW + CW:SW + CW + SH])
                p.I("dve", "memset", ap=dtT.v(), constant=0.0)
                p.I("dve", "memset", ap=acT.v(), constant=0.0)
                for tt in range(NT):
                    ts = slice(tt * TT, (tt + 1) * TT)
                    ps_ = pss[tt % 4]
                    for dc in range(DC):
                        p.I("pe", "matmul", out=ps_[0:HN, 0:TT], lhsT=wdt[:, dc, :], rhs=hT[dc][:, ts], start=(dc == 0), stop=(dc == DC - 1))
                    p.I("act", "activation", out=dtT[0:HN, ts], in_=ps_[0:HN, 0:TT], func=AF.Exp, bias=hv[0:HN, 0:1], scale=1.0)
                p.I("act", "activation", out=dtT[0:HN, :], in_=dtT[0:HN, :], func=AF.Ln, bias=1.0, scale=1.0)
            if cfg.stop <= 2:
                return False
            with p.scope():
                eaT = p.sb("eaT", [128, S], F32)
                na = p.sb("na", [64, 1], F32)
                t1 = p.sb("st1", [128, S], F32)
                Dg = p.sb("Dg", [64, 64], F32)
                psq = [p.ps("spsq", [128, 512], F32) for _ in range(2)]
                p.I("act", "activation", out=na[0:HN, :], in_=hv[0:HN, 1:2], func=AF.Exp)
                p.I("dve", "tensor_scalar", out=na[0:HN, :], in0=na[0:HN, :], scalar1=-1.0, scalar2=None, op0=ALU.mult)
                p.I("dve", "tensor_scalar", out=t1[0:HN, :], in0=dtT[0:HN, :], scalar1=na[0:HN, 0:1], scalar2=None, op0=ALU.mult)
                p.I("dve", "tensor_tensor_scan", out=acT[0:HN, :], data0=rmask128[0:HN, :], data1=t1[0:HN, :], initial=0.0,
                    op0=ALU.mult, op1=ALU.add)
                p.I("dve", "memset", ap=nacT.v(), constant=0.0)
                p.I("dve", "memset", ap=eaT.v(), constant=0.0)
                p.I("dve", "tensor_scalar", out=nacT[0:HN, :], in0=acT[0:HN, :], scalar1=-1.0, scalar2=None, op0=ALU.mult)
                p.I("act", "activation", out=eaT[0:HN, :], in_=acT[0:HN, :], func=AF.Exp)
                for n in range(NCH):
                    ns = slice(n * 128, (n + 1) * 128)
                    last = acT[0:HN, n * 128 + 127:n * 128 + 128]
                    p.I("act", "activation", out=t1[0:HN, ns], in_=acT[0:HN, ns], func=AF.Exp, bias=last, scale=-1.0)
                    p.I("dve", "tensor_tensor", out=dtT[64:64 + HN, ns], in0=t1[0:HN, ns], in1=dtT[0:HN, ns], op=ALU.mult)
                    ps_ = psq[n % 2]
                    p.I("pe", "transpose", out=ps_[:, 0:128], in_=dtT[:, ns], identity=ident32)
                    p.I("pe", "transpose", out=ps_[:, 128:256], in_=eaT[:, ns], identity=ident32)
                    p.I("dve", "tensor_scalar", out=Dg[0:HN, 0:HN], in0=ident32[0:HN, 0:HN], scalar1=last, scalar2=None, op0=ALU.mult)
                    p.I("pe", "matmul", out=ps_[:, 256:256 + HN], lhsT=ones32[0:HN, :], rhs=Dg[0:HN, 0:HN], start=True, stop=True)
                    p.I("act", "copy", out=wtm[:, n, :], in_=ps_[:, 0:128])
                    p.I("dve", "tensor_copy", out=eatm[:, n, :], in_=ps_[:, 128:192])
                    p.I("act", "activation", out=decbc[:, n, 0:HN], in_=ps_[:, 256:256 + HN], func=AF.Exp)
            if cfg.stop <= 3:
                return False
            nhalf = 2 if SG >= 2 else 1
            GPH = SG // nhalf
            yg = [p.sb("ygs", [128, S], BF16) for _ in range(GPH * 4)]
            for half in range(nhalf):
              with p.scope():
                xg = [p.sb("xg", [128, 4, 128], F32) for _ in range(2)]
                bg = p.sb("bg", [128, S], F32)
                BT = p.sb("BT", [128, S], BF16)
                CT = p.sb("CT", [128, S], BF16)
                ngb = p.sb("ngb", [128, 512], F32)
                prev32 = p.sb("prev32", [128, 512], F32)
                prevb = p.sb("prevb", [128, 512], BF16)
                xtm = [p.sb("xtm", [128, 512], F32) for _ in range(2)]
                xc = [p.sb("xc", [128, 512], BF16) for _ in range(2)]
                xcd = [p.sb("xcd", [128, 512], BF16) for _ in range(2)]
                Btm = [p.sb("Btm", [128, 128], BF16) for _ in range(2)]
                cbT = [p.sb("cbT", [128, 128], BF16) for _ in range(2)]
                eM = [p.sb("eM", [128, 4, 128], F32) for _ in range(2)]
                Mm = [p.sb("Mm", [128, 4, 128], BF16) for _ in range(2)]
                z32 = [p.sb("z32", [128, 512], F32) for _ in range(2)]
                ty = [p.sb("ty", [128, 512], F32) for _ in range(2)]
                tu = [p.sb("tu", [128, 512], F32) for _ in range(2)]
                junk = p.sb("sjunk", [128, 512], F32)
                ssq = [p.sb("sssq", [128, 1], F32) for _ in range(2)]
                ybf = [p.sb("ybf", [128, 512], BF16) for _ in range(2)]
                psX = p.ps("spsX", [128, 512], F32)
                psB = p.ps("spsB", [128, 8, 128], BF16)
                psC = p.ps("spsC", [128, 512], F32)
                psM = [p.ps("spsM", [128, 4, 128], F32) for _ in range(2)]
                psY = p.ps("spsY", [128, 512], F32)
                psYo = p.ps("spsYo", [128, 512], F32)
                psSt = p.ps("spsSt", [128, 512], F32)
                for g in range(half * GPH, (half + 1) * GPH):
                    p.dma("sp", bg.v(), sxbc[SW // 128 + g].v())
                    p.I("act", "copy", out=BT.v(), in_=bg.v())
                    p.dma("sp", bg.v(), sxbc[SW // 128 + SG + g].v())
                    p.I("dve", "tensor_copy", out=CT.v(), in_=bg.v())
                    p.dma("sp", ngb.v(), sd_ngb.v()[:, j, g * 512:(g + 1) * 512])
                    p.I("dve", "memset", ap=prev32.v(), constant=0.0)
                    p.I("dve", "memset", ap=prevb.v(), constant=0.0)
                    for n in range(NCH):
                        ns = slice(n * 128, (n + 1) * 128)
                        b2 = n % 2
                        hs8 = slice(g * 8, (g + 1) * 8)
                        p.dma("sp", z32[b2].v(), sz[g][ns, :])
                        for i4 in range(4):
                            p.dma("sp", xg[b2][:, i4, :], sxbc[g * 4 + i4][:, ns])
                        for i4 in range(4):
                            p.I("pe", "transpose", out=psX[:, i4 * 128:(i4 + 1) * 128], in_=xg[b2][:, i4, :], identity=ident32)
                        p.I("act", "copy", out=xtm[b2].v(), in_=psX.v())
                        x3 = xtm[b2].v().re("p (h x) -> p h x", x=64)
                        p.I("dve", "tensor_tensor", out=xc[b2].v().re("p (h x) -> p h x", x=64), in0=x3,
                            in1=wtm[:, n, g * 8:(g + 1) * 8].bc([2], [128, 8, 64]), op=ALU.mult)
                        p.I("dve", "tensor_tensor", out=xcd[b2].v().re("p (h x) -> p h x", x=64), in0=x3,
                            in1=wtm[:, n, 64 + g * 8:64 + (g + 1) * 8].bc([2], [128, 8, 64]), op=ALU.mult)
                        p.I("pe", "transpose", out=psB[:, 0, :], in_=BT[:, ns], identity=ident_bf.v())
                        p.I("dve", "tensor_copy", out=Btm[b2].v(), in_=psB[:, 0, :])
                        p.I("pe", "matmul", out=psC[:, 0:128], lhsT=BT[:, ns], rhs=CT[:, ns], start=True, stop=True)
                        p.I("act", "copy", out=cbT[b2].v(), in_=psC[:, 0:128])
                        for hq in range(2):
                            pm = psM[hq]
                            for h4 in range(4):
                                h = g * 8 + hq * 4 + h4
                                sel = ident32[0:HN, h:h + 1].bc([], [HN, 128])
                                p.I("pe", "matmul", out=pm[:, h4, :], lhsT=sel, rhs=acT[0:HN, ns], start=True, stop=False)
                                p.I("pe", "matmul", out=pm[:, h4, :], lhsT=nacT[0:HN, ns], rhs=sel, start=False, stop=False)
                                p.I("pe", "matmul", out=pm[:, h4, :], lhsT=ident32, rhs=maskb, start=False, stop=True)
                            p.I("act", "activation", out=eM[hq].v(), in_=pm.v(), func=AF.Exp)
                            p.I("dve", "tensor_tensor", out=Mm[hq].v(), in0=eM[hq].v(),
                                in1=cbT[b2].v().bc([1], [128, 4, 128]), op=ALU.mult)
                        for hq in range(2):
                            for h4 in range(4):
                                hl = hq * 4 + h4
                                p.I("pe", "matmul", out=psY[:, hl * 64:(hl + 1) * 64], lhsT=Mm[hq][:, h4, :],
                                    rhs=xc[b2][:, hl * 64:(hl + 1) * 64], start=True, stop=True)
                        p.I("pe", "matmul", out=psYo.v(), lhsT=CT[:, ns], rhs=prevb.v(), start=True, stop=True)
                        p.I("pe", "matmul", out=psSt.v(), lhsT=Btm[b2].v(), rhs=xcd[b2].v(), start=True, stop=True)
                        t_ = ty[b2]
                        u_ = tu[b2]
                        t3 = t_.v().re("p (h x) -> p h x", x=64)
                        u3 = u_.v().re("p (h x) -> p h x", x=64)
                        p.I("dve", "tensor_tensor", out=t3, in0=psYo.v().re("p (h x) -> p h x", x=64),
                            in1=eatm[:, n, hs8].bc([2], [128, 8, 64]), op=ALU.mult)
                        p.I("dve", "tensor_tensor", out=t_.v(), in0=psY.v(), in1=t_.v(), op=ALU.add)
                        p.I("dve", "tensor_tensor", out=u3, in0=x3, in1=dsb[:, hs8].bc([2], [128, 8, 64]), op=ALU.mult)
                        p.I("dve", "tensor_tensor", out=t_.v(), in0=t_.v(), in1=u_.v(), op=ALU.add)
                        p32 = prev32.v().re("p (h x) -> p h x", x=64)
                        p.I("dve", "tensor_tensor", out=p32, in0=p32, in1=decbc[:, n, hs8].bc([2], [128, 8, 64]), op=ALU.mult)
                        p.I("dve", "tensor_tensor", out=prev32.v(), in0=psSt.v(), in1=prev32.v(), op=ALU.add)
                        p.I("act", "copy", out=prevb.v(), in_=prev32.v())
                        p.I("act", "activation", out=u_.v(), in_=z32[b2].v(), func=AF.Silu)
                        p.I("dve", "tensor_tensor", out=t_.v(), in0=t_.v(), in1=u_.v(), op=ALU.mult)
                        p.I("act", "activation", out=junk.v(), in_=t_.v(), func=AF.Square, accum_out=ssq[b2].v())
                        p.I("act", "activation", out=ssq[b2].v(), in_=ssq[b2].v(), func=AF.Sqrt, bias=1e-5, scale=1.0 / 512)
                        p.I("dve", "reciprocal", out=ssq[b2].v(), in_=ssq[b2].v())
                        p.I("dve", "scalar_tensor_tensor", out=ybf[b2].v(), in0=t_.v(), scalar=ssq[b2].v(), in1=ngb.v(),
                            op0=ALU.mult, op1=ALU.mult)
                        for i4 in range(4):
                            p.I("pe", "transpose", out=psB[:, 4 + i4, :], in_=ybf[b2][:, i4 * 128:(i4 + 1) * 128], identity=ident_bf.v())
                        for i4 in range(4):
                            p.I("act" if i4 % 2 == 0 else "dve", "copy" if i4 % 2 == 0 else "tensor_copy",
                                out=yg[(g - half * GPH) * 4 + i4][:, ns], in_=psB[:, 4 + i4, :])
              if cfg.stop <= 9:
                  return False
              nci = GPH * 4
              dsth = nextbuf()
              out_proj(yg, sd_out.v()[j][half * nci * 128:(half + 1) * nci * 128, :], nci,
                       lambda ft: modT[:, l, 2 * DC + ft:2 * DC + ft + 1], srcv[0], dsth.v())
              srcv[0] = dsth.v()
            return srcv[0]

    def rwkv_layer(l, j, src, dst):
        CB, TB = cfg.CB, cfg.TB
        NCHB = TB // 64
        with p.scope():
            yg = [p.sb("yg", [128, S], BF16) for _ in range(HP)]
            xs = yg
            lw1 = p.sb("lw1", [R, S], BF16)
            la1 = p.sb("la1", [R, S], BF16)
            vec = p.sb("rvec", [128, 7, HP], F32)
            omka = p.sb("omka", [128, HP], F32)
            p.dma("sp", vec.v(), rw_vecT.v()[:, j])
            p.I("dve", "tensor_scalar", out=omka.v(), in0=vec[:, 3, :], scalar1=-1.0, scalar2=1.0,
                op0=ALU.mult, op1=ALU.add)
            with p.scope():
                hT = [p.sb("hT", [128, S], BF16) for _ in range(DC)]
                mu = p.sb("mu", [128, 6, DC], F32)
                omm = p.sb("omm", [128, 6, DC], F32)
                p.dma("sp", mu.v(), rw_muT.v()[:, j])
                p.I("dve", "tensor_scalar", out=omm.v(), in0=mu.v(), scalar1=-1.0, scalar2=1.0,
                    op0=ALU.mult, op1=ALU.add)
                norm_phase(src, hT, lambda dc: gsT[:, l, dc:dc + 1], lambda dc: modT[:, l, dc:dc + 1])
                if cfg.stop <= 1:
                    return False
                wts = [p.sb("wi", [128, DC, 512], BF16) for _ in range(2)]
                w1t = p.sb("w1t", [128, DC, R], BF16)
                pss = [p.ps("psp", [128, 512], F32) for _ in range(4)]
                stg = [p.sb("stg", [128, TT], F32) for _ in range(4)]
                wv = rw_in.v()[j].re("(dc p) f -> p dc f", p=128)
                sk = [0]

                def mix(c):
                    for dc in range(DC):
                        p.I("dve", "memset", ap=xs[dc][:, 0:1], constant=0.0)
                        p.I("act", "mul", out=xs[dc][:, 1:S], in_=hT[dc][:, 0:S - 1], mul=mu[:, c, dc:dc + 1])
                        p.I("dve", "scalar_tensor_tensor", out=xs[dc].v(), in0=hT[dc].v(), scalar=omm[:, c, dc:dc + 1],
                            in1=xs[dc].v(), op0=ALU.mult, op1=ALU.add)

                import os as _os
                for c in range(4):
                    if not _os.environ.get("NOMIX") or c == 0:
                        mix(c)

                    def sink(ft, tt, ps_, c=c):
                        s_ = stg[sk[0] % 4]
                        e = "act" if sk[0] % 2 == 0 else "dve"
                        sk[0] += 1
                        if e == "act":
                            p.I("act", "copy", out=s_.v(), in_=ps_[:, 0:TT])
                        else:
                            p.I("dve", "tensor_copy", out=s_.v(), in_=ps_[:, 0:TT])
                        if not _os.environ.get("NOSTORE"):
                            p.dma("sp", projT[c][ft][:, tt * TT:(tt + 1) * TT], s_.v(), acc_w=True)

                    proj_fm(xs, wv, c * W, HP, sink, wts, pss)
                for c, (w1d, dstl, fn) in ((4, (rw_dw1, lw1, AF.Tanh)), (5, (rw_aw1, la1, AF.Copy))):
                    mix(c)
                    p.dma("pool", w1t.v(), w1d.v()[j].re("(dc p) r -> p dc r", p=128))
                    for tt in range(NT):
                        ts = slice(tt * TT, (tt + 1) * TT)
                        ps_ = pss[tt % 4]
                        for dc in range(DC):
                            p.I("pe", "matmul", out=ps_[0:R, 0:TT], lhsT=w1t[:, dc, :], rhs=xs[dc][:, ts],
                                start=(dc == 0), stop=(dc == DC - 1))
                        p.I("act", "activation", out=dstl[:, ts], in_=ps_[0:R, 0:TT], func=fn)
            if cfg.stop <= 2:
                return False
            with p.scope():
                dw2 = p.sb("dw2", [R, W], BF16)
                aw2 = p.sb("aw2", [R, W], BF16)
                p.dma("pool", dw2.v(), rw_dw2.v()[j])
                p.dma("pool", aw2.v(), rw_aw2.v()[j])
                ld = {nm: p.sb("ld_" + nm, [128, TB], F32) for nm in ("r", "k", "v", "g")}
                tm = {nm: p.sb("tm_" + nm, [128, TB], F32) for nm in
                      ("lw", "cum", "e1", "e2", "e3", "a", "kk", "kf", "t1", "t2", "t3", "bv", "y")}
                BKT = p.sb("BKT", [128, NCHB, 2, 64], BF16)
                KRT = p.sb("KRT", [128, NCHB, 2, 64], BF16)
                KKVT = p.sb("KKVT", [128, NCHB, 2, 64], BF16)
                CBS, NSTR = 2, 2
                STR = []
                psTrS = p.ps("psTr", [128, 4, 2, 128], BF16)
                for si in range(NSTR):
                    STR.append(dict(
                        BK=p.sb("BK", [128, CBS, 128], BF16), UV=p.sb("UV", [128, CBS, 128], BF16),
                        Xr=p.sb("Xr", [64, CBS * 2, 2, 64], BF16), A_sb=p.sb("A_sb", [128, CBS * 2, 128], BF16),
                        NXT=p.sb("NXT", [64, CBS * 2, 192], BF16), MU=p.sb("MU", [64, CBS * 2, 128], BF16),
                        GT=p.sb("GT", [128, CBS, 64], BF16), PpT=p.sb("PpT", [128, CBS, 64], BF16),
                        PG=p.ps("PG", [128, 2, 512], F32), psTr=psTrS, toff=si * CBS))
                Tst = [p.sb("Tst", [128, 64], BF16) for _ in range(3)]
                psP = [p.ps("psP", [128, 512], F32) for _ in range(2)]
                psAV = p.ps("psAV", [128, CBS * 2, 128], F32)
                NTB = TB // TT if TB >= TT else 1
                TTB = min(TT, TB)
                import os as _os2
                for hp in range(int(_os2.environ.get('RW_HP', HP))):
                    hsl = slice(hp * 128, (hp + 1) * 128)
                    vcol = lambda i: vec[:, i, hp:hp + 1]
                    ti = 0
                    p.I("dve", "memset", ap=Tst[0].v(), constant=0.0)
                    for tb in range(S // TB):
                        tbs = slice(tb * TB, (tb + 1) * TB)
                        for c, nm in enumerate(("r", "k", "v", "g")):
                            p.dma("sp", ld[nm].v(), projT[c][hp][:, tbs])
                        r_, k_, v_, g_ = ld["r"], ld["k"], ld["v"], ld["g"]
                        lw, cum, e1, e2, e3, a_, kk, kf, t1, t2, t3, bv, yT = (tm[n] for n in (
                            "lw", "cum", "e1", "e2", "e3", "a", "kk", "kf", "t1", "t2", "t3", "bv", "y"))
                        for tt in range(NTB):
                            ts = slice(tt * TTB, (tt + 1) * TTB)
                            gs_ = slice(tb * TB + tt * TTB, tb * TB + (tt + 1) * TTB)
                            ps_ = psP[0]
                            p.I("pe", "matmul", out=ps_[:, 0:TTB], lhsT=dw2[:, hsl], rhs=lw1[:, gs_], start=True, stop=True)
                            p.I("act", "activation", out=lw[:, ts], in_=ps_[:, 0:TTB], func=AF.Sigmoid, bias=vcol(0), scale=1.0)
                            ps_ = psP[1]
                            p.I("pe", "matmul", out=ps_[:, 0:TTB], lhsT=aw2[:, hsl], rhs=la1[:, gs_], start=True, stop=True)
                            p.I("act", "activation", out=a_[:, ts], in_=ps_[:, 0:TTB], func=AF.Sigmoid, bias=vcol(1), scale=1.0)
                        p.I("dve", "tensor_scalar", out=lw.v(), in0=lw.v(), scalar1=NEG_EXP_HALF, scalar2=None, op0=ALU.mult)
                        p.I("dve", "tensor_tensor_scan", out=cum.v(), data0=rmask[:, 0:TB], data1=lw.v(), initial=0.0,
                            op0=ALU.mult, op1=ALU.add)
                        p.I("act", "activation", out=e1.v(), in_=cum.v(), func=AF.Exp)
                        p.I("act", "activation", out=e2.v(), in_=cum.v(), func=AF.Exp, scale=-1.0)
                        p.I("dve", "tensor_tensor", out=t1.v(), in0=cum.v(), in1=lw.v(), op=ALU.subtract)
                        p.I("act", "activation", out=e3.v(), in_=t1.v(), func=AF.Exp)
                        p.I("act", "activation", out=t2.v(), in_=k_.v(), func=AF.Square, scale=vcol(2))
                        for tt in range(NTB):
                            ts = slice(tt * TTB, (tt + 1) * TTB)
                            ps_ = psP[tt % 2]
                            p.I("pe", "matmul", out=ps_[:, 0:TTB], lhsT=bones32, rhs=t2[:, ts], start=True, stop=True)
                            p.I("act", "activation", out=t3[:, ts], in_=ps_[:, 0:TTB], func=AF.Sqrt)
                        p.I("dve", "tensor_scalar", out=t3.v(), in0=t3.v(), scalar1=1e-12, scalar2=None, op0=ALU.max)
                        p.I("dve", "reciprocal", out=t3.v(), in_=t3.v())
                        p.I("dve", "scalar_tensor_tensor", out=kk.v(), in0=k_.v(), scalar=vcol(2), in1=t3.v(), op0=ALU.mult, op1=ALU.mult)
                        p.I("dve", "tensor_scalar", out=t1.v(), in0=a_.v(), scalar1=vcol(3), scalar2=omka[:, hp:hp + 1],
                            op0=ALU.mult, op1=ALU.add)
                        p.I("dve", "tensor_tensor", out=kf.v(), in0=k_.v(), in1=t1.v(), op=ALU.mult)
                        p.I("dve", "tensor_tensor", out=t2.v(), in0=kk.v(), in1=a_.v(), op=ALU.mult)
                        ch = lambda t: t.v().re("p (n c) -> p n c", c=64)
                        p.I("dve", "tensor_tensor", out=KRT[:, :, 1, :], in0=ch(r_), in1=ch(e1), op=ALU.mult)
                        p.I("dve", "tensor_tensor", out=BKT[:, :, 1, :], in0=ch(kf), in1=ch(e2), op=ALU.mult)
                        p.I("dve", "tensor_tensor", out=BKT[:, :, 0, :], in0=ch(t2), in1=ch(e2), op=ALU.mult)
                        p.I("dve", "tensor_tensor", out=KRT[:, :, 0, :], in0=ch(kk), in1=ch(e3), op=ALU.mult)
                        p.I("act", "copy", out=KKVT[:, :, 0, :], in_=KRT[:, :, 0, :])
                        p.I("act", "copy", out=KKVT[:, :, 1, :], in_=ch(v_))
                        p.I("dve", "scalar_tensor_tensor", out=t1.v(), in0=r_.v(), scalar=vcol(4), in1=kf.v(),
                            op0=ALU.mult, op1=ALU.mult)
                        for tt in range(NTB):
                            ts = slice(tt * TTB, (tt + 1) * TTB)
                            ps_ = psP[tt % 2]
                            p.I("pe", "matmul", out=ps_[:, 0:TTB], lhsT=bones32, rhs=t1[:, ts], start=True, stop=True)
                            p.I("dve", "tensor_tensor", out=bv[:, ts], in0=ps_[:, 0:TTB], in1=v_[:, ts], op=ALU.mult)
                        if cfg.stop <= 3:
                            return False
                        tiref = [ti]

                        def group_stream(c0, cb_n, T):
                            BK, UV, Xr, A_sb, NXT, MU, GT, PpT, PG, psTr = (T[k_] for k_ in
                                ("BK", "UV", "Xr", "A_sb", "NXT", "MU", "GT", "PpT", "PG", "psTr"))
                            psTr = psTr[:, T["toff"]:T["toff"] + CBS]
                            PGv = PG.v().re("p h (c x) -> p h c x", c=CBS)
                            hc = lambda t: t.v().re("p (h c) x -> p h c x", h=2)[:, :, 0:cb_n, :]
                            for cb in range(cb_n):
                                n = c0 + cb
                                p.I("pe", "transpose", out=psTr[:, cb, 0, :], in_=BKT[:, n].re("p a c -> p (a c)"), identity=ident_bf.v())
                                p.I("pe", "transpose", out=psTr[:, cb, 1, :], in_=KKVT[:, n].re("p a c -> p (a c)"), identity=ident_bf.v())
                            p.I("dve", "tensor_copy", out=BK[:, 0:cb_n, :], in_=psTr[:, 0:cb_n, 0, :])
                            p.I("dve", "tensor_copy", out=Xr.v().re("p (h c) a x -> p h c a x", h=2)[:, :, 0:cb_n, 0, :],
                                in_=psTr[0:64, 0:cb_n, 1, :].re("p c (h x) -> p h c x", h=2))
                            p.I("dve", "tensor_copy", out=UV[64:128, 0:cb_n, :], in_=psTr[64:128, 0:cb_n, 1, :])
                            for cb in range(cb_n):
                                n = c0 + cb
                                for h in range(2):
                                    hs = slice(h * 64, (h + 1) * 64)
                                    p.I("pe", "matmul", out=PGv[:, h, cb, 0:128],
                                        lhsT=BKT[hs, n].re("p a c -> p (a c)"), rhs=KRT[hs, n].re("p a c -> p (a c)"),
                                        start=True, stop=True)
                                    p.I("pe", "matmul", out=PGv[0:64, h, cb, 128:192],
                                        lhsT=KRT[hs, n, 0, :], rhs=BKT[hs, n, 0, :], start=True, stop=True)
                            pgA = PGv[:, :, 0:cb_n, 0:128]
                            p.I("act", "copy", out=hc(A_sb), in_=pgA)
                            p.I("dve", "tensor_tensor", out=hc(A_sb), in0=hc(A_sb),
                                in1=maskA.bc([1, 1], [128, 2, cb_n, 128]), op=ALU.mult)
                            nx = hc(NXT)
                            p.I("dve", "tensor_tensor", out=nx[:, :, :, 0:64], in0=hc(A_sb)[0:64, :, :, 0:64],
                                in1=negSU.bc([1, 1], [64, 2, cb_n, 64]), op=ALU.mult)
                            p.I("dve", "tensor_tensor", out=nx[:, :, :, 64:128], in0=nx[:, :, :, 0:64],
                                in1=cst[0:64, 5, 0:64].bc([1, 1], [64, 2, cb_n, 64]), op=ALU.add)
                            p.I("dve", "tensor_tensor", out=nx[:, :, :, 128:192],
                                in0=PGv[0:64, :, 0:cb_n, 128:192],
                                in1=negSL.bc([1, 1], [64, 2, cb_n, 64]), op=ALU.mult)
                            yield
                            for cb in range(cb_n):
                                for h in range(2):
                                    q = h * CBS + cb
                                    p.I("pe", "matmul", out=psAV[0:64, q, 0:64], lhsT=A_sb[64:128, q, 0:64],
                                        rhs=UV[64:128, cb, h * 64:(h + 1) * 64], start=True, stop=True)
                            pgI = PGv[0:64, :, 0:cb_n, 0:192]
                            for rnd in range(6):
                                for cb in range(cb_n):
                                    for h in range(2):
                                        q = h * CBS + cb
                                        if rnd == 0:
                                            p.I("pe", "matmul", out=PGv[0:64, h, cb, 0:64], lhsT=NXT[:, q, 128:192],
                                                rhs=NXT[:, q, 0:64], start=True, stop=True)
                                        elif rnd < 5:
                                            p.I("pe", "matmul", out=PGv[0:64, h, cb, 0:128], lhsT=NXT[:, q, 128:192],
                                                rhs=NXT[:, q, 0:128], start=True, stop=True)
                                        else:
                                            p.I("pe", "matmul", out=PGv[0:64, h, cb, 64:128], lhsT=NXT[:, q, 128:192],
                                                rhs=NXT[:, q, 64:128], start=True, stop=True)
                                        if rnd < 5:
                                            p.I("pe", "matmul", out=PGv[0:64, h, cb, 128:192], lhsT=NXT[:, q, 0:64],
                                                rhs=NXT[:, q, 128:192], start=True, stop=True)
                                if rnd == 0:
                                    p.I("act", "copy", out=Xr.v().re("p (h c) a x -> p h c a x", h=2)[:, :, 0:cb_n, 1, :],
                                        in_=psAV.v().re("p (h c) x -> p h c x", h=2)[0:64, :, 0:cb_n, 0:64])
                                if rnd > 0:
                                    p.I("dve", "tensor_tensor", out=nx[:, :, :, 64:128], in0=pgI[:, :, :, 64:128],
                                        in1=nx[:, :, :, 64:128], op=ALU.add)
                                if rnd < 5:
                                    p.I("act", "copy", out=nx[:, :, :, 0:64], in_=pgI[:, :, :, 0:64])
                                    p.I("act", "copy", out=nx[:, :, :, 128:192], in_=pgI[:, :, :, 128:192])
                                yield
                            for cb in range(cb_n):
                                for h in range(2):
                                    q = h * CBS + cb
                                    p.I("pe", "matmul", out=PGv[0:64, h, cb, 0:128], lhsT=NXT[:, q, 64:128],
                                        rhs=Xr[:, q].re("p a c -> p (a c)"), start=True, stop=True)
                            pgM = PGv[0:64, :, 0:cb_n, 0:128]
                            p.I("act", "mul", out=hc(MU), in_=pgM, mul=-1.0)
                            p.I("dve", "tensor_scalar", out=UV[0:64, 0:cb_n, :].re("p c (h x) -> p h c x", h=2),
                                in0=pgM[:, :, :, 64:128], scalar1=-1.0, scalar2=None, op0=ALU.mult)
                            yield
                            for cb in range(cb_n):
                                for h in range(2):
                                    q = h * CBS + cb
                                    hs = slice(h * 64, (h + 1) * 64)
                                    p.I("pe", "matmul", out=PGv[hs, h, cb, 0:64], lhsT=MU[:, q, 0:64], rhs=A_sb[0:64, q, 64:128],
                                        start=True, stop=True)
                                    p.I("pe", "matmul", out=PGv[hs, h, cb, 64:128], lhsT=MU[:, q, 0:64], rhs=BK[0:64, cb, h * 64:(h + 1) * 64],
                                        start=True, stop=True)
                            for h in range(2):
                                hs = slice(h * 64, (h + 1) * 64)
                                p.I("dve", "tensor_tensor", out=GT[hs, 0:cb_n, :], in0=PGv[hs, h, 0:cb_n, 0:64],
                                    in1=KRT[hs, c0:c0 + cb_n, 1, :], op=ALU.add)
                                p.I("dve", "tensor_tensor", out=PpT[hs, 0:cb_n, :], in0=PGv[hs, h, 0:cb_n, 64:128],
                                    in1=cst[hs, 5, 0:64].bc([1], [64, cb_n, 64]), op=ALU.add)
                            yield
                            for cb in range(cb_n):
                                n = c0 + cb
                                Tc, Tn = Tst[tiref[0] % 3], Tst[(tiref[0] + 1) % 3]
                                tiref[0] += 1
                                for h in range(2):
                                    q = h * CBS + cb
                                    hs = slice(h * 64, (h + 1) * 64)
                                    p.I("pe", "matmul", out=PGv[hs, h, cb, 128:192], lhsT=PpT[hs, cb, :], rhs=Tc[hs, :], start=True, stop=False)
                                    p.I("pe", "matmul", out=PGv[hs, h, cb, 128:192], lhsT=BK[:, cb, hs], rhs=UV[:, cb, hs], start=False, stop=True)
                                    p.I("pe", "matmul", out=PGv[hs, h, cb, 192:256], lhsT=Tc[hs, :], rhs=GT[hs, cb, :], start=True, stop=False)
                                    p.I("pe", "matmul", out=PGv[hs, h, cb, 192:256], lhsT=UV[:, cb, hs], rhs=A_sb[:, q, 64:128], start=False, stop=True)
                                for h in range(2):
                                    hs = slice(h * 64, (h + 1) * 64)
                                    p.I("dve", "tensor_scalar", out=Tn[hs, :], in0=PGv[hs, h, cb, 128:192],
                                        scalar1=e1[hs, n * 64 + 63:n * 64 + 64], scalar2=None, op0=ALU.mult)
                            for h in range(2):
                                hs = slice(h * 64, (h + 1) * 64)
                                p.I("act", "copy", out=yT.v().re("p (n c) -> p n c", c=64)[hs, c0:c0 + cb_n, :],
                                    in_=PGv[hs, h, 0:cb_n, 192:256])
                            yield

                        for g0 in range(0, NCHB, CBS * NSTR):
                            gens = []
                            for si in range(NSTR):
                                c0 = g0 + si * CBS
                                if c0 < NCHB:
                                    gens.append(group_stream(c0, min(CBS, NCHB - c0), STR[si]))
                            alive = list(gens)
                            while alive:
                                nxt = []
                                for gq in alive:
                                    try:
                                        next(gq)
                                        nxt.append(gq)
                                    except StopIteration:
                                        pass
                                alive = nxt
                        ti = tiref[0]
                        if cfg.stop <= 8:
                            return False
                        p.I("act", "activation", out=t2.v(), in_=yT.v(), func=AF.Square)
                        for tt in range(NTB):
                            ts = slice(tt * TTB, (tt + 1) * TTB)
                            p.I("pe", "matmul", out=psP[0][:, 0:TTB], lhsT=bones32, rhs=yT[:, ts], start=True, stop=True)
                            p.I("pe", "matmul", out=psP[1][:, 0:TTB], lhsT=bones32, rhs=t2[:, ts], start=True, stop=True)
                            p.I("act", "mul", out=t1[:, ts], in_=psP[0][:, 0:TTB], mul=1.0 / 64)
                            p.I("dve", "tensor_tensor", out=t3[:, ts], in0=t1[:, ts], in1=t1[:, ts], op=ALU.mult)
                            p.I("dve", "scalar_tensor_tensor", out=t3[:, ts], in0=psP[1][:, 0:TTB], scalar=1.0 / 64, in1=t3[:, ts],
                                op0=ALU.mult, op1=ALU.subtract)
                        p.I("act", "activation", out=t3.v(), in_=t3.v(), func=AF.Sqrt, bias=RWKV_GN_EPS, scale=1.0)
                        p.I("dve", "reciprocal", out=t3.v(), in_=t3.v())
                        p.I("dve", "tensor_tensor", out=t1.v(), in0=yT.v(), in1=t1.v(), op=ALU.subtract)
                        p.I("dve", "tensor_tensor", out=t1.v(), in0=t1.v(), in1=t3.v(), op=ALU.mult)
                        p.I("act", "activation", out=t1.v(), in_=t1.v(), func=AF.Identity, bias=vcol(6), scale=vcol(5))
                        p.I("dve", "tensor_tensor", out=t1.v(), in0=t1.v(), in1=bv.v(), op=ALU.add)
                        p.I("act", "activation", out=t2.v(), in_=g_.v(), func=AF.Silu)
                        p.I("dve", "tensor_tensor", out=yg[hp][:, tbs], in0=t1.v(), in1=t2.v(), op=ALU.mult)
            if cfg.stop <= 9:
                return False
            out_proj(yg, rw_out.v()[j], HP, lambda ft: modT[:, l, 2 * DC + ft:2 * DC + ft + 1], src, dst)
            return True

    bufs = xres
    bi = [0]

    def nextbuf():
        b_ = bufs[bi[0] % len(bufs)]
        bi[0] += 1
        return b_

    cur = xT.v()
    counters = {0: 0, 1: 0, 2: 0}
    for l, kind in enumerate(cfg.kinds):
        j = counters[kind]
        counters[kind] += 1
        if kind in (0, 1):
            dst = nextbuf()
            ok = (rwkv_layer if kind == 0 else gla_layer)(l, j, cur, dst.v())
            if ok:
                cur = dst.v()
        else:
            r_ = ssd_layer(l, j, cur, nextbuf)
            if r_ is not False:
                cur = r_
    with p.scope():
        fg = p.sb("fg", [128, DC], F32)
        p.dma("sp", fg.v(), final_gT.v())
        norm_phase(cur, None, lambda dc: fg[:, dc:dc + 1], None, out_dram=outT.v())
    p.emit()
    return nc, p


def _pp(vec, nchunk):
    v = np.asarray(vec, np.float32)
    lead = v.shape[:-1]
    v = v.reshape(lead + (nchunk, 128))
    return np.ascontiguousarray(np.moveaxis(v, -1, 0))


def prepare_inputs(cfg, inp, n_cores, batch_of_core):
    D, S, DC, L = cfg.D, cfg.S, cfg.DC, cfg.L
    consts, rmask, rmask128 = make_consts(cfg)
    shared = {
        "ada_w": np.ascontiguousarray(inp["ada_w"], dtype=np.float32),
        "ada_bT": _pp(inp["ada_b"], 3 * DC),
        "norm_gT": _pp(inp["norm_g"], DC),
        "final_gT": _pp(inp["final_g"], DC),
        "consts": consts, "rmask": rmask, "rmask128": rmask128,
    }
    if cfg.nR:
        HP = D // 128
        for k in ("rwkv_w_in", "rwkv_w_out", "rwkv_dec_w1", "rwkv_dec_w2", "rwkv_iclr_w1", "rwkv_iclr_w2"):
            shared[k] = np.ascontiguousarray(inp[k], dtype=np.float32)
        shared["rwkv_muT"] = _pp(inp["rwkv_mu"], DC)
        vecs = np.stack([inp["rwkv_dec_w0"], inp["rwkv_iclr_w0"], inp["rwkv_k_k"], inp["rwkv_k_a"],
                         np.asarray(inp["rwkv_r_k"]).reshape(cfg.nR, -1), inp["rwkv_gn_w"], inp["rwkv_gn_b"]], axis=1)
        shared["rwkv_vecT"] = _pp(vecs, HP)
    if cfg.nG:
        for k in ("gla_w_in", "gla_w_out", "gla_gate_w2"):
            shared[k] = np.ascontiguousarray(inp[k], dtype=np.float32)
        shared["gla_nbT"] = _pp(inp["gla_gate_b"], (D // 2) // 128)
        hg = np.asarray(inp["gla_head_g"], np.float32)
        shared["gla_hgb"] = np.ascontiguousarray(np.broadcast_to(hg[None], (128,) + hg.shape))
    if cfg.nS:
        SW = 2 * D
        SH = SW // 64
        for k in ("ssd_w_in", "ssd_w_out"):
            shared[k] = np.ascontiguousarray(inp[k], dtype=np.float32)
        cwk = np.asarray(inp["ssd_conv_w"], np.float32)
        shared["ssd_cwT"] = _pp(np.moveaxis(cwk, 1, 2).reshape(cfg.nS, -1).reshape(cfg.nS, cwk.shape[2], 4).transpose(0, 2, 1), cwk.shape[2] // 128).transpose(0, 1, 3, 2).copy()
        shared["ssd_cbT"] = _pp(inp["ssd_conv_b"], cwk.shape[2] // 128)
        hv = np.zeros((64, cfg.nS, 2), np.float32)
        hv[:SH, :, 0] = np.asarray(inp["ssd_dt_bias"], np.float32).T
        hv[:SH, :, 1] = np.asarray(inp["ssd_a_log"], np.float32).T
        shared["ssd_hv"] = hv
        dsk = np.asarray(inp["ssd_d"], np.float32)
        shared["ssd_dsb"] = np.ascontiguousarray(np.broadcast_to(dsk[None], (128,) + dsk.shape))
        ng = np.asarray(inp["ssd_norm_g"], np.float32)
        shared["ssd_ngb"] = np.ascontiguousarray(np.broadcast_to(ng[None], (128,) + ng.shape))
    maps = []
    for core in range(n_cores):
        b = batch_of_core[core]
        m = dict(shared)
        m["xT"] = np.ascontiguousarray(np.asarray(inp["x"][b], np.float32).T)
        m["cT"] = _pp(inp["c"][b], DC)
        maps.append(m)
    return maps


_CACHE = {}


def kernel(**inputs):
    cfg = Cfg()
    B = inputs["x"].shape[0]
    n_cores = 8
    batch_of_core = [c % B for c in range(n_cores)]
    if "nc" not in _CACHE:
        _CACHE["nc"] = build(cfg)[0]
    nc = _CACHE["nc"]
    maps = prepare_inputs(cfg, inputs, n_cores, batch_of_core)
    res = run_bass_kernel_spmd(nc, maps, core_ids=list(range(n_cores)))
    out = np.empty((B, cfg.S, cfg.D), np.float32)
    for b in range(B):
        out[b] = res.results[b]["outT"].T
    return out
```

```python
from contextlib import ExitStack
import math
import numpy as np
import concourse.bass as bass
import concourse.mybir as mybir
from concourse.bass_utils import run_bass_kernel_spmd

F32 = mybir.dt.float32
BF16 = mybir.dt.bfloat16
AF = mybir.ActivationFunctionType
ALU = mybir.AluOpType
AX = mybir.AxisListType


class V:
    __slots__ = ("ap", "tl")

    def __init__(self, ap, tl):
        self.ap = ap
        self.tl = tl

    def __getitem__(self, idx):
        return V(self.ap[idx], self.tl)

    def re(self, pat, **kw):
        return V(self.ap.rearrange(pat, **kw), self.tl)

    def bc(self, axes, shape):
        a = self.ap
        for ax in axes:
            a = a.unsqueeze(ax)
        return V(a.broadcast_to(list(shape)), self.tl)


class Tl:
    __slots__ = ("t", "lw", "rd", "name", "excl")

    def __init__(self, t, name="", excl=False):
        self.t = t
        self.lw = []
        self.rd = []
        self.name = name
        self.excl = excl

    def __getitem__(self, idx):
        return V(self.t[idx], self)

    def v(self):
        return V(self.t[:], self)


ENGS = ("pe", "act", "dve", "pool", "sp")
DMA_ENGS = ("sp", "pool", "act")
NDMA_SLOTS = 12
WRITE_KW = ("out", "accum_out", "ap")


def _compress(toks):
    best = {}
    for s, v, src in toks:
        k = id(s)
        if k not in best or best[k][1] < v:
            best[k] = (s, v, src)
    return list(best.values())


class Prog:
    def __init__(self, nc):
        self.nc = nc
        self.stacks = [ExitStack()]
        self.q = {e: [] for e in ENGS}
        self.cnt = {e: 0 for e in ENGS}
        self.sem = {e: self.stacks[0].enter_context(nc.semaphore("s_" + e)) for e in ENGS}
        self.seen = {e: {} for e in ENGS}
        self.dsem, self.dval, self.dnext = {}, {}, {}
        for e in DMA_ENGS:
            self.dsem[e] = [self.stacks[0].enter_context(nc.semaphore("d_%s%d" % (e, i))) for i in range(NDMA_SLOTS)]
            self.dval[e] = [0] * NDMA_SLOTS
            self.dnext[e] = 0
        self.n_inst = 0
        self.uid = 0

    def _nm(self, name):
        self.uid += 1
        return "%s_%d" % (name, self.uid)

    def sb(self, name, shape, dt=F32):
        t = self.stacks[-1].enter_context(self.nc.sbuf_tensor(self._nm(name), list(shape), dt))
        return Tl(t, name)

    def ps(self, name, shape, dt=F32):
        nbytes = int(np.prod(shape[1:])) * (4 if dt == F32 else 2)
        assert nbytes % 2048 == 0, "PSUM tiles must cover whole banks"
        t = self.stacks[-1].enter_context(self.nc.psum_tensor(self._nm(name), list(shape), dt))
        return Tl(t, name, excl=True)

    def dram(self, name, shape, dt=F32, kind="Internal"):
        t = self.nc.dram_tensor(name, list(shape), dt, kind=kind)
        return Tl(t.ap(), name)

    class _Scope:
        def __init__(self, p):
            self.p = p

        def __enter__(self):
            self.p.stacks.append(ExitStack())

        def __exit__(self, *a):
            self.p.barrier()
            self.p.stacks.pop().close()
            return False

    def scope(self):
        return Prog._Scope(self)

    def _deps(self, eng, reads, writes, acc_w=False):
        waits = {}

        def need(tok):
            sem, val, src = tok
            if src == "pe" and eng == "pe":
                return
            k = id(sem)
            if self.seen[eng].get(k, 0) >= val:
                return
            if k not in waits or waits[k][1] < val:
                waits[k] = (sem, val)

        for tl in reads:
            for tok in tl.lw:
                need(tok)
        for tl in writes:
            if not acc_w:
                for tok in tl.lw:
                    need(tok)
            for tok in tl.rd:
                need(tok)
        for k, (sem, val) in waits.items():
            self.seen[eng][k] = val
        return list(waits.values())

    def _commit(self, tok, reads, writes, acc_w=False):
        for tl in writes:
            if acc_w:
                tl.lw.append(tok)
                if len(tl.lw) > 48:
                    tl.lw = _compress(tl.lw)
            else:
                tl.lw = [tok]
            tl.rd = []
        for tl in reads:
            if tl not in writes:
                tl.rd.append(tok)
                if len(tl.rd) > 48:
                    tl.rd = _compress(tl.rd)

    def I(self, eng, fn, *, acc_w=False, **kw):
        reads, writes, args = [], [], {}
        for k, a in kw.items():
            if isinstance(a, V):
                args[k] = a.ap
                (writes if (k in WRITE_KW or a.tl.excl) else reads).append(a.tl)
            else:
                args[k] = a
        waits = self._deps(eng, reads, writes, acc_w)
        self.cnt[eng] += 1
        tok = (self.sem[eng], self.cnt[eng], eng)
        self._commit(tok, reads, writes, acc_w)
        self.q[eng].append((waits, fn, args, (self.sem[eng], 1)))
        self.n_inst += 1

    def dma(self, eng, out, in_, acc_w=False, **kw):
        reads, writes = [in_.tl], [out.tl]
        waits = self._deps(eng, reads, writes, acc_w)
        s = self.dnext[eng]
        self.dnext[eng] = (s + 1) % NDMA_SLOTS
        sem = self.dsem[eng][s]
        prev = self.dval[eng][s]
        if prev > 0 and self.seen[eng].get(id(sem), 0) < prev:
            waits.append((sem, prev))
            self.seen[eng][id(sem)] = prev
        self.dval[eng][s] = prev + 16
        tok = (sem, prev + 16, "dma")
        self._commit(tok, reads, writes, acc_w)
        args = dict(out=out.ap, in_=in_.ap)
        args.update(kw)
        self.q[eng].append((waits, "dma_start", args, (sem, 16)))
        self.n_inst += 1

    def barrier(self):
        for e in ENGS:
            waits = []
            for e2 in ENGS:
                if self.cnt[e2] > 0 and self.seen[e].get(id(self.sem[e2]), 0) < self.cnt[e2] and e2 != e:
                    waits.append((self.sem[e2], self.cnt[e2]))
                    self.seen[e][id(self.sem[e2])] = self.cnt[e2]
            for de in DMA_ENGS:
                for s in range(NDMA_SLOTS):
                    v = self.dval[de][s]
                    sem = self.dsem[de][s]
                    if v > 0 and self.seen[e].get(id(sem), 0) < v:
                        waits.append((sem, v))
                        self.seen[e][id(sem)] = v
            if waits:
                self.q[e].append((waits, None, None, None))

    def emit(self):
        nc = self.nc
        self.barrier()
        with nc.Block() as block:
            def run(engname):
                def f(e):
                    for waits, fn, args, inc in self.q[engname]:
                        for sem, val in waits:
                            e.wait_ge(sem, val)
                        if fn is not None:
                            getattr(e, fn)(**args).then_inc(inc[0], inc[1])
                return f
            block.tensor(run("pe"))
            block.scalar(run("act"))
            block.vector(run("dve"))
            block.gpsimd(run("pool"))
            block.sync(run("sp"))
        while self.stacks:
            self.stacks.pop().close()


class Cfg:
    def __init__(self, D=2048, S=2048, kinds=(0, 1, 2, 0), lora=96,
                 gla_heads=4, gla_rank=16, ssm_groups=8):
        self.D, self.S, self.kinds, self.lora = D, S, tuple(kinds), lora
        self.gla_heads, self.gla_rank, self.ssm_groups = gla_heads, gla_rank, ssm_groups
        self.DC = D // 128
        self.TT = min(512, S)
        self.NT = S // self.TT
        self.TA = min(256, S)
        self.L = len(kinds)
        self.nR = sum(1 for k in kinds if k == 0)
        self.nG = sum(1 for k in kinds if k == 1)
        self.nS = sum(1 for k in kinds if k == 2)
        self.TB = min(1024, S)
        self.CB = 4
        self.stop = 99


NEG_EXP_HALF = -math.exp(-0.5)
NORM_EPS = 1e-6
RWKV_GN_EPS = 64e-5


def make_consts(cfg):
    c = np.zeros((128, 8, 128), np.float32)
    c[:, 0, :] = np.eye(128)
    c[:, 1, :] = 1.0
    c[0:64, 2, 0:64] = 1.0
    c[64:128, 2, 64:128] = 1.0
    su = np.triu(np.ones((64, 64), np.float32), 1)
    iu = np.triu(np.ones((64, 64), np.float32), 0)
    c[0:64, 3, 0:64] = su
    c[64:128, 3, 0:64] = su
    c[0:64, 3, 64:128] = iu
    c[64:128, 3, 64:128] = iu
    c[0:64, 4, 0:64] = -su
    c[0:64, 4, 64:128] = -su.T
    c[0:64, 5, 0:64] = np.eye(64)
    c[64:128, 5, 0:64] = np.eye(64)
    c[:, 6, :] = np.triu(np.ones((128, 128), np.float32), 0)
    c[:, 7, :] = np.where(np.triu(np.ones((128, 128)), 0) > 0, 0.0, -30000.0)
    rmask = np.ones((128, cfg.S), np.float32)
    rmask[:, 0::64] = 0.0
    rmask128 = np.ones((128, cfg.S), np.float32)
    rmask128[:, 0::128] = 0.0
    return c.reshape(128, 8 * 128), rmask, rmask128


def build(cfg):
    nc = bass.Bass("TRN2", target_bir_lowering=False)
    p = Prog(nc)
    D, S, DC, TT, NT, L = cfg.D, cfg.S, cfg.DC, cfg.TT, cfg.NT, cfg.L
    EI = "ExternalInput"
    xT = p.dram("xT", [D, S], F32, EI)
    cT = p.dram("cT", [128, DC], F32, EI)
    ada_w = p.dram("ada_w", [L, D, 3 * D], F32, EI)
    ada_bT = p.dram("ada_bT", [128, L, 3 * DC], F32, EI)
    norm_gT = p.dram("norm_gT", [128, L, DC], F32, EI)
    final_gT = p.dram("final_gT", [128, DC], F32, EI)
    consts_d = p.dram("consts", [128, 8 * 128], F32, EI)
    rmask_d = p.dram("rmask", [128, S], F32, EI)
    rmask128_d = p.dram("rmask128", [128, S], F32, EI)
    outT = p.dram("outT", [D, S], F32, "ExternalOutput")
    xres = [p.dram("xres%d" % i, [D, S], F32) for i in range(3)]
    W = D
    HP = W // 128
    R = cfg.lora
    if cfg.nR:
        nR = cfg.nR
        rw_in = p.dram("rwkv_w_in", [nR, D, 4 * W], F32, EI)
        rw_out = p.dram("rwkv_w_out", [nR, W, D], F32, EI)
        rw_dw1 = p.dram("rwkv_dec_w1", [nR, D, R], F32, EI)
        rw_dw2 = p.dram("rwkv_dec_w2", [nR, R, W], F32, EI)
        rw_aw1 = p.dram("rwkv_iclr_w1", [nR, D, R], F32, EI)
        rw_aw2 = p.dram("rwkv_iclr_w2", [nR, R, W], F32, EI)
        rw_muT = p.dram("rwkv_muT", [128, nR, 6, DC], F32, EI)
        rw_vecT = p.dram("rwkv_vecT", [128, nR, 7, HP], F32, EI)
        projT = [[Tl(t.t[f * 128:(f + 1) * 128, :], "projT") for f in range(HP)]
                 for t in [p.dram("projT%d" % c, [W, S], F32) for c in range(4)]]

    GH = cfg.gla_heads
    KW, VW = D // 2, D
    DK, DV = KW // GH, VW // GH
    KC, VC = max(DK // 128, 1), DV // 128
    GR = cfg.gla_rank
    if cfg.nG:
        nG = cfg.nG
        assert DK % 128 == 0 and DV % 128 == 0 and DV <= 512
        gl_in = p.dram("gla_w_in", [nG, D, 2 * KW + 2 * VW + GR], F32, EI)
        gl_out = p.dram("gla_w_out", [nG, VW, D], F32, EI)
        gl_w2 = p.dram("gla_gate_w2", [nG, GR, KW], F32, EI)
        gl_nbT = p.dram("gla_nbT", [128, nG, KW // 128], F32, EI)
        gl_hgb = p.dram("gla_hgb", [128, nG, DV], F32, EI)
        gqk = [[Tl(t.t[f * 128:(f + 1) * 128, :], "gqk") for f in range(KW // 128)]
               for t in [p.dram("gqk%d" % c, [KW, S], F32) for c in range(2)]]
        gvg_t = [p.dram("gvg%d" % c, [S, VW], F32) for c in range(2)]
        gvg = [[Tl(t.t[:, hh * DV:(hh + 1) * DV], "gvg") for hh in range(GH)] for t in gvg_t]

    SW = 2 * D
    SH = SW // 64
    SG = SW // 512
    SN = 128
    CW = SW + 2 * SG * SN
    SIN = SW + CW + SH
    if cfg.nS:
        nS = cfg.nS
        sd_in = p.dram("ssd_w_in", [nS, D, SIN], F32, EI)
        sd_out = p.dram("ssd_w_out", [nS, SW, D], F32, EI)
        sd_cwT = p.dram("ssd_cwT", [128, nS, CW // 128, 4], F32, EI)
        sd_cbT = p.dram("ssd_cbT", [128, nS, CW // 128], F32, EI)
        sd_hv = p.dram("ssd_hv", [64, nS, 2], F32, EI)
        sd_dsb = p.dram("ssd_dsb", [128, nS, SH], F32, EI)
        sd_ngb = p.dram("ssd_ngb", [128, nS, SW], F32, EI)
        sxbc_t = p.dram("sxbc", [CW, S], F32)
        sxbc = [Tl(sxbc_t.t[f * 128:(f + 1) * 128, :], "sxbc") for f in range(CW // 128)]
        sz_t = p.dram("sz", [S, SW], F32)
        sz = [Tl(sz_t.t[:, g * 512:(g + 1) * 512], "sz") for g in range(SG)]

    cst = p.sb("cst", [128, 8, 128], F32)
    p.dma("sp", cst.v().re("p a b -> p (a b)"), consts_d.v())
    ident_bf = p.sb("ident_bf", [128, 128], BF16)
    p.I("dve", "tensor_copy", out=ident_bf.v(), in_=cst[:, 0, :])
    ones32 = cst[:, 1, :]
    bones32 = cst[:, 2, :]
    maskA = cst[:, 3, :]
    negSU = cst[0:64, 4, 0:64]
    negSL = cst[0:64, 4, 64:128]
    ident2 = cst[:, 5, 0:64]
    rmask = p.sb("rmask", [128, S], BF16)
    p.dma("pool", rmask.v(), rmask_d.v())
    rmask128 = p.sb("rmask128", [128, S], BF16)
    p.dma("pool", rmask128.v(), rmask128_d.v())
    iu128 = cst[:, 6, :]

    modT = p.sb("modT", [128, L, 3 * DC], F32)
    gsT = p.sb("gsT", [128, L, DC], F32)
    with p.scope():
        cact = p.sb("cact", [128, DC], F32)
        abT = p.sb("abT", [128, L, 3 * DC], F32)
        ngT = p.sb("ngT", [128, L, DC], F32)
        p.dma("sp", cact.v(), cT.v())
        p.dma("sp", abT.v(), ada_bT.v())
        p.dma("sp", ngT.v(), norm_gT.v())
        p.I("act", "activation", out=cact.v(), in_=cact.v(), func=AF.Silu)
        EG = 4 if (3 * DC) % 4 == 0 else 2
        cact_bf = p.sb("cact_bf", [128, DC], BF16)
        p.I("dve", "tensor_copy", out=cact_bf.v(), in_=cact.v())
        wst = [p.sb("adaw", [128, DC, EG * 128], BF16) for _ in range(3)]
        psm = p.ps("psmod", [128, 512], F32)
        gi = 0
        for l in range(L):
            wv = ada_w.v()[l].re("(dc p) e -> p dc e", p=128)
            for eg in range(3 * DC // EG):
                wt = wst[gi % 3]
                p.dma("pool", wt.v(), wv[:, :, eg * EG * 128:(eg + 1) * EG * 128])
                gi += 1
                for j in range(EG):
                    col = l * 3 * DC + eg * EG + j
                    for dc in range(DC):
                        p.I("pe", "matmul", out=psm[:, col:col + 1], lhsT=wt[:, dc, j * 128:(j + 1) * 128],
                            rhs=cact_bf[:, dc:dc + 1], start=(dc == 0), stop=(dc == DC - 1))
        p.I("dve", "tensor_tensor", out=modT.v().re("p l e -> p (l e)"), in0=psm[:, 0:L * 3 * DC],
            in1=abT.v().re("p l e -> p (l e)"), op=ALU.add)
        p.I("dve", "scalar_tensor_tensor", out=gsT.v(), in0=modT[:, :, DC:2 * DC], scalar=1.0, in1=ngT.v(),
            op0=ALU.add, op1=ALU.mult)

    def norm_phase(src, dst_tiles, g_of_dc, sh_of_dc, out_dram=None):
        TA = cfg.TA
        with p.scope():
            xt = [p.sb("xt", [128, DC, TA], F32) for _ in range(2)]
            sq = [p.sb("sq", [128, TA], F32) for _ in range(2)]
            rstd = [p.sb("rstd", [128, TA], F32) for _ in range(2)]
            tmp = [p.sb("ntmp", [128, TA], F32) for _ in range(4)]
            pss = [p.ps("psn", [128, 512], F32) for _ in range(2)]
            k = 0
            for ta in range(S // TA):
                x_ = xt[ta % 2]
                ts = slice(ta * TA, (ta + 1) * TA)
                p.dma("sp", x_.v(), src.re("(dc p) s -> p dc s", p=128)[:, :, ts])
                ps_ = pss[ta % 2]
                for dc in range(DC):
                    s_ = sq[dc % 2]
                    if dc % 2 == 0:
                        p.I("act", "activation", out=s_.v(), in_=x_[:, dc, :], func=AF.Square)
                    else:
                        p.I("dve", "tensor_tensor", out=s_.v(), in0=x_[:, dc, :], in1=x_[:, dc, :], op=ALU.mult)
                    p.I("pe", "matmul", out=ps_[:, 0:TA], lhsT=ones32, rhs=s_.v(), start=(dc == 0), stop=(dc == DC - 1))
                r_ = rstd[ta % 2]
                p.I("act", "activation", out=r_.v(), in_=ps_[:, 0:TA], func=AF.Sqrt, bias=NORM_EPS, scale=1.0 / D)
                p.I("dve", "reciprocal", out=r_.v(), in_=r_.v())
                for dc in range(DC):
                    t_ = tmp[k % 4]
                    k += 1
                    p.I("dve", "scalar_tensor_tensor", out=t_.v(), in0=x_[:, dc, :],
                        scalar=g_of_dc(dc), in1=r_.v(), op0=ALU.mult, op1=ALU.mult)
                    if out_dram is None:
                        p.I("act", "activation", out=dst_tiles[dc][:, ts], in_=t_.v(), func=AF.Identity,
                            bias=sh_of_dc(dc), scale=1.0)
                    else:
                        p.dma("sp", out_dram[dc * 128:(dc + 1) * 128, ts], t_.v(), acc_w=True)

    wring = {}

    def out_proj(yg_tiles, w_dram, nci, gate_of_ft, src, dst):
        with p.scope():
            wts = [p.sb("wo", [128, nci, 512], BF16) for _ in range(2)]
            pso = [p.ps("pso", [128, 512], F32) for _ in range(4)]
            xin = [p.sb("xin", [128, TT], F32) for _ in range(4)]
            wv = w_dram.re("(ci p) f -> p ci f", p=128)
            k = 0
            G = 4 if (D // 128) % 4 == 0 else 2
            for fg in range(D // (128 * G)):
                wt = wts[fg % 2]
                p.dma("pool", wt[:, :, 0:G * 128], wv[:, :, fg * G * 128:(fg + 1) * G * 128])
                for j in range(G):
                    ft = fg * G + j
                    for tt in range(NT):
                        ts = slice(tt * TT, (tt + 1) * TT)
                        ps_ = pso[k % 4]
                        x_ = xin[k % 4]
                        k += 1
                        p.dma("sp", x_.v(), src[ft * 128:(ft + 1) * 128, ts])
                        for ci in range(nci):
                            p.I("pe", "matmul", out=ps_[:, 0:TT], lhsT=wt[:, ci, j * 128:(j + 1) * 128],
                                rhs=yg_tiles[ci][:, ts], start=(ci == 0), stop=(ci == nci - 1))
                        p.I("dve", "scalar_tensor_tensor", out=x_.v(), in0=ps_[:, 0:TT], scalar=gate_of_ft(ft),
                            in1=x_.v(), op0=ALU.mult, op1=ALU.add)
                        p.dma("sp", dst[ft * 128:(ft + 1) * 128, ts], x_.v(), acc_w=True)

    def proj_fm(xs_tiles, wv, f0, nft, sink, wts, pss, kdim=DC):
        k = 0
        G = 4 if nft % 4 == 0 else (2 if nft % 2 == 0 else 1)
        gi = 0
        for fg in range(nft // G):
            wt = wts[gi % 2]
            gi += 1
            p.dma("pool", wt[:, :, 0:G * 128], wv[:, :, f0 + fg * G * 128:f0 + (fg + 1) * G * 128])
            for j in range(G):
                ft = fg * G + j
                for tt in range(NT):
                    ts = slice(tt * TT, (tt + 1) * TT)
                    ps_ = pss[k % len(pss)]
                    k += 1
                    for dc in range(kdim):
                        p.I("pe", "matmul", out=ps_[:, 0:TT], lhsT=wt[:, dc, j * 128:(j + 1) * 128],
                            rhs=xs_tiles[dc][:, ts], start=(dc == 0), stop=(dc == kdim - 1))
                    sink(ft, tt, ps_)


    def proj_tm(hT, wv, f0, ngroups, gw, sink, wts, pss):
        k = 0
        for gi in range(ngroups):
            wt = wts[gi % 2]
            p.dma("pool", wt[:, :, 0:gw], wv[:, :, f0 + gi * gw:f0 + (gi + 1) * gw])
            for tk in range(S // 128):
                ps_ = pss[k % len(pss)]
                k += 1
                for dc in range(DC):
                    p.I("pe", "matmul", out=ps_[:, 0:gw], lhsT=hT[dc][:, tk * 128:(tk + 1) * 128], rhs=wt[:, dc, 0:gw],
                        start=(dc == 0), stop=(dc == DC - 1))
                sink(gi, tk, ps_)

    def gla_layer(l, j, src, dst):
        NCH = S // 128
        with p.scope():
            yg = [p.sb("ygg", [128, S], BF16) for _ in range(VW // 128)]
            lowT = p.sb("lowT", [GR, S], BF16)
            gw2 = p.sb("gw2", [GR, KW], BF16)
            nb = p.sb("gnb", [128, KW // 128], F32)
            hgb = p.sb("hgb", [128, DV], F32)
            p.dma("pool", gw2.v(), gl_w2.v()[j])
            p.dma("sp", nb.v(), gl_nbT.v()[:, j])
            p.dma("sp", hgb.v(), gl_hgb.v()[:, j])
            p.I("dve", "tensor_scalar", out=nb.v(), in0=nb.v(), scalar1=-1.0, scalar2=None, op0=ALU.mult)
            wv = gl_in.v()[j].re("(dc p) f -> p dc f", p=128)
            with p.scope():
                hT = [p.sb("hT", [128, S], BF16) for _ in range(DC)]
                norm_phase(src, hT, lambda dc: gsT[:, l, dc:dc + 1], lambda dc: modT[:, l, dc:dc + 1])
                wts = [p.sb("wi", [128, DC, 512], BF16) for _ in range(2)]
                wl = p.sb("wl", [128, DC, GR], BF16)
                pss = [p.ps("psp", [128, 512], F32) for _ in range(4)]
                stg = [p.sb("stg", [128, 512], F32) for _ in range(4)]
                sk = [0]

                def evac(ps_ap, dst_ap, width):
                    s_ = stg[sk[0] % 4]
                    e = "act" if sk[0] % 2 == 0 else "dve"
                    sk[0] += 1
                    if e == "act":
                        p.I("act", "copy", out=s_[:, 0:width], in_=ps_ap)
                    else:
                        p.I("dve", "tensor_copy", out=s_[:, 0:width], in_=ps_ap)
                    p.dma("sp", dst_ap, s_[:, 0:width], acc_w=True)

                for c in range(2):
                    proj_fm(hT, wv, c * KW, KW // 128,
                            lambda ft, tt, ps_, c=c: evac(ps_[:, 0:TT], gqk[c][ft][:, tt * TT:(tt + 1) * TT], TT), wts, pss)
                for c in range(2):
                    proj_tm(hT, wv, 2 * KW + c * VW, GH, DV,
                            lambda gi, tk, ps_, c=c: evac(ps_[:, 0:DV], gvg[c][gi][tk * 128:(tk + 1) * 128, :], DV), wts, pss)
                p.dma("pool", wl.v(), wv[:, :, 2 * KW + 2 * VW:2 * KW + 2 * VW + GR])
                for tt in range(NT):
                    ts = slice(tt * TT, (tt + 1) * TT)
                    ps_ = pss[tt % 4]
                    for dc in range(DC):
                        p.I("pe", "matmul", out=ps_[0:GR, 0:TT], lhsT=wl[:, dc, :], rhs=hT[dc][:, ts],
                            start=(dc == 0), stop=(dc == DC - 1))
                    p.I("act", "copy", out=lowT[:, ts], in_=ps_[0:GR, 0:TT])
            if cfg.stop <= 2:
                return False
            with p.scope():
                ldq = [p.sb("ldq", [128, S], F32) for _ in range(KC)]
                ldk = [p.sb("ldk", [128, S], F32) for _ in range(KC)]
                QT = [p.sb("QT", [128, S], BF16) for _ in range(KC)]
                KT = [p.sb("KT", [128, S], BF16) for _ in range(KC)]
                eb = [p.sb("eb", [128, S], F32) for _ in range(KC)]
                t1 = p.sb("gt1", [128, S], F32)
                t2 = p.sb("gt2", [128, S], F32)
                S32 = [p.sb("S32", [128, DV], F32) for _ in range(KC)]
                Sb = [p.sb("Sb", [128, DV], BF16) for _ in range(KC)]
                v32 = [p.sb("v32", [128, DV], F32) for _ in range(2)]
                g32 = [p.sb("g32", [128, DV], F32) for _ in range(2)]
                Vb = [p.sb("Vb", [128, DV], BF16) for _ in range(2)]
                SG = [p.sb("SG", [128, DV], F32) for _ in range(2)]
                KTM = [p.sb("KTM", [128, KC * 128], BF16) for _ in range(2)]
                ST = [p.sb("ST", [128, 128], BF16) for _ in range(2)]
                junk = p.sb("junk", [128, DV], F32)
                ssq = [p.sb("ssq", [128, 1], F32) for _ in range(2)]
                y32 = [p.sb("y32", [128, DV], F32) for _ in range(2)]
                yb = [p.sb("yb", [128, DV], BF16) for _ in range(2)]
                psP = p.ps("gpsP", [128, 512], F32)
                psTr = p.ps("gpsTr", [128, 8, 128], BF16)
                psS = p.ps("gpsS", [128, 512], F32)
                psO = [p.ps("gpsO", [128, 512], F32) for _ in range(2)]
                psSt = [p.ps("gpsSt", [128, 512], F32) for _ in range(2)]
                psTr2 = p.ps("gpsTr2", [128, 8, 128], BF16)
                for hh in range(GH):
                    for kc in range(KC):
                        ft = hh * KC + kc
                        p.dma("sp", ldq[kc].v(), gqk[0][ft].v())
                        p.dma("sp", ldk[kc].v(), gqk[1][ft].v())
                        for tt in range(NT):
                            ts = slice(tt * TT, (tt + 1) * TT)
                            p.I("pe", "matmul", out=psP[:, 0:TT], lhsT=gw2[:, ft * 128:(ft + 1) * 128], rhs=lowT[:, ts], start=True, stop=True)
                            p.I("act", "activation", out=t1[:, ts], in_=psP[:, 0:TT], func=AF.Exp, bias=nb[:, ft:ft + 1], scale=-1.0)
                        p.I("act", "activation", out=t1.v(), in_=t1.v(), func=AF.Ln, bias=1.0, scale=1.0)
                        p.I("dve", "tensor_scalar", out=t1.v(), in0=t1.v(), scalar1=-1.0 / 16.0, scalar2=None, op0=ALU.mult)
                        p.I("dve", "tensor_tensor_scan", out=t2.v(), data0=rmask128.v(), data1=t1.v(), initial=0.0,
                            op0=ALU.mult, op1=ALU.add)
                        p.I("act", "activation", out=eb[kc].v(), in_=t2.v(), func=AF.Exp)
                        p.I("act", "activation", out=t1.v(), in_=t2.v(), func=AF.Exp, scale=-1.0)
                        p.I("dve", "scalar_tensor_tensor", out=QT[kc].v(), in0=ldq[kc].v(), scalar=float(DK) ** -0.5, in1=eb[kc].v(),
                            op0=ALU.mult, op1=ALU.mult)
                        p.I("dve", "tensor_tensor", out=KT[kc].v(), in0=ldk[kc].v(), in1=t1.v(), op=ALU.mult)
                        p.I("dve", "memset", ap=S32[kc].v(), constant=0.0)
                        p.I("dve", "memset", ap=Sb[kc].v(), constant=0.0)
                    for n in range(NCH):
                        ns = slice(n * 128, (n + 1) * 128)
                        b2 = n % 2
                        p.dma("sp", v32[b2].v(), gvg[0][hh][ns, :])
                        p.dma("sp", g32[b2].v(), gvg[1][hh][ns, :])
                        p.I("act", "copy", out=Vb[b2].v(), in_=v32[b2].v())
                        p.I("act", "activation", out=SG[b2].v(), in_=g32[b2].v(), func=AF.Silu)
                        for kc in range(KC):
                            p.I("pe", "transpose", out=psTr[:, kc, :], in_=KT[kc][:, ns], identity=ident_bf.v())
                        p.I("dve", "tensor_copy", out=KTM[b2].v().re("p (k x) -> p k x", x=128), in_=psTr[:, 0:KC, :])
                        for kc in range(KC):
                            p.I("pe", "matmul", out=psS[:, 0:128], lhsT=KT[kc][:, ns], rhs=QT[kc][:, ns], start=(kc == 0), stop=(kc == KC - 1))
                        p.I("dve", "tensor_tensor", out=ST[b2].v(), in0=psS[:, 0:128], in1=iu128, op=ALU.mult)
                        po = psO[b2]
                        p.I("pe", "matmul", out=po[:, 0:DV], lhsT=ST[b2].v(), rhs=Vb[b2].v(), start=True, stop=False)
                        for kc in range(KC):
                            p.I("pe", "matmul", out=po[:, 0:DV], lhsT=QT[kc][:, ns], rhs=Sb[kc].v(), start=False, stop=(kc == KC - 1))
                        for kc in range(KC):
                            pst = psSt[kc % 2]
                            p.I("pe", "matmul", out=pst[:, 0:DV], lhsT=KTM[b2][:, kc * 128:(kc + 1) * 128], rhs=Vb[b2].v(), start=True, stop=True)
                            dcol = eb[kc][:, n * 128 + 127:n * 128 + 128]
                            p.I("dve", "tensor_scalar", out=S32[kc].v(), in0=S32[kc].v(), scalar1=dcol, scalar2=None, op0=ALU.mult)
                            p.I("dve", "scalar_tensor_tensor", out=S32[kc].v(), in0=pst[:, 0:DV], scalar=dcol, in1=S32[kc].v(),
                                op0=ALU.mult, op1=ALU.add)
                            p.I("act", "copy", out=Sb[kc].v(), in_=S32[kc].v())
                        p.I("act", "activation", out=junk.v(), in_=po[:, 0:DV], func=AF.Square, accum_out=ssq[b2].v())
                        p.I("act", "activation", out=ssq[b2].v(), in_=ssq[b2].v(), func=AF.Sqrt, bias=NORM_EPS, scale=1.0 / DV)
                        p.I("dve", "reciprocal", out=ssq[b2].v(), in_=ssq[b2].v())
                        p.I("dve", "scalar_tensor_tensor", out=y32[b2].v(), in0=po[:, 0:DV], scalar=ssq[b2].v(), in1=hgb.v(),
                            op0=ALU.mult, op1=ALU.mult)
                        p.I("dve", "tensor_tensor", out=yb[b2].v(), in0=y32[b2].v(), in1=SG[b2].v(), op=ALU.mult)
                        for vc in range(VC):
                            p.I("pe", "transpose", out=psTr2[:, vc, :], in_=yb[b2][:, vc * 128:(vc + 1) * 128], identity=ident_bf.v())
                        for vc in range(VC):
                            p.I("act" if vc % 2 == 0 else "dve", "copy" if vc % 2 == 0 else "tensor_copy",
                                out=yg[hh * VC + vc][:, ns], in_=psTr2[:, vc, :])
            if cfg.stop <= 9:
                return False
            out_proj(yg, gl_out.v()[j], VW // 128, lambda ft: modT[:, l, 2 * DC + ft:2 * DC + ft + 1], src, dst)
            return True


    def ssd_layer(l, j, src, nextbuf):
        NCH = S // 128
        srcv = [src]
        HN = SH
        maskb = cst[:, 7, :]
        ident32 = cst[:, 0, :]
        with p.scope():
            dtT = p.sb("dtT", [128, S], F32)
            acT = p.sb("acT", [128, S], F32)
            nacT = p.sb("nacT", [128, S], F32)
            hv = p.sb("hv", [64, 2], F32)
            dsb = p.sb("dsb", [128, SH], F32)
            cw = p.sb("cw", [128, CW // 128, 4], F32)
            cbv = p.sb("cbv", [128, CW // 128], F32)
            wtm = p.sb("wtm", [128, NCH, 128], F32)
            eatm = p.sb("eatm", [128, NCH, 64], F32)
            decbc = p.sb("decbc", [128, NCH, 64], F32)
            p.dma("sp", hv.v(), sd_hv.v()[:, j])
            p.dma("sp", dsb.v(), sd_dsb.v()[:, j])
            p.dma("sp", cw.v(), sd_cwT.v()[:, j])
            p.dma("sp", cbv.v(), sd_cbT.v()[:, j])
            wv = sd_in.v()[j].re("(dc p) f -> p dc f", p=128)
            with p.scope():
                hT = [p.sb("hT", [128, S], BF16) for _ in range(DC)]
                norm_phase(src, hT, lambda dc: gsT[:, l, dc:dc + 1], lambda dc: modT[:, l, dc:dc + 1])
                wts = [p.sb("wi", [128, DC, 512], BF16) for _ in range(2)]
                wdt = p.sb("wdt", [128, DC, SH], BF16)
                pss = [p.ps("psp", [128, 512], F32) for _ in range(4)]
                stg = [p.sb("stg", [128, 512], F32) for _ in range(4)]
                xst = [p.sb("xst", [128, S + 3], F32) for _ in range(2)]
                acc = [p.sb("cacc", [128, S], F32) for _ in range(2)]
                sk = [0]

                def zsink(gi, tk, ps_):
                    s_ = stg[sk[0] % 4]
                    e = "act" if sk[0] % 2 == 0 else "dve"
                    sk[0] += 1
                    if e == "act":
                        p.I("act", "copy", out=s_.v(), in_=ps_.v())
                    else:
                        p.I("dve", "tensor_copy", out=s_.v(), in_=ps_.v())
                    p.dma("sp", sz[gi][tk * 128:(tk + 1) * 128, :], s_.v(), acc_w=True)

                proj_tm(hT, wv, 0, SG, 512, zsink, wts, pss)
                for b in range(2):
                    p.I("dve", "memset", ap=xst[b][:, 0:3], constant=0.0)

                def csink(ft, tt, ps_):
                    x_ = xst[ft % 2]
                    e = "act" if (ft + tt) % 2 == 0 else "dve"
                    if e == "act":
                        p.I("act", "copy", out=x_[:, 3 + tt * TT:3 + (tt + 1) * TT], in_=ps_[:, 0:TT])
                    else:
                        p.I("dve", "tensor_copy", out=x_[:, 3 + tt * TT:3 + (tt + 1) * TT], in_=ps_[:, 0:TT])
                    if tt == NT - 1:
                        a_ = acc[ft % 2]
                        p.I("act", "mul", out=a_.v(), in_=x_[:, 3:S + 3], mul=cw[:, ft, 3:4])
                        for kk_ in range(3):
                            p.I("dve", "scalar_tensor_tensor", out=a_.v(), in0=x_[:, kk_:S + kk_], scalar=cw[:, ft, kk_:kk_ + 1],
                                in1=a_.v(), op0=ALU.mult, op1=ALU.add)
                        p.I("act", "activation", out=a_.v(), in_=a_.v(), func=AF.Silu, bias=cbv[:, ft:ft + 1], scale=1.0)
                        p.dma("sp", sxbc[ft].v(), a_.v())

                proj_fm(hT, wv, SW, CW // 128, csink, wts, pss)
                p.dma("pool", wdt.v(), wv[:, :, SW + CW:SW + CW + SH])
                p.I("dve", "memset", ap=dtT.v(), constant=0.0)
                p.I("dve", "memset", ap=acT.v(), constant=0.0)
                for tt in range(NT):
                    ts = slice(tt * TT, (tt + 1) * TT)
                    ps_ = pss[tt % 4]
                    for dc in range(DC):
                        p.I("pe", "matmul", out=ps_[0:HN, 0:TT], lhsT=wdt[:, dc, :], rhs=hT[dc][:, ts], start=(dc == 0), stop=(dc == DC - 1))
                    p.I("act", "activation", out=dtT[0:HN, ts], in_=ps_[0:HN, 0:TT], func=AF.Exp, bias=hv[0:HN, 0:1], scale=1.0)
                p.I("act", "activation", out=dtT[0:HN, :], in_=dtT[0:HN, :], func=AF.Ln, bias=1.0, scale=1.0)
            if cfg.stop <= 2:
                return False
            with p.scope():
                eaT = p.sb("eaT", [128, S], F32)
                na = p.sb("na", [64, 1], F32)
                t1 = p.sb("st1", [128, S], F32)
                Dg = p.sb("Dg", [64, 64], F32)
                psq = [p.ps("spsq", [128, 512], F32) for _ in range(2)]
                p.I("act", "activation", out=na[0:HN, :], in_=hv[0:HN, 1:2], func=AF.Exp)
                p.I("dve", "tensor_scalar", out=na[0:HN, :], in0=na[0:HN, :], scalar1=-1.0, scalar2=None, op0=ALU.mult)
                p.I("dve", "tensor_scalar", out=t1[0:HN, :], in0=dtT[0:HN, :], scalar1=na[0:HN, 0:1], scalar2=None, op0=ALU.mult)
                p.I("dve", "tensor_tensor_scan", out=acT[0:HN, :], data0=rmask128[0:HN, :], data1=t1[0:HN, :], initial=0.0,
                    op0=ALU.mult, op1=ALU.add)
                p.I("dve", "memset", ap=nacT.v(), constant=0.0)
                p.I("dve", "memset", ap=eaT.v(), constant=0.0)
                p.I("dve", "tensor_scalar", out=nacT[0:HN, :], in0=acT[0:HN, :], scalar1=-1.0, scalar2=None, op0=ALU.mult)
                p.I("act", "activation", out=eaT[0:HN, :], in_=acT[0:HN, :], func=AF.Exp)
                for n in range(NCH):
                    ns = slice(n * 128, (n + 1) * 128)
                    last = acT[0:HN, n * 128 + 127:n * 128 + 128]
                    p.I("act", "activation", out=t1[0:HN, ns], in_=acT[0:HN, ns], func=AF.Exp, bias=last, scale=-1.0)
                    p.I("dve", "tensor_tensor", out=dtT[64:64 + HN, ns], in0=t1[0:HN, ns], in1=dtT[0:HN, ns], op=ALU.mult)
                    ps_ = psq[n % 2]
                    p.I("pe", "transpose", out=ps_[:, 0:128], in_=dtT[:, ns], identity=ident32)
                    p.I("pe", "transpose", out=ps_[:, 128:256], in_=eaT[:, ns], identity=ident32)
                    p.I("dve", "tensor_scalar", out=Dg[0:HN, 0:HN], in0=ident32[0:HN, 0:HN], scalar1=last, scalar2=None, op0=ALU.mult)
                    p.I("pe", "matmul", out=ps_[:, 256:256 + HN], lhsT=ones32[0:HN, :], rhs=Dg[0:HN, 0:HN], start=True, stop=True)
                    p.I("act", "copy", out=wtm[:, n, :], in_=ps_[:, 0:128])
                    p.I("dve", "tensor_copy", out=eatm[:, n, :], in_=ps_[:, 128:192])
                    p.I("act", "activation", out=decbc[:, n, 0:HN], in_=ps_[:, 256:256 + HN], func=AF.Exp)
            if cfg.stop <= 3:
                return False
            nhalf = 2 if SG >= 2 else 1
            GPH = SG // nhalf
            yg = [p.sb("ygs", [128, S], BF16) for _ in range(GPH * 4)]
            for half in range(nhalf):
              with p.scope():
                xg = [p.sb("xg", [128, 4, 128], F32) for _ in range(2)]
                bg = p.sb("bg", [128, S], F32)
                BT = p.sb("BT", [128, S], BF16)
                CT = p.sb("CT", [128, S], BF16)
                ngb = p.sb("ngb", [128, 512], F32)
                prev32 = p.sb("prev32", [128, 512], F32)
                prevb = p.sb("prevb", [128, 512], BF16)
                xtm = [p.sb("xtm", [128, 512], F32) for _ in range(2)]
                xc = [p.sb("xc", [128, 512], BF16) for _ in range(2)]
                xcd = [p.sb("xcd", [128, 512], BF16) for _ in range(2)]
                Btm = [p.sb("Btm", [128, 128], BF16) for _ in range(2)]
                cbT = [p.sb("cbT", [128, 128], BF16) for _ in range(2)]
                eM = [p.sb("eM", [128, 4, 128], F32) for _ in range(2)]
                Mm = [p.sb("Mm", [128, 4, 128], BF16) for _ in range(2)]
                z32 = [p.sb("z32", [128, 512], F32) for _ in range(2)]
                ty = [p.sb("ty", [128, 512], F32) for _ in range(2)]
                tu = [p.sb("tu", [128, 512], F32) for _ in range(2)]
                junk = p.sb("sjunk", [128, 512], F32)
                ssq = [p.sb("sssq", [128, 1], F32) for _ in range(2)]
                ybf = [p.sb("ybf", [128, 512], BF16) for _ in range(2)]
                psX = p.ps("spsX", [128, 512], F32)
                psB = p.ps("spsB", [128, 8, 128], BF16)
                psC = p.ps("spsC", [128, 512], F32)
                psM = [p.ps("spsM", [128, 4, 128], F32) for _ in range(2)]
                psY = p.ps("spsY", [128, 512], F32)
                psYo = p.ps("spsYo", [128, 512], F32)
                psSt = p.ps("spsSt", [128, 512], F32)
                for g in range(half * GPH, (half + 1) * GPH):
                    p.dma("sp", bg.v(), sxbc[SW // 128 + g].v())
                    p.I("act", "copy", out=BT.v(), in_=bg.v())
                    p.dma("sp", bg.v(), sxbc[SW // 128 + SG + g].v())
                    p.I("dve", "tensor_copy", out=CT.v(), in_=bg.v())
                    p.dma("sp", ngb.v(), sd_ngb.v()[:, j, g * 512:(g + 1) * 512])
                    p.I("dve", "memset", ap=prev32.v(), constant=0.0)
                    p.I("dve", "memset", ap=prevb.v(), constant=0.0)
                    for n in range(NCH):
                        ns = slice(n * 128, (n + 1) * 128)
                        b2 = n % 2
                        hs8 = slice(g * 8, (g + 1) * 8)
                        p.dma("sp", z32[b2].v(), sz[g][ns, :])
                        for i4 in range(4):
                            p.dma("sp", xg[b2][:, i4, :], sxbc[g * 4 + i4][:, ns])
                        for i4 in range(4):
                            p.I("pe", "transpose", out=psX[:, i4 * 128:(i4 + 1) * 128], in_=xg[b2][:, i4, :], identity=ident32)
                        p.I("act", "copy", out=xtm[b2].v(), in_=psX.v())
                        x3 = xtm[b2].v().re("p (h x) -> p h x", x=64)
                        p.I("dve", "tensor_tensor", out=xc[b2].v().re("p (h x) -> p h x", x=64), in0=x3,
                            in1=wtm[:, n, g * 8:(g + 1) * 8].bc([2], [128, 8, 64]), op=ALU.mult)
                        p.I("dve", "tensor_tensor", out=xcd[b2].v().re("p (h x) -> p h x", x=64), in0=x3,
                            in1=wtm[:, n, 64 + g * 8:64 + (g + 1) * 8].bc([2], [128, 8, 64]), op=ALU.mult)
                        p.I("pe", "transpose", out=psB[:, 0, :], in_=BT[:, ns], identity=ident_bf.v())
                        p.I("dve", "tensor_copy", out=Btm[b2].v(), in_=psB[:, 0, :])
                        p.I("pe", "matmul", out=psC[:, 0:128], lhsT=BT[:, ns], rhs=CT[:, ns], start=True, stop=True)
                        p.I("act", "copy", out=cbT[b2].v(), in_=psC[:, 0:128])
                        for hq in range(2):
                            pm = psM[hq]
                            for h4 in range(4):
                                h = g * 8 + hq * 4 + h4
                                sel = ident32[0:HN, h:h + 1].bc([], [HN, 128])
                                p.I("pe", "matmul", out=pm[:, h4, :], lhsT=sel, rhs=acT[0:HN, ns], start=True, stop=False)
                                p.I("pe", "matmul", out=pm[:, h4, :], lhsT=nacT[0:HN, ns], rhs=sel, start=False, stop=False)
                                p.I("pe", "matmul", out=pm[:, h4, :], lhsT=ident32, rhs=maskb, start=False, stop=True)
                            p.I("act", "activation", out=eM[hq].v(), in_=pm.v(), func=AF.Exp)
                            p.I("dve", "tensor_tensor", out=Mm[hq].v(), in0=eM[hq].v(),
                                in1=cbT[b2].v().bc([1], [128, 4, 128]), op=ALU.mult)
                        for hq in range(2):
                            for h4 in range(4):
                                hl = hq * 4 + h4
                                p.I("pe", "matmul", out=psY[:, hl * 64:(hl + 1) * 64], lhsT=Mm[hq][:, h4, :],
                                    rhs=xc[b2][:, hl * 64:(hl + 1) * 64], start=True, stop=True)
                        p.I("pe", "matmul", out=psYo.v(), lhsT=CT[:, ns], rhs=prevb.v(), start=True, stop=True)
                        p.I("pe", "matmul", out=psSt.v(), lhsT=Btm[b2].v(), rhs=xcd[b2].v(), start=True, stop=True)
                        t_ = ty[b2]
                        u_ = tu[b2]
                        t3 = t_.v().re("p (h x) -> p h x", x=64)
                        u3 = u_.v().re("p (h x) -> p h x", x=64)
                        p.I("dve", "tensor_tensor", out=t3, in0=psYo.v().re("p (h x) -> p h x", x=64),
                            in1=eatm[:, n, hs8].bc([2], [128, 8, 64]), op=ALU.mult)
                        p.I("dve", "tensor_tensor", out=t_.v(), in0=psY.v(), in1=t_.v(), op=ALU.add)
                        p.I("dve", "tensor_tensor", out=u3, in0=x3, in1=dsb[:, hs8].bc([2], [128, 8, 64]), op=ALU.mult)
                        p.I("dve", "tensor_tensor", out=t_.v(), in0=t_.v(), in1=u_.v(), op=ALU.add)
                        p32 = prev32.v().re("p (h x) -> p h x", x=64)
                        p.I("dve", "tensor_tensor", out=p32, in0=p32, in1=decbc[:, n, hs8].bc([2], [128, 8, 64]), op=ALU.mult)
                        p.I("dve", "tensor_tensor", out=prev32.v(), in0=psSt.v(), in1=prev32.v(), op=ALU.add)
                        p.I("act", "copy", out=prevb.v(), in_=prev32.v())
                        p.I("act", "activation", out=u_.v(), in_=z32[b2].v(), func=AF.Silu)
                        p.I("dve", "tensor_tensor", out=t_.v(), in0=t_.v(), in1=u_.v(), op=ALU.mult)
                        p.I("act", "activation", out=junk.v(), in_=t_.v(), func=AF.Square, accum_out=ssq[b2].v())
                        p.I("act", "activation", out=ssq[b2].v(), in_=ssq[b2].v(), func=AF.Sqrt, bias=1e-5, scale=1.0 / 512)
                        p.I("dve", "reciprocal", out=ssq[b2].v(), in_=ssq[b2].v())
                        p.I("dve", "scalar_tensor_tensor", out=ybf[b2].v(), in0=t_.v(), scalar=ssq[b2].v(), in1=ngb.v(),
                            op0=ALU.mult, op1=ALU.mult)
                        for i4 in range(4):
                            p.I("pe", "transpose", out=psB[:, 4 + i4, :], in_=ybf[b2][:, i4 * 128:(i4 + 1) * 128], identity=ident_bf.v())
                        for i4 in range(4):
                            p.I("act" if i4 % 2 == 0 else "dve", "copy" if i4 % 2 == 0 else "tensor_copy",
                                out=yg[(g - half * GPH) * 4 + i4][:, ns], in_=psB[:, 4 + i4, :])
              if cfg.stop <= 9:
                  return False
              nci = GPH * 4
              dsth = nextbuf()
              out_proj(yg, sd_out.v()[j][half * nci * 128:(half + 1) * nci * 128, :], nci,
                       lambda ft: modT[:, l, 2 * DC + ft:2 * DC + ft + 1], srcv[0], dsth.v())
              srcv[0] = dsth.v()
            return srcv[0]

    def rwkv_layer(l, j, src, dst):
        CB, TB = cfg.CB, cfg.TB
        NCHB = TB // 64
        with p.scope():
            yg = [p.sb("yg", [128, S], BF16) for _ in range(HP)]
            xs = yg
            lw1 = p.sb("lw1", [R, S], BF16)
            la1 = p.sb("la1", [R, S], BF16)
            vec = p.sb("rvec", [128, 7, HP], F32)
            omka = p.sb("omka", [128, HP], F32)
            p.dma("sp", vec.v(), rw_vecT.v()[:, j])
            p.I("dve", "tensor_scalar", out=omka.v(), in0=vec[:, 3, :], scalar1=-1.0, scalar2=1.0,
                op0=ALU.mult, op1=ALU.add)
            with p.scope():
                hT = [p.sb("hT", [128, S], BF16) for _ in range(DC)]
                mu = p.sb("mu", [128, 6, DC], F32)
                omm = p.sb("omm", [128, 6, DC], F32)
                p.dma("sp", mu.v(), rw_muT.v()[:, j])
                p.I("dve", "tensor_scalar", out=omm.v(), in0=mu.v(), scalar1=-1.0, scalar2=1.0,
                    op0=ALU.mult, op1=ALU.add)
                norm_phase(src, hT, lambda dc: gsT[:, l, dc:dc + 1], lambda dc: modT[:, l, dc:dc + 1])
                if cfg.stop <= 1:
                    return False
                wts = [p.sb("wi", [128, DC, 512], BF16) for _ in range(2)]
                w1t = p.sb("w1t", [128, DC, R], BF16)
                pss = [p.ps("psp", [128, 512], F32) for _ in range(4)]
                stg = [p.sb("stg", [128, TT], F32) for _ in range(4)]
                wv = rw_in.v()[j].re("(dc p) f -> p dc f", p=128)
                sk = [0]

                def mix(c):
                    for dc in range(DC):
                        p.I("dve", "memset", ap=xs[dc][:, 0:1], constant=0.0)
                        p.I("act", "mul", out=xs[dc][:, 1:S], in_=hT[dc][:, 0:S - 1], mul=mu[:, c, dc:dc + 1])
                        p.I("dve", "scalar_tensor_tensor", out=xs[dc].v(), in0=hT[dc].v(), scalar=omm[:, c, dc:dc + 1],
                            in1=xs[dc].v(), op0=ALU.mult, op1=ALU.add)

                import os as _os
                for c in range(4):
                    if not _os.environ.get("NOMIX") or c == 0:
                        mix(c)

                    def sink(ft, tt, ps_, c=c):
                        s_ = stg[sk[0] % 4]
                        e = "act" if sk[0] % 2 == 0 else "dve"
                        sk[0] += 1
                        if e == "act":
                            p.I("act", "copy", out=s_.v(), in_=ps_[:, 0:TT])
                        else:
                            p.I("dve", "tensor_copy", out=s_.v(), in_=ps_[:, 0:TT])
                        if not _os.environ.get("NOSTORE"):
                            p.dma("sp", projT[c][ft][:, tt * TT:(tt + 1) * TT], s_.v(), acc_w=True)

                    proj_fm(xs, wv, c * W, HP, sink, wts, pss)
                for c, (w1d, dstl, fn) in ((4, (rw_dw1, lw1, AF.Tanh)), (5, (rw_aw1, la1, AF.Copy))):
                    mix(c)
                    p.dma("pool", w1t.v(), w1d.v()[j].re("(dc p) r -> p dc r", p=128))
                    for tt in range(NT):
                        ts = slice(tt * TT, (tt + 1) * TT)
                        ps_ = pss[tt % 4]
                        for dc in range(DC):
                            p.I("pe", "matmul", out=ps_[0:R, 0:TT], lhsT=w1t[:, dc, :], rhs=xs[dc][:, ts],
                                start=(dc == 0), stop=(dc == DC - 1))
                        p.I("act", "activation", out=dstl[:, ts], in_=ps_[0:R, 0:TT], func=fn)
            if cfg.stop <= 2:
                return False
            with p.scope():
                dw2 = p.sb("dw2", [R, W], BF16)
                aw2 = p.sb("aw2", [R, W], BF16)
                p.dma("pool", dw2.v(), rw_dw2.v()[j])
                p.dma("pool", aw2.v(), rw_aw2.v()[j])
                ld = {nm: p.sb("ld_" + nm, [128, TB], F32) for nm in ("r", "k", "v", "g")}
                tm = {nm: p.sb("tm_" + nm, [128, TB], F32) for nm in
                      ("lw", "cum", "e1", "e2", "e3", "a", "kk", "kf", "t1", "t2", "t3", "bv", "y")}
                BKT = p.sb("BKT", [128, NCHB, 2, 64], BF16)
                KRT = p.sb("KRT", [128, NCHB, 2, 64], BF16)
                KKVT = p.sb("KKVT", [128, NCHB, 2, 64], BF16)
                CBS, NSTR = 2, 2
                STR = []
                psTrS = p.ps("psTr", [128, 4, 2, 128], BF16)
                for si in range(NSTR):
                    pg_ = p.ps("PG", [128, 2, 512], F32)
                    xr_ = p.sb("Xr", [64, CBS * 2, 2, 64], BF16)
                    nxt_ = p.sb("NXT", [64, CBS * 2, 192], BF16)
                    mu_ = p.sb("MU", [64, CBS * 2, 128], BF16)
                    STR.append([dict(
                        BK=p.sb("BK", [128, CBS, 128], BF16), UV=p.sb("UV", [128, CBS, 128], BF16),
                        Xr=xr_, A_sb=p.sb("A_sb", [128, CBS * 2, 128], BF16), NXT=nxt_, MU=mu_,
                        GT=p.sb("GT", [128, CBS, 64], BF16), PpT=p.sb("PpT", [128, CBS, 64], BF16),
                        PG=pg_, psTr=psTrS, toff=si * CBS) for _par in range(2)])
                psS5 = p.ps("psS5", [128, 4, 128], F32)
                Tst = [p.sb("Tst", [128, 64], BF16) for _ in range(3)]
                psP1 = p.ps("psP", [128, 512], F32)
                psP = [psP1, psP1]
                psAV = p.ps("psAV", [128, CBS * 2, 128], F32)
                NTB = TB // TT if TB >= TT else 1
                TTB = min(TT, TB)
                import os as _os2
                for hp in range(int(_os2.environ.get('RW_HP', HP))):
                    hsl = slice(hp * 128, (hp + 1) * 128)
                    vcol = lambda i: vec[:, i, hp:hp + 1]
                    ti = 0
                    p.I("dve", "memset", ap=Tst[0].v(), constant=0.0)
                    for tb in range(S // TB):
                        tbs = slice(tb * TB, (tb + 1) * TB)
                        for c, nm in enumerate(("r", "k", "v", "g")):
                            p.dma("sp", ld[nm].v(), projT[c][hp][:, tbs])
                        r_, k_, v_, g_ = ld["r"], ld["k"], ld["v"], ld["g"]
                        lw, cum, e1, e2, e3, a_, kk, kf, t1, t2, t3, bv, yT = (tm[n] for n in (
                            "lw", "cum", "e1", "e2", "e3", "a", "kk", "kf", "t1", "t2", "t3", "bv", "y"))
                        for tt in range(NTB):
                            ts = slice(tt * TTB, (tt + 1) * TTB)
                            gs_ = slice(tb * TB + tt * TTB, tb * TB + (tt + 1) * TTB)
                            ps_ = psP[0]
                            p.I("pe", "matmul", out=ps_[:, 0:TTB], lhsT=dw2[:, hsl], rhs=lw1[:, gs_], start=True, stop=True)
                            p.I("act", "activation", out=lw[:, ts], in_=ps_[:, 0:TTB], func=AF.Sigmoid, bias=vcol(0), scale=1.0)
                            ps_ = psP[1]
                            p.I("pe", "matmul", out=ps_[:, 0:TTB], lhsT=aw2[:, hsl], rhs=la1[:, gs_], start=True, stop=True)
                            p.I("act", "activation", out=a_[:, ts], in_=ps_[:, 0:TTB], func=AF.Sigmoid, bias=vcol(1), scale=1.0)
                        p.I("dve", "tensor_scalar", out=lw.v(), in0=lw.v(), scalar1=NEG_EXP_HALF, scalar2=None, op0=ALU.mult)
                        p.I("dve", "tensor_tensor_scan", out=cum.v(), data0=rmask[:, 0:TB], data1=lw.v(), initial=0.0,
                            op0=ALU.mult, op1=ALU.add)
                        p.I("act", "activation", out=e1.v(), in_=cum.v(), func=AF.Exp)
                        p.I("act", "activation", out=e2.v(), in_=cum.v(), func=AF.Exp, scale=-1.0)
                        p.I("dve", "tensor_tensor", out=t1.v(), in0=cum.v(), in1=lw.v(), op=ALU.subtract)
                        p.I("act", "activation", out=e3.v(), in_=t1.v(), func=AF.Exp)
                        p.I("act", "activation", out=t2.v(), in_=k_.v(), func=AF.Square, scale=vcol(2))
                        for tt in range(NTB):
                            ts = slice(tt * TTB, (tt + 1) * TTB)
                            ps_ = psP[tt % 2]
                            p.I("pe", "matmul", out=ps_[:, 0:TTB], lhsT=bones32, rhs=t2[:, ts], start=True, stop=True)
                            p.I("act", "activation", out=t3[:, ts], in_=ps_[:, 0:TTB], func=AF.Sqrt)
                        p.I("dve", "tensor_scalar", out=t3.v(), in0=t3.v(), scalar1=1e-12, scalar2=None, op0=ALU.max)
                        p.I("dve", "reciprocal", out=t3.v(), in_=t3.v())
                        p.I("dve", "scalar_tensor_tensor", out=kk.v(), in0=k_.v(), scalar=vcol(2), in1=t3.v(), op0=ALU.mult, op1=ALU.mult)
                        p.I("dve", "tensor_scalar", out=t1.v(), in0=a_.v(), scalar1=vcol(3), scalar2=omka[:, hp:hp + 1],
                            op0=ALU.mult, op1=ALU.add)
                        p.I("dve", "tensor_tensor", out=kf.v(), in0=k_.v(), in1=t1.v(), op=ALU.mult)
                        p.I("dve", "tensor_tensor", out=t2.v(), in0=kk.v(), in1=a_.v(), op=ALU.mult)
                        ch = lambda t: t.v().re("p (n c) -> p n c", c=64)
                        p.I("dve", "tensor_tensor", out=KRT[:, :, 1, :], in0=ch(r_), in1=ch(e1), op=ALU.mult)
                        p.I("dve", "tensor_tensor", out=BKT[:, :, 1, :], in0=ch(kf), in1=ch(e2), op=ALU.mult)
                        p.I("dve", "tensor_tensor", out=BKT[:, :, 0, :], in0=ch(t2), in1=ch(e2), op=ALU.mult)
                        p.I("dve", "tensor_tensor", out=KRT[:, :, 0, :], in0=ch(kk), in1=ch(e3), op=ALU.mult)
                        p.I("act", "copy", out=KKVT[:, :, 0, :], in_=KRT[:, :, 0, :])
                        p.I("act", "copy", out=KKVT[:, :, 1, :], in_=ch(v_))
                        p.I("dve", "scalar_tensor_tensor", out=t1.v(), in0=r_.v(), scalar=vcol(4), in1=kf.v(),
                            op0=ALU.mult, op1=ALU.mult)
                        for tt in range(NTB):
                            ts = slice(tt * TTB, (tt + 1) * TTB)
                            ps_ = psP[tt % 2]
                            p.I("pe", "matmul", out=ps_[:, 0:TTB], lhsT=bones32, rhs=t1[:, ts], start=True, stop=True)
                            p.I("dve", "tensor_tensor", out=bv[:, ts], in0=ps_[:, 0:TTB], in1=v_[:, ts], op=ALU.mult)
                        if cfg.stop <= 3:
                            return False
                        tiref = [ti]

                        def group_stream(c0, cb_n, T):
                            BK, UV, Xr, A_sb, NXT, MU, GT, PpT, PG, psTr = (T[k_] for k_ in
                                ("BK", "UV", "Xr", "A_sb", "NXT", "MU", "GT", "PpT", "PG", "psTr"))
                            psTr = psTr[:, T["toff"]:T["toff"] + CBS]
                            PGv = PG.v().re("p h (c x) -> p h c x", c=CBS)
                            hc = lambda t: t.v().re("p (h c) x -> p h c x", h=2)[:, :, 0:cb_n, :]
                            for cb in range(cb_n):
                                n = c0 + cb
                                p.I("pe", "transpose", out=psTr[:, cb, 0, :], in_=BKT[:, n].re("p a c -> p (a c)"), identity=ident_bf.v())
                                p.I("pe", "transpose", out=psTr[:, cb, 1, :], in_=KKVT[:, n].re("p a c -> p (a c)"), identity=ident_bf.v())
                            p.I("dve", "tensor_copy", out=BK[:, 0:cb_n, :], in_=psTr[:, 0:cb_n, 0, :])
                            p.I("dve", "tensor_copy", out=Xr.v().re("p (h c) a x -> p h c a x", h=2)[:, :, 0:cb_n, 0, :],
                                in_=psTr[0:64, 0:cb_n, 1, :].re("p c (h x) -> p h c x", h=2))
                            p.I("dve", "tensor_copy", out=UV[64:128, 0:cb_n, :], in_=psTr[64:128, 0:cb_n, 1, :])
                            for cb in range(cb_n):
                                n = c0 + cb
                                for h in range(2):
                                    hs = slice(h * 64, (h + 1) * 64)
                                    p.I("pe", "matmul", out=PGv[:, h, cb, 0:128],
                                        lhsT=BKT[hs, n].re("p a c -> p (a c)"), rhs=KRT[hs, n].re("p a c -> p (a c)"),
                                        start=True, stop=True)
                                    p.I("pe", "matmul", out=PGv[0:64, h, cb, 128:192],
                                        lhsT=KRT[hs, n, 0, :], rhs=BKT[hs, n, 0, :], start=True, stop=True)
                            pgA = PGv[:, :, 0:cb_n, 0:128]
                            p.I("act", "copy", out=hc(A_sb), in_=pgA)
                            p.I("dve", "tensor_tensor", out=hc(A_sb), in0=hc(A_sb),
                                in1=maskA.bc([1, 1], [128, 2, cb_n, 128]), op=ALU.mult)
                            nx = hc(NXT)
                            p.I("dve", "tensor_tensor", out=nx[:, :, :, 0:64], in0=hc(A_sb)[0:64, :, :, 0:64],
                                in1=negSU.bc([1, 1], [64, 2, cb_n, 64]), op=ALU.mult)
                            p.I("dve", "tensor_tensor", out=nx[:, :, :, 64:128], in0=nx[:, :, :, 0:64],
                                in1=cst[0:64, 5, 0:64].bc([1, 1], [64, 2, cb_n, 64]), op=ALU.add)
                            p.I("dve", "tensor_tensor", out=nx[:, :, :, 128:192],
                                in0=PGv[0:64, :, 0:cb_n, 128:192],
                                in1=negSL.bc([1, 1], [64, 2, cb_n, 64]), op=ALU.mult)
                            yield
                            for cb in range(cb_n):
                                for h in range(2):
                                    q = h * CBS + cb
                                    p.I("pe", "matmul", out=psAV[0:64, q, 0:64], lhsT=A_sb[64:128, q, 0:64],
                                        rhs=UV[64:128, cb, h * 64:(h + 1) * 64], start=True, stop=True)
                            pgI = PGv[0:64, :, 0:cb_n, 0:192]
                            for rnd in range(6):
                                for cb in range(cb_n):
                                    for h in range(2):
                                        q = h * CBS + cb
                                        if rnd == 0:
                                            p.I("pe", "matmul", out=PGv[0:64, h, cb, 0:64], lhsT=NXT[:, q, 128:192],
                                                rhs=NXT[:, q, 0:64], start=True, stop=True)
                                        elif rnd < 5:
                                            p.I("pe", "matmul", out=PGv[0:64, h, cb, 0:128], lhsT=NXT[:, q, 128:192],
                                                rhs=NXT[:, q, 0:128], start=True, stop=True)
                                        else:
                                            p.I("pe", "matmul", out=PGv[0:64, h, cb, 64:128], lhsT=NXT[:, q, 128:192],
                                                rhs=NXT[:, q, 64:128], start=True, stop=True)
                                        if rnd < 5:
                                            p.I("pe", "matmul", out=PGv[0:64, h, cb, 128:192], lhsT=NXT[:, q, 0:64],
                                                rhs=NXT[:, q, 128:192], start=True, stop=True)
                                if rnd == 0:
                                    p.I("act", "copy", out=Xr.v().re("p (h c) a x -> p h c a x", h=2)[:, :, 0:cb_n, 1, :],
                                        in_=psAV.v().re("p (h c) x -> p h c x", h=2)[0:64, :, 0:cb_n, 0:64])
                                if rnd > 0:
                                    p.I("dve", "tensor_tensor", out=nx[:, :, :, 64:128], in0=pgI[:, :, :, 64:128],
                                        in1=nx[:, :, :, 64:128], op=ALU.add)
                                if rnd < 5:
                                    p.I("act", "copy", out=nx[:, :, :, 0:64], in_=pgI[:, :, :, 0:64])
                                    p.I("act", "copy", out=nx[:, :, :, 128:192], in_=pgI[:, :, :, 128:192])
                                yield
                            for cb in range(cb_n):
                                for h in range(2):
                                    q = h * CBS + cb
                                    p.I("pe", "matmul", out=PGv[0:64, h, cb, 0:128], lhsT=NXT[:, q, 64:128],
                                        rhs=Xr[:, q].re("p a c -> p (a c)"), start=True, stop=True)
                            pgM = PGv[0:64, :, 0:cb_n, 0:128]
                            p.I("act", "mul", out=hc(MU), in_=pgM, mul=-1.0)
                            p.I("dve", "tensor_scalar", out=UV[0:64, 0:cb_n, :].re("p c (h x) -> p h c x", h=2),
                                in0=pgM[:, :, :, 64:128], scalar1=-1.0, scalar2=None, op0=ALU.mult)
                            yield
                            for cb in range(cb_n):
                                for h in range(2):
                                    q = h * CBS + cb
                                    hs = slice(h * 64, (h + 1) * 64)
                                    p.I("pe", "matmul", out=PGv[hs, h, cb, 0:64], lhsT=MU[:, q, 0:64], rhs=A_sb[0:64, q, 64:128],
                                        start=True, stop=True)
                                    p.I("pe", "matmul", out=PGv[hs, h, cb, 64:128], lhsT=MU[:, q, 0:64], rhs=BK[0:64, cb, h * 64:(h + 1) * 64],
                                        start=True, stop=True)
                            for h in range(2):
                                hs = slice(h * 64, (h + 1) * 64)
                                p.I("dve", "tensor_tensor", out=GT[hs, 0:cb_n, :], in0=PGv[hs, h, 0:cb_n, 0:64],
                                    in1=KRT[hs, c0:c0 + cb_n, 1, :], op=ALU.add)
                                p.I("dve", "tensor_tensor", out=PpT[hs, 0:cb_n, :], in0=PGv[hs, h, 0:cb_n, 64:128],
                                    in1=cst[hs, 5, 0:64].bc([1], [64, cb_n, 64]), op=ALU.add)
                            yield
                            return

                        def back(sets):
                            slot = 0
                            c0g = sets[0][1]
                            for (T, c0, cb_n) in sets:
                                BK, UV, A_sb, GT, PpT = (T[k_] for k_ in ("BK", "UV", "A_sb", "GT", "PpT"))
                                for cb in range(cb_n):
                                    n = c0 + cb
                                    Tc, Tn = Tst[tiref[0] % 3], Tst[(tiref[0] + 1) % 3]
                                    tiref[0] += 1
                                    for h in range(2):
                                        hs = slice(h * 64, (h + 1) * 64)
                                        p.I("pe", "matmul", out=psS5[hs, slot, 0:64], lhsT=PpT[hs, cb, :], rhs=Tc[hs, :], start=True, stop=False)
                                        p.I("pe", "matmul", out=psS5[hs, slot, 0:64], lhsT=BK[:, cb, hs], rhs=UV[:, cb, hs], start=False, stop=True)
                                    yield
                                    for h in range(2):
                                        hs = slice(h * 64, (h + 1) * 64)
                                        p.I("dve", "tensor_scalar", out=Tn[hs, :], in0=psS5[hs, slot, 0:64],
                                            scalar1=e1[hs, n * 64 + 63:n * 64 + 64], scalar2=None, op0=ALU.mult)
                                    for h in range(2):
                                        q = h * CBS + cb
                                        hs = slice(h * 64, (h + 1) * 64)
                                        p.I("pe", "matmul", out=psS5[hs, slot, 64:128], lhsT=Tc[hs, :], rhs=GT[hs, cb, :], start=True, stop=False)
                                        p.I("pe", "matmul", out=psS5[hs, slot, 64:128], lhsT=UV[:, cb, hs], rhs=A_sb[:, q, 64:128], start=False, stop=True)
                                    slot += 1
                                    yield
                            for h in range(2):
                                hs = slice(h * 64, (h + 1) * 64)
                                p.I("act", "copy", out=yT.v().re("p (n c) -> p n c", c=64)[hs, c0g:c0g + slot, :],
                                    in_=psS5[hs, 0:slot, 64:128])

                        def drive(gens):
                            alive = list(gens)
                            while alive:
                                nxt = []
                                for gq in alive:
                                    try:
                                        next(gq)
                                        nxt.append(gq)
                                    except StopIteration:
                                        pass
                                alive = nxt

                        prev_sets = None
                        for gi_, g0 in enumerate(range(0, NCHB, CBS * NSTR)):
                            gens, sets = [], []
                            for si in range(NSTR):
                                c0 = g0 + si * CBS
                                if c0 < NCHB:
                                    T_ = STR[si][gi_ % 2]
                                    cbn_ = min(CBS, NCHB - c0)
                                    gens.append(group_stream(c0, cbn_, T_))
                                    sets.append((T_, c0, cbn_))
                            if prev_sets is not None:
                                gens.append(back(prev_sets))
                            drive(gens)
                            prev_sets = sets
                        drive([back(prev_sets)])
                        ti = tiref[0]
                        if cfg.stop <= 8:
                            return False
                        p.I("act", "activation", out=t2.v(), in_=yT.v(), func=AF.Square)
                        HW_ = min(256, TTB)
                        for tt in range(TB // HW_):
                            ts = slice(tt * HW_, (tt + 1) * HW_)
                            p.I("pe", "matmul", out=psP1[:, 0:HW_], lhsT=bones32, rhs=yT[:, ts], start=True, stop=True)
                            p.I("pe", "matmul", out=psP1[:, 256:256 + HW_], lhsT=bones32, rhs=t2[:, ts], start=True, stop=True)
                            p.I("act", "mul", out=t1[:, ts], in_=psP1[:, 0:HW_], mul=1.0 / 64)
                            p.I("dve", "tensor_tensor", out=t3[:, ts], in0=t1[:, ts], in1=t1[:, ts], op=ALU.mult)
                            p.I("dve", "scalar_tensor_tensor", out=t3[:, ts], in0=psP1[:, 256:256 + HW_], scalar=1.0 / 64, in1=t3[:, ts],
                                op0=ALU.mult, op1=ALU.subtract)
                        p.I("act", "activation", out=t3.v(), in_=t3.v(), func=AF.Sqrt, bias=RWKV_GN_EPS, scale=1.0)
                        p.I("dve", "reciprocal", out=t3.v(), in_=t3.v())
                        p.I("dve", "tensor_tensor", out=t1.v(), in0=yT.v(), in1=t1.v(), op=ALU.subtract)
                        p.I("dve", "tensor_tensor", out=t1.v(), in0=t1.v(), in1=t3.v(), op=ALU.mult)
                        p.I("act", "activation", out=t1.v(), in_=t1.v(), func=AF.Identity, bias=vcol(6), scale=vcol(5))
                        p.I("dve", "tensor_tensor", out=t1.v(), in0=t1.v(), in1=bv.v(), op=ALU.add)
                        p.I("act", "activation", out=t2.v(), in_=g_.v(), func=AF.Silu)
                        p.I("dve", "tensor_tensor", out=yg[hp][:, tbs], in0=t1.v(), in1=t2.v(), op=ALU.mult)
            if cfg.stop <= 9:
                return False
            out_proj(yg, rw_out.v()[j], HP, lambda ft: modT[:, l, 2 * DC + ft:2 * DC + ft + 1], src, dst)
            return True

    bufs = xres
    bi = [0]

    def nextbuf():
        b_ = bufs[bi[0] % len(bufs)]
        bi[0] += 1
        return b_

    cur = xT.v()
    counters = {0: 0, 1: 0, 2: 0}
    for l, kind in enumerate(cfg.kinds):
        j = counters[kind]
        counters[kind] += 1
        if kind in (0, 1):
            dst = nextbuf()
            ok = (rwkv_layer if kind == 0 else gla_layer)(l, j, cur, dst.v())
            if ok:
                cur = dst.v()
        else:
            r_ = ssd_layer(l, j, cur, nextbuf)
            if r_ is not False:
                cur = r_
    with p.scope():
        fg = p.sb("fg", [128, DC], F32)
        p.dma("sp", fg.v(), final_gT.v())
        norm_phase(cur, None, lambda dc: fg[:, dc:dc + 1], None, out_dram=outT.v())
    p.emit()
    return nc, p


def _pp(vec, nchunk):
    v = np.asarray(vec, np.float32)
    lead = v.shape[:-1]
    v = v.reshape(lead + (nchunk, 128))
    return np.ascontiguousarray(np.moveaxis(v, -1, 0))


def prepare_inputs(cfg, inp, n_cores, batch_of_core):
    D, S, DC, L = cfg.D, cfg.S, cfg.DC, cfg.L
    consts, rmask, rmask128 = make_consts(cfg)
    shared = {
        "ada_w": np.ascontiguousarray(inp["ada_w"], dtype=np.float32),
        "ada_bT": _pp(inp["ada_b"], 3 * DC),
        "norm_gT": _pp(inp["norm_g"], DC),
        "final_gT": _pp(inp["final_g"], DC),
        "consts": consts, "rmask": rmask, "rmask128": rmask128,
    }
    if cfg.nR:
        HP = D // 128
        for k in ("rwkv_w_in", "rwkv_w_out", "rwkv_dec_w1", "rwkv_dec_w2", "rwkv_iclr_w1", "rwkv_iclr_w2"):
            shared[k] = np.ascontiguousarray(inp[k], dtype=np.float32)
        shared["rwkv_muT"] = _pp(inp["rwkv_mu"], DC)
        vecs = np.stack([inp["rwkv_dec_w0"], inp["rwkv_iclr_w0"], inp["rwkv_k_k"], inp["rwkv_k_a"],
                         np.asarray(inp["rwkv_r_k"]).reshape(cfg.nR, -1), inp["rwkv_gn_w"], inp["rwkv_gn_b"]], axis=1)
        shared["rwkv_vecT"] = _pp(vecs, HP)
    if cfg.nG:
        for k in ("gla_w_in", "gla_w_out", "gla_gate_w2"):
            shared[k] = np.ascontiguousarray(inp[k], dtype=np.float32)
        shared["gla_nbT"] = _pp(inp["gla_gate_b"], (D // 2) // 128)
        hg = np.asarray(inp["gla_head_g"], np.float32)
        shared["gla_hgb"] = np.ascontiguousarray(np.broadcast_to(hg[None], (128,) + hg.shape))
    if cfg.nS:
        SW = 2 * D
        SH = SW // 64
        for k in ("ssd_w_in", "ssd_w_out"):
            shared[k] = np.ascontiguousarray(inp[k], dtype=np.float32)
        cwk = np.asarray(inp["ssd_conv_w"], np.float32)
        shared["ssd_cwT"] = _pp(np.moveaxis(cwk, 1, 2).reshape(cfg.nS, -1).reshape(cfg.nS, cwk.shape[2], 4).transpose(0, 2, 1), cwk.shape[2] // 128).transpose(0, 1, 3, 2).copy()
        shared["ssd_cbT"] = _pp(inp["ssd_conv_b"], cwk.shape[2] // 128)
        hv = np.zeros((64, cfg.nS, 2), np.float32)
        hv[:SH, :, 0] = np.asarray(inp["ssd_dt_bias"], np.float32).T
        hv[:SH, :, 1] = np.asarray(inp["ssd_a_log"], np.float32).T
        shared["ssd_hv"] = hv
        dsk = np.asarray(inp["ssd_d"], np.float32)
        shared["ssd_dsb"] = np.ascontiguousarray(np.broadcast_to(dsk[None], (128,) + dsk.shape))
        ng = np.asarray(inp["ssd_norm_g"], np.float32)
        shared["ssd_ngb"] = np.ascontiguousarray(np.broadcast_to(ng[None], (128,) + ng.shape))
    maps = []
    for core in range(n_cores):
        b = batch_of_core[core]
        m = dict(shared)
        m["xT"] = np.ascontiguousarray(np.asarray(inp["x"][b], np.float32).T)
        m["cT"] = _pp(inp["c"][b], DC)
        maps.append(m)
    return maps


_CACHE = {}


def kernel(**inputs):
    cfg = Cfg()
    B = inputs["x"].shape[0]
    n_cores = 8
    batch_of_core = [c % B for c in range(n_cores)]
    if "nc" not in _CACHE:
        _CACHE["nc"] = build(cfg)[0]
    nc = _CACHE["nc"]
    maps = prepare_inputs(cfg, inputs, n_cores, batch_of_core)
    res = run_bass_kernel_spmd(nc, maps, core_ids=list(range(n_cores)))
    out = np.empty((B, cfg.S, cfg.D), np.float32)
    for b in range(B):
        out[b] = res.results[b]["outT"].T
    return out
```

```python
from contextlib import ExitStack
import math
import numpy as np
import concourse.bass as bass
import concourse.mybir as mybir
from concourse.bass_utils import run_bass_kernel_spmd

F32 = mybir.dt.float32
BF16 = mybir.dt.bfloat16
AF = mybir.ActivationFunctionType
ALU = mybir.AluOpType
AX = mybir.AxisListType


class V:
    __slots__ = ("ap", "tl")

    def __init__(self, ap, tl):
        self.ap = ap
        self.tl = tl

    def __getitem__(self, idx):
        return V(self.ap[idx], self.tl)

    def re(self, pat, **kw):
        return V(self.ap.rearrange(pat, **kw), self.tl)

    def bc(self, axes, shape):
        a = self.ap
        for ax in axes:
            a = a.unsqueeze(ax)
        return V(a.broadcast_to(list(shape)), self.tl)


class Tl:
    __slots__ = ("t", "lw", "rd", "name", "excl")

    def __init__(self, t, name="", excl=False):
        self.t = t
        self.lw = []
        self.rd = []
        self.name = name
        self.excl = excl

    def __getitem__(self, idx):
        return V(self.t[idx], self)

    def v(self):
        return V(self.t[:], self)


ENGS = ("pe", "act", "dve", "pool", "sp")
DMA_ENGS = ("sp", "pool", "act")
NDMA_SLOTS = 12
WRITE_KW = ("out", "accum_out", "ap")


def _compress(toks):
    best = {}
    for s, v, src in toks:
        k = id(s)
        if k not in best or best[k][1] < v:
            best[k] = (s, v, src)
    return list(best.values())


class Prog:
    def __init__(self, nc):
        self.nc = nc
        self.stacks = [ExitStack()]
        self.q = {e: [] for e in ENGS}
        self.cnt = {e: 0 for e in ENGS}
        self.sem = {e: self.stacks[0].enter_context(nc.semaphore("s_" + e)) for e in ENGS}
        self.seen = {e: {} for e in ENGS}
        self.dsem, self.dval, self.dnext = {}, {}, {}
        for e in DMA_ENGS:
            self.dsem[e] = [self.stacks[0].enter_context(nc.semaphore("d_%s%d" % (e, i))) for i in range(NDMA_SLOTS)]
            self.dval[e] = [0] * NDMA_SLOTS
            self.dnext[e] = 0
        self.n_inst = 0
        self.uid = 0
        self.marks = []

    def mark(self, label):
        self.marks.append((label, dict(self.cnt)))

    def _nm(self, name):
        self.uid += 1
        return "%s_%d" % (name, self.uid)

    def sb(self, name, shape, dt=F32):
        t = self.stacks[-1].enter_context(self.nc.sbuf_tensor(self._nm(name), list(shape), dt))
        return Tl(t, name)

    def ps(self, name, shape, dt=F32):
        nbytes = int(np.prod(shape[1:])) * (4 if dt == F32 else 2)
        assert nbytes % 2048 == 0, "PSUM tiles must cover whole banks"
        t = self.stacks[-1].enter_context(self.nc.psum_tensor(self._nm(name), list(shape), dt))
        return Tl(t, name, excl=True)

    def dram(self, name, shape, dt=F32, kind="Internal"):
        t = self.nc.dram_tensor(name, list(shape), dt, kind=kind)
        return Tl(t.ap(), name)

    class _Scope:
        def __init__(self, p):
            self.p = p

        def __enter__(self):
            self.p.stacks.append(ExitStack())

        def __exit__(self, *a):
            self.p.barrier()
            self.p.stacks.pop().close()
            return False

    def scope(self):
        return Prog._Scope(self)

    def _deps(self, eng, reads, writes, acc_w=False):
        waits = {}

        def need(tok):
            sem, val, src = tok
            if src == "pe" and eng == "pe":
                return
            k = id(sem)
            if self.seen[eng].get(k, 0) >= val:
                return
            if k not in waits or waits[k][1] < val:
                waits[k] = (sem, val)

        for tl in reads:
            for tok in tl.lw:
                need(tok)
        for tl in writes:
            if not acc_w:
                for tok in tl.lw:
                    need(tok)
            for tok in tl.rd:
                need(tok)
        for k, (sem, val) in waits.items():
            self.seen[eng][k] = val
        return list(waits.values())

    def _commit(self, tok, reads, writes, acc_w=False):
        for tl in writes:
            if acc_w:
                tl.lw.append(tok)
                if len(tl.lw) > 48:
                    tl.lw = _compress(tl.lw)
            else:
                tl.lw = [tok]
            tl.rd = []
        for tl in reads:
            if tl not in writes:
                tl.rd.append(tok)
                if len(tl.rd) > 48:
                    tl.rd = _compress(tl.rd)

    def I(self, eng, fn, *, acc_w=False, **kw):
        reads, writes, args = [], [], {}
        for k, a in kw.items():
            if isinstance(a, V):
                args[k] = a.ap
                (writes if (k in WRITE_KW or a.tl.excl) else reads).append(a.tl)
            else:
                args[k] = a
        waits = self._deps(eng, reads, writes, acc_w)
        self.cnt[eng] += 1
        tok = (self.sem[eng], self.cnt[eng], eng)
        self._commit(tok, reads, writes, acc_w)
        self.q[eng].append((waits, fn, args, (self.sem[eng], 1)))
        self.n_inst += 1

    def dma(self, eng, out, in_, acc_w=False, **kw):
        reads, writes = [in_.tl], [out.tl]
        waits = self._deps(eng, reads, writes, acc_w)
        s = self.dnext[eng]
        self.dnext[eng] = (s + 1) % NDMA_SLOTS
        sem = self.dsem[eng][s]
        prev = self.dval[eng][s]
        if prev > 0 and self.seen[eng].get(id(sem), 0) < prev:
            waits.append((sem, prev))
            self.seen[eng][id(sem)] = prev
        self.dval[eng][s] = prev + 16
        tok = (sem, prev + 16, "dma")
        self._commit(tok, reads, writes, acc_w)
        args = dict(out=out.ap, in_=in_.ap)
        args.update(kw)
        self.q[eng].append((waits, "dma_start", args, (sem, 16)))
        self.n_inst += 1

    def barrier(self):
        for e in ENGS:
            waits = []
            for e2 in ENGS:
                if self.cnt[e2] > 0 and self.seen[e].get(id(self.sem[e2]), 0) < self.cnt[e2] and e2 != e:
                    waits.append((self.sem[e2], self.cnt[e2]))
                    self.seen[e][id(self.sem[e2])] = self.cnt[e2]
            for de in DMA_ENGS:
                for s in range(NDMA_SLOTS):
                    v = self.dval[de][s]
                    sem = self.dsem[de][s]
                    if v > 0 and self.seen[e].get(id(sem), 0) < v:
                        waits.append((sem, v))
                        self.seen[e][id(sem)] = v
            if waits:
                self.q[e].append((waits, None, None, None))

    def emit(self):
        nc = self.nc
        self.barrier()
        with nc.Block() as block:
            def run(engname):
                def f(e):
                    for waits, fn, args, inc in self.q[engname]:
                        for sem, val in waits:
                            e.wait_ge(sem, val)
                        if fn is not None:
                            getattr(e, fn)(**args).then_inc(inc[0], inc[1])
                return f
            block.tensor(run("pe"))
            block.scalar(run("act"))
            block.vector(run("dve"))
            block.gpsimd(run("pool"))
            block.sync(run("sp"))
        while self.stacks:
            self.stacks.pop().close()


class Cfg:
    def __init__(self, D=2048, S=2048, kinds=(0, 1, 2, 0), lora=96,
                 gla_heads=4, gla_rank=16, ssm_groups=8):
        self.D, self.S, self.kinds, self.lora = D, S, tuple(kinds), lora
        self.gla_heads, self.gla_rank, self.ssm_groups = gla_heads, gla_rank, ssm_groups
        self.DC = D // 128
        self.TT = min(512, S)
        self.NT = S // self.TT
        self.TA = min(256, S)
        self.L = len(kinds)
        self.nR = sum(1 for k in kinds if k == 0)
        self.nG = sum(1 for k in kinds if k == 1)
        self.nS = sum(1 for k in kinds if k == 2)
        self.TB = min(512, S)
        self.CB = 4
        self.stop = 99


NEG_EXP_HALF = -math.exp(-0.5)
NORM_EPS = 1e-6
RWKV_GN_EPS = 64e-5


def make_consts(cfg):
    c = np.zeros((128, 8, 128), np.float32)
    c[:, 0, :] = np.eye(128)
    c[:, 1, :] = 1.0
    c[0:64, 2, 0:64] = 1.0
    c[64:128, 2, 64:128] = 1.0
    su = np.triu(np.ones((64, 64), np.float32), 1)
    iu = np.triu(np.ones((64, 64), np.float32), 0)
    c[0:64, 3, 0:64] = su
    c[64:128, 3, 0:64] = su
    c[0:64, 3, 64:128] = iu
    c[64:128, 3, 64:128] = iu
    c[0:64, 4, 0:64] = -su
    c[0:64, 4, 64:128] = -su.T
    c[0:64, 5, 0:64] = np.eye(64)
    c[64:128, 5, 0:64] = np.eye(64)
    c[:, 6, :] = np.triu(np.ones((128, 128), np.float32), 0)
    c[:, 7, :] = np.where(np.triu(np.ones((128, 128)), 0) > 0, 0.0, -30000.0)
    rmask = np.ones((128, cfg.S), np.float32)
    rmask[:, 0::64] = 0.0
    rmask128 = np.ones((128, cfg.S), np.float32)
    rmask128[:, 0::128] = 0.0
    return c.reshape(128, 8 * 128), rmask, rmask128


def build(cfg):
    nc = bass.Bass("TRN2", target_bir_lowering=False)
    p = Prog(nc)
    D, S, DC, TT, NT, L = cfg.D, cfg.S, cfg.DC, cfg.TT, cfg.NT, cfg.L
    EI = "ExternalInput"
    xT = p.dram("xT", [D, S], F32, EI)
    cT = p.dram("cT", [128, DC], F32, EI)
    ada_w = p.dram("ada_w", [L, D, 3 * D], F32, EI)
    ada_bT = p.dram("ada_bT", [128, L, 3 * DC], F32, EI)
    norm_gT = p.dram("norm_gT", [128, L, DC], F32, EI)
    final_gT = p.dram("final_gT", [128, DC], F32, EI)
    consts_d = p.dram("consts", [128, 8 * 128], F32, EI)
    rmask_d = p.dram("rmask", [128, S], F32, EI)
    rmask128_d = p.dram("rmask128", [128, S], F32, EI)
    outT = p.dram("outT", [D, S], F32, "ExternalOutput")
    xres = [p.dram("xres%d" % i, [D, S], F32) for i in range(3)]
    W = D
    HP = W // 128
    R = cfg.lora
    if cfg.nR:
        nR = cfg.nR
        rw_in = p.dram("rwkv_w_in", [nR, D, 4 * W], F32, EI)
        rw_out = p.dram("rwkv_w_out", [nR, W, D], F32, EI)
        rw_dw1 = p.dram("rwkv_dec_w1", [nR, D, R], F32, EI)
        rw_dw2 = p.dram("rwkv_dec_w2", [nR, R, W], F32, EI)
        rw_aw1 = p.dram("rwkv_iclr_w1", [nR, D, R], F32, EI)
        rw_aw2 = p.dram("rwkv_iclr_w2", [nR, R, W], F32, EI)
        rw_muT = p.dram("rwkv_muT", [128, nR, 6, DC], F32, EI)
        rw_vecT = p.dram("rwkv_vecT", [128, nR, 7, HP], F32, EI)
        projT = [[Tl(t.t[f * 128:(f + 1) * 128, :], "projT") for f in range(HP)]
                 for t in [p.dram("projT%d" % c, [W, S], F32) for c in range(4)]]

    GH = cfg.gla_heads
    KW, VW = D // 2, D
    DK, DV = KW // GH, VW // GH
    KC, VC = max(DK // 128, 1), DV // 128
    GR = cfg.gla_rank
    if cfg.nG:
        nG = cfg.nG
        assert DK % 128 == 0 and DV % 128 == 0 and DV <= 512
        gl_in = p.dram("gla_w_in", [nG, D, 2 * KW + 2 * VW + GR], F32, EI)
        gl_out = p.dram("gla_w_out", [nG, VW, D], F32, EI)
        gl_w2 = p.dram("gla_gate_w2", [nG, GR, KW], F32, EI)
        gl_nbT = p.dram("gla_nbT", [128, nG, KW // 128], F32, EI)
        gl_hgb = p.dram("gla_hgb", [128, nG, DV], F32, EI)
        gqk = [[Tl(t.t[f * 128:(f + 1) * 128, :], "gqk") for f in range(KW // 128)]
               for t in [p.dram("gqk%d" % c, [KW, S], F32) for c in range(2)]]
        gvg_t = [p.dram("gvg%d" % c, [S, VW], F32) for c in range(2)]
        gvg = [[Tl(t.t[:, hh * DV:(hh + 1) * DV], "gvg") for hh in range(GH)] for t in gvg_t]

    SW = 2 * D
    SH = SW // 64
    SG = SW // 512
    SN = 128
    CW = SW + 2 * SG * SN
    SIN = SW + CW + SH
    if cfg.nS:
        nS = cfg.nS
        sd_in = p.dram("ssd_w_in", [nS, D, SIN], F32, EI)
        sd_out = p.dram("ssd_w_out", [nS, SW, D], F32, EI)
        sd_cwT = p.dram("ssd_cwT", [128, nS, CW // 128, 4], F32, EI)
        sd_cbT = p.dram("ssd_cbT", [128, nS, CW // 128], F32, EI)
        sd_hv = p.dram("ssd_hv", [64, nS, 2], F32, EI)
        sd_dsb = p.dram("ssd_dsb", [128, nS, SH], F32, EI)
        sd_ngb = p.dram("ssd_ngb", [128, nS, SW], F32, EI)
        sxbc_t = p.dram("sxbc", [CW, S], F32)
        sxbc = [Tl(sxbc_t.t[f * 128:(f + 1) * 128, :], "sxbc") for f in range(CW // 128)]
        sz_t = p.dram("sz", [S, SW], F32)
        sz = [Tl(sz_t.t[:, g * 512:(g + 1) * 512], "sz") for g in range(SG)]

    cst = p.sb("cst", [128, 8, 128], F32)
    p.dma("sp", cst.v().re("p a b -> p (a b)"), consts_d.v())
    ident_bf = p.sb("ident_bf", [128, 128], BF16)
    p.I("dve", "tensor_copy", out=ident_bf.v(), in_=cst[:, 0, :])
    ones32 = cst[:, 1, :]
    bones32 = cst[:, 2, :]
    maskA = cst[:, 3, :]
    negSU = cst[0:64, 4, 0:64]
    negSL = cst[0:64, 4, 64:128]
    ident2 = cst[:, 5, 0:64]
    rmask = p.sb("rmask", [128, S], BF16)
    p.dma("pool", rmask.v(), rmask_d.v())
    rmask128 = p.sb("rmask128", [128, S], BF16)
    p.dma("pool", rmask128.v(), rmask128_d.v())
    iu128 = cst[:, 6, :]

    modT = p.sb("modT", [128, L, 3 * DC], F32)
    gsT = p.sb("gsT", [128, L, DC], F32)
    with p.scope():
        cact = p.sb("cact", [128, DC], F32)
        abT = p.sb("abT", [128, L, 3 * DC], F32)
        ngT = p.sb("ngT", [128, L, DC], F32)
        p.dma("sp", cact.v(), cT.v())
        p.dma("sp", abT.v(), ada_bT.v())
        p.dma("sp", ngT.v(), norm_gT.v())
        p.I("act", "activation", out=cact.v(), in_=cact.v(), func=AF.Silu)
        EG = 4 if (3 * DC) % 4 == 0 else 2
        cact_bf = p.sb("cact_bf", [128, DC], BF16)
        p.I("dve", "tensor_copy", out=cact_bf.v(), in_=cact.v())
        wst = [p.sb("adaw", [128, DC, EG * 128], BF16) for _ in range(3)]
        psm = p.ps("psmod", [128, 512], F32)
        gi = 0
        for l in range(L):
            wv = ada_w.v()[l].re("(dc p) e -> p dc e", p=128)
            for eg in range(3 * DC // EG):
                wt = wst[gi % 3]
                p.dma("pool", wt.v(), wv[:, :, eg * EG * 128:(eg + 1) * EG * 128])
                gi += 1
                for j in range(EG):
                    col = l * 3 * DC + eg * EG + j
                    for dc in range(DC):
                        p.I("pe", "matmul", out=psm[:, col:col + 1], lhsT=wt[:, dc, j * 128:(j + 1) * 128],
                            rhs=cact_bf[:, dc:dc + 1], start=(dc == 0), stop=(dc == DC - 1))
        p.I("dve", "tensor_tensor", out=modT.v().re("p l e -> p (l e)"), in0=psm[:, 0:L * 3 * DC],
            in1=abT.v().re("p l e -> p (l e)"), op=ALU.add)
        p.I("dve", "scalar_tensor_tensor", out=gsT.v(), in0=modT[:, :, DC:2 * DC], scalar=1.0, in1=ngT.v(),
            op0=ALU.add, op1=ALU.mult)

    def norm_phase(src, dst_tiles, g_of_dc, sh_of_dc, out_dram=None):
        TA = cfg.TA
        with p.scope():
            xt = [p.sb("xt", [128, DC, TA], F32) for _ in range(2)]
            sq = [p.sb("sq", [128, TA], F32) for _ in range(2)]
            rstd = [p.sb("rstd", [128, TA], F32) for _ in range(2)]
            tmp = [p.sb("ntmp", [128, TA], F32) for _ in range(4)]
            pss = [p.ps("psn", [128, 512], F32) for _ in range(2)]
            k = 0
            for ta in range(S // TA):
                x_ = xt[ta % 2]
                ts = slice(ta * TA, (ta + 1) * TA)
                p.dma("sp", x_.v(), src.re("(dc p) s -> p dc s", p=128)[:, :, ts])
                ps_ = pss[ta % 2]
                for dc in range(DC):
                    s_ = sq[dc % 2]
                    if dc % 2 == 0:
                        p.I("act", "activation", out=s_.v(), in_=x_[:, dc, :], func=AF.Square)
                    else:
                        p.I("dve", "tensor_tensor", out=s_.v(), in0=x_[:, dc, :], in1=x_[:, dc, :], op=ALU.mult)
                    p.I("pe", "matmul", out=ps_[:, 0:TA], lhsT=ones32, rhs=s_.v(), start=(dc == 0), stop=(dc == DC - 1))
                r_ = rstd[ta % 2]
                p.I("act", "activation", out=r_.v(), in_=ps_[:, 0:TA], func=AF.Sqrt, bias=NORM_EPS, scale=1.0 / D)
                p.I("dve", "reciprocal", out=r_.v(), in_=r_.v())
                for dc in range(DC):
                    t_ = tmp[k % 4]
                    k += 1
                    p.I("dve", "scalar_tensor_tensor", out=t_.v(), in0=x_[:, dc, :],
                        scalar=g_of_dc(dc), in1=r_.v(), op0=ALU.mult, op1=ALU.mult)
                    if out_dram is None:
                        p.I("act", "activation", out=dst_tiles[dc][:, ts], in_=t_.v(), func=AF.Identity,
                            bias=sh_of_dc(dc), scale=1.0)
                    else:
                        p.dma("sp", out_dram[dc * 128:(dc + 1) * 128, ts], t_.v(), acc_w=True)

    wring = {}

    def out_proj(yg_tiles, w_dram, nci, gate_of_ft, src, dst):
        with p.scope():
            wts = [p.sb("wo", [128, nci, 512], BF16) for _ in range(2)]
            pso = [p.ps("pso", [128, 512], F32) for _ in range(4)]
            xin = [p.sb("xin", [128, TT], F32) for _ in range(4)]
            wv = w_dram.re("(ci p) f -> p ci f", p=128)
            k = 0
            G = 4 if (D // 128) % 4 == 0 else 2
            for fg in range(D // (128 * G)):
                wt = wts[fg % 2]
                p.dma("pool", wt[:, :, 0:G * 128], wv[:, :, fg * G * 128:(fg + 1) * G * 128])
                for j in range(G):
                    ft = fg * G + j
                    for tt in range(NT):
                        ts = slice(tt * TT, (tt + 1) * TT)
                        ps_ = pso[k % 4]
                        x_ = xin[k % 4]
                        k += 1
                        p.dma("sp", x_.v(), src[ft * 128:(ft + 1) * 128, ts])
                        for ci in range(nci):
                            p.I("pe", "matmul", out=ps_[:, 0:TT], lhsT=wt[:, ci, j * 128:(j + 1) * 128],
                                rhs=yg_tiles[ci][:, ts], start=(ci == 0), stop=(ci == nci - 1))
                        p.I("dve", "scalar_tensor_tensor", out=x_.v(), in0=ps_[:, 0:TT], scalar=gate_of_ft(ft),
                            in1=x_.v(), op0=ALU.mult, op1=ALU.add)
                        p.dma("sp", dst[ft * 128:(ft + 1) * 128, ts], x_.v(), acc_w=True)

    def proj_fm(xs_tiles, wv, f0, nft, sink, wts, pss, kdim=DC):
        k = 0
        G = 4 if nft % 4 == 0 else (2 if nft % 2 == 0 else 1)
        gi = 0
        for fg in range(nft // G):
            wt = wts[gi % 2]
            gi += 1
            p.dma("pool", wt[:, :, 0:G * 128], wv[:, :, f0 + fg * G * 128:f0 + (fg + 1) * G * 128])
            for j in range(G):
                ft = fg * G + j
                for tt in range(NT):
                    ts = slice(tt * TT, (tt + 1) * TT)
                    ps_ = pss[k % len(pss)]
                    k += 1
                    for dc in range(kdim):
                        p.I("pe", "matmul", out=ps_[:, 0:TT], lhsT=wt[:, dc, j * 128:(j + 1) * 128],
                            rhs=xs_tiles[dc][:, ts], start=(dc == 0), stop=(dc == kdim - 1))
                    sink(ft, tt, ps_)


    def proj_tm(hT, wv, f0, ngroups, gw, sink, wts, pss):
        k = 0
        for gi in range(ngroups):
            wt = wts[gi % 2]
            p.dma("pool", wt[:, :, 0:gw], wv[:, :, f0 + gi * gw:f0 + (gi + 1) * gw])
            for tk in range(S // 128):
                ps_ = pss[k % len(pss)]
                k += 1
                for dc in range(DC):
                    p.I("pe", "matmul", out=ps_[:, 0:gw], lhsT=hT[dc][:, tk * 128:(tk + 1) * 128], rhs=wt[:, dc, 0:gw],
                        start=(dc == 0), stop=(dc == DC - 1))
                sink(gi, tk, ps_)

    def gla_layer(l, j, src, dst):
        NCH = S // 128
        with p.scope():
            yg = [p.sb("ygg", [128, S], BF16) for _ in range(VW // 128)]
            lowT = p.sb("lowT", [GR, S], BF16)
            gw2 = p.sb("gw2", [GR, KW], BF16)
            nb = p.sb("gnb", [128, KW // 128], F32)
            hgb = p.sb("hgb", [128, DV], F32)
            p.dma("pool", gw2.v(), gl_w2.v()[j])
            p.dma("sp", nb.v(), gl_nbT.v()[:, j])
            p.dma("sp", hgb.v(), gl_hgb.v()[:, j])
            p.I("dve", "tensor_scalar", out=nb.v(), in0=nb.v(), scalar1=-1.0, scalar2=None, op0=ALU.mult)
            wv = gl_in.v()[j].re("(dc p) f -> p dc f", p=128)
            with p.scope():
                hT = [p.sb("hT", [128, S], BF16) for _ in range(DC)]
                norm_phase(src, hT, lambda dc: gsT[:, l, dc:dc + 1], lambda dc: modT[:, l, dc:dc + 1])
                wts = [p.sb("wi", [128, DC, 512], BF16) for _ in range(2)]
                wl = p.sb("wl", [128, DC, GR], BF16)
                pss = [p.ps("psp", [128, 512], F32) for _ in range(4)]
                stg = [p.sb("stg", [128, 512], F32) for _ in range(4)]
                sk = [0]

                def evac(ps_ap, dst_ap, width):
                    s_ = stg[sk[0] % 4]
                    e = "act" if sk[0] % 2 == 0 else "dve"
                    sk[0] += 1
                    if e == "act":
                        p.I("act", "copy", out=s_[:, 0:width], in_=ps_ap)
                    else:
                        p.I("dve", "tensor_copy", out=s_[:, 0:width], in_=ps_ap)
                    p.dma("sp", dst_ap, s_[:, 0:width], acc_w=True)

                for c in range(2):
                    proj_fm(hT, wv, c * KW, KW // 128,
                            lambda ft, tt, ps_, c=c: evac(ps_[:, 0:TT], gqk[c][ft][:, tt * TT:(tt + 1) * TT], TT), wts, pss)
                for c in range(2):
                    proj_tm(hT, wv, 2 * KW + c * VW, GH, DV,
                            lambda gi, tk, ps_, c=c: evac(ps_[:, 0:DV], gvg[c][gi][tk * 128:(tk + 1) * 128, :], DV), wts, pss)
                p.dma("pool", wl.v(), wv[:, :, 2 * KW + 2 * VW:2 * KW + 2 * VW + GR])
                for tt in range(NT):
                    ts = slice(tt * TT, (tt + 1) * TT)
                    ps_ = pss[tt % 4]
                    for dc in range(DC):
                        p.I("pe", "matmul", out=ps_[0:GR, 0:TT], lhsT=wl[:, dc, :], rhs=hT[dc][:, ts],
                            start=(dc == 0), stop=(dc == DC - 1))
                    p.I("act", "copy", out=lowT[:, ts], in_=ps_[0:GR, 0:TT])
            if cfg.stop <= 2:
                return False
            with p.scope():
                ldq = [p.sb("ldq", [128, S], F32) for _ in range(KC)]
                ldk = [p.sb("ldk", [128, S], F32) for _ in range(KC)]
                QT = [p.sb("QT", [128, S], BF16) for _ in range(KC)]
                KT = [p.sb("KT", [128, S], BF16) for _ in range(KC)]
                eb = [p.sb("eb", [128, S], F32) for _ in range(KC)]
                t1 = p.sb("gt1", [128, S], F32)
                t2 = p.sb("gt2", [128, S], F32)
                S32 = [p.sb("S32", [128, DV], F32) for _ in range(KC)]
                Sb = [p.sb("Sb", [128, DV], BF16) for _ in range(KC)]
                v32 = [p.sb("v32", [128, DV], F32) for _ in range(2)]
                g32 = [p.sb("g32", [128, DV], F32) for _ in range(2)]
                Vb = [p.sb("Vb", [128, DV], BF16) for _ in range(2)]
                SG = [p.sb("SG", [128, DV], F32) for _ in range(2)]
                KTM = [p.sb("KTM", [128, KC * 128], BF16) for _ in range(2)]
                ST = [p.sb("ST", [128, 128], BF16) for _ in range(2)]
                junk = p.sb("junk", [128, DV], F32)
                ssq = [p.sb("ssq", [128, 1], F32) for _ in range(2)]
                y32 = [p.sb("y32", [128, DV], F32) for _ in range(2)]
                yb = [p.sb("yb", [128, DV], BF16) for _ in range(2)]
                psP = p.ps("gpsP", [128, 512], F32)
                psTr = p.ps("gpsTr", [128, 8, 128], BF16)
                psS = p.ps("gpsS", [128, 512], F32)
                psO = [p.ps("gpsO", [128, 512], F32) for _ in range(2)]
                psSt = [p.ps("gpsSt", [128, 512], F32) for _ in range(2)]
                psTr2 = p.ps("gpsTr2", [128, 8, 128], BF16)
                for hh in range(GH):
                    for kc in range(KC):
                        ft = hh * KC + kc
                        p.dma("sp", ldq[kc].v(), gqk[0][ft].v())
                        p.dma("sp", ldk[kc].v(), gqk[1][ft].v())
                        for tt in range(NT):
                            ts = slice(tt * TT, (tt + 1) * TT)
                            p.I("pe", "matmul", out=psP[:, 0:TT], lhsT=gw2[:, ft * 128:(ft + 1) * 128], rhs=lowT[:, ts], start=True, stop=True)
                            p.I("act", "activation", out=t1[:, ts], in_=psP[:, 0:TT], func=AF.Exp, bias=nb[:, ft:ft + 1], scale=-1.0)
                        p.I("act", "activation", out=t1.v(), in_=t1.v(), func=AF.Ln, bias=1.0, scale=1.0)
                        p.I("dve", "tensor_scalar", out=t1.v(), in0=t1.v(), scalar1=-1.0 / 16.0, scalar2=None, op0=ALU.mult)
                        p.I("dve", "tensor_tensor_scan", out=t2.v(), data0=rmask128.v(), data1=t1.v(), initial=0.0,
                            op0=ALU.mult, op1=ALU.add)
                        p.I("act", "activation", out=eb[kc].v(), in_=t2.v(), func=AF.Exp)
                        p.I("act", "activation", out=t1.v(), in_=t2.v(), func=AF.Exp, scale=-1.0)
                        p.I("dve", "scalar_tensor_tensor", out=QT[kc].v(), in0=ldq[kc].v(), scalar=float(DK) ** -0.5, in1=eb[kc].v(),
                            op0=ALU.mult, op1=ALU.mult)
                        p.I("dve", "tensor_tensor", out=KT[kc].v(), in0=ldk[kc].v(), in1=t1.v(), op=ALU.mult)
                        p.I("dve", "memset", ap=S32[kc].v(), constant=0.0)
                        p.I("dve", "memset", ap=Sb[kc].v(), constant=0.0)
                    for n in range(NCH):
                        ns = slice(n * 128, (n + 1) * 128)
                        b2 = n % 2
                        p.dma("sp", v32[b2].v(), gvg[0][hh][ns, :])
                        p.dma("sp", g32[b2].v(), gvg[1][hh][ns, :])
                        p.I("act", "copy", out=Vb[b2].v(), in_=v32[b2].v())
                        p.I("act", "activation", out=SG[b2].v(), in_=g32[b2].v(), func=AF.Silu)
                        for kc in range(KC):
                            p.I("pe", "transpose", out=psTr[:, kc, :], in_=KT[kc][:, ns], identity=ident_bf.v())
                        p.I("dve", "tensor_copy", out=KTM[b2].v().re("p (k x) -> p k x", x=128), in_=psTr[:, 0:KC, :])
                        for kc in range(KC):
                            p.I("pe", "matmul", out=psS[:, 0:128], lhsT=KT[kc][:, ns], rhs=QT[kc][:, ns], start=(kc == 0), stop=(kc == KC - 1))
                        p.I("dve", "tensor_tensor", out=ST[b2].v(), in0=psS[:, 0:128], in1=iu128, op=ALU.mult)
                        po = psO[b2]
                        p.I("pe", "matmul", out=po[:, 0:DV], lhsT=ST[b2].v(), rhs=Vb[b2].v(), start=True, stop=False)
                        for kc in range(KC):
                            p.I("pe", "matmul", out=po[:, 0:DV], lhsT=QT[kc][:, ns], rhs=Sb[kc].v(), start=False, stop=(kc == KC - 1))
                        for kc in range(KC):
                            pst = psSt[kc % 2]
                            p.I("pe", "matmul", out=pst[:, 0:DV], lhsT=KTM[b2][:, kc * 128:(kc + 1) * 128], rhs=Vb[b2].v(), start=True, stop=True)
                            dcol = eb[kc][:, n * 128 + 127:n * 128 + 128]
                            p.I("dve", "tensor_scalar", out=S32[kc].v(), in0=S32[kc].v(), scalar1=dcol, scalar2=None, op0=ALU.mult)
                            p.I("dve", "scalar_tensor_tensor", out=S32[kc].v(), in0=pst[:, 0:DV], scalar=dcol, in1=S32[kc].v(),
                                op0=ALU.mult, op1=ALU.add)
                            p.I("act", "copy", out=Sb[kc].v(), in_=S32[kc].v())
                        p.I("act", "activation", out=junk.v(), in_=po[:, 0:DV], func=AF.Square, accum_out=ssq[b2].v())
                        p.I("act", "activation", out=ssq[b2].v(), in_=ssq[b2].v(), func=AF.Sqrt, bias=NORM_EPS, scale=1.0 / DV)
                        p.I("dve", "reciprocal", out=ssq[b2].v(), in_=ssq[b2].v())
                        p.I("dve", "scalar_tensor_tensor", out=y32[b2].v(), in0=po[:, 0:DV], scalar=ssq[b2].v(), in1=hgb.v(),
                            op0=ALU.mult, op1=ALU.mult)
                        p.I("dve", "tensor_tensor", out=yb[b2].v(), in0=y32[b2].v(), in1=SG[b2].v(), op=ALU.mult)
                        for vc in range(VC):
                            p.I("pe", "transpose", out=psTr2[:, vc, :], in_=yb[b2][:, vc * 128:(vc + 1) * 128], identity=ident_bf.v())
                        for vc in range(VC):
                            p.I("act" if vc % 2 == 0 else "dve", "copy" if vc % 2 == 0 else "tensor_copy",
                                out=yg[hh * VC + vc][:, ns], in_=psTr2[:, vc, :])
            if cfg.stop <= 9:
                return False
            out_proj(yg, gl_out.v()[j], VW // 128, lambda ft: modT[:, l, 2 * DC + ft:2 * DC + ft + 1], src, dst)
            return True


    def ssd_layer(l, j, src, nextbuf):
        NCH = S // 128
        srcv = [src]
        HN = SH
        maskb = cst[:, 7, :]
        ident32 = cst[:, 0, :]
        with p.scope():
            dtT = p.sb("dtT", [128, S], F32)
            acT = p.sb("acT", [128, S], F32)
            nacT = p.sb("nacT", [128, S], F32)
            hv = p.sb("hv", [64, 2], F32)
            dsb = p.sb("dsb", [128, SH], F32)
            cw = p.sb("cw", [128, CW // 128, 4], F32)
            cbv = p.sb("cbv", [128, CW // 128], F32)
            wtm = p.sb("wtm", [128, NCH, 128], F32)
            eatm = p.sb("eatm", [128, NCH, 64], F32)
            decbc = p.sb("decbc", [128, NCH, 64], F32)
            p.dma("sp", hv.v(), sd_hv.v()[:, j])
            p.dma("sp", dsb.v(), sd_dsb.v()[:, j])
            p.dma("sp", cw.v(), sd_cwT.v()[:, j])
            p.dma("sp", cbv.v(), sd_cbT.v()[:, j])
            wv = sd_in.v()[j].re("(dc p) f -> p dc f", p=128)
            with p.scope():
                hT = [p.sb("hT", [128, S], BF16) for _ in range(DC)]
                norm_phase(src, hT, lambda dc: gsT[:, l, dc:dc + 1], lambda dc: modT[:, l, dc:dc + 1])
                wts = [p.sb("wi", [128, DC, 512], BF16) for _ in range(2)]
                wdt = p.sb("wdt", [128, DC, SH], BF16)
                pss = [p.ps("psp", [128, 512], F32) for _ in range(4)]
                stg = [p.sb("stg", [128, 512], F32) for _ in range(4)]
                xst = [p.sb("xst", [128, S + 3], F32) for _ in range(2)]
                acc = [p.sb("cacc", [128, S], F32) for _ in range(2)]
                sk = [0]

                def zsink(gi, tk, ps_):
                    s_ = stg[sk[0] % 4]
                    e = "act" if sk[0] % 2 == 0 else "dve"
                    sk[0] += 1
                    if e == "act":
                        p.I("act", "copy", out=s_.v(), in_=ps_.v())
                    else:
                        p.I("dve", "tensor_copy", out=s_.v(), in_=ps_.v())
                    p.dma("sp", sz[gi][tk * 128:(tk + 1) * 128, :], s_.v(), acc_w=True)

                proj_tm(hT, wv, 0, SG, 512, zsink, wts, pss)
                for b in range(2):
                    p.I("dve", "memset", ap=xst[b][:, 0:3], constant=0.0)

                def csink(ft, tt, ps_):
                    x_ = xst[ft % 2]
                    e = "act" if (ft + tt) % 2 == 0 else "dve"
                    if e == "act":
                        p.I("act", "copy", out=x_[:, 3 + tt * TT:3 + (tt + 1) * TT], in_=ps_[:, 0:TT])
                    else:
                        p.I("dve", "tensor_copy", out=x_[:, 3 + tt * TT:3 + (tt + 1) * TT], in_=ps_[:, 0:TT])
                    if tt == NT - 1:
                        a_ = acc[ft % 2]
                        p.I("act", "mul", out=a_.v(), in_=x_[:, 3:S + 3], mul=cw[:, ft, 3:4])
                        for kk_ in range(3):
                            p.I("dve", "scalar_tensor_tensor", out=a_.v(), in0=x_[:, kk_:S + kk_], scalar=cw[:, ft, kk_:kk_ + 1],
                                in1=a_.v(), op0=ALU.mult, op1=ALU.add)
                        p.I("act", "activation", out=a_.v(), in_=a_.v(), func=AF.Silu, bias=cbv[:, ft:ft + 1], scale=1.0)
                        p.dma("sp", sxbc[ft].v(), a_.v())

                proj_fm(hT, wv, SW, CW // 128, csink, wts, pss)
                p.dma("pool", wdt.v(), wv[:, :, SW + CW:SW + CW + SH])
                p.I("dve", "memset", ap=dtT.v(), constant=0.0)
                p.I("dve", "memset", ap=acT.v(), constant=0.0)
                for tt in range(NT):
                    ts = slice(tt * TT, (tt + 1) * TT)
                    ps_ = pss[tt % 4]
                    for dc in range(DC):
                        p.I("pe", "matmul", out=ps_[0:HN, 0:TT], lhsT=wdt[:, dc, :], rhs=hT[dc][:, ts], start=(dc == 0), stop=(dc == DC - 1))
                    p.I("act", "activation", out=dtT[0:HN, ts], in_=ps_[0:HN, 0:TT], func=AF.Exp, bias=hv[0:HN, 0:1], scale=1.0)
                p.I("act", "activation", out=dtT[0:HN, :], in_=dtT[0:HN, :], func=AF.Ln, bias=1.0, scale=1.0)
            if cfg.stop <= 2:
                return False
            with p.scope():
                eaT = p.sb("eaT", [128, S], F32)
                na = p.sb("na", [64, 1], F32)
                t1 = p.sb("st1", [128, S], F32)
                Dg = p.sb("Dg", [64, 64], F32)
                psq = [p.ps("spsq", [128, 512], F32) for _ in range(2)]
                p.I("act", "activation", out=na[0:HN, :], in_=hv[0:HN, 1:2], func=AF.Exp)
                p.I("dve", "tensor_scalar", out=na[0:HN, :], in0=na[0:HN, :], scalar1=-1.0, scalar2=None, op0=ALU.mult)
                p.I("dve", "tensor_scalar", out=t1[0:HN, :], in0=dtT[0:HN, :], scalar1=na[0:HN, 0:1], scalar2=None, op0=ALU.mult)
                p.I("dve", "tensor_tensor_scan", out=acT[0:HN, :], data0=rmask128[0:HN, :], data1=t1[0:HN, :], initial=0.0,
                    op0=ALU.mult, op1=ALU.add)
                p.I("dve", "memset", ap=nacT.v(), constant=0.0)
                p.I("dve", "memset", ap=eaT.v(), constant=0.0)
                p.I("dve", "tensor_scalar", out=nacT[0:HN, :], in0=acT[0:HN, :], scalar1=-1.0, scalar2=None, op0=ALU.mult)
                p.I("act", "activation", out=eaT[0:HN, :], in_=acT[0:HN, :], func=AF.Exp)
                for n in range(NCH):
                    ns = slice(n * 128, (n + 1) * 128)
                    last = acT[0:HN, n * 128 + 127:n * 128 + 128]
                    p.I("act", "activation", out=t1[0:HN, ns], in_=acT[0:HN, ns], func=AF.Exp, bias=last, scale=-1.0)
                    p.I("dve", "tensor_tensor", out=dtT[64:64 + HN, ns], in0=t1[0:HN, ns], in1=dtT[0:HN, ns], op=ALU.mult)
                    ps_ = psq[n % 2]
                    p.I("pe", "transpose", out=ps_[:, 0:128], in_=dtT[:, ns], identity=ident32)
                    p.I("pe", "transpose", out=ps_[:, 128:256], in_=eaT[:, ns], identity=ident32)
                    p.I("dve", "tensor_scalar", out=Dg[0:HN, 0:HN], in0=ident32[0:HN, 0:HN], scalar1=last, scalar2=None, op0=ALU.mult)
                    p.I("pe", "matmul", out=ps_[:, 256:256 + HN], lhsT=ones32[0:HN, :], rhs=Dg[0:HN, 0:HN], start=True, stop=True)
                    p.I("act", "copy", out=wtm[:, n, :], in_=ps_[:, 0:128])
                    p.I("dve", "tensor_copy", out=eatm[:, n, :], in_=ps_[:, 128:192])
                    p.I("act", "activation", out=decbc[:, n, 0:HN], in_=ps_[:, 256:256 + HN], func=AF.Exp)
            if cfg.stop <= 3:
                return False
            nhalf = 2 if SG >= 2 else 1
            GPH = SG // nhalf
            yg = [p.sb("ygs", [128, S], BF16) for _ in range(GPH * 4)]
            for half in range(nhalf):
              with p.scope():
                xg = [p.sb("xg", [128, 4, 128], F32) for _ in range(2)]
                bg = p.sb("bg", [128, S], F32)
                BT = p.sb("BT", [128, S], BF16)
                CT = p.sb("CT", [128, S], BF16)
                ngb = p.sb("ngb", [128, 512], F32)
                prev32 = p.sb("prev32", [128, 512], F32)
                prevb = p.sb("prevb", [128, 512], BF16)
                xtm = [p.sb("xtm", [128, 512], F32) for _ in range(2)]
                xc = [p.sb("xc", [128, 512], BF16) for _ in range(2)]
                xcd = [p.sb("xcd", [128, 512], BF16) for _ in range(2)]
                Btm = [p.sb("Btm", [128, 128], BF16) for _ in range(2)]
                cbT = [p.sb("cbT", [128, 128], BF16) for _ in range(2)]
                eM = [p.sb("eM", [128, 4, 128], F32) for _ in range(2)]
                Mm = [p.sb("Mm", [128, 4, 128], BF16) for _ in range(2)]
                z32 = [p.sb("z32", [128, 512], F32) for _ in range(2)]
                ty = [p.sb("ty", [128, 512], F32) for _ in range(2)]
                tu = [p.sb("tu", [128, 512], F32) for _ in range(2)]
                junk = p.sb("sjunk", [128, 512], F32)
                ssq = [p.sb("sssq", [128, 1], F32) for _ in range(2)]
                ybf = [p.sb("ybf", [128, 512], BF16) for _ in range(2)]
                psX = p.ps("spsX", [128, 512], F32)
                psB = p.ps("spsB", [128, 8, 128], BF16)
                psC = p.ps("spsC", [128, 512], F32)
                psM = [p.ps("spsM", [128, 4, 128], F32) for _ in range(2)]
                psY = p.ps("spsY", [128, 512], F32)
                psYo = p.ps("spsYo", [128, 512], F32)
                psSt = p.ps("spsSt", [128, 512], F32)
                for g in range(half * GPH, (half + 1) * GPH):
                    p.dma("sp", bg.v(), sxbc[SW // 128 + g].v())
                    p.I("act", "copy", out=BT.v(), in_=bg.v())
                    p.dma("sp", bg.v(), sxbc[SW // 128 + SG + g].v())
                    p.I("dve", "tensor_copy", out=CT.v(), in_=bg.v())
                    p.dma("sp", ngb.v(), sd_ngb.v()[:, j, g * 512:(g + 1) * 512])
                    p.I("dve", "memset", ap=prev32.v(), constant=0.0)
                    p.I("dve", "memset", ap=prevb.v(), constant=0.0)
                    for n in range(NCH):
                        ns = slice(n * 128, (n + 1) * 128)
                        b2 = n % 2
                        hs8 = slice(g * 8, (g + 1) * 8)
                        p.dma("sp", z32[b2].v(), sz[g][ns, :])
                        for i4 in range(4):
                            p.dma("sp", xg[b2][:, i4, :], sxbc[g * 4 + i4][:, ns])
                        for i4 in range(4):
                            p.I("pe", "transpose", out=psX[:, i4 * 128:(i4 + 1) * 128], in_=xg[b2][:, i4, :], identity=ident32)
                        p.I("act", "copy", out=xtm[b2].v(), in_=psX.v())
                        x3 = xtm[b2].v().re("p (h x) -> p h x", x=64)
                        p.I("dve", "tensor_tensor", out=xc[b2].v().re("p (h x) -> p h x", x=64), in0=x3,
                            in1=wtm[:, n, g * 8:(g + 1) * 8].bc([2], [128, 8, 64]), op=ALU.mult)
                        p.I("dve", "tensor_tensor", out=xcd[b2].v().re("p (h x) -> p h x", x=64), in0=x3,
                            in1=wtm[:, n, 64 + g * 8:64 + (g + 1) * 8].bc([2], [128, 8, 64]), op=ALU.mult)
                        p.I("pe", "transpose", out=psB[:, 0, :], in_=BT[:, ns], identity=ident_bf.v())
                        p.I("dve", "tensor_copy", out=Btm[b2].v(), in_=psB[:, 0, :])
                        p.I("pe", "matmul", out=psC[:, 0:128], lhsT=BT[:, ns], rhs=CT[:, ns], start=True, stop=True)
                        p.I("act", "copy", out=cbT[b2].v(), in_=psC[:, 0:128])
                        for hq in range(2):
                            pm = psM[hq]
                            for h4 in range(4):
                                h = g * 8 + hq * 4 + h4
                                sel = ident32[0:HN, h:h + 1].bc([], [HN, 128])
                                p.I("pe", "matmul", out=pm[:, h4, :], lhsT=sel, rhs=acT[0:HN, ns], start=True, stop=False)
                                p.I("pe", "matmul", out=pm[:, h4, :], lhsT=nacT[0:HN, ns], rhs=sel, start=False, stop=False)
                                p.I("pe", "matmul", out=pm[:, h4, :], lhsT=ident32, rhs=maskb, start=False, stop=True)
                            p.I("act", "activation", out=eM[hq].v(), in_=pm.v(), func=AF.Exp)
                            p.I("dve", "tensor_tensor", out=Mm[hq].v(), in0=eM[hq].v(),
                                in1=cbT[b2].v().bc([1], [128, 4, 128]), op=ALU.mult)
                        for hq in range(2):
                            for h4 in range(4):
                                hl = hq * 4 + h4
                                p.I("pe", "matmul", out=psY[:, hl * 64:(hl + 1) * 64], lhsT=Mm[hq][:, h4, :],
                                    rhs=xc[b2][:, hl * 64:(hl + 1) * 64], start=True, stop=True)
                        p.I("pe", "matmul", out=psYo.v(), lhsT=CT[:, ns], rhs=prevb.v(), start=True, stop=True)
                        p.I("pe", "matmul", out=psSt.v(), lhsT=Btm[b2].v(), rhs=xcd[b2].v(), start=True, stop=True)
                        t_ = ty[b2]
                        u_ = tu[b2]
                        t3 = t_.v().re("p (h x) -> p h x", x=64)
                        u3 = u_.v().re("p (h x) -> p h x", x=64)
                        p.I("dve", "tensor_tensor", out=t3, in0=psYo.v().re("p (h x) -> p h x", x=64),
                            in1=eatm[:, n, hs8].bc([2], [128, 8, 64]), op=ALU.mult)
                        p.I("dve", "tensor_tensor", out=t_.v(), in0=psY.v(), in1=t_.v(), op=ALU.add)
                        p.I("dve", "tensor_tensor", out=u3, in0=x3, in1=dsb[:, hs8].bc([2], [128, 8, 64]), op=ALU.mult)
                        p.I("dve", "tensor_tensor", out=t_.v(), in0=t_.v(), in1=u_.v(), op=ALU.add)
                        p32 = prev32.v().re("p (h x) -> p h x", x=64)
                        p.I("dve", "tensor_tensor", out=p32, in0=p32, in1=decbc[:, n, hs8].bc([2], [128, 8, 64]), op=ALU.mult)
                        p.I("dve", "tensor_tensor", out=prev32.v(), in0=psSt.v(), in1=prev32.v(), op=ALU.add)
                        p.I("act", "copy", out=prevb.v(), in_=prev32.v())
                        p.I("act", "activation", out=u_.v(), in_=z32[b2].v(), func=AF.Silu)
                        p.I("dve", "tensor_tensor", out=t_.v(), in0=t_.v(), in1=u_.v(), op=ALU.mult)
                        p.I("act", "activation", out=junk.v(), in_=t_.v(), func=AF.Square, accum_out=ssq[b2].v())
                        p.I("act", "activation", out=ssq[b2].v(), in_=ssq[b2].v(), func=AF.Sqrt, bias=1e-5, scale=1.0 / 512)
                        p.I("dve", "reciprocal", out=ssq[b2].v(), in_=ssq[b2].v())
                        p.I("dve", "scalar_tensor_tensor", out=ybf[b2].v(), in0=t_.v(), scalar=ssq[b2].v(), in1=ngb.v(),
                            op0=ALU.mult, op1=ALU.mult)
                        for i4 in range(4):
                            p.I("pe", "transpose", out=psB[:, 4 + i4, :], in_=ybf[b2][:, i4 * 128:(i4 + 1) * 128], identity=ident_bf.v())
                        for i4 in range(4):
                            p.I("act" if i4 % 2 == 0 else "dve", "copy" if i4 % 2 == 0 else "tensor_copy",
                                out=yg[(g - half * GPH) * 4 + i4][:, ns], in_=psB[:, 4 + i4, :])
              if cfg.stop <= 9:
                  return False
              nci = GPH * 4
              dsth = nextbuf()
              out_proj(yg, sd_out.v()[j][half * nci * 128:(half + 1) * nci * 128, :], nci,
                       lambda ft: modT[:, l, 2 * DC + ft:2 * DC + ft + 1], srcv[0], dsth.v())
              srcv[0] = dsth.v()
            return srcv[0]

    def rwkv_layer(l, j, src, dst):
        CB, TB = cfg.CB, cfg.TB
        NCHB = TB // 64
        with p.scope():
            yg = [p.sb("yg", [128, S], BF16) for _ in range(HP)]
            xs = yg
            lw1 = p.sb("lw1", [R, S], BF16)
            la1 = p.sb("la1", [R, S], BF16)
            vec = p.sb("rvec", [128, 7, HP], F32)
            omka = p.sb("omka", [128, HP], F32)
            p.dma("sp", vec.v(), rw_vecT.v()[:, j])
            p.I("dve", "tensor_scalar", out=omka.v(), in0=vec[:, 3, :], scalar1=-1.0, scalar2=1.0,
                op0=ALU.mult, op1=ALU.add)
            with p.scope():
                hT = [p.sb("hT", [128, S], BF16) for _ in range(DC)]
                mu = p.sb("mu", [128, 6, DC], F32)
                omm = p.sb("omm", [128, 6, DC], F32)
                p.dma("sp", mu.v(), rw_muT.v()[:, j])
                p.I("dve", "tensor_scalar", out=omm.v(), in0=mu.v(), scalar1=-1.0, scalar2=1.0,
                    op0=ALU.mult, op1=ALU.add)
                p.mark('rw_norm_start')
                norm_phase(src, hT, lambda dc: gsT[:, l, dc:dc + 1], lambda dc: modT[:, l, dc:dc + 1])
                p.mark('rw_proj_start')
                if cfg.stop <= 1:
                    return False
                wts = [p.sb("wi", [128, DC, 512], BF16) for _ in range(2)]
                w1t = p.sb("w1t", [128, DC, R], BF16)
                pss = [p.ps("psp", [128, 512], F32) for _ in range(4)]
                stg = [p.sb("stg", [128, TT], F32) for _ in range(4)]
                wv = rw_in.v()[j].re("(dc p) f -> p dc f", p=128)
                sk = [0]

                def mix(c):
                    for dc in range(DC):
                        p.I("dve", "memset", ap=xs[dc][:, 0:1], constant=0.0)
                        p.I("act", "mul", out=xs[dc][:, 1:S], in_=hT[dc][:, 0:S - 1], mul=mu[:, c, dc:dc + 1])
                        p.I("dve", "scalar_tensor_tensor", out=xs[dc].v(), in0=hT[dc].v(), scalar=omm[:, c, dc:dc + 1],
                            in1=xs[dc].v(), op0=ALU.mult, op1=ALU.add)

                import os as _os
                for c in range(4):
                    if not _os.environ.get("NOMIX") or c == 0:
                        mix(c)

                    def sink(ft, tt, ps_, c=c):
                        s_ = stg[sk[0] % 4]
                        e = "act" if sk[0] % 2 == 0 else "dve"
                        sk[0] += 1
                        if e == "act":
                            p.I("act", "copy", out=s_.v(), in_=ps_[:, 0:TT])
                        else:
                            p.I("dve", "tensor_copy", out=s_.v(), in_=ps_[:, 0:TT])
                        if not _os.environ.get("NOSTORE"):
                            p.dma("sp", projT[c][ft][:, tt * TT:(tt + 1) * TT], s_.v(), acc_w=True)

                    proj_fm(xs, wv, c * W, HP, sink, wts, pss)
                for c, (w1d, dstl, fn) in ((4, (rw_dw1, lw1, AF.Tanh)), (5, (rw_aw1, la1, AF.Copy))):
                    mix(c)
                    p.dma("pool", w1t.v(), w1d.v()[j].re("(dc p) r -> p dc r", p=128))
                    for tt in range(NT):
                        ts = slice(tt * TT, (tt + 1) * TT)
                        ps_ = pss[tt % 4]
                        for dc in range(DC):
                            p.I("pe", "matmul", out=ps_[0:R, 0:TT], lhsT=w1t[:, dc, :], rhs=xs[dc][:, ts],
                                start=(dc == 0), stop=(dc == DC - 1))
                        p.I("act", "activation", out=dstl[:, ts], in_=ps_[0:R, 0:TT], func=fn)
            if cfg.stop <= 2:
                return False
            p.mark('rw_scan_start')
            with p.scope():
                dw2 = p.sb("dw2", [R, W], BF16)
                aw2 = p.sb("aw2", [R, W], BF16)
                p.dma("pool", dw2.v(), rw_dw2.v()[j])
                p.dma("pool", aw2.v(), rw_aw2.v()[j])
                CBS, NSTR = 2, 2
                STR = []
                psTrS = p.ps("psTr", [128, 4, 2, 128], BF16)
                for si in range(NSTR):
                    pg_ = p.ps("PG", [128, 2, 512], F32)
                    xr_ = p.sb("Xr", [64, CBS * 2, 2, 64], BF16)
                    nxt_ = p.sb("NXT", [64, CBS * 2, 192], BF16)
                    mu_ = p.sb("MU", [64, CBS * 2, 128], BF16)
                    STR.append([dict(
                        BK=p.sb("BK", [128, CBS, 128], BF16), UV=p.sb("UV", [128, CBS, 128], BF16),
                        Xr=xr_, A_sb=p.sb("A_sb", [128, CBS * 2, 128], BF16), NXT=nxt_, MU=mu_,
                        GT=p.sb("GT", [128, CBS, 64], BF16), PpT=p.sb("PpT", [128, CBS, 64], BF16),
                        PG=pg_, psTr=psTrS, toff=si * CBS) for _par in range(2)])
                psS5 = p.ps("psS5", [128, 4, 128], F32)
                Tst = [p.sb("Tst", [128, 64], BF16) for _ in range(3)]
                psP1 = p.ps("psP", [128, 512], F32)
                psP = [psP1, psP1]
                psAV = p.ps("psAV", [128, CBS * 2, 128], F32)
                NTB = TB // TT if TB >= TT else 1
                TTB = min(TT, TB)
                tiref = [0]

                def item(hp, tb, SET):
                    hsl = slice(hp * 128, (hp + 1) * 128)
                    vcol = lambda i: vec[:, i, hp:hp + 1]
                    tbs = slice(tb * TB, (tb + 1) * TB)
                    ld, tm, BKT, KRT, KKVT = SET["ld"], SET["tm"], SET["BKT"], SET["KRT"], SET["KKVT"]
                    for c, nm in enumerate(("r", "k", "v", "g")):
                        p.dma("sp", ld[nm].v(), projT[c][hp][:, tbs])
                    r_, k_, v_, g_ = ld["r"], ld["k"], ld["v"], ld["g"]
                    if hp == 3:
                        p.mark('rw_prep_start_tb%d' % tb)
                    lw, cum, e1, e2, e3, a_, kk, kf, t1, t2, t3, bv, yT = (tm[n] for n in (
                        "lw", "cum", "e1", "e2", "e3", "a", "kk", "kf", "t1", "t2", "t3", "bv", "y"))
                    for tt in range(NTB):
                        ts = slice(tt * TTB, (tt + 1) * TTB)
                        gs_ = slice(tb * TB + tt * TTB, tb * TB + (tt + 1) * TTB)
                        ps_ = psP[0]
                        p.I("pe", "matmul", out=ps_[:, 0:TTB], lhsT=dw2[:, hsl], rhs=lw1[:, gs_], start=True, stop=True)
                        p.I("act", "activation", out=lw[:, ts], in_=ps_[:, 0:TTB], func=AF.Sigmoid, bias=vcol(0), scale=1.0)
                        ps_ = psP[1]
                        p.I("pe", "matmul", out=ps_[:, 0:TTB], lhsT=aw2[:, hsl], rhs=la1[:, gs_], start=True, stop=True)
                        p.I("act", "activation", out=a_[:, ts], in_=ps_[:, 0:TTB], func=AF.Sigmoid, bias=vcol(1), scale=1.0)
                    p.I("dve", "tensor_scalar", out=lw.v(), in0=lw.v(), scalar1=NEG_EXP_HALF, scalar2=None, op0=ALU.mult)
                    yield "P"
                    p.I("dve", "tensor_tensor_scan", out=cum.v(), data0=rmask[:, 0:TB], data1=lw.v(), initial=0.0,
                        op0=ALU.mult, op1=ALU.add)
                    yield "P"
                    p.I("act", "activation", out=e1.v(), in_=cum.v(), func=AF.Exp)
                    yield "P"
                    p.I("act", "activation", out=e2.v(), in_=cum.v(), func=AF.Exp, scale=-1.0)
                    yield "P"
                    p.I("dve", "tensor_tensor", out=t1.v(), in0=cum.v(), in1=lw.v(), op=ALU.subtract)
                    yield "P"
                    p.I("act", "activation", out=e3.v(), in_=t1.v(), func=AF.Exp)
                    yield "P"
                    p.I("act", "activation", out=t2.v(), in_=k_.v(), func=AF.Square, scale=vcol(2))
                    yield "P"
                    for tt in range(NTB):
                        ts = slice(tt * TTB, (tt + 1) * TTB)
                        ps_ = psP[tt % 2]
                        p.I("pe", "matmul", out=ps_[:, 0:TTB], lhsT=bones32, rhs=t2[:, ts], start=True, stop=True)
                        p.I("act", "activation", out=t3[:, ts], in_=ps_[:, 0:TTB], func=AF.Sqrt)
                    p.I("dve", "tensor_scalar", out=t3.v(), in0=t3.v(), scalar1=1e-12, scalar2=None, op0=ALU.max)
                    yield "P"
                    p.I("dve", "reciprocal", out=t3.v(), in_=t3.v())
                    yield "P"
                    p.I("dve", "scalar_tensor_tensor", out=kk.v(), in0=k_.v(), scalar=vcol(2), in1=t3.v(), op0=ALU.mult, op1=ALU.mult)
                    yield "P"
                    p.I("dve", "tensor_scalar", out=t1.v(), in0=a_.v(), scalar1=vcol(3), scalar2=omka[:, hp:hp + 1],
                        op0=ALU.mult, op1=ALU.add)
                    yield "P"
                    p.I("dve", "tensor_tensor", out=kf.v(), in0=k_.v(), in1=t1.v(), op=ALU.mult)
                    yield "P"
                    p.I("dve", "tensor_tensor", out=t2.v(), in0=kk.v(), in1=a_.v(), op=ALU.mult)
                    yield "P"
                    ch = lambda t: t.v().re("p (n c) -> p n c", c=64)
                    p.I("dve", "tensor_tensor", out=KRT[:, :, 1, :], in0=ch(r_), in1=ch(e1), op=ALU.mult)
                    yield "P"
                    p.I("dve", "tensor_tensor", out=BKT[:, :, 1, :], in0=ch(kf), in1=ch(e2), op=ALU.mult)
                    yield "P"
                    p.I("dve", "tensor_tensor", out=BKT[:, :, 0, :], in0=ch(t2), in1=ch(e2), op=ALU.mult)
                    yield "P"
                    p.I("dve", "tensor_tensor", out=KRT[:, :, 0, :], in0=ch(kk), in1=ch(e3), op=ALU.mult)
                    yield "P"
                    p.I("act", "copy", out=KKVT[:, :, 0, :], in_=KRT[:, :, 0, :])
                    yield "P"
                    p.I("act", "copy", out=KKVT[:, :, 1, :], in_=ch(v_))
                    yield "P"
                    p.I("dve", "scalar_tensor_tensor", out=t1.v(), in0=r_.v(), scalar=vcol(4), in1=kf.v(),
                        op0=ALU.mult, op1=ALU.mult)
                    yield "P"
                    for tt in range(NTB):
                        ts = slice(tt * TTB, (tt + 1) * TTB)
                        ps_ = psP[tt % 2]
                        p.I("pe", "matmul", out=ps_[:, 0:TTB], lhsT=bones32, rhs=t1[:, ts], start=True, stop=True)
                        p.I("dve", "tensor_tensor", out=bv[:, ts], in0=ps_[:, 0:TTB], in1=v_[:, ts], op=ALU.mult)
                    if cfg.stop <= 3:
                        return False
                    if hp == 3:
                        p.mark('rw_groups_start_tb%d' % tb)
                    yield "P_DONE"
                    if tb == 0:
                        p.I("dve", "memset", ap=Tst[tiref[0] % 3].v(), constant=0.0)

                    def group_stream(c0, cb_n, T):
                        BK, UV, Xr, A_sb, NXT, MU, GT, PpT, PG, psTr = (T[k_] for k_ in
                            ("BK", "UV", "Xr", "A_sb", "NXT", "MU", "GT", "PpT", "PG", "psTr"))
                        psTr = psTr[:, T["toff"]:T["toff"] + CBS]
                        PGv = PG.v().re("p h (c x) -> p h c x", c=CBS)
                        hc = lambda t: t.v().re("p (h c) x -> p h c x", h=2)[:, :, 0:cb_n, :]
                        for cb in range(cb_n):
                            n = c0 + cb
                            p.I("pe", "transpose", out=psTr[:, cb, 0, :], in_=BKT[:, n].re("p a c -> p (a c)"), identity=ident_bf.v())
                            p.I("pe", "transpose", out=psTr[:, cb, 1, :], in_=KKVT[:, n].re("p a c -> p (a c)"), identity=ident_bf.v())
                        p.I("dve", "tensor_copy", out=BK[:, 0:cb_n, :], in_=psTr[:, 0:cb_n, 0, :])
                        p.I("dve", "tensor_copy", out=Xr.v().re("p (h c) a x -> p h c a x", h=2)[:, :, 0:cb_n, 0, :],
                            in_=psTr[0:64, 0:cb_n, 1, :].re("p c (h x) -> p h c x", h=2))
                        p.I("dve", "tensor_copy", out=UV[64:128, 0:cb_n, :], in_=psTr[64:128, 0:cb_n, 1, :])
                        for cb in range(cb_n):
                            n = c0 + cb
                            for h in range(2):
                                hs = slice(h * 64, (h + 1) * 64)
                                p.I("pe", "matmul", out=PGv[:, h, cb, 0:128],
                                    lhsT=BKT[hs, n].re("p a c -> p (a c)"), rhs=KRT[hs, n].re("p a c -> p (a c)"),
                                    start=True, stop=True)
                                p.I("pe", "matmul", out=PGv[0:64, h, cb, 128:192],
                                    lhsT=KRT[hs, n, 0, :], rhs=BKT[hs, n, 0, :], start=True, stop=True)
                        pgA = PGv[:, :, 0:cb_n, 0:128]
                        p.I("act", "copy", out=hc(A_sb), in_=pgA)
                        p.I("dve", "tensor_tensor", out=hc(A_sb), in0=hc(A_sb),
                            in1=maskA.bc([1, 1], [128, 2, cb_n, 128]), op=ALU.mult)
                        nx = hc(NXT)
                        p.I("dve", "tensor_tensor", out=nx[:, :, :, 0:64], in0=hc(A_sb)[0:64, :, :, 0:64],
                            in1=negSU.bc([1, 1], [64, 2, cb_n, 64]), op=ALU.mult)
                        p.I("dve", "tensor_tensor", out=nx[:, :, :, 64:128], in0=nx[:, :, :, 0:64],
                            in1=cst[0:64, 5, 0:64].bc([1, 1], [64, 2, cb_n, 64]), op=ALU.add)
                        p.I("dve", "tensor_tensor", out=nx[:, :, :, 128:192],
                            in0=PGv[0:64, :, 0:cb_n, 128:192],
                            in1=negSL.bc([1, 1], [64, 2, cb_n, 64]), op=ALU.mult)
                        yield
                        for cb in range(cb_n):
                            for h in range(2):
                                q = h * CBS + cb
                                p.I("pe", "matmul", out=psAV[0:64, q, 0:64], lhsT=A_sb[64:128, q, 0:64],
                                    rhs=UV[64:128, cb, h * 64:(h + 1) * 64], start=True, stop=True)
                        pgI = PGv[0:64, :, 0:cb_n, 0:192]
                        for rnd in range(6):
                            for cb in range(cb_n):
                                for h in range(2):
                                    q = h * CBS + cb
                                    if rnd == 0:
                                        p.I("pe", "matmul", out=PGv[0:64, h, cb, 0:64], lhsT=NXT[:, q, 128:192],
                                            rhs=NXT[:, q, 0:64], start=True, stop=True)
                                    elif rnd < 5:
                                        p.I("pe", "matmul", out=PGv[0:64, h, cb, 0:128], lhsT=NXT[:, q, 128:192],
                                            rhs=NXT[:, q, 0:128], start=True, stop=True)
                                    else:
                                        p.I("pe", "matmul", out=PGv[0:64, h, cb, 64:128], lhsT=NXT[:, q, 128:192],
                                            rhs=NXT[:, q, 64:128], start=True, stop=True)
                                    if rnd < 5:
                                        p.I("pe", "matmul", out=PGv[0:64, h, cb, 128:192], lhsT=NXT[:, q, 0:64],
                                            rhs=NXT[:, q, 128:192], start=True, stop=True)
                            if rnd == 0:
                                p.I("act", "copy", out=Xr.v().re("p (h c) a x -> p h c a x", h=2)[:, :, 0:cb_n, 1, :],
                                    in_=psAV.v().re("p (h c) x -> p h c x", h=2)[0:64, :, 0:cb_n, 0:64])
                            if rnd > 0:
                                p.I("dve", "tensor_tensor", out=nx[:, :, :, 64:128], in0=pgI[:, :, :, 64:128],
                                    in1=nx[:, :, :, 64:128], op=ALU.add)
                            if rnd < 5:
                                p.I("act", "copy", out=nx[:, :, :, 0:64], in_=pgI[:, :, :, 0:64])
                                p.I("act", "copy", out=nx[:, :, :, 128:192], in_=pgI[:, :, :, 128:192])
                            yield
                        for cb in range(cb_n):
                            for h in range(2):
                                q = h * CBS + cb
                                p.I("pe", "matmul", out=PGv[0:64, h, cb, 0:128], lhsT=NXT[:, q, 64:128],
                                    rhs=Xr[:, q].re("p a c -> p (a c)"), start=True, stop=True)
                        pgM = PGv[0:64, :, 0:cb_n, 0:128]
                        p.I("act", "mul", out=hc(MU), in_=pgM, mul=-1.0)
                        p.I("dve", "tensor_scalar", out=UV[0:64, 0:cb_n, :].re("p c (h x) -> p h c x", h=2),
                            in0=pgM[:, :, :, 64:128], scalar1=-1.0, scalar2=None, op0=ALU.mult)
                        yield
                        for cb in range(cb_n):
                            for h in range(2):
                                q = h * CBS + cb
                                hs = slice(h * 64, (h + 1) * 64)
                                p.I("pe", "matmul", out=PGv[hs, h, cb, 0:64], lhsT=MU[:, q, 0:64], rhs=A_sb[0:64, q, 64:128],
                                    start=True, stop=True)
                                p.I("pe", "matmul", out=PGv[hs, h, cb, 64:128], lhsT=MU[:, q, 0:64], rhs=BK[0:64, cb, h * 64:(h + 1) * 64],
                                    start=True, stop=True)
                        for h in range(2):
                            hs = slice(h * 64, (h + 1) * 64)
                            p.I("dve", "tensor_tensor", out=GT[hs, 0:cb_n, :], in0=PGv[hs, h, 0:cb_n, 0:64],
                                in1=KRT[hs, c0:c0 + cb_n, 1, :], op=ALU.add)
                            p.I("dve", "tensor_tensor", out=PpT[hs, 0:cb_n, :], in0=PGv[hs, h, 0:cb_n, 64:128],
                                in1=cst[hs, 5, 0:64].bc([1], [64, cb_n, 64]), op=ALU.add)
                        yield
                        return

                    def back(sets):
                        slot = 0
                        c0g = sets[0][1]
                        for (T, c0, cb_n) in sets:
                            BK, UV, A_sb, GT, PpT = (T[k_] for k_ in ("BK", "UV", "A_sb", "GT", "PpT"))
                            for cb in range(cb_n):
                                n = c0 + cb
                                Tc, Tn = Tst[tiref[0] % 3], Tst[(tiref[0] + 1) % 3]
                                tiref[0] += 1
                                for h in range(2):
                                    hs = slice(h * 64, (h + 1) * 64)
                                    p.I("pe", "matmul", out=psS5[hs, slot, 0:64], lhsT=PpT[hs, cb, :], rhs=Tc[hs, :], start=True, stop=False)
                                    p.I("pe", "matmul", out=psS5[hs, slot, 0:64], lhsT=BK[:, cb, hs], rhs=UV[:, cb, hs], start=False, stop=True)
                                yield
                                for h in range(2):
                                    hs = slice(h * 64, (h + 1) * 64)
                                    p.I("dve", "tensor_scalar", out=Tn[hs, :], in0=psS5[hs, slot, 0:64],
                                        scalar1=e1[hs, n * 64 + 63:n * 64 + 64], scalar2=None, op0=ALU.mult)
                                for h in range(2):
                                    q = h * CBS + cb
                                    hs = slice(h * 64, (h + 1) * 64)
                                    p.I("pe", "matmul", out=psS5[hs, slot, 64:128], lhsT=Tc[hs, :], rhs=GT[hs, cb, :], start=True, stop=False)
                                    p.I("pe", "matmul", out=psS5[hs, slot, 64:128], lhsT=UV[:, cb, hs], rhs=A_sb[:, q, 64:128], start=False, stop=True)
                                slot += 1
                                yield
                        for h in range(2):
                            hs = slice(h * 64, (h + 1) * 64)
                            p.I("act", "copy", out=yT.v().re("p (n c) -> p n c", c=64)[hs, c0g:c0g + slot, :],
                                in_=psS5[hs, 0:slot, 64:128])

                    def drive(gens):
                        alive = list(gens)
                        while alive:
                            nxt = []
                            for gq in alive:
                                try:
                                    next(gq)
                                    nxt.append(gq)
                                except StopIteration:
                                    pass
                            alive = nxt
                            yield "G"

                    prev_sets = None
                    for gi_, g0 in enumerate(range(0, NCHB, CBS * NSTR)):
                        gens, sets = [], []
                        for si in range(NSTR):
                            c0 = g0 + si * CBS
                            if c0 < NCHB:
                                T_ = STR[si][gi_ % 2]
                                cbn_ = min(CBS, NCHB - c0)
                                gens.append(group_stream(c0, cbn_, T_))
                                sets.append((T_, c0, cbn_))
                        if prev_sets is not None:
                            gens.append(back(prev_sets))
                        yield from drive(gens)
                        prev_sets = sets
                    yield from drive([back(prev_sets)])
                    yield "G_DONE"
                    if hp == 3:
                        p.mark('rw_post_start_tb%d' % tb)
                    if cfg.stop <= 8:
                        return False
                    p.I("act", "activation", out=t2.v(), in_=yT.v(), func=AF.Square)
                    yield "Q"
                    HW_ = min(256, TTB)
                    for tt in range(TB // HW_):
                        ts = slice(tt * HW_, (tt + 1) * HW_)
                        p.I("pe", "matmul", out=psP1[:, 0:HW_], lhsT=bones32, rhs=yT[:, ts], start=True, stop=True)
                        p.I("pe", "matmul", out=psP1[:, 256:256 + HW_], lhsT=bones32, rhs=t2[:, ts], start=True, stop=True)
                        p.I("act", "mul", out=t1[:, ts], in_=psP1[:, 0:HW_], mul=1.0 / 64)
                        p.I("dve", "tensor_tensor", out=t3[:, ts], in0=t1[:, ts], in1=t1[:, ts], op=ALU.mult)
                        p.I("dve", "scalar_tensor_tensor", out=t3[:, ts], in0=psP1[:, 256:256 + HW_], scalar=1.0 / 64, in1=t3[:, ts],
                            op0=ALU.mult, op1=ALU.subtract)
                    p.I("act", "activation", out=t3.v(), in_=t3.v(), func=AF.Sqrt, bias=RWKV_GN_EPS, scale=1.0)
                    yield "Q"
                    p.I("dve", "reciprocal", out=t3.v(), in_=t3.v())
                    yield "Q"
                    p.I("dve", "tensor_tensor", out=t1.v(), in0=yT.v(), in1=t1.v(), op=ALU.subtract)
                    yield "Q"
                    p.I("dve", "tensor_tensor", out=t1.v(), in0=t1.v(), in1=t3.v(), op=ALU.mult)
                    yield "Q"
                    p.I("act", "activation", out=t1.v(), in_=t1.v(), func=AF.Identity, bias=vcol(6), scale=vcol(5))
                    yield "Q"
                    p.I("dve", "tensor_tensor", out=t1.v(), in0=t1.v(), in1=bv.v(), op=ALU.add)
                    yield "Q"
                    p.I("act", "activation", out=t2.v(), in_=g_.v(), func=AF.Silu)
                    yield "Q"
                    p.I("dve", "tensor_tensor", out=yg[hp][:, tbs], in0=t1.v(), in1=t2.v(), op=ALU.mult)
                    yield "Q"

                SETS = []
                for _si in range(2):
                    SETS.append(dict(
                        ld={nm: p.sb("ld_" + nm, [128, TB], F32) for nm in ("r", "k", "v", "g")},
                        tm={nm: p.sb("tm_" + nm, [128, TB], F32) for nm in
                            ("lw", "cum", "e1", "e2", "e3", "a", "kk", "kf", "t1", "t2", "t3", "bv", "y")},
                        BKT=p.sb("BKT", [128, NCHB, 2, 64], BF16), KRT=p.sb("KRT", [128, NCHB, 2, 64], BF16),
                        KKVT=p.sb("KKVT", [128, NCHB, 2, 64], BF16)))
                import os as _os2
                items = [(hp_, tb_) for hp_ in range(int(_os2.environ.get('RW_HP', HP))) for tb_ in range(S // TB)]
                gens_ = [item(hp_, tb_, SETS[ix % 2]) for ix, (hp_, tb_) in enumerate(items)]
                phase_ = ["P"] * len(items)
                lo = 0
                while lo < len(items):
                    hi = min(lo + 3, len(items))
                    for ix in range(lo, hi):
                        ph = phase_[ix]
                        if ph == "D":
                            continue
                        if ph == "P" and ((ix >= 2 and phase_[ix - 2] != "D") or (ix >= 1 and phase_[ix - 1] == "P")):
                            continue
                        if ph == "G" and ix >= 1 and phase_[ix - 1] in ("P", "G"):
                            continue
                        try:
                            tag = next(gens_[ix])
                            if tag == "P_DONE":
                                phase_[ix] = "G"
                            elif tag == "G_DONE":
                                phase_[ix] = "Q"
                        except StopIteration:
                            phase_[ix] = "D"
                    while lo < len(items) and phase_[lo] == "D":
                        lo += 1
            if cfg.stop <= 9:
                return False
            p.mark('rw_outproj_start')
            out_proj(yg, rw_out.v()[j], HP, lambda ft: modT[:, l, 2 * DC + ft:2 * DC + ft + 1], src, dst)
            p.mark('rw_outproj_end')
            return True

    bufs = xres
    bi = [0]

    def nextbuf():
        b_ = bufs[bi[0] % len(bufs)]
        bi[0] += 1
        return b_

    cur = xT.v()
    counters = {0: 0, 1: 0, 2: 0}
    for l, kind in enumerate(cfg.kinds):
        j = counters[kind]
        counters[kind] += 1
        if kind in (0, 1):
            dst = nextbuf()
            ok = (rwkv_layer if kind == 0 else gla_layer)(l, j, cur, dst.v())
            if ok:
                cur = dst.v()
        else:
            r_ = ssd_layer(l, j, cur, nextbuf)
            if r_ is not False:
                cur = r_
    with p.scope():
        fg = p.sb("fg", [128, DC], F32)
        p.dma("sp", fg.v(), final_gT.v())
        norm_phase(cur, None, lambda dc: fg[:, dc:dc + 1], None, out_dram=outT.v())
    p.emit()
    return nc, p


def _pp(vec, nchunk):
    v = np.asarray(vec, np.float32)
    lead = v.shape[:-1]
    v = v.reshape(lead + (nchunk, 128))
    return np.ascontiguousarray(np.moveaxis(v, -1, 0))


def prepare_inputs(cfg, inp, n_cores, batch_of_core):
    D, S, DC, L = cfg.D, cfg.S, cfg.DC, cfg.L
    consts, rmask, rmask128 = make_consts(cfg)
    shared = {
        "ada_w": np.ascontiguousarray(inp["ada_w"], dtype=np.float32),
        "ada_bT": _pp(inp["ada_b"], 3 * DC),
        "norm_gT": _pp(inp["norm_g"], DC),
        "final_gT": _pp(inp["final_g"], DC),
        "consts": consts, "rmask": rmask, "rmask128": rmask128,
    }
    if cfg.nR:
        HP = D // 128
        for k in ("rwkv_w_in", "rwkv_w_out", "rwkv_dec_w1", "rwkv_dec_w2", "rwkv_iclr_w1", "rwkv_iclr_w2"):
            shared[k] = np.ascontiguousarray(inp[k], dtype=np.float32)
        shared["rwkv_muT"] = _pp(inp["rwkv_mu"], DC)
        vecs = np.stack([inp["rwkv_dec_w0"], inp["rwkv_iclr_w0"], inp["rwkv_k_k"], inp["rwkv_k_a"],
                         np.asarray(inp["rwkv_r_k"]).reshape(cfg.nR, -1), inp["rwkv_gn_w"], inp["rwkv_gn_b"]], axis=1)
        shared["rwkv_vecT"] = _pp(vecs, HP)
    if cfg.nG:
        for k in ("gla_w_in", "gla_w_out", "gla_gate_w2"):
            shared[k] = np.ascontiguousarray(inp[k], dtype=np.float32)
        shared["gla_nbT"] = _pp(inp["gla_gate_b"], (D // 2) // 128)
        hg = np.asarray(inp["gla_head_g"], np.float32)
        shared["gla_hgb"] = np.ascontiguousarray(np.broadcast_to(hg[None], (128,) + hg.shape))
    if cfg.nS:
        SW = 2 * D
        SH = SW // 64
        for k in ("ssd_w_in", "ssd_w_out"):
            shared[k] = np.ascontiguousarray(inp[k], dtype=np.float32)
        cwk = np.asarray(inp["ssd_conv_w"], np.float32)
        shared["ssd_cwT"] = _pp(np.moveaxis(cwk, 1, 2).reshape(cfg.nS, -1).reshape(cfg.nS, cwk.shape[2], 4).transpose(0, 2, 1), cwk.shape[2] // 128).transpose(0, 1, 3, 2).copy()
        shared["ssd_cbT"] = _pp(inp["ssd_conv_b"], cwk.shape[2] // 128)
        hv = np.zeros((64, cfg.nS, 2), np.float32)
        hv[:SH, :, 0] = np.asarray(inp["ssd_dt_bias"], np.float32).T
        hv[:SH, :, 1] = np.asarray(inp["ssd_a_log"], np.float32).T
        shared["ssd_hv"] = hv
        dsk = np.asarray(inp["ssd_d"], np.float32)
        shared["ssd_dsb"] = np.ascontiguousarray(np.broadcast_to(dsk[None], (128,) + dsk.shape))
        ng = np.asarray(inp["ssd_norm_g"], np.float32)
        shared["ssd_ngb"] = np.ascontiguousarray(np.broadcast_to(ng[None], (128,) + ng.shape))
    maps = []
    for core in range(n_cores):
        b = batch_of_core[core]
        m = dict(shared)
        m["xT"] = np.ascontiguousarray(np.asarray(inp["x"][b], np.float32).T)
        m["cT"] = _pp(inp["c"][b], DC)
        maps.append(m)
    return maps


_CACHE = {}


def kernel(**inputs):
    cfg = Cfg()
    B = inputs["x"].shape[0]
    n_cores = 8
    batch_of_core = [c % B for c in range(n_cores)]
    if "nc" not in _CACHE:
        _CACHE["nc"] = build(cfg)[0]
    nc = _CACHE["nc"]
    maps = prepare_inputs(cfg, inputs, n_cores, batch_of_core)
    res = run_bass_kernel_spmd(nc, maps, core_ids=list(range(n_cores)))
    out = np.empty((B, cfg.S, cfg.D), np.float32)
    for b in range(B):
        out[b] = res.results[b]["outT"].T
    return out
```

```python
from contextlib import ExitStack
import math
import numpy as np
import concourse.bass as bass
import concourse.mybir as mybir
from concourse.bass_utils import run_bass_kernel_spmd

F32 = mybir.dt.float32
BF16 = mybir.dt.bfloat16
AF = mybir.ActivationFunctionType
ALU = mybir.AluOpType
AX = mybir.AxisListType


class V:
    __slots__ = ("ap", "tl")

    def __init__(self, ap, tl):
        self.ap = ap
        self.tl = tl

    def __getitem__(self, idx):
        return V(self.ap[idx], self.tl)

    def re(self, pat, **kw):
        return V(self.ap.rearrange(pat, **kw), self.tl)

    def bc(self, axes, shape):
        a = self.ap
        for ax in axes:
            a = a.unsqueeze(ax)
        return V(a.broadcast_to(list(shape)), self.tl)


class Tl:
    __slots__ = ("t", "lw", "rd", "name", "excl")

    def __init__(self, t, name="", excl=False):
        self.t = t
        self.lw = []
        self.rd = []
        self.name = name
        self.excl = excl

    def __getitem__(self, idx):
        return V(self.t[idx], self)

    def v(self):
        return V(self.t[:], self)


ENGS = ("pe", "act", "dve", "pool", "sp")
DMA_ENGS = ("sp", "pool", "act")
NDMA_SLOTS = 12
WRITE_KW = ("out", "accum_out", "ap")


def _compress(toks):
    best = {}
    for s, v, src in toks:
        k = id(s)
        if k not in best or best[k][1] < v:
            best[k] = (s, v, src)
    return list(best.values())


class Prog:
    def __init__(self, nc):
        self.nc = nc
        self.stacks = [ExitStack()]
        self.q = {e: [] for e in ENGS}
        self.cnt = {e: 0 for e in ENGS}
        self.sem = {e: self.stacks[0].enter_context(nc.semaphore("s_" + e)) for e in ENGS}
        self.seen = {e: {} for e in ENGS}
        self.dsem, self.dval, self.dnext = {}, {}, {}
        for e in DMA_ENGS:
            self.dsem[e] = [self.stacks[0].enter_context(nc.semaphore("d_%s%d" % (e, i))) for i in range(NDMA_SLOTS)]
            self.dval[e] = [0] * NDMA_SLOTS
            self.dnext[e] = 0
        self.n_inst = 0
        self.uid = 0
        self.marks = []

    def mark(self, label):
        self.marks.append((label, dict(self.cnt)))

    def _nm(self, name):
        self.uid += 1
        return "%s_%d" % (name, self.uid)

    def sb(self, name, shape, dt=F32):
        t = self.stacks[-1].enter_context(self.nc.sbuf_tensor(self._nm(name), list(shape), dt))
        return Tl(t, name)

    def ps(self, name, shape, dt=F32):
        nbytes = int(np.prod(shape[1:])) * (4 if dt == F32 else 2)
        assert nbytes % 2048 == 0, "PSUM tiles must cover whole banks"
        t = self.stacks[-1].enter_context(self.nc.psum_tensor(self._nm(name), list(shape), dt))
        return Tl(t, name, excl=True)

    def dram(self, name, shape, dt=F32, kind="Internal"):
        t = self.nc.dram_tensor(name, list(shape), dt, kind=kind)
        return Tl(t.ap(), name)

    class _Scope:
        def __init__(self, p):
            self.p = p

        def __enter__(self):
            self.p.stacks.append(ExitStack())

        def __exit__(self, *a):
            self.p.barrier()
            self.p.stacks.pop().close()
            return False

    def scope(self):
        return Prog._Scope(self)

    def _deps(self, eng, reads, writes, acc_w=False):
        waits = {}

        def need(tok):
            sem, val, src = tok
            if src == "pe" and eng == "pe":
                return
            k = id(sem)
            if self.seen[eng].get(k, 0) >= val:
                return
            if k not in waits or waits[k][1] < val:
                waits[k] = (sem, val)

        for tl in reads:
            for tok in tl.lw:
                need(tok)
        for tl in writes:
            if not acc_w:
                for tok in tl.lw:
                    need(tok)
            for tok in tl.rd:
                need(tok)
        for k, (sem, val) in waits.items():
            self.seen[eng][k] = val
        return list(waits.values())

    def _commit(self, tok, reads, writes, acc_w=False):
        for tl in writes:
            if acc_w:
                tl.lw.append(tok)
                if len(tl.lw) > 48:
                    tl.lw = _compress(tl.lw)
            else:
                tl.lw = [tok]
            tl.rd = []
        for tl in reads:
            if tl not in writes:
                tl.rd.append(tok)
                if len(tl.rd) > 48:
                    tl.rd = _compress(tl.rd)

    def I(self, eng, fn, *, acc_w=False, **kw):
        reads, writes, args = [], [], {}
        for k, a in kw.items():
            if isinstance(a, V):
                args[k] = a.ap
                (writes if (k in WRITE_KW or a.tl.excl) else reads).append(a.tl)
            else:
                args[k] = a
        waits = self._deps(eng, reads, writes, acc_w)
        self.cnt[eng] += 1
        tok = (self.sem[eng], self.cnt[eng], eng)
        self._commit(tok, reads, writes, acc_w)
        self.q[eng].append((waits, fn, args, (self.sem[eng], 1)))
        self.n_inst += 1

    def dma(self, eng, out, in_, acc_w=False, **kw):
        reads, writes = [in_.tl], [out.tl]
        waits = self._deps(eng, reads, writes, acc_w)
        s = self.dnext[eng]
        self.dnext[eng] = (s + 1) % NDMA_SLOTS
        sem = self.dsem[eng][s]
        prev = self.dval[eng][s]
        if prev > 0 and self.seen[eng].get(id(sem), 0) < prev:
            waits.append((sem, prev))
            self.seen[eng][id(sem)] = prev
        self.dval[eng][s] = prev + 16
        tok = (sem, prev + 16, "dma")
        self._commit(tok, reads, writes, acc_w)
        args = dict(out=out.ap, in_=in_.ap)
        args.update(kw)
        self.q[eng].append((waits, "dma_start", args, (sem, 16)))
        self.n_inst += 1

    def barrier(self):
        for e in ENGS:
            waits = []
            for e2 in ENGS:
                if self.cnt[e2] > 0 and self.seen[e].get(id(self.sem[e2]), 0) < self.cnt[e2] and e2 != e:
                    waits.append((self.sem[e2], self.cnt[e2]))
                    self.seen[e][id(self.sem[e2])] = self.cnt[e2]
            for de in DMA_ENGS:
                for s in range(NDMA_SLOTS):
                    v = self.dval[de][s]
                    sem = self.dsem[de][s]
                    if v > 0 and self.seen[e].get(id(sem), 0) < v:
                        waits.append((sem, v))
                        self.seen[e][id(sem)] = v
            if waits:
                self.q[e].append((waits, None, None, None))

    def emit(self):
        nc = self.nc
        self.barrier()
        with nc.Block() as block:
            def run(engname):
                def f(e):
                    for waits, fn, args, inc in self.q[engname]:
                        for sem, val in waits:
                            e.wait_ge(sem, val)
                        if fn is not None:
                            getattr(e, fn)(**args).then_inc(inc[0], inc[1])
                return f
            block.tensor(run("pe"))
            block.scalar(run("act"))
            block.vector(run("dve"))
            block.gpsimd(run("pool"))
            block.sync(run("sp"))
        while self.stacks:
            self.stacks.pop().close()


class Cfg:
    def __init__(self, D=2048, S=2048, kinds=(0, 1, 2, 0), lora=96,
                 gla_heads=4, gla_rank=16, ssm_groups=8):
        self.D, self.S, self.kinds, self.lora = D, S, tuple(kinds), lora
        self.gla_heads, self.gla_rank, self.ssm_groups = gla_heads, gla_rank, ssm_groups
        self.DC = D // 128
        self.TT = min(512, S)
        self.NT = S // self.TT
        self.TA = min(256, S)
        self.L = len(kinds)
        self.nR = sum(1 for k in kinds if k == 0)
        self.nG = sum(1 for k in kinds if k == 1)
        self.nS = sum(1 for k in kinds if k == 2)
        self.TB = min(512, S)
        self.CB = 4
        self.stop = 99


NEG_EXP_HALF = -math.exp(-0.5)
NORM_EPS = 1e-6
RWKV_GN_EPS = 64e-5


def make_consts(cfg):
    c = np.zeros((128, 8, 128), np.float32)
    c[:, 0, :] = np.eye(128)
    c[:, 1, :] = 1.0
    c[0:64, 2, 0:64] = 1.0
    c[64:128, 2, 64:128] = 1.0
    su = np.triu(np.ones((64, 64), np.float32), 1)
    iu = np.triu(np.ones((64, 64), np.float32), 0)
    c[0:64, 3, 0:64] = su
    c[64:128, 3, 0:64] = su
    c[0:64, 3, 64:128] = iu
    c[64:128, 3, 64:128] = iu
    c[0:64, 4, 0:64] = -su
    c[0:64, 4, 64:128] = -su.T
    c[0:64, 5, 0:64] = np.eye(64)
    c[64:128, 5, 0:64] = np.eye(64)
    c[:, 6, :] = np.triu(np.ones((128, 128), np.float32), 0)
    c[:, 7, :] = np.where(np.triu(np.ones((128, 128)), 0) > 0, 0.0, -30000.0)
    rmask = np.ones((128, cfg.S), np.float32)
    rmask[:, 0::64] = 0.0
    rmask128 = np.ones((128, cfg.S), np.float32)
    rmask128[:, 0::128] = 0.0
    return c.reshape(128, 8 * 128), rmask, rmask128


def build(cfg):
    nc = bass.Bass("TRN2", target_bir_lowering=False)
    p = Prog(nc)
    D, S, DC, TT, NT, L = cfg.D, cfg.S, cfg.DC, cfg.TT, cfg.NT, cfg.L
    EI = "ExternalInput"
    xT = p.dram("xT", [D, S], F32, EI)
    cT = p.dram("cT", [128, DC], F32, EI)
    ada_w = p.dram("ada_w", [L, D, 3 * D], F32, EI)
    ada_bT = p.dram("ada_bT", [128, L, 3 * DC], F32, EI)
    norm_gT = p.dram("norm_gT", [128, L, DC], F32, EI)
    final_gT = p.dram("final_gT", [128, DC], F32, EI)
    consts_d = p.dram("consts", [128, 8 * 128], F32, EI)
    rmask_d = p.dram("rmask", [128, S], F32, EI)
    rmask128_d = p.dram("rmask128", [128, S], F32, EI)
    outT = p.dram("outT", [D, S], F32, "ExternalOutput")
    xres = [p.dram("xres%d" % i, [D, S], F32) for i in range(3)]
    W = D
    HP = W // 128
    R = cfg.lora
    if cfg.nR:
        nR = cfg.nR
        rw_in = p.dram("rwkv_w_in", [nR, D, 4 * W], F32, EI)
        rw_out = p.dram("rwkv_w_out", [nR, W, D], F32, EI)
        rw_dw1 = p.dram("rwkv_dec_w1", [nR, D, R], F32, EI)
        rw_dw2 = p.dram("rwkv_dec_w2", [nR, R, W], F32, EI)
        rw_aw1 = p.dram("rwkv_iclr_w1", [nR, D, R], F32, EI)
        rw_aw2 = p.dram("rwkv_iclr_w2", [nR, R, W], F32, EI)
        rw_muT = p.dram("rwkv_muT", [128, nR, 6, DC], F32, EI)
        rw_vecT = p.dram("rwkv_vecT", [128, nR, 7, HP], F32, EI)
        projT = [[Tl(t.t[f * 128:(f + 1) * 128, :], "projT") for f in range(HP)]
                 for t in [p.dram("projT%d" % c, [W, S], F32) for c in range(4)]]

    GH = cfg.gla_heads
    KW, VW = D // 2, D
    DK, DV = KW // GH, VW // GH
    KC, VC = max(DK // 128, 1), DV // 128
    GR = cfg.gla_rank
    if cfg.nG:
        nG = cfg.nG
        assert DK % 128 == 0 and DV % 128 == 0 and DV <= 512
        gl_in = p.dram("gla_w_in", [nG, D, 2 * KW + 2 * VW + GR], F32, EI)
        gl_out = p.dram("gla_w_out", [nG, VW, D], F32, EI)
        gl_w2 = p.dram("gla_gate_w2", [nG, GR, KW], F32, EI)
        gl_nbT = p.dram("gla_nbT", [128, nG, KW // 128], F32, EI)
        gl_hgb = p.dram("gla_hgb", [128, nG, DV], F32, EI)
        gqk = [[Tl(t.t[f * 128:(f + 1) * 128, :], "gqk") for f in range(KW // 128)]
               for t in [p.dram("gqk%d" % c, [KW, S], F32) for c in range(2)]]
        gvg_t = [p.dram("gvg%d" % c, [S, VW], F32) for c in range(2)]
        gvg = [[Tl(t.t[:, hh * DV:(hh + 1) * DV], "gvg") for hh in range(GH)] for t in gvg_t]

    SW = 2 * D
    SH = SW // 64
    SG = SW // 512
    SN = 128
    CW = SW + 2 * SG * SN
    SIN = SW + CW + SH
    if cfg.nS:
        nS = cfg.nS
        sd_in = p.dram("ssd_w_in", [nS, D, SIN], F32, EI)
        sd_out = p.dram("ssd_w_out", [nS, SW, D], F32, EI)
        sd_cwT = p.dram("ssd_cwT", [128, nS, CW // 128, 4], F32, EI)
        sd_cbT = p.dram("ssd_cbT", [128, nS, CW // 128], F32, EI)
        sd_hv = p.dram("ssd_hv", [64, nS, 2], F32, EI)
        sd_dsb = p.dram("ssd_dsb", [128, nS, SH], F32, EI)
        sd_ngb = p.dram("ssd_ngb", [128, nS, SW], F32, EI)
        sxbc_t = p.dram("sxbc", [CW, S], F32)
        sxbc = [Tl(sxbc_t.t[f * 128:(f + 1) * 128, :], "sxbc") for f in range(CW // 128)]
        sz_t = p.dram("sz", [S, SW], F32)
        sz = [Tl(sz_t.t[:, g * 512:(g + 1) * 512], "sz") for g in range(SG)]

    cst = p.sb("cst", [128, 8, 128], F32)
    p.dma("sp", cst.v().re("p a b -> p (a b)"), consts_d.v())
    ident_bf = p.sb("ident_bf", [128, 128], BF16)
    p.I("dve", "tensor_copy", out=ident_bf.v(), in_=cst[:, 0, :])
    ones32 = cst[:, 1, :]
    bones32 = cst[:, 2, :]
    maskA = cst[:, 3, :]
    negSU = cst[0:64, 4, 0:64]
    negSL = cst[0:64, 4, 64:128]
    ident2 = cst[:, 5, 0:64]
    rmask = p.sb("rmask", [128, S], BF16)
    p.dma("pool", rmask.v(), rmask_d.v())
    rmask128 = p.sb("rmask128", [128, S], BF16)
    p.dma("pool", rmask128.v(), rmask128_d.v())
    iu128 = cst[:, 6, :]

    modT = p.sb("modT", [128, L, 3 * DC], F32)
    gsT = p.sb("gsT", [128, L, DC], F32)
    with p.scope():
        cact = p.sb("cact", [128, DC], F32)
        abT = p.sb("abT", [128, L, 3 * DC], F32)
        ngT = p.sb("ngT", [128, L, DC], F32)
        p.dma("sp", cact.v(), cT.v())
        p.dma("sp", abT.v(), ada_bT.v())
        p.dma("sp", ngT.v(), norm_gT.v())
        p.I("act", "activation", out=cact.v(), in_=cact.v(), func=AF.Silu)
        EG = 4 if (3 * DC) % 4 == 0 else 2
        cact_bf = p.sb("cact_bf", [128, DC], BF16)
        p.I("dve", "tensor_copy", out=cact_bf.v(), in_=cact.v())
        wst = [p.sb("adaw", [128, DC, EG * 128], BF16) for _ in range(3)]
        psm = p.ps("psmod", [128, 512], F32)
        gi = 0
        for l in range(L):
            wv = ada_w.v()[l].re("(dc p) e -> p dc e", p=128)
            for eg in range(3 * DC // EG):
                wt = wst[gi % 3]
                p.dma("pool", wt.v(), wv[:, :, eg * EG * 128:(eg + 1) * EG * 128])
                gi += 1
                for j in range(EG):
                    col = l * 3 * DC + eg * EG + j
                    for dc in range(DC):
                        p.I("pe", "matmul", out=psm[:, col:col + 1], lhsT=wt[:, dc, j * 128:(j + 1) * 128],
                            rhs=cact_bf[:, dc:dc + 1], start=(dc == 0), stop=(dc == DC - 1))
        p.I("dve", "tensor_tensor", out=modT.v().re("p l e -> p (l e)"), in0=psm[:, 0:L * 3 * DC],
            in1=abT.v().re("p l e -> p (l e)"), op=ALU.add)
        p.I("dve", "scalar_tensor_tensor", out=gsT.v(), in0=modT[:, :, DC:2 * DC], scalar=1.0, in1=ngT.v(),
            op0=ALU.add, op1=ALU.mult)

    def norm_phase(src, dst_tiles, g_of_dc, sh_of_dc, out_dram=None):
        TA = cfg.TA
        with p.scope():
            xt = [p.sb("xt", [128, DC, TA], F32) for _ in range(2)]
            sq = [p.sb("sq", [128, TA], F32) for _ in range(2)]
            rstd = [p.sb("rstd", [128, TA], F32) for _ in range(2)]
            tmp = [p.sb("ntmp", [128, TA], F32) for _ in range(4)]
            pss = [p.ps("psn", [128, 512], F32) for _ in range(2)]
            k = 0
            for ta in range(S // TA):
                x_ = xt[ta % 2]
                ts = slice(ta * TA, (ta + 1) * TA)
                p.dma("sp", x_.v(), src.re("(dc p) s -> p dc s", p=128)[:, :, ts])
                ps_ = pss[ta % 2]
                for dc in range(DC):
                    s_ = sq[dc % 2]
                    if dc % 2 == 0:
                        p.I("act", "activation", out=s_.v(), in_=x_[:, dc, :], func=AF.Square)
                    else:
                        p.I("dve", "tensor_tensor", out=s_.v(), in0=x_[:, dc, :], in1=x_[:, dc, :], op=ALU.mult)
                    p.I("pe", "matmul", out=ps_[:, 0:TA], lhsT=ones32, rhs=s_.v(), start=(dc == 0), stop=(dc == DC - 1))
                r_ = rstd[ta % 2]
                p.I("act", "activation", out=r_.v(), in_=ps_[:, 0:TA], func=AF.Sqrt, bias=NORM_EPS, scale=1.0 / D)
                p.I("dve", "reciprocal", out=r_.v(), in_=r_.v())
                for dc in range(DC):
                    t_ = tmp[k % 4]
                    k += 1
                    p.I("dve", "scalar_tensor_tensor", out=t_.v(), in0=x_[:, dc, :],
                        scalar=g_of_dc(dc), in1=r_.v(), op0=ALU.mult, op1=ALU.mult)
                    if out_dram is None:
                        p.I("act", "activation", out=dst_tiles[dc][:, ts], in_=t_.v(), func=AF.Identity,
                            bias=sh_of_dc(dc), scale=1.0)
                    else:
                        p.dma("sp", out_dram[dc * 128:(dc + 1) * 128, ts], t_.v(), acc_w=True)

    wring = {}

    def out_proj(yg_tiles, w_dram, nci, gate_of_ft, src, dst):
        with p.scope():
            wts = [p.sb("wo", [128, nci, 512], BF16) for _ in range(2)]
            pso = [p.ps("pso", [128, 512], F32) for _ in range(4)]
            xin = [p.sb("xin", [128, TT], F32) for _ in range(4)]
            wv = w_dram.re("(ci p) f -> p ci f", p=128)
            k = 0
            G = 4 if (D // 128) % 4 == 0 else 2
            for fg in range(D // (128 * G)):
                wt = wts[fg % 2]
                p.dma("pool", wt[:, :, 0:G * 128], wv[:, :, fg * G * 128:(fg + 1) * G * 128])
                for j in range(G):
                    ft = fg * G + j
                    for tt in range(NT):
                        ts = slice(tt * TT, (tt + 1) * TT)
                        ps_ = pso[k % 4]
                        x_ = xin[k % 4]
                        k += 1
                        p.dma("sp", x_.v(), src[ft * 128:(ft + 1) * 128, ts])
                        for ci in range(nci):
                            p.I("pe", "matmul", out=ps_[:, 0:TT], lhsT=wt[:, ci, j * 128:(j + 1) * 128],
                                rhs=yg_tiles[ci][:, ts], start=(ci == 0), stop=(ci == nci - 1))
                        p.I("dve", "scalar_tensor_tensor", out=x_.v(), in0=ps_[:, 0:TT], scalar=gate_of_ft(ft),
                            in1=x_.v(), op0=ALU.mult, op1=ALU.add)
                        p.dma("sp", dst[ft * 128:(ft + 1) * 128, ts], x_.v(), acc_w=True)

    def proj_fm(xs_tiles, wv, f0, nft, sink, wts, pss, kdim=DC):
        k = 0
        G = 4 if nft % 4 == 0 else (2 if nft % 2 == 0 else 1)
        gi = 0
        for fg in range(nft // G):
            wt = wts[gi % 2]
            gi += 1
            p.dma("pool", wt[:, :, 0:G * 128], wv[:, :, f0 + fg * G * 128:f0 + (fg + 1) * G * 128])
            for j in range(G):
                ft = fg * G + j
                for tt in range(NT):
                    ts = slice(tt * TT, (tt + 1) * TT)
                    ps_ = pss[k % len(pss)]
                    k += 1
                    for dc in range(kdim):
                        p.I("pe", "matmul", out=ps_[:, 0:TT], lhsT=wt[:, dc, j * 128:(j + 1) * 128],
                            rhs=xs_tiles[dc][:, ts], start=(dc == 0), stop=(dc == kdim - 1))
                    sink(ft, tt, ps_)


    def proj_tm(hT, wv, f0, ngroups, gw, sink, wts, pss):
        k = 0
        for gi in range(ngroups):
            wt = wts[gi % 2]
            p.dma("pool", wt[:, :, 0:gw], wv[:, :, f0 + gi * gw:f0 + (gi + 1) * gw])
            for tk in range(S // 128):
                ps_ = pss[k % len(pss)]
                k += 1
                for dc in range(DC):
                    p.I("pe", "matmul", out=ps_[:, 0:gw], lhsT=hT[dc][:, tk * 128:(tk + 1) * 128], rhs=wt[:, dc, 0:gw],
                        start=(dc == 0), stop=(dc == DC - 1))
                sink(gi, tk, ps_)

    def gla_layer(l, j, src, dst):
        NCH = S // 128
        with p.scope():
            yg = [p.sb("ygg", [128, S], BF16) for _ in range(VW // 128)]
            lowT = p.sb("lowT", [GR, S], BF16)
            gw2 = p.sb("gw2", [GR, KW], BF16)
            nb = p.sb("gnb", [128, KW // 128], F32)
            hgb = p.sb("hgb", [128, DV], F32)
            p.dma("pool", gw2.v(), gl_w2.v()[j])
            p.dma("sp", nb.v(), gl_nbT.v()[:, j])
            p.dma("sp", hgb.v(), gl_hgb.v()[:, j])
            p.I("dve", "tensor_scalar", out=nb.v(), in0=nb.v(), scalar1=-1.0, scalar2=None, op0=ALU.mult)
            wv = gl_in.v()[j].re("(dc p) f -> p dc f", p=128)
            with p.scope():
                hT = [p.sb("hT", [128, S], BF16) for _ in range(DC)]
                norm_phase(src, hT, lambda dc: gsT[:, l, dc:dc + 1], lambda dc: modT[:, l, dc:dc + 1])
                wts = [p.sb("wi", [128, DC, 512], BF16) for _ in range(2)]
                wl = p.sb("wl", [128, DC, GR], BF16)
                pss = [p.ps("psp", [128, 512], F32) for _ in range(4)]
                stg = [p.sb("stg", [128, 512], F32) for _ in range(4)]
                sk = [0]

                def evac(ps_ap, dst_ap, width):
                    s_ = stg[sk[0] % 4]
                    e = "act" if sk[0] % 2 == 0 else "dve"
                    sk[0] += 1
                    if e == "act":
                        p.I("act", "copy", out=s_[:, 0:width], in_=ps_ap)
                    else:
                        p.I("dve", "tensor_copy", out=s_[:, 0:width], in_=ps_ap)
                    p.dma("sp", dst_ap, s_[:, 0:width], acc_w=True)

                for c in range(2):
                    proj_fm(hT, wv, c * KW, KW // 128,
                            lambda ft, tt, ps_, c=c: evac(ps_[:, 0:TT], gqk[c][ft][:, tt * TT:(tt + 1) * TT], TT), wts, pss)
                for c in range(2):
                    proj_tm(hT, wv, 2 * KW + c * VW, GH, DV,
                            lambda gi, tk, ps_, c=c: evac(ps_[:, 0:DV], gvg[c][gi][tk * 128:(tk + 1) * 128, :], DV), wts, pss)
                p.dma("pool", wl.v(), wv[:, :, 2 * KW + 2 * VW:2 * KW + 2 * VW + GR])
                for tt in range(NT):
                    ts = slice(tt * TT, (tt + 1) * TT)
                    ps_ = pss[tt % 4]
                    for dc in range(DC):
                        p.I("pe", "matmul", out=ps_[0:GR, 0:TT], lhsT=wl[:, dc, :], rhs=hT[dc][:, ts],
                            start=(dc == 0), stop=(dc == DC - 1))
                    p.I("act", "copy", out=lowT[:, ts], in_=ps_[0:GR, 0:TT])
            if cfg.stop <= 2:
                return False
            with p.scope():
                ldq = [p.sb("ldq", [128, S], F32) for _ in range(KC)]
                ldk = [p.sb("ldk", [128, S], F32) for _ in range(KC)]
                QT = [p.sb("QT", [128, S], BF16) for _ in range(KC)]
                KT = [p.sb("KT", [128, S], BF16) for _ in range(KC)]
                eb = [p.sb("eb", [128, S], F32) for _ in range(KC)]
                t1 = p.sb("gt1", [128, S], F32)
                t2 = p.sb("gt2", [128, S], F32)
                S32 = [p.sb("S32", [128, DV], F32) for _ in range(KC)]
                Sb = [p.sb("Sb", [128, DV], BF16) for _ in range(KC)]
                v32 = [p.sb("v32", [128, DV], F32) for _ in range(2)]
                g32 = [p.sb("g32", [128, DV], F32) for _ in range(2)]
                Vb = [p.sb("Vb", [128, DV], BF16) for _ in range(2)]
                SG = [p.sb("SG", [128, DV], F32) for _ in range(2)]
                KTM = [p.sb("KTM", [128, KC * 128], BF16) for _ in range(2)]
                ST = [p.sb("ST", [128, 128], BF16) for _ in range(2)]
                junk = p.sb("junk", [128, DV], F32)
                ssq = [p.sb("ssq", [128, 1], F32) for _ in range(2)]
                y32 = [p.sb("y32", [128, DV], F32) for _ in range(2)]
                yb = [p.sb("yb", [128, DV], BF16) for _ in range(2)]
                psP = p.ps("gpsP", [128, 512], F32)
                psTr = p.ps("gpsTr", [128, 8, 128], BF16)
                psS = p.ps("gpsS", [128, 512], F32)
                psO = [p.ps("gpsO", [128, 512], F32) for _ in range(2)]
                psSt = [p.ps("gpsSt", [128, 512], F32) for _ in range(2)]
                psTr2 = p.ps("gpsTr2", [128, 8, 128], BF16)
                for hh in range(GH):
                    for kc in range(KC):
                        ft = hh * KC + kc
                        p.dma("sp", ldq[kc].v(), gqk[0][ft].v())
                        p.dma("sp", ldk[kc].v(), gqk[1][ft].v())
                        for tt in range(NT):
                            ts = slice(tt * TT, (tt + 1) * TT)
                            p.I("pe", "matmul", out=psP[:, 0:TT], lhsT=gw2[:, ft * 128:(ft + 1) * 128], rhs=lowT[:, ts], start=True, stop=True)
                            p.I("act", "activation", out=t1[:, ts], in_=psP[:, 0:TT], func=AF.Exp, bias=nb[:, ft:ft + 1], scale=-1.0)
                        p.I("act", "activation", out=t1.v(), in_=t1.v(), func=AF.Ln, bias=1.0, scale=1.0)
                        p.I("dve", "tensor_scalar", out=t1.v(), in0=t1.v(), scalar1=-1.0 / 16.0, scalar2=None, op0=ALU.mult)
                        p.I("dve", "tensor_tensor_scan", out=t2.v(), data0=rmask128.v(), data1=t1.v(), initial=0.0,
                            op0=ALU.mult, op1=ALU.add)
                        p.I("act", "activation", out=eb[kc].v(), in_=t2.v(), func=AF.Exp)
                        p.I("act", "activation", out=t1.v(), in_=t2.v(), func=AF.Exp, scale=-1.0)
                        p.I("dve", "scalar_tensor_tensor", out=QT[kc].v(), in0=ldq[kc].v(), scalar=float(DK) ** -0.5, in1=eb[kc].v(),
                            op0=ALU.mult, op1=ALU.mult)
                        p.I("dve", "tensor_tensor", out=KT[kc].v(), in0=ldk[kc].v(), in1=t1.v(), op=ALU.mult)
                        p.I("dve", "memset", ap=S32[kc].v(), constant=0.0)
                        p.I("dve", "memset", ap=Sb[kc].v(), constant=0.0)
                    for n in range(NCH):
                        ns = slice(n * 128, (n + 1) * 128)
                        b2 = n % 2
                        p.dma("sp", v32[b2].v(), gvg[0][hh][ns, :])
                        p.dma("sp", g32[b2].v(), gvg[1][hh][ns, :])
                        p.I("act", "copy", out=Vb[b2].v(), in_=v32[b2].v())
                        p.I("act", "activation", out=SG[b2].v(), in_=g32[b2].v(), func=AF.Silu)
                        for kc in range(KC):
                            p.I("pe", "transpose", out=psTr[:, kc, :], in_=KT[kc][:, ns], identity=ident_bf.v())
                        p.I("dve", "tensor_copy", out=KTM[b2].v().re("p (k x) -> p k x", x=128), in_=psTr[:, 0:KC, :])
                        for kc in range(KC):
                            p.I("pe", "matmul", out=psS[:, 0:128], lhsT=KT[kc][:, ns], rhs=QT[kc][:, ns], start=(kc == 0), stop=(kc == KC - 1))
                        p.I("dve", "tensor_tensor", out=ST[b2].v(), in0=psS[:, 0:128], in1=iu128, op=ALU.mult)
                        po = psO[b2]
                        p.I("pe", "matmul", out=po[:, 0:DV], lhsT=ST[b2].v(), rhs=Vb[b2].v(), start=True, stop=False)
                        for kc in range(KC):
                            p.I("pe", "matmul", out=po[:, 0:DV], lhsT=QT[kc][:, ns], rhs=Sb[kc].v(), start=False, stop=(kc == KC - 1))
                        for kc in range(KC):
                            pst = psSt[kc % 2]
                            p.I("pe", "matmul", out=pst[:, 0:DV], lhsT=KTM[b2][:, kc * 128:(kc + 1) * 128], rhs=Vb[b2].v(), start=True, stop=True)
                            dcol = eb[kc][:, n * 128 + 127:n * 128 + 128]
                            p.I("dve", "tensor_scalar", out=S32[kc].v(), in0=S32[kc].v(), scalar1=dcol, scalar2=None, op0=ALU.mult)
                            p.I("dve", "scalar_tensor_tensor", out=S32[kc].v(), in0=pst[:, 0:DV], scalar=dcol, in1=S32[kc].v(),
                                op0=ALU.mult, op1=ALU.add)
                            p.I("act", "copy", out=Sb[kc].v(), in_=S32[kc].v())
                        p.I("act", "activation", out=junk.v(), in_=po[:, 0:DV], func=AF.Square, accum_out=ssq[b2].v())
                        p.I("act", "activation", out=ssq[b2].v(), in_=ssq[b2].v(), func=AF.Sqrt, bias=NORM_EPS, scale=1.0 / DV)
                        p.I("dve", "reciprocal", out=ssq[b2].v(), in_=ssq[b2].v())
                        p.I("dve", "scalar_tensor_tensor", out=y32[b2].v(), in0=po[:, 0:DV], scalar=ssq[b2].v(), in1=hgb.v(),
                            op0=ALU.mult, op1=ALU.mult)
                        p.I("dve", "tensor_tensor", out=yb[b2].v(), in0=y32[b2].v(), in1=SG[b2].v(), op=ALU.mult)
                        for vc in range(VC):
                            p.I("pe", "transpose", out=psTr2[:, vc, :], in_=yb[b2][:, vc * 128:(vc + 1) * 128], identity=ident_bf.v())
                        for vc in range(VC):
                            p.I("act" if vc % 2 == 0 else "dve", "copy" if vc % 2 == 0 else "tensor_copy",
                                out=yg[hh * VC + vc][:, ns], in_=psTr2[:, vc, :])
            if cfg.stop <= 9:
                return False
            out_proj(yg, gl_out.v()[j], VW // 128, lambda ft: modT[:, l, 2 * DC + ft:2 * DC + ft + 1], src, dst)
            return True


    def ssd_layer(l, j, src, nextbuf):
        NCH = S // 128
        srcv = [src]
        HN = SH
        maskb = cst[:, 7, :]
        ident32 = cst[:, 0, :]
        with p.scope():
            dtT = p.sb("dtT", [128, S], F32)
            acT = p.sb("acT", [128, S], F32)
            nacT = p.sb("nacT", [128, S], F32)
            hv = p.sb("hv", [64, 2], F32)
            dsb = p.sb("dsb", [128, SH], F32)
            cw = p.sb("cw", [128, CW // 128, 4], F32)
            cbv = p.sb("cbv", [128, CW // 128], F32)
            wtm = p.sb("wtm", [128, NCH, 128], F32)
            eatm = p.sb("eatm", [128, NCH, 64], F32)
            decbc = p.sb("decbc", [128, NCH, 64], F32)
            p.dma("sp", hv.v(), sd_hv.v()[:, j])
            p.dma("sp", dsb.v(), sd_dsb.v()[:, j])
            p.dma("sp", cw.v(), sd_cwT.v()[:, j])
            p.dma("sp", cbv.v(), sd_cbT.v()[:, j])
            wv = sd_in.v()[j].re("(dc p) f -> p dc f", p=128)
            with p.scope():
                hT = [p.sb("hT", [128, S], BF16) for _ in range(DC)]
                norm_phase(src, hT, lambda dc: gsT[:, l, dc:dc + 1], lambda dc: modT[:, l, dc:dc + 1])
                wts = [p.sb("wi", [128, DC, 512], BF16) for _ in range(2)]
                wdt = p.sb("wdt", [128, DC, SH], BF16)
                pss = [p.ps("psp", [128, 512], F32) for _ in range(4)]
                stg = [p.sb("stg", [128, 512], F32) for _ in range(4)]
                xst = [p.sb("xst", [128, S + 3], F32) for _ in range(2)]
                acc = [p.sb("cacc", [128, S], F32) for _ in range(2)]
                sk = [0]

                def zsink(gi, tk, ps_):
                    s_ = stg[sk[0] % 4]
                    e = "act" if sk[0] % 2 == 0 else "dve"
                    sk[0] += 1
                    if e == "act":
                        p.I("act", "copy", out=s_.v(), in_=ps_.v())
                    else:
                        p.I("dve", "tensor_copy", out=s_.v(), in_=ps_.v())
                    p.dma("sp", sz[gi][tk * 128:(tk + 1) * 128, :], s_.v(), acc_w=True)

                proj_tm(hT, wv, 0, SG, 512, zsink, wts, pss)
                for b in range(2):
                    p.I("dve", "memset", ap=xst[b][:, 0:3], constant=0.0)

                def csink(ft, tt, ps_):
                    x_ = xst[ft % 2]
                    e = "act" if (ft + tt) % 2 == 0 else "dve"
                    if e == "act":
                        p.I("act", "copy", out=x_[:, 3 + tt * TT:3 + (tt + 1) * TT], in_=ps_[:, 0:TT])
                    else:
                        p.I("dve", "tensor_copy", out=x_[:, 3 + tt * TT:3 + (tt + 1) * TT], in_=ps_[:, 0:TT])
                    if tt == NT - 1:
                        a_ = acc[ft % 2]
                        p.I("act", "mul", out=a_.v(), in_=x_[:, 3:S + 3], mul=cw[:, ft, 3:4])
                        for kk_ in range(3):
                            p.I("dve", "scalar_tensor_tensor", out=a_.v(), in0=x_[:, kk_:S + kk_], scalar=cw[:, ft, kk_:kk_ + 1],
                                in1=a_.v(), op0=ALU.mult, op1=ALU.add)
                        p.I("act", "activation", out=a_.v(), in_=a_.v(), func=AF.Silu, bias=cbv[:, ft:ft + 1], scale=1.0)
                        p.dma("sp", sxbc[ft].v(), a_.v())

                proj_fm(hT, wv, SW, CW // 128, csink, wts, pss)
                p.dma("pool", wdt.v(), wv[:, :, SW + CW:SW + CW + SH])
                p.I("dve", "memset", ap=dtT.v(), constant=0.0)
                p.I("dve", "memset", ap=acT.v(), constant=0.0)
                for tt in range(NT):
                    ts = slice(tt * TT, (tt + 1) * TT)
                    ps_ = pss[tt % 4]
                    for dc in range(DC):
                        p.I("pe", "matmul", out=ps_[0:HN, 0:TT], lhsT=wdt[:, dc, :], rhs=hT[dc][:, ts], start=(dc == 0), stop=(dc == DC - 1))
                    p.I("act", "activation", out=dtT[0:HN, ts], in_=ps_[0:HN, 0:TT], func=AF.Exp, bias=hv[0:HN, 0:1], scale=1.0)
                p.I("act", "activation", out=dtT[0:HN, :], in_=dtT[0:HN, :], func=AF.Ln, bias=1.0, scale=1.0)
            if cfg.stop <= 2:
                return False
            with p.scope():
                eaT = p.sb("eaT", [128, S], F32)
                na = p.sb("na", [64, 1], F32)
                t1 = p.sb("st1", [128, S], F32)
                Dg = p.sb("Dg", [64, 64], F32)
                psq = [p.ps("spsq", [128, 512], F32) for _ in range(2)]
                p.I("act", "activation", out=na[0:HN, :], in_=hv[0:HN, 1:2], func=AF.Exp)
                p.I("dve", "tensor_scalar", out=na[0:HN, :], in0=na[0:HN, :], scalar1=-1.0, scalar2=None, op0=ALU.mult)
                p.I("dve", "tensor_scalar", out=t1[0:HN, :], in0=dtT[0:HN, :], scalar1=na[0:HN, 0:1], scalar2=None, op0=ALU.mult)
                p.I("dve", "tensor_tensor_scan", out=acT[0:HN, :], data0=rmask128[0:HN, :], data1=t1[0:HN, :], initial=0.0,
                    op0=ALU.mult, op1=ALU.add)
                p.I("dve", "memset", ap=nacT.v(), constant=0.0)
                p.I("dve", "memset", ap=eaT.v(), constant=0.0)
                p.I("dve", "tensor_scalar", out=nacT[0:HN, :], in0=acT[0:HN, :], scalar1=-1.0, scalar2=None, op0=ALU.mult)
                p.I("act", "activation", out=eaT[0:HN, :], in_=acT[0:HN, :], func=AF.Exp)
                for n in range(NCH):
                    ns = slice(n * 128, (n + 1) * 128)
                    last = acT[0:HN, n * 128 + 127:n * 128 + 128]
                    p.I("act", "activation", out=t1[0:HN, ns], in_=acT[0:HN, ns], func=AF.Exp, bias=last, scale=-1.0)
                    p.I("dve", "tensor_tensor", out=dtT[64:64 + HN, ns], in0=t1[0:HN, ns], in1=dtT[0:HN, ns], op=ALU.mult)
                    ps_ = psq[n % 2]
                    p.I("pe", "transpose", out=ps_[:, 0:128], in_=dtT[:, ns], identity=ident32)
                    p.I("pe", "transpose", out=ps_[:, 128:256], in_=eaT[:, ns], identity=ident32)
                    p.I("dve", "tensor_scalar", out=Dg[0:HN, 0:HN], in0=ident32[0:HN, 0:HN], scalar1=last, scalar2=None, op0=ALU.mult)
                    p.I("pe", "matmul", out=ps_[:, 256:256 + HN], lhsT=ones32[0:HN, :], rhs=Dg[0:HN, 0:HN], start=True, stop=True)
                    p.I("act", "copy", out=wtm[:, n, :], in_=ps_[:, 0:128])
                    p.I("dve", "tensor_copy", out=eatm[:, n, :], in_=ps_[:, 128:192])
                    p.I("act", "activation", out=decbc[:, n, 0:HN], in_=ps_[:, 256:256 + HN], func=AF.Exp)
            if cfg.stop <= 3:
                return False
            nhalf = 2 if SG >= 2 else 1
            GPH = SG // nhalf
            yg = [p.sb("ygs", [128, S], BF16) for _ in range(GPH * 4)]
            for half in range(nhalf):
              with p.scope():
                xg = [p.sb("xg", [128, 4, 128], F32) for _ in range(2)]
                bg = p.sb("bg", [128, S], F32)
                BT = p.sb("BT", [128, S], BF16)
                CT = p.sb("CT", [128, S], BF16)
                ngb = p.sb("ngb", [128, 512], F32)
                prev32 = p.sb("prev32", [128, 512], F32)
                prevb = p.sb("prevb", [128, 512], BF16)
                xtm = [p.sb("xtm", [128, 512], F32) for _ in range(2)]
                xc = [p.sb("xc", [128, 512], BF16) for _ in range(2)]
                xcd = [p.sb("xcd", [128, 512], BF16) for _ in range(2)]
                Btm = [p.sb("Btm", [128, 128], BF16) for _ in range(2)]
                cbT = [p.sb("cbT", [128, 128], BF16) for _ in range(2)]
                eM = [p.sb("eM", [128, 4, 128], F32) for _ in range(2)]
                Mm = [[p.sb("Mm", [128, 4, 128], BF16) for _ in range(2)] for _b in range(2)]
                maskb_bf = p.sb("maskb_bf", [128, 128], BF16)
                p.I("dve", "tensor_copy", out=maskb_bf.v(), in_=maskb)
                z32 = [p.sb("z32", [128, 512], F32) for _ in range(2)]
                ty = [p.sb("ty", [128, 512], F32) for _ in range(2)]
                tu = [p.sb("tu", [128, 512], F32) for _ in range(2)]
                junk = p.sb("sjunk", [128, 512], F32)
                ssq = [p.sb("sssq", [128, 1], F32) for _ in range(2)]
                ybf = [p.sb("ybf", [128, 512], BF16) for _ in range(2)]
                psX = p.ps("spsX", [128, 512], F32)
                psB = p.ps("spsB", [128, 8, 128], BF16)
                psC = p.ps("spsC", [128, 512], F32)
                psM = [p.ps("spsM", [128, 4, 128], F32) for _ in range(2)]
                psY = p.ps("spsY", [128, 512], F32)
                psYo = p.ps("spsYo", [128, 512], F32)
                psSt = p.ps("spsSt", [128, 512], F32)
                for g in range(half * GPH, (half + 1) * GPH):
                    p.dma("sp", bg.v(), sxbc[SW // 128 + g].v())
                    p.I("act", "copy", out=BT.v(), in_=bg.v())
                    p.dma("sp", bg.v(), sxbc[SW // 128 + SG + g].v())
                    p.I("dve", "tensor_copy", out=CT.v(), in_=bg.v())
                    p.dma("sp", ngb.v(), sd_ngb.v()[:, j, g * 512:(g + 1) * 512])
                    p.I("dve", "memset", ap=prev32.v(), constant=0.0)
                    p.I("dve", "memset", ap=prevb.v(), constant=0.0)
                    def partA(n):
                            ns = slice(n * 128, (n + 1) * 128)
                            b2 = n % 2
                            hs8 = slice(g * 8, (g + 1) * 8)
                            p.dma("sp", z32[b2].v(), sz[g][ns, :])
                            for i4 in range(4):
                                p.dma("sp", xg[b2][:, i4, :], sxbc[g * 4 + i4][:, ns])
                            for i4 in range(4):
                                p.I("pe", "transpose", out=psX[:, i4 * 128:(i4 + 1) * 128], in_=xg[b2][:, i4, :], identity=ident32)
                            p.I("act", "copy", out=xtm[b2].v(), in_=psX.v())
                            x3 = xtm[b2].v().re("p (h x) -> p h x", x=64)
                            p.I("dve", "tensor_tensor", out=xc[b2].v().re("p (h x) -> p h x", x=64), in0=x3,
                                in1=wtm[:, n, g * 8:(g + 1) * 8].bc([2], [128, 8, 64]), op=ALU.mult)
                            p.I("dve", "tensor_tensor", out=xcd[b2].v().re("p (h x) -> p h x", x=64), in0=x3,
                                in1=wtm[:, n, 64 + g * 8:64 + (g + 1) * 8].bc([2], [128, 8, 64]), op=ALU.mult)
                            p.I("pe", "matmul", out=psC[:, 128:256], lhsT=BT[:, ns], rhs=ident_bf.v(), start=True, stop=True)
                            p.I("pe", "matmul", out=psC[:, 0:128], lhsT=BT[:, ns], rhs=CT[:, ns], start=True, stop=True)
                            p.I("dve", "tensor_copy", out=Btm[b2].v(), in_=psC[:, 128:256])
                            p.I("act", "copy", out=cbT[b2].v(), in_=psC[:, 0:128])
                            for hq in range(2):
                                pm = psM[hq]
                                for h4 in range(4):
                                    h = g * 8 + hq * 4 + h4
                                    sel = ident32[0:HN, h:h + 1].bc([], [HN, 128])
                                    p.I("pe", "matmul", out=pm[:, h4, :], lhsT=sel, rhs=acT[0:HN, ns], start=True, stop=False)
                                    p.I("pe", "matmul", out=pm[:, h4, :], lhsT=nacT[0:HN, ns], rhs=sel, start=False, stop=False)
                                    p.I("pe", "matmul", out=pm[:, h4, :], lhsT=ident_bf.v(), rhs=maskb_bf.v(), start=False, stop=True)
                                p.I("act", "activation", out=eM[hq].v(), in_=pm.v(), func=AF.Exp)
                                p.I("dve", "tensor_tensor", out=Mm[b2][hq].v(), in0=eM[hq].v(),
                                    in1=cbT[b2].v().bc([1], [128, 4, 128]), op=ALU.mult)

                    def partB(n):
                            ns = slice(n * 128, (n + 1) * 128)
                            b2 = n % 2
                            hs8 = slice(g * 8, (g + 1) * 8)
                            x3 = xtm[b2].v().re("p (h x) -> p h x", x=64)
                            for hq in range(2):
                                for h4 in range(4):
                                    hl = hq * 4 + h4
                                    p.I("pe", "matmul", out=psY[:, hl * 64:(hl + 1) * 64], lhsT=Mm[b2][hq][:, h4, :],
                                        rhs=xc[b2][:, hl * 64:(hl + 1) * 64], start=True, stop=True)
                            p.I("pe", "matmul", out=psYo.v(), lhsT=CT[:, ns], rhs=prevb.v(), start=True, stop=True)
                            p.I("pe", "matmul", out=psSt.v(), lhsT=Btm[b2].v(), rhs=xcd[b2].v(), start=True, stop=True)
                            t_ = ty[b2]
                            u_ = tu[b2]
                            t3 = t_.v().re("p (h x) -> p h x", x=64)
                            u3 = u_.v().re("p (h x) -> p h x", x=64)
                            p.I("dve", "tensor_tensor", out=t3, in0=psYo.v().re("p (h x) -> p h x", x=64),
                                in1=eatm[:, n, hs8].bc([2], [128, 8, 64]), op=ALU.mult)
                            p.I("dve", "tensor_tensor", out=t_.v(), in0=psY.v(), in1=t_.v(), op=ALU.add)
                            p.I("dve", "tensor_tensor", out=u3, in0=x3, in1=dsb[:, hs8].bc([2], [128, 8, 64]), op=ALU.mult)
                            p.I("dve", "tensor_tensor", out=t_.v(), in0=t_.v(), in1=u_.v(), op=ALU.add)
                            p32 = prev32.v().re("p (h x) -> p h x", x=64)
                            p.I("dve", "tensor_tensor", out=p32, in0=p32, in1=decbc[:, n, hs8].bc([2], [128, 8, 64]), op=ALU.mult)
                            p.I("dve", "tensor_tensor", out=prev32.v(), in0=psSt.v(), in1=prev32.v(), op=ALU.add)
                            p.I("act", "copy", out=prevb.v(), in_=prev32.v())
                            p.I("act", "activation", out=u_.v(), in_=z32[b2].v(), func=AF.Silu)
                            p.I("dve", "tensor_tensor", out=t_.v(), in0=t_.v(), in1=u_.v(), op=ALU.mult)
                            p.I("act", "activation", out=junk.v(), in_=t_.v(), func=AF.Square, accum_out=ssq[b2].v())
                            p.I("act", "activation", out=ssq[b2].v(), in_=ssq[b2].v(), func=AF.Sqrt, bias=1e-5, scale=1.0 / 512)
                            p.I("dve", "reciprocal", out=ssq[b2].v(), in_=ssq[b2].v())
                            p.I("dve", "scalar_tensor_tensor", out=ybf[b2].v(), in0=t_.v(), scalar=ssq[b2].v(), in1=ngb.v(),
                                op0=ALU.mult, op1=ALU.mult)
                            for i4 in range(4):
                                p.I("pe", "transpose", out=psB[:, 4 + i4, :], in_=ybf[b2][:, i4 * 128:(i4 + 1) * 128], identity=ident_bf.v())
                            for i4 in range(4):
                                p.I("act" if i4 % 2 == 0 else "dve", "copy" if i4 % 2 == 0 else "tensor_copy",
                                    out=yg[(g - half * GPH) * 4 + i4][:, ns], in_=psB[:, 4 + i4, :])

                    partA(0)
                    for n in range(NCH):
                        if n + 1 < NCH:
                            partA(n + 1)
                        partB(n)
              if cfg.stop <= 9:
                  return False
              nci = GPH * 4
              dsth = nextbuf()
              out_proj(yg, sd_out.v()[j][half * nci * 128:(half + 1) * nci * 128, :], nci,
                       lambda ft: modT[:, l, 2 * DC + ft:2 * DC + ft + 1], srcv[0], dsth.v())
              srcv[0] = dsth.v()
            return srcv[0]

    def rwkv_layer(l, j, src, dst):
        CB, TB = cfg.CB, cfg.TB
        NCHB = TB // 64
        with p.scope():
            yg = [p.sb("yg", [128, S], BF16) for _ in range(HP)]
            xs = yg
            lw1 = p.sb("lw1", [R, S], BF16)
            la1 = p.sb("la1", [R, S], BF16)
            vec = p.sb("rvec", [128, 7, HP], F32)
            omka = p.sb("omka", [128, HP], F32)
            p.dma("sp", vec.v(), rw_vecT.v()[:, j])
            p.I("dve", "tensor_scalar", out=omka.v(), in0=vec[:, 3, :], scalar1=-1.0, scalar2=1.0,
                op0=ALU.mult, op1=ALU.add)
            with p.scope():
                hT = [p.sb("hT", [128, S], BF16) for _ in range(DC)]
                mu = p.sb("mu", [128, 6, DC], F32)
                omm = p.sb("omm", [128, 6, DC], F32)
                p.dma("sp", mu.v(), rw_muT.v()[:, j])
                p.I("dve", "tensor_scalar", out=omm.v(), in0=mu.v(), scalar1=-1.0, scalar2=1.0,
                    op0=ALU.mult, op1=ALU.add)
                p.mark('rw_norm_start')
                norm_phase(src, hT, lambda dc: gsT[:, l, dc:dc + 1], lambda dc: modT[:, l, dc:dc + 1])
                p.mark('rw_proj_start')
                if cfg.stop <= 1:
                    return False
                wts = [p.sb("wi", [128, DC, 512], BF16) for _ in range(2)]
                w1t = p.sb("w1t", [128, DC, R], BF16)
                pss = [p.ps("psp", [128, 512], F32) for _ in range(4)]
                stg = [p.sb("stg", [128, TT], F32) for _ in range(4)]
                wv = rw_in.v()[j].re("(dc p) f -> p dc f", p=128)
                sk = [0]

                def mix(c):
                    for dc in range(DC):
                        p.I("dve", "memset", ap=xs[dc][:, 0:1], constant=0.0)
                        p.I("act", "mul", out=xs[dc][:, 1:S], in_=hT[dc][:, 0:S - 1], mul=mu[:, c, dc:dc + 1])
                        p.I("dve", "scalar_tensor_tensor", out=xs[dc].v(), in0=hT[dc].v(), scalar=omm[:, c, dc:dc + 1],
                            in1=xs[dc].v(), op0=ALU.mult, op1=ALU.add)

                import os as _os
                for c in range(4):
                    if not _os.environ.get("NOMIX") or c == 0:
                        mix(c)

                    def sink(ft, tt, ps_, c=c):
                        s_ = stg[sk[0] % 4]
                        e = "act" if sk[0] % 2 == 0 else "dve"
                        sk[0] += 1
                        if e == "act":
                            p.I("act", "copy", out=s_.v(), in_=ps_[:, 0:TT])
                        else:
                            p.I("dve", "tensor_copy", out=s_.v(), in_=ps_[:, 0:TT])
                        if not _os.environ.get("NOSTORE"):
                            p.dma("sp", projT[c][ft][:, tt * TT:(tt + 1) * TT], s_.v(), acc_w=True)

                    proj_fm(xs, wv, c * W, HP, sink, wts, pss)
                for c, (w1d, dstl, fn) in ((4, (rw_dw1, lw1, AF.Tanh)), (5, (rw_aw1, la1, AF.Copy))):
                    mix(c)
                    p.dma("pool", w1t.v(), w1d.v()[j].re("(dc p) r -> p dc r", p=128))
                    for tt in range(NT):
                        ts = slice(tt * TT, (tt + 1) * TT)
                        ps_ = pss[tt % 4]
                        for dc in range(DC):
                            p.I("pe", "matmul", out=ps_[0:R, 0:TT], lhsT=w1t[:, dc, :], rhs=xs[dc][:, ts],
                                start=(dc == 0), stop=(dc == DC - 1))
                        p.I("act", "activation", out=dstl[:, ts], in_=ps_[0:R, 0:TT], func=fn)
            if cfg.stop <= 2:
                return False
            p.mark('rw_scan_start')
            with p.scope():
                dw2 = p.sb("dw2", [R, W], BF16)
                aw2 = p.sb("aw2", [R, W], BF16)
                p.dma("pool", dw2.v(), rw_dw2.v()[j])
                p.dma("pool", aw2.v(), rw_aw2.v()[j])
                CBS, NSTR = 2, 2
                STR = []
                psTrS = p.ps("psTr", [128, 4, 2, 128], BF16)
                for si in range(NSTR):
                    pg_ = p.ps("PG", [128, 2, 512], F32)
                    xr_ = p.sb("Xr", [64, CBS * 2, 2, 64], BF16)
                    nxt_ = p.sb("NXT", [64, CBS * 2, 192], BF16)
                    mu_ = p.sb("MU", [64, CBS * 2, 128], BF16)
                    STR.append([dict(
                        BK=p.sb("BK", [128, CBS, 128], BF16), UV=p.sb("UV", [128, CBS, 128], BF16),
                        Xr=xr_, A_sb=p.sb("A_sb", [128, CBS * 2, 128], BF16), NXT=nxt_, MU=mu_,
                        GT=p.sb("GT", [128, CBS, 64], BF16), PpT=p.sb("PpT", [128, CBS, 64], BF16),
                        PG=pg_, psTr=psTrS, toff=si * CBS) for _par in range(2)])
                psS5 = p.ps("psS5", [128, 4, 128], F32)
                Tst = [p.sb("Tst", [128, 64], BF16) for _ in range(3)]
                psP1 = p.ps("psP", [128, 512], F32)
                psP = [psP1, psP1]
                psAV = p.ps("psAV", [128, CBS * 2, 128], F32)
                NTB = TB // TT if TB >= TT else 1
                TTB = min(TT, TB)
                tiref = [0]

                def item(hp, tb, SET):
                    hsl = slice(hp * 128, (hp + 1) * 128)
                    vcol = lambda i: vec[:, i, hp:hp + 1]
                    tbs = slice(tb * TB, (tb + 1) * TB)
                    ld, tm, BKT, KRT, KKVT = SET["ld"], SET["tm"], SET["BKT"], SET["KRT"], SET["KKVT"]
                    for c, nm in enumerate(("r", "k", "v", "g")):
                        p.dma("sp", ld[nm].v(), projT[c][hp][:, tbs])
                    r_, k_, v_, g_ = ld["r"], ld["k"], ld["v"], ld["g"]
                    if hp == 3:
                        p.mark('rw_prep_start_tb%d' % tb)
                    lw, cum, e1, e2, e3, a_, kk, kf, t1, t2, t3, bv, yT = (tm[n] for n in (
                        "lw", "cum", "e1", "e2", "e3", "a", "kk", "kf", "t1", "t2", "t3", "bv", "y"))
                    for tt in range(NTB):
                        ts = slice(tt * TTB, (tt + 1) * TTB)
                        gs_ = slice(tb * TB + tt * TTB, tb * TB + (tt + 1) * TTB)
                        ps_ = psP[0]
                        p.I("pe", "matmul", out=ps_[:, 0:TTB], lhsT=dw2[:, hsl], rhs=lw1[:, gs_], start=True, stop=True)
                        p.I("act", "activation", out=lw[:, ts], in_=ps_[:, 0:TTB], func=AF.Sigmoid, bias=vcol(0), scale=1.0)
                        ps_ = psP[1]
                        p.I("pe", "matmul", out=ps_[:, 0:TTB], lhsT=aw2[:, hsl], rhs=la1[:, gs_], start=True, stop=True)
                        p.I("act", "activation", out=a_[:, ts], in_=ps_[:, 0:TTB], func=AF.Sigmoid, bias=vcol(1), scale=1.0)
                    p.I("dve", "tensor_scalar", out=lw.v(), in0=lw.v(), scalar1=NEG_EXP_HALF, scalar2=None, op0=ALU.mult)
                    yield "P"
                    p.I("dve", "tensor_tensor_scan", out=cum.v(), data0=rmask[:, 0:TB], data1=lw.v(), initial=0.0,
                        op0=ALU.mult, op1=ALU.add)
                    yield "P"
                    p.I("act", "activation", out=e1.v(), in_=cum.v(), func=AF.Exp)
                    yield "P"
                    p.I("act", "activation", out=e2.v(), in_=cum.v(), func=AF.Exp, scale=-1.0)
                    yield "P"
                    p.I("dve", "tensor_tensor", out=t1.v(), in0=cum.v(), in1=lw.v(), op=ALU.subtract)
                    yield "P"
                    p.I("act", "activation", out=e3.v(), in_=t1.v(), func=AF.Exp)
                    yield "P"
                    p.I("act", "activation", out=t2.v(), in_=k_.v(), func=AF.Square, scale=vcol(2))
                    yield "P"
                    for tt in range(NTB):
                        ts = slice(tt * TTB, (tt + 1) * TTB)
                        ps_ = psP[tt % 2]
                        p.I("pe", "matmul", out=ps_[:, 0:TTB], lhsT=bones32, rhs=t2[:, ts], start=True, stop=True)
                        p.I("act", "activation", out=t3[:, ts], in_=ps_[:, 0:TTB], func=AF.Sqrt)
                    p.I("dve", "tensor_scalar", out=t3.v(), in0=t3.v(), scalar1=1e-12, scalar2=None, op0=ALU.max)
                    yield "P"
                    p.I("dve", "reciprocal", out=t3.v(), in_=t3.v())
                    yield "P"
                    p.I("dve", "scalar_tensor_tensor", out=kk.v(), in0=k_.v(), scalar=vcol(2), in1=t3.v(), op0=ALU.mult, op1=ALU.mult)
                    yield "P"
                    p.I("dve", "tensor_scalar", out=t1.v(), in0=a_.v(), scalar1=vcol(3), scalar2=omka[:, hp:hp + 1],
                        op0=ALU.mult, op1=ALU.add)
                    yield "P"
                    p.I("dve", "tensor_tensor", out=kf.v(), in0=k_.v(), in1=t1.v(), op=ALU.mult)
                    yield "P"
                    p.I("dve", "tensor_tensor", out=t2.v(), in0=kk.v(), in1=a_.v(), op=ALU.mult)
                    yield "P"
                    ch = lambda t: t.v().re("p (n c) -> p n c", c=64)
                    p.I("dve", "tensor_tensor", out=KRT[:, :, 1, :], in0=ch(r_), in1=ch(e1), op=ALU.mult)
                    yield "P"
                    p.I("dve", "tensor_tensor", out=BKT[:, :, 1, :], in0=ch(kf), in1=ch(e2), op=ALU.mult)
                    yield "P"
                    p.I("dve", "tensor_tensor", out=BKT[:, :, 0, :], in0=ch(t2), in1=ch(e2), op=ALU.mult)
                    yield "P"
                    p.I("dve", "tensor_tensor", out=KRT[:, :, 0, :], in0=ch(kk), in1=ch(e3), op=ALU.mult)
                    yield "P"
                    p.I("act", "copy", out=KKVT[:, :, 0, :], in_=KRT[:, :, 0, :])
                    yield "P"
                    p.I("act", "copy", out=KKVT[:, :, 1, :], in_=ch(v_))
                    yield "P"
                    p.I("dve", "scalar_tensor_tensor", out=t1.v(), in0=r_.v(), scalar=vcol(4), in1=kf.v(),
                        op0=ALU.mult, op1=ALU.mult)
                    yield "P"
                    for tt in range(NTB):
                        ts = slice(tt * TTB, (tt + 1) * TTB)
                        ps_ = psP[tt % 2]
                        p.I("pe", "matmul", out=ps_[:, 0:TTB], lhsT=bones32, rhs=t1[:, ts], start=True, stop=True)
                        p.I("dve", "tensor_tensor", out=bv[:, ts], in0=ps_[:, 0:TTB], in1=v_[:, ts], op=ALU.mult)
                    if cfg.stop <= 3:
                        return False
                    if hp == 3:
                        p.mark('rw_groups_start_tb%d' % tb)
                    yield "P_DONE"
                    if tb == 0:
                        p.I("dve", "memset", ap=Tst[tiref[0] % 3].v(), constant=0.0)

                    def group_stream(c0, cb_n, T):
                        BK, UV, Xr, A_sb, NXT, MU, GT, PpT, PG, psTr = (T[k_] for k_ in
                            ("BK", "UV", "Xr", "A_sb", "NXT", "MU", "GT", "PpT", "PG", "psTr"))
                        psTr = psTr[:, T["toff"]:T["toff"] + CBS]
                        PGv = PG.v().re("p h (c x) -> p h c x", c=CBS)
                        hc = lambda t: t.v().re("p (h c) x -> p h c x", h=2)[:, :, 0:cb_n, :]
                        for cb in range(cb_n):
                            n = c0 + cb
                            p.I("pe", "transpose", out=psTr[:, cb, 0, :], in_=BKT[:, n].re("p a c -> p (a c)"), identity=ident_bf.v())
                            p.I("pe", "transpose", out=psTr[:, cb, 1, :], in_=KKVT[:, n].re("p a c -> p (a c)"), identity=ident_bf.v())
                        p.I("dve", "tensor_copy", out=BK[:, 0:cb_n, :], in_=psTr[:, 0:cb_n, 0, :])
                        p.I("dve", "tensor_copy", out=Xr.v().re("p (h c) a x -> p h c a x", h=2)[:, :, 0:cb_n, 0, :],
                            in_=psTr[0:64, 0:cb_n, 1, :].re("p c (h x) -> p h c x", h=2))
                        p.I("dve", "tensor_copy", out=UV[64:128, 0:cb_n, :], in_=psTr[64:128, 0:cb_n, 1, :])
                        for cb in range(cb_n):
                            n = c0 + cb
                            for h in range(2):
                                hs = slice(h * 64, (h + 1) * 64)
                                p.I("pe", "matmul", out=PGv[:, h, cb, 0:128],
                                    lhsT=BKT[hs, n].re("p a c -> p (a c)"), rhs=KRT[hs, n].re("p a c -> p (a c)"),
                                    start=True, stop=True)
                                p.I("pe", "matmul", out=PGv[0:64, h, cb, 128:192],
                                    lhsT=KRT[hs, n, 0, :], rhs=BKT[hs, n, 0, :], start=True, stop=True)
                        pgA = PGv[:, :, 0:cb_n, 0:128]
                        p.I("act", "copy", out=hc(A_sb), in_=pgA)
                        p.I("dve", "tensor_tensor", out=hc(A_sb), in0=hc(A_sb),
                            in1=maskA.bc([1, 1], [128, 2, cb_n, 128]), op=ALU.mult)
                        nx = hc(NXT)
                        p.I("dve", "tensor_tensor", out=nx[:, :, :, 0:64], in0=hc(A_sb)[0:64, :, :, 0:64],
                            in1=negSU.bc([1, 1], [64, 2, cb_n, 64]), op=ALU.mult)
                        p.I("dve", "tensor_tensor", out=nx[:, :, :, 64:128], in0=nx[:, :, :, 0:64],
                            in1=cst[0:64, 5, 0:64].bc([1, 1], [64, 2, cb_n, 64]), op=ALU.add)
                        p.I("dve", "tensor_tensor", out=nx[:, :, :, 128:192],
                            in0=PGv[0:64, :, 0:cb_n, 128:192],
                            in1=negSL.bc([1, 1], [64, 2, cb_n, 64]), op=ALU.mult)
                        yield
                        for cb in range(cb_n):
                            for h in range(2):
                                q = h * CBS + cb
                                p.I("pe", "matmul", out=psAV[0:64, q, 0:64], lhsT=A_sb[64:128, q, 0:64],
                                    rhs=UV[64:128, cb, h * 64:(h + 1) * 64], start=True, stop=True)
                        pgI = PGv[0:64, :, 0:cb_n, 0:192]
                        for rnd in range(6):
                            for cb in range(cb_n):
                                for h in range(2):
                                    q = h * CBS + cb
                                    if rnd == 0:
                                        p.I("pe", "matmul", out=PGv[0:64, h, cb, 0:64], lhsT=NXT[:, q, 128:192],
                                            rhs=NXT[:, q, 0:64], start=True, stop=True)
                                    elif rnd < 5:
                                        p.I("pe", "matmul", out=PGv[0:64, h, cb, 0:128], lhsT=NXT[:, q, 128:192],
                                            rhs=NXT[:, q, 0:128], start=True, stop=True)
                                    else:
                                        p.I("pe", "matmul", out=PGv[0:64, h, cb, 64:128], lhsT=NXT[:, q, 128:192],
                                            rhs=NXT[:, q, 64:128], start=True, stop=True)
                                    if rnd < 5:
                                        p.I("pe", "matmul", out=PGv[0:64, h, cb, 128:192], lhsT=NXT[:, q, 0:64],
                                            rhs=NXT[:, q, 128:192], start=True, stop=True)
                            if rnd == 0:
                                p.I("act", "copy", out=Xr.v().re("p (h c) a x -> p h c a x", h=2)[:, :, 0:cb_n, 1, :],
                                    in_=psAV.v().re("p (h c) x -> p h c x", h=2)[0:64, :, 0:cb_n, 0:64])
                            if rnd > 0:
                                p.I("dve", "tensor_tensor", out=nx[:, :, :, 64:128], in0=pgI[:, :, :, 64:128],
                                    in1=nx[:, :, :, 64:128], op=ALU.add)
                            if rnd < 5:
                                p.I("act", "copy", out=nx[:, :, :, 0:64], in_=pgI[:, :, :, 0:64])
                                p.I("act", "copy", out=nx[:, :, :, 128:192], in_=pgI[:, :, :, 128:192])
                            yield
                        for cb in range(cb_n):
                            for h in range(2):
                                q = h * CBS + cb
                                p.I("pe", "matmul", out=PGv[0:64, h, cb, 0:128], lhsT=NXT[:, q, 64:128],
                                    rhs=Xr[:, q].re("p a c -> p (a c)"), start=True, stop=True)
                        pgM = PGv[0:64, :, 0:cb_n, 0:128]
                        p.I("act", "mul", out=hc(MU), in_=pgM, mul=-1.0)
                        p.I("dve", "tensor_scalar", out=UV[0:64, 0:cb_n, :].re("p c (h x) -> p h c x", h=2),
                            in0=pgM[:, :, :, 64:128], scalar1=-1.0, scalar2=None, op0=ALU.mult)
                        yield
                        for cb in range(cb_n):
                            for h in range(2):
                                q = h * CBS + cb
                                hs = slice(h * 64, (h + 1) * 64)
                                p.I("pe", "matmul", out=PGv[hs, h, cb, 0:64], lhsT=MU[:, q, 0:64], rhs=A_sb[0:64, q, 64:128],
                                    start=True, stop=True)
                                p.I("pe", "matmul", out=PGv[hs, h, cb, 64:128], lhsT=MU[:, q, 0:64], rhs=BK[0:64, cb, h * 64:(h + 1) * 64],
                                    start=True, stop=True)
                        for h in range(2):
                            hs = slice(h * 64, (h + 1) * 64)
                            p.I("dve", "tensor_tensor", out=GT[hs, 0:cb_n, :], in0=PGv[hs, h, 0:cb_n, 0:64],
                                in1=KRT[hs, c0:c0 + cb_n, 1, :], op=ALU.add)
                            p.I("dve", "tensor_tensor", out=PpT[hs, 0:cb_n, :], in0=PGv[hs, h, 0:cb_n, 64:128],
                                in1=cst[hs, 5, 0:64].bc([1], [64, cb_n, 64]), op=ALU.add)
                        yield
                        return

                    def back(sets):
                        slot = 0
                        c0g = sets[0][1]
                        for (T, c0, cb_n) in sets:
                            BK, UV, A_sb, GT, PpT = (T[k_] for k_ in ("BK", "UV", "A_sb", "GT", "PpT"))
                            for cb in range(cb_n):
                                n = c0 + cb
                                Tc, Tn = Tst[tiref[0] % 3], Tst[(tiref[0] + 1) % 3]
                                tiref[0] += 1
                                for h in range(2):
                                    hs = slice(h * 64, (h + 1) * 64)
                                    p.I("pe", "matmul", out=psS5[hs, slot, 0:64], lhsT=PpT[hs, cb, :], rhs=Tc[hs, :], start=True, stop=False)
                                    p.I("pe", "matmul", out=psS5[hs, slot, 0:64], lhsT=BK[:, cb, hs], rhs=UV[:, cb, hs], start=False, stop=True)
                                yield
                                for h in range(2):
                                    hs = slice(h * 64, (h + 1) * 64)
                                    p.I("dve", "tensor_scalar", out=Tn[hs, :], in0=psS5[hs, slot, 0:64],
                                        scalar1=e1[hs, n * 64 + 63:n * 64 + 64], scalar2=None, op0=ALU.mult)
                                for h in range(2):
                                    q = h * CBS + cb
                                    hs = slice(h * 64, (h + 1) * 64)
                                    p.I("pe", "matmul", out=psS5[hs, slot, 64:128], lhsT=Tc[hs, :], rhs=GT[hs, cb, :], start=True, stop=False)
                                    p.I("pe", "matmul", out=psS5[hs, slot, 64:128], lhsT=UV[:, cb, hs], rhs=A_sb[:, q, 64:128], start=False, stop=True)
                                slot += 1
                                yield
                        for h in range(2):
                            hs = slice(h * 64, (h + 1) * 64)
                            p.I("act", "copy", out=yT.v().re("p (n c) -> p n c", c=64)[hs, c0g:c0g + slot, :],
                                in_=psS5[hs, 0:slot, 64:128])

                    def drive(gens):
                        alive = list(gens)
                        while alive:
                            nxt = []
                            for gq in alive:
                                try:
                                    next(gq)
                                    nxt.append(gq)
                                except StopIteration:
                                    pass
                            alive = nxt
                            yield "G"

                    prev_sets = None
                    for gi_, g0 in enumerate(range(0, NCHB, CBS * NSTR)):
                        gens, sets = [], []
                        for si in range(NSTR):
                            c0 = g0 + si * CBS
                            if c0 < NCHB:
                                T_ = STR[si][gi_ % 2]
                                cbn_ = min(CBS, NCHB - c0)
                                gens.append(group_stream(c0, cbn_, T_))
                                sets.append((T_, c0, cbn_))
                        if prev_sets is not None:
                            gens.append(back(prev_sets))
                        yield from drive(gens)
                        prev_sets = sets
                    yield from drive([back(prev_sets)])
                    yield "G_DONE"
                    if hp == 3:
                        p.mark('rw_post_start_tb%d' % tb)
                    if cfg.stop <= 8:
                        return False
                    p.I("act", "activation", out=t2.v(), in_=yT.v(), func=AF.Square)
                    yield "Q"
                    HW_ = min(256, TTB)
                    for tt in range(TB // HW_):
                        ts = slice(tt * HW_, (tt + 1) * HW_)
                        p.I("pe", "matmul", out=psP1[:, 0:HW_], lhsT=bones32, rhs=yT[:, ts], start=True, stop=True)
                        p.I("pe", "matmul", out=psP1[:, 256:256 + HW_], lhsT=bones32, rhs=t2[:, ts], start=True, stop=True)
                        p.I("act", "mul", out=t1[:, ts], in_=psP1[:, 0:HW_], mul=1.0 / 64)
                        p.I("dve", "tensor_tensor", out=t3[:, ts], in0=t1[:, ts], in1=t1[:, ts], op=ALU.mult)
                        p.I("dve", "scalar_tensor_tensor", out=t3[:, ts], in0=psP1[:, 256:256 + HW_], scalar=1.0 / 64, in1=t3[:, ts],
                            op0=ALU.mult, op1=ALU.subtract)
                    p.I("act", "activation", out=t3.v(), in_=t3.v(), func=AF.Sqrt, bias=RWKV_GN_EPS, scale=1.0)
                    yield "Q"
                    p.I("dve", "reciprocal", out=t3.v(), in_=t3.v())
                    yield "Q"
                    p.I("dve", "tensor_tensor", out=t1.v(), in0=yT.v(), in1=t1.v(), op=ALU.subtract)
                    yield "Q"
                    p.I("dve", "tensor_tensor", out=t1.v(), in0=t1.v(), in1=t3.v(), op=ALU.mult)
                    yield "Q"
                    p.I("act", "activation", out=t1.v(), in_=t1.v(), func=AF.Identity, bias=vcol(6), scale=vcol(5))
                    yield "Q"
                    p.I("dve", "tensor_tensor", out=t1.v(), in0=t1.v(), in1=bv.v(), op=ALU.add)
                    yield "Q"
                    p.I("act", "activation", out=t2.v(), in_=g_.v(), func=AF.Silu)
                    yield "Q"
                    p.I("dve", "tensor_tensor", out=yg[hp][:, tbs], in0=t1.v(), in1=t2.v(), op=ALU.mult)
                    yield "Q"

                SETS = []
                for _si in range(2):
                    SETS.append(dict(
                        ld={nm: p.sb("ld_" + nm, [128, TB], F32) for nm in ("r", "k", "v", "g")},
                        tm={nm: p.sb("tm_" + nm, [128, TB], F32) for nm in
                            ("lw", "cum", "e1", "e2", "e3", "a", "kk", "kf", "t1", "t2", "t3", "bv", "y")},
                        BKT=p.sb("BKT", [128, NCHB, 2, 64], BF16), KRT=p.sb("KRT", [128, NCHB, 2, 64], BF16),
                        KKVT=p.sb("KKVT", [128, NCHB, 2, 64], BF16)))
                import os as _os2
                items = [(hp_, tb_) for hp_ in range(int(_os2.environ.get('RW_HP', HP))) for tb_ in range(S // TB)]
                gens_ = [item(hp_, tb_, SETS[ix % 2]) for ix, (hp_, tb_) in enumerate(items)]
                phase_ = ["P"] * len(items)
                lo = 0
                while lo < len(items):
                    hi = min(lo + 3, len(items))
                    for ix in range(lo, hi):
                        ph = phase_[ix]
                        if ph == "D":
                            continue
                        if ph == "P" and ((ix >= 2 and phase_[ix - 2] != "D") or (ix >= 1 and phase_[ix - 1] == "P")):
                            continue
                        if ph == "G" and ix >= 1 and phase_[ix - 1] in ("P", "G"):
                            continue
                        try:
                            tag = next(gens_[ix])
                            if tag == "P_DONE":
                                phase_[ix] = "G"
                            elif tag == "G_DONE":
                                phase_[ix] = "Q"
                        except StopIteration:
                            phase_[ix] = "D"
                    while lo < len(items) and phase_[lo] == "D":
                        lo += 1
            if cfg.stop <= 9:
                return False
            p.mark('rw_outproj_start')
            out_proj(yg, rw_out.v()[j], HP, lambda ft: modT[:, l, 2 * DC + ft:2 * DC + ft + 1], src, dst)
            p.mark('rw_outproj_end')
            return True

    bufs = xres
    bi = [0]

    def nextbuf():
        b_ = bufs[bi[0] % len(bufs)]
        bi[0] += 1
        return b_

    cur = xT.v()
    counters = {0: 0, 1: 0, 2: 0}
    for l, kind in enumerate(cfg.kinds):
        j = counters[kind]
        counters[kind] += 1
        if kind in (0, 1):
            dst = nextbuf()
            ok = (rwkv_layer if kind == 0 else gla_layer)(l, j, cur, dst.v())
            if ok:
                cur = dst.v()
        else:
            r_ = ssd_layer(l, j, cur, nextbuf)
            if r_ is not False:
                cur = r_
    with p.scope():
        fg = p.sb("fg", [128, DC], F32)
        p.dma("sp", fg.v(), final_gT.v())
        norm_phase(cur, None, lambda dc: fg[:, dc:dc + 1], None, out_dram=outT.v())
    p.emit()
    return nc, p


def _pp(vec, nchunk):
    v = np.asarray(vec, np.float32)
    lead = v.shape[:-1]
    v = v.reshape(lead + (nchunk, 128))
    return np.ascontiguousarray(np.moveaxis(v, -1, 0))


def prepare_inputs(cfg, inp, n_cores, batch_of_core):
    D, S, DC, L = cfg.D, cfg.S, cfg.DC, cfg.L
    consts, rmask, rmask128 = make_consts(cfg)
    shared = {
        "ada_w": np.ascontiguousarray(inp["ada_w"], dtype=np.float32),
        "ada_bT": _pp(inp["ada_b"], 3 * DC),
        "norm_gT": _pp(inp["norm_g"], DC),
        "final_gT": _pp(inp["final_g"], DC),
        "consts": consts, "rmask": rmask, "rmask128": rmask128,
    }
    if cfg.nR:
        HP = D // 128
        for k in ("rwkv_w_in", "rwkv_w_out", "rwkv_dec_w1", "rwkv_dec_w2", "rwkv_iclr_w1", "rwkv_iclr_w2"):
            shared[k] = np.ascontiguousarray(inp[k], dtype=np.float32)
        shared["rwkv_muT"] = _pp(inp["rwkv_mu"], DC)
        vecs = np.stack([inp["rwkv_dec_w0"], inp["rwkv_iclr_w0"], inp["rwkv_k_k"], inp["rwkv_k_a"],
                         np.asarray(inp["rwkv_r_k"]).reshape(cfg.nR, -1), inp["rwkv_gn_w"], inp["rwkv_gn_b"]], axis=1)
        shared["rwkv_vecT"] = _pp(vecs, HP)
    if cfg.nG:
        for k in ("gla_w_in", "gla_w_out", "gla_gate_w2"):
            shared[k] = np.ascontiguousarray(inp[k], dtype=np.float32)
        shared["gla_nbT"] = _pp(inp["gla_gate_b"], (D // 2) // 128)
        hg = np.asarray(inp["gla_head_g"], np.float32)
        shared["gla_hgb"] = np.ascontiguousarray(np.broadcast_to(hg[None], (128,) + hg.shape))
    if cfg.nS:
        SW = 2 * D
        SH = SW // 64
        for k in ("ssd_w_in", "ssd_w_out"):
            shared[k] = np.ascontiguousarray(inp[k], dtype=np.float32)
        cwk = np.asarray(inp["ssd_conv_w"], np.float32)
        shared["ssd_cwT"] = _pp(np.moveaxis(cwk, 1, 2).reshape(cfg.nS, -1).reshape(cfg.nS, cwk.shape[2], 4).transpose(0, 2, 1), cwk.shape[2] // 128).transpose(0, 1, 3, 2).copy()
        shared["ssd_cbT"] = _pp(inp["ssd_conv_b"], cwk.shape[2] // 128)
        hv = np.zeros((64, cfg.nS, 2), np.float32)
        hv[:SH, :, 0] = np.asarray(inp["ssd_dt_bias"], np.float32).T
        hv[:SH, :, 1] = np.asarray(inp["ssd_a_log"], np.float32).T
        shared["ssd_hv"] = hv
        dsk = np.asarray(inp["ssd_d"], np.float32)
        shared["ssd_dsb"] = np.ascontiguousarray(np.broadcast_to(dsk[None], (128,) + dsk.shape))
        ng = np.asarray(inp["ssd_norm_g"], np.float32)
        shared["ssd_ngb"] = np.ascontiguousarray(np.broadcast_to(ng[None], (128,) + ng.shape))
    maps = []
    for core in range(n_cores):
        b = batch_of_core[core]
        m = dict(shared)
        m["xT"] = np.ascontiguousarray(np.asarray(inp["x"][b], np.float32).T)
        m["cT"] = _pp(inp["c"][b], DC)
        maps.append(m)
    return maps


_CACHE = {}


def kernel(**inputs):
    cfg = Cfg()
    B = inputs["x"].shape[0]
    n_cores = 8
    batch_of_core = [c % B for c in range(n_cores)]
    if "nc" not in _CACHE:
        _CACHE["nc"] = build(cfg)[0]
    nc = _CACHE["nc"]
    maps = prepare_inputs(cfg, inputs, n_cores, batch_of_core)
    res = run_bass_kernel_spmd(nc, maps, core_ids=list(range(n_cores)))
    out = np.empty((B, cfg.S, cfg.D), np.float32)
    for b in range(B):
        out[b] = res.results[b]["outT"].T
    return out
```

```python
from contextlib import ExitStack
import math
import numpy as np
import concourse.bass as bass
import concourse.mybir as mybir
from concourse.bass_utils import run_bass_kernel_spmd

F32 = mybir.dt.float32
BF16 = mybir.dt.bfloat16
AF = mybir.ActivationFunctionType
ALU = mybir.AluOpType
AX = mybir.AxisListType


class V:
    __slots__ = ("ap", "tl")

    def __init__(self, ap, tl):
        self.ap = ap
        self.tl = tl

    def __getitem__(self, idx):
        return V(self.ap[idx], self.tl)

    def re(self, pat, **kw):
        return V(self.ap.rearrange(pat, **kw), self.tl)

    def bc(self, axes, shape):
        a = self.ap
        for ax in axes:
            a = a.unsqueeze(ax)
        return V(a.broadcast_to(list(shape)), self.tl)


class Tl:
    __slots__ = ("t", "lw", "rd", "name", "excl")

    def __init__(self, t, name="", excl=False):
        self.t = t
        self.lw = []
        self.rd = []
        self.name = name
        self.excl = excl

    def __getitem__(self, idx):
        return V(self.t[idx], self)

    def v(self):
        return V(self.t[:], self)


ENGS = ("pe", "act", "dve", "pool", "sp")
DMA_ENGS = ("sp", "pool", "act")
NDMA_SLOTS = 12
WRITE_KW = ("out", "accum_out", "ap")


def _compress(toks):
    best = {}
    for s, v, src in toks:
        k = id(s)
        if k not in best or best[k][1] < v:
            best[k] = (s, v, src)
    return list(best.values())


class Prog:
    def __init__(self, nc):
        self.nc = nc
        self.stacks = [ExitStack()]
        self.q = {e: [] for e in ENGS}
        self.cnt = {e: 0 for e in ENGS}
        self.sem = {e: self.stacks[0].enter_context(nc.semaphore("s_" + e)) for e in ENGS}
        self.seen = {e: {} for e in ENGS}
        self.dsem, self.dval, self.dnext = {}, {}, {}
        for e in DMA_ENGS:
            self.dsem[e] = [self.stacks[0].enter_context(nc.semaphore("d_%s%d" % (e, i))) for i in range(NDMA_SLOTS)]
            self.dval[e] = [0] * NDMA_SLOTS
            self.dnext[e] = 0
        self.n_inst = 0
        self.uid = 0
        self.marks = []

    def mark(self, label):
        self.marks.append((label, dict(self.cnt)))

    def _nm(self, name):
        self.uid += 1
        return "%s_%d" % (name, self.uid)

    def sb(self, name, shape, dt=F32):
        t = self.stacks[-1].enter_context(self.nc.sbuf_tensor(self._nm(name), list(shape), dt))
        return Tl(t, name)

    def ps(self, name, shape, dt=F32):
        nbytes = int(np.prod(shape[1:])) * (4 if dt == F32 else 2)
        assert nbytes % 2048 == 0, "PSUM tiles must cover whole banks"
        t = self.stacks[-1].enter_context(self.nc.psum_tensor(self._nm(name), list(shape), dt))
        return Tl(t, name, excl=True)

    def dram(self, name, shape, dt=F32, kind="Internal"):
        t = self.nc.dram_tensor(name, list(shape), dt, kind=kind)
        return Tl(t.ap(), name)

    class _Scope:
        def __init__(self, p):
            self.p = p

        def __enter__(self):
            self.p.stacks.append(ExitStack())

        def __exit__(self, *a):
            self.p.barrier()
            self.p.stacks.pop().close()
            return False

    def scope(self):
        return Prog._Scope(self)

    def _deps(self, eng, reads, writes, acc_w=False):
        waits = {}

        def need(tok):
            sem, val, src = tok
            if src == "pe" and eng == "pe":
                return
            k = id(sem)
            if self.seen[eng].get(k, 0) >= val:
                return
            if k not in waits or waits[k][1] < val:
                waits[k] = (sem, val)

        for tl in reads:
            for tok in tl.lw:
                need(tok)
        for tl in writes:
            if not acc_w:
                for tok in tl.lw:
                    need(tok)
            for tok in tl.rd:
                need(tok)
        for k, (sem, val) in waits.items():
            self.seen[eng][k] = val
        return list(waits.values())

    def _commit(self, tok, reads, writes, acc_w=False):
        for tl in writes:
            if acc_w:
                tl.lw.append(tok)
                if len(tl.lw) > 48:
                    tl.lw = _compress(tl.lw)
            else:
                tl.lw = [tok]
            tl.rd = []
        for tl in reads:
            if tl not in writes:
                tl.rd.append(tok)
                if len(tl.rd) > 48:
                    tl.rd = _compress(tl.rd)

    def I(self, eng, fn, *, acc_w=False, **kw):
        reads, writes, args = [], [], {}
        for k, a in kw.items():
            if isinstance(a, V):
                args[k] = a.ap
                (writes if (k in WRITE_KW or a.tl.excl) else reads).append(a.tl)
            else:
                args[k] = a
        waits = self._deps(eng, reads, writes, acc_w)
        self.cnt[eng] += 1
        tok = (self.sem[eng], self.cnt[eng], eng)
        self._commit(tok, reads, writes, acc_w)
        self.q[eng].append((waits, fn, args, (self.sem[eng], 1)))
        self.n_inst += 1

    def dma(self, eng, out, in_, acc_w=False, **kw):
        reads, writes = [in_.tl], [out.tl]
        waits = self._deps(eng, reads, writes, acc_w)
        s = self.dnext[eng]
        self.dnext[eng] = (s + 1) % NDMA_SLOTS
        sem = self.dsem[eng][s]
        prev = self.dval[eng][s]
        if prev > 0 and self.seen[eng].get(id(sem), 0) < prev:
            waits.append((sem, prev))
            self.seen[eng][id(sem)] = prev
        self.dval[eng][s] = prev + 16
        tok = (sem, prev + 16, "dma")
        self._commit(tok, reads, writes, acc_w)
        args = dict(out=out.ap, in_=in_.ap)
        args.update(kw)
        self.q[eng].append((waits, "dma_start", args, (sem, 16)))
        self.n_inst += 1

    def barrier(self):
        for e in ENGS:
            waits = []
            for e2 in ENGS:
                if self.cnt[e2] > 0 and self.seen[e].get(id(self.sem[e2]), 0) < self.cnt[e2] and e2 != e:
                    waits.append((self.sem[e2], self.cnt[e2]))
                    self.seen[e][id(self.sem[e2])] = self.cnt[e2]
            for de in DMA_ENGS:
                for s in range(NDMA_SLOTS):
                    v = self.dval[de][s]
                    sem = self.dsem[de][s]
                    if v > 0 and self.seen[e].get(id(sem), 0) < v:
                        waits.append((sem, v))
                        self.seen[e][id(sem)] = v
            if waits:
                self.q[e].append((waits, None, None, None))

    def emit(self):
        nc = self.nc
        self.barrier()
        with nc.Block() as block:
            def run(engname):
                def f(e):
                    for waits, fn, args, inc in self.q[engname]:
                        for sem, val in waits:
                            e.wait_ge(sem, val)
                        if fn is not None:
                            getattr(e, fn)(**args).then_inc(inc[0], inc[1])
                return f
            block.tensor(run("pe"))
            block.scalar(run("act"))
            block.vector(run("dve"))
            block.gpsimd(run("pool"))
            block.sync(run("sp"))
        while self.stacks:
            self.stacks.pop().close()


class Cfg:
    def __init__(self, D=2048, S=2048, kinds=(0, 1, 2, 0), lora=96,
                 gla_heads=4, gla_rank=16, ssm_groups=8):
        self.D, self.S, self.kinds, self.lora = D, S, tuple(kinds), lora
        self.gla_heads, self.gla_rank, self.ssm_groups = gla_heads, gla_rank, ssm_groups
        self.DC = D // 128
        self.TT = min(512, S)
        self.NT = S // self.TT
        self.TA = min(256, S)
        self.L = len(kinds)
        self.nR = sum(1 for k in kinds if k == 0)
        self.nG = sum(1 for k in kinds if k == 1)
        self.nS = sum(1 for k in kinds if k == 2)
        self.TB = min(512, S)
        self.CB = 4
        self.stop = 99


NEG_EXP_HALF = -math.exp(-0.5)
NORM_EPS = 1e-6
RWKV_GN_EPS = 64e-5


def make_consts(cfg):
    c = np.zeros((128, 8, 128), np.float32)
    c[:, 0, :] = np.eye(128)
    c[:, 1, :] = 1.0
    c[0:64, 2, 0:64] = 1.0
    c[64:128, 2, 64:128] = 1.0
    su = np.triu(np.ones((64, 64), np.float32), 1)
    iu = np.triu(np.ones((64, 64), np.float32), 0)
    c[0:64, 3, 0:64] = su
    c[64:128, 3, 0:64] = su
    c[0:64, 3, 64:128] = iu
    c[64:128, 3, 64:128] = iu
    c[0:64, 4, 0:64] = -su
    c[0:64, 4, 64:128] = -su.T
    c[0:64, 5, 0:64] = np.eye(64)
    c[64:128, 5, 0:64] = np.eye(64)
    c[:, 6, :] = np.triu(np.ones((128, 128), np.float32), 0)
    c[:, 7, :] = np.where(np.triu(np.ones((128, 128)), 0) > 0, 0.0, -30000.0)
    rmask = np.ones((128, cfg.S), np.float32)
    rmask[:, 0::64] = 0.0
    rmask128 = np.ones((128, cfg.S), np.float32)
    rmask128[:, 0::128] = 0.0
    return c.reshape(128, 8 * 128), rmask, rmask128


def build(cfg):
    nc = bass.Bass("TRN2", target_bir_lowering=False)
    p = Prog(nc)
    D, S, DC, TT, NT, L = cfg.D, cfg.S, cfg.DC, cfg.TT, cfg.NT, cfg.L
    EI = "ExternalInput"
    xT = p.dram("xT", [D, S], F32, EI)
    cT = p.dram("cT", [128, DC], F32, EI)
    ada_w = p.dram("ada_w", [L, D, 3 * D], F32, EI)
    ada_bT = p.dram("ada_bT", [128, L, 3 * DC], F32, EI)
    norm_gT = p.dram("norm_gT", [128, L, DC], F32, EI)
    final_gT = p.dram("final_gT", [128, DC], F32, EI)
    consts_d = p.dram("consts", [128, 8 * 128], F32, EI)
    rmask_d = p.dram("rmask", [128, S], F32, EI)
    rmask128_d = p.dram("rmask128", [128, S], F32, EI)
    outT = p.dram("outT", [D, S], F32, "ExternalOutput")
    xres = [p.dram("xres%d" % i, [D, S], F32) for i in range(3)]
    W = D
    HP = W // 128
    R = cfg.lora
    if cfg.nR:
        nR = cfg.nR
        rw_in = p.dram("rwkv_w_in", [nR, D, 4 * W], F32, EI)
        rw_out = p.dram("rwkv_w_out", [nR, W, D], F32, EI)
        rw_dw1 = p.dram("rwkv_dec_w1", [nR, D, R], F32, EI)
        rw_dw2 = p.dram("rwkv_dec_w2", [nR, R, W], F32, EI)
        rw_aw1 = p.dram("rwkv_iclr_w1", [nR, D, R], F32, EI)
        rw_aw2 = p.dram("rwkv_iclr_w2", [nR, R, W], F32, EI)
        rw_muT = p.dram("rwkv_muT", [128, nR, 6, DC], F32, EI)
        rw_vecT = p.dram("rwkv_vecT", [128, nR, 7, HP], F32, EI)
        projT = [[Tl(t.t[f * 128:(f + 1) * 128, :], "projT") for f in range(HP)]
                 for t in [p.dram("projT%d" % c, [W, S], F32) for c in range(4)]]

    GH = cfg.gla_heads
    KW, VW = D // 2, D
    DK, DV = KW // GH, VW // GH
    KC, VC = max(DK // 128, 1), DV // 128
    GR = cfg.gla_rank
    if cfg.nG:
        nG = cfg.nG
        assert DK % 128 == 0 and DV % 128 == 0 and DV <= 512
        gl_in = p.dram("gla_w_in", [nG, D, 2 * KW + 2 * VW + GR], F32, EI)
        gl_out = p.dram("gla_w_out", [nG, VW, D], F32, EI)
        gl_w2 = p.dram("gla_gate_w2", [nG, GR, KW], F32, EI)
        gl_nbT = p.dram("gla_nbT", [128, nG, KW // 128], F32, EI)
        gl_hgb = p.dram("gla_hgb", [128, nG, DV], F32, EI)
        gqk = [[Tl(t.t[f * 128:(f + 1) * 128, :], "gqk") for f in range(KW // 128)]
               for t in [p.dram("gqk%d" % c, [KW, S], F32) for c in range(2)]]
        gvg_t = [p.dram("gvg%d" % c, [S, VW], F32) for c in range(2)]
        gvg = [[Tl(t.t[:, hh * DV:(hh + 1) * DV], "gvg") for hh in range(GH)] for t in gvg_t]

    SW = 2 * D
    SH = SW // 64
    SG = SW // 512
    SN = 128
    CW = SW + 2 * SG * SN
    SIN = SW + CW + SH
    if cfg.nS:
        nS = cfg.nS
        sd_in = p.dram("ssd_w_in", [nS, D, SIN], F32, EI)
        sd_out = p.dram("ssd_w_out", [nS, SW, D], F32, EI)
        sd_cwT = p.dram("ssd_cwT", [128, nS, CW // 128, 4], F32, EI)
        sd_cbT = p.dram("ssd_cbT", [128, nS, CW // 128], F32, EI)
        sd_hv = p.dram("ssd_hv", [64, nS, 2], F32, EI)
        sd_dsb = p.dram("ssd_dsb", [128, nS, SH], F32, EI)
        sd_ngb = p.dram("ssd_ngb", [128, nS, SW], F32, EI)
        sxbc_t = p.dram("sxbc", [CW, S], F32)
        sxbc = [Tl(sxbc_t.t[f * 128:(f + 1) * 128, :], "sxbc") for f in range(CW // 128)]
        sz_t = p.dram("sz", [S, SW], F32)
        sz = [Tl(sz_t.t[:, g * 512:(g + 1) * 512], "sz") for g in range(SG)]

    cst = p.sb("cst", [128, 8, 128], F32)
    p.dma("sp", cst.v().re("p a b -> p (a b)"), consts_d.v())
    ident_bf = p.sb("ident_bf", [128, 128], BF16)
    p.I("dve", "tensor_copy", out=ident_bf.v(), in_=cst[:, 0, :])
    ones32 = cst[:, 1, :]
    bones32 = cst[:, 2, :]
    maskA = cst[:, 3, :]
    negSU = cst[0:64, 4, 0:64]
    negSL = cst[0:64, 4, 64:128]
    ident2 = cst[:, 5, 0:64]
    rmask = p.sb("rmask", [128, S], BF16)
    p.dma("pool", rmask.v(), rmask_d.v())
    rmask128 = p.sb("rmask128", [128, S], BF16)
    p.dma("pool", rmask128.v(), rmask128_d.v())
    iu128 = cst[:, 6, :]

    modT = p.sb("modT", [128, L, 3 * DC], F32)
    gsT = p.sb("gsT", [128, L, DC], F32)
    with p.scope():
        cact = p.sb("cact", [128, DC], F32)
        abT = p.sb("abT", [128, L, 3 * DC], F32)
        ngT = p.sb("ngT", [128, L, DC], F32)
        p.dma("sp", cact.v(), cT.v())
        p.dma("sp", abT.v(), ada_bT.v())
        p.dma("sp", ngT.v(), norm_gT.v())
        p.I("act", "activation", out=cact.v(), in_=cact.v(), func=AF.Silu)
        EG = 4 if (3 * DC) % 4 == 0 else 2
        cact_bf = p.sb("cact_bf", [128, DC], BF16)
        p.I("dve", "tensor_copy", out=cact_bf.v(), in_=cact.v())
        wst = [p.sb("adaw", [128, DC, EG * 128], BF16) for _ in range(3)]
        psm = p.ps("psmod", [128, 512], F32)
        gi = 0
        for l in range(L):
            wv = ada_w.v()[l].re("(dc p) e -> p dc e", p=128)
            for eg in range(3 * DC // EG):
                wt = wst[gi % 3]
                p.dma("pool", wt.v(), wv[:, :, eg * EG * 128:(eg + 1) * EG * 128])
                gi += 1
                for j in range(EG):
                    col = l * 3 * DC + eg * EG + j
                    for dc in range(DC):
                        p.I("pe", "matmul", out=psm[:, col:col + 1], lhsT=wt[:, dc, j * 128:(j + 1) * 128],
                            rhs=cact_bf[:, dc:dc + 1], start=(dc == 0), stop=(dc == DC - 1))
        p.I("dve", "tensor_tensor", out=modT.v().re("p l e -> p (l e)"), in0=psm[:, 0:L * 3 * DC],
            in1=abT.v().re("p l e -> p (l e)"), op=ALU.add)
        p.I("dve", "scalar_tensor_tensor", out=gsT.v(), in0=modT[:, :, DC:2 * DC], scalar=1.0, in1=ngT.v(),
            op0=ALU.add, op1=ALU.mult)

    def norm_phase(src, dst_tiles, g_of_dc, sh_of_dc, out_dram=None):
        TA = cfg.TA
        with p.scope():
            xt = [p.sb("xt", [128, DC, TA], F32) for _ in range(2)]
            sq = [p.sb("sq", [128, TA], F32) for _ in range(2)]
            rstd = [p.sb("rstd", [128, TA], F32) for _ in range(2)]
            tmp = [p.sb("ntmp", [128, TA], F32) for _ in range(4)]
            pss = [p.ps("psn", [128, 512], F32) for _ in range(2)]
            k = 0
            for ta in range(S // TA):
                x_ = xt[ta % 2]
                ts = slice(ta * TA, (ta + 1) * TA)
                p.dma("sp", x_.v(), src.re("(dc p) s -> p dc s", p=128)[:, :, ts])
                ps_ = pss[ta % 2]
                for dc in range(DC):
                    s_ = sq[dc % 2]
                    if dc % 2 == 0:
                        p.I("act", "activation", out=s_.v(), in_=x_[:, dc, :], func=AF.Square)
                    else:
                        p.I("dve", "tensor_tensor", out=s_.v(), in0=x_[:, dc, :], in1=x_[:, dc, :], op=ALU.mult)
                    p.I("pe", "matmul", out=ps_[:, 0:TA], lhsT=ones32, rhs=s_.v(), start=(dc == 0), stop=(dc == DC - 1))
                r_ = rstd[ta % 2]
                p.I("act", "activation", out=r_.v(), in_=ps_[:, 0:TA], func=AF.Sqrt, bias=NORM_EPS, scale=1.0 / D)
                p.I("dve", "reciprocal", out=r_.v(), in_=r_.v())
                for dc in range(DC):
                    t_ = tmp[k % 4]
                    k += 1
                    p.I("dve", "scalar_tensor_tensor", out=t_.v(), in0=x_[:, dc, :],
                        scalar=g_of_dc(dc), in1=r_.v(), op0=ALU.mult, op1=ALU.mult)
                    if out_dram is None:
                        p.I("act", "activation", out=dst_tiles[dc][:, ts], in_=t_.v(), func=AF.Identity,
                            bias=sh_of_dc(dc), scale=1.0)
                    else:
                        p.dma("sp", out_dram[dc * 128:(dc + 1) * 128, ts], t_.v(), acc_w=True)

    wring = {}

    def out_proj(yg_tiles, w_dram, nci, gate_of_ft, src, dst):
        with p.scope():
            wts = [p.sb("wo", [128, nci, 512], BF16) for _ in range(2)]
            pso = [p.ps("pso", [128, 512], F32) for _ in range(4)]
            xin = [p.sb("xin", [128, TT], F32) for _ in range(4)]
            wv = w_dram.re("(ci p) f -> p ci f", p=128)
            k = 0
            G = 4 if (D // 128) % 4 == 0 else 2
            for fg in range(D // (128 * G)):
                wt = wts[fg % 2]
                p.dma("pool", wt[:, :, 0:G * 128], wv[:, :, fg * G * 128:(fg + 1) * G * 128])
                for j in range(G):
                    ft = fg * G + j
                    for tt in range(NT):
                        ts = slice(tt * TT, (tt + 1) * TT)
                        ps_ = pso[k % 4]
                        x_ = xin[k % 4]
                        k += 1
                        p.dma("sp", x_.v(), src[ft * 128:(ft + 1) * 128, ts])
                        for ci in range(nci):
                            p.I("pe", "matmul", out=ps_[:, 0:TT], lhsT=wt[:, ci, j * 128:(j + 1) * 128],
                                rhs=yg_tiles[ci][:, ts], start=(ci == 0), stop=(ci == nci - 1))
                        p.I("dve", "scalar_tensor_tensor", out=x_.v(), in0=ps_[:, 0:TT], scalar=gate_of_ft(ft),
                            in1=x_.v(), op0=ALU.mult, op1=ALU.add)
                        p.dma("sp", dst[ft * 128:(ft + 1) * 128, ts], x_.v(), acc_w=True)

    def proj_fm(xs_tiles, wv, f0, nft, sink, wts, pss, kdim=DC):
        k = 0
        G = 4 if nft % 4 == 0 else (2 if nft % 2 == 0 else 1)
        gi = 0
        for fg in range(nft // G):
            wt = wts[gi % 2]
            gi += 1
            p.dma("pool", wt[:, :, 0:G * 128], wv[:, :, f0 + fg * G * 128:f0 + (fg + 1) * G * 128])
            for j in range(G):
                ft = fg * G + j
                for tt in range(NT):
                    ts = slice(tt * TT, (tt + 1) * TT)
                    ps_ = pss[k % len(pss)]
                    k += 1
                    for dc in range(kdim):
                        p.I("pe", "matmul", out=ps_[:, 0:TT], lhsT=wt[:, dc, j * 128:(j + 1) * 128],
                            rhs=xs_tiles[dc][:, ts], start=(dc == 0), stop=(dc == kdim - 1))
                    sink(ft, tt, ps_)


    def proj_tm(hT, wv, f0, ngroups, gw, sink, wts, pss):
        k = 0
        for gi in range(ngroups):
            wt = wts[gi % 2]
            p.dma("pool", wt[:, :, 0:gw], wv[:, :, f0 + gi * gw:f0 + (gi + 1) * gw])
            for tk in range(S // 128):
                ps_ = pss[k % len(pss)]
                k += 1
                for dc in range(DC):
                    p.I("pe", "matmul", out=ps_[:, 0:gw], lhsT=hT[dc][:, tk * 128:(tk + 1) * 128], rhs=wt[:, dc, 0:gw],
                        start=(dc == 0), stop=(dc == DC - 1))
                sink(gi, tk, ps_)

    def gla_layer(l, j, src, dst):
        NCH = S // 128
        with p.scope():
            yg = [p.sb("ygg", [128, S], BF16) for _ in range(VW // 128)]
            lowT = p.sb("lowT", [GR, S], BF16)
            gw2 = p.sb("gw2", [GR, KW], BF16)
            nb = p.sb("gnb", [128, KW // 128], F32)
            hgb = p.sb("hgb", [128, DV], F32)
            p.dma("pool", gw2.v(), gl_w2.v()[j])
            p.dma("sp", nb.v(), gl_nbT.v()[:, j])
            p.dma("sp", hgb.v(), gl_hgb.v()[:, j])
            p.I("dve", "tensor_scalar", out=nb.v(), in0=nb.v(), scalar1=-1.0, scalar2=None, op0=ALU.mult)
            wv = gl_in.v()[j].re("(dc p) f -> p dc f", p=128)
            with p.scope():
                hT = [p.sb("hT", [128, S], BF16) for _ in range(DC)]
                norm_phase(src, hT, lambda dc: gsT[:, l, dc:dc + 1], lambda dc: modT[:, l, dc:dc + 1])
                wts = [p.sb("wi", [128, DC, 512], BF16) for _ in range(2)]
                wl = p.sb("wl", [128, DC, GR], BF16)
                pss = [p.ps("psp", [128, 512], F32) for _ in range(4)]
                stg = [p.sb("stg", [128, 512], F32) for _ in range(4)]
                sk = [0]

                def evac(ps_ap, dst_ap, width):
                    s_ = stg[sk[0] % 4]
                    e = "act" if sk[0] % 2 == 0 else "dve"
                    sk[0] += 1
                    if e == "act":
                        p.I("act", "copy", out=s_[:, 0:width], in_=ps_ap)
                    else:
                        p.I("dve", "tensor_copy", out=s_[:, 0:width], in_=ps_ap)
                    p.dma("sp", dst_ap, s_[:, 0:width], acc_w=True)

                for c in range(2):
                    proj_fm(hT, wv, c * KW, KW // 128,
                            lambda ft, tt, ps_, c=c: evac(ps_[:, 0:TT], gqk[c][ft][:, tt * TT:(tt + 1) * TT], TT), wts, pss)
                for c in range(2):
                    proj_tm(hT, wv, 2 * KW + c * VW, GH, DV,
                            lambda gi, tk, ps_, c=c: evac(ps_[:, 0:DV], gvg[c][gi][tk * 128:(tk + 1) * 128, :], DV), wts, pss)
                p.dma("pool", wl.v(), wv[:, :, 2 * KW + 2 * VW:2 * KW + 2 * VW + GR])
                for tt in range(NT):
                    ts = slice(tt * TT, (tt + 1) * TT)
                    ps_ = pss[tt % 4]
                    for dc in range(DC):
                        p.I("pe", "matmul", out=ps_[0:GR, 0:TT], lhsT=wl[:, dc, :], rhs=hT[dc][:, ts],
                            start=(dc == 0), stop=(dc == DC - 1))
                    p.I("act", "copy", out=lowT[:, ts], in_=ps_[0:GR, 0:TT])
            if cfg.stop <= 2:
                return False
            with p.scope():
                ldq = [p.sb("ldq", [128, S], F32) for _ in range(KC)]
                ldk = [p.sb("ldk", [128, S], F32) for _ in range(KC)]
                QT = [p.sb("QT", [128, S], BF16) for _ in range(KC)]
                KT = [p.sb("KT", [128, S], BF16) for _ in range(KC)]
                eb = [p.sb("eb", [128, S], F32) for _ in range(KC)]
                t1 = p.sb("gt1", [128, S], F32)
                t2 = p.sb("gt2", [128, S], F32)
                S32 = [p.sb("S32", [128, DV], F32) for _ in range(KC)]
                Sb = [p.sb("Sb", [128, DV], BF16) for _ in range(KC)]
                v32 = [p.sb("v32", [128, DV], F32) for _ in range(2)]
                g32 = [p.sb("g32", [128, DV], F32) for _ in range(2)]
                Vb = [p.sb("Vb", [128, DV], BF16) for _ in range(2)]
                SG = [p.sb("SG", [128, DV], F32) for _ in range(2)]
                KTM = [p.sb("KTM", [128, KC * 128], BF16) for _ in range(2)]
                ST = [p.sb("ST", [128, 128], BF16) for _ in range(2)]
                junk = p.sb("junk", [128, DV], F32)
                ssq = [p.sb("ssq", [128, 1], F32) for _ in range(2)]
                y32 = [p.sb("y32", [128, DV], F32) for _ in range(2)]
                yb = [p.sb("yb", [128, DV], BF16) for _ in range(2)]
                psP = p.ps("gpsP", [128, 512], F32)
                psTr = p.ps("gpsTr", [128, 8, 128], BF16)
                psS = p.ps("gpsS", [128, 512], F32)
                psO = [p.ps("gpsO", [128, 512], F32) for _ in range(2)]
                psSt = [p.ps("gpsSt", [128, 512], F32) for _ in range(2)]
                psTr2 = p.ps("gpsTr2", [128, 8, 128], BF16)
                for hh in range(GH):
                    for kc in range(KC):
                        ft = hh * KC + kc
                        p.dma("sp", ldq[kc].v(), gqk[0][ft].v())
                        p.dma("sp", ldk[kc].v(), gqk[1][ft].v())
                        for tt in range(NT):
                            ts = slice(tt * TT, (tt + 1) * TT)
                            p.I("pe", "matmul", out=psP[:, 0:TT], lhsT=gw2[:, ft * 128:(ft + 1) * 128], rhs=lowT[:, ts], start=True, stop=True)
                            p.I("act", "activation", out=t1[:, ts], in_=psP[:, 0:TT], func=AF.Exp, bias=nb[:, ft:ft + 1], scale=-1.0)
                        p.I("act", "activation", out=t1.v(), in_=t1.v(), func=AF.Ln, bias=1.0, scale=1.0)
                        p.I("dve", "tensor_scalar", out=t1.v(), in0=t1.v(), scalar1=-1.0 / 16.0, scalar2=None, op0=ALU.mult)
                        p.I("dve", "tensor_tensor_scan", out=t2.v(), data0=rmask128.v(), data1=t1.v(), initial=0.0,
                            op0=ALU.mult, op1=ALU.add)
                        p.I("act", "activation", out=eb[kc].v(), in_=t2.v(), func=AF.Exp)
                        p.I("act", "activation", out=t1.v(), in_=t2.v(), func=AF.Exp, scale=-1.0)
                        p.I("dve", "scalar_tensor_tensor", out=QT[kc].v(), in0=ldq[kc].v(), scalar=float(DK) ** -0.5, in1=eb[kc].v(),
                            op0=ALU.mult, op1=ALU.mult)
                        p.I("dve", "tensor_tensor", out=KT[kc].v(), in0=ldk[kc].v(), in1=t1.v(), op=ALU.mult)
                        p.I("dve", "memset", ap=S32[kc].v(), constant=0.0)
                        p.I("dve", "memset", ap=Sb[kc].v(), constant=0.0)
                    for n in range(NCH):
                        ns = slice(n * 128, (n + 1) * 128)
                        b2 = n % 2
                        p.dma("sp", v32[b2].v(), gvg[0][hh][ns, :])
                        p.dma("sp", g32[b2].v(), gvg[1][hh][ns, :])
                        p.I("act", "copy", out=Vb[b2].v(), in_=v32[b2].v())
                        p.I("act", "activation", out=SG[b2].v(), in_=g32[b2].v(), func=AF.Silu)
                        for kc in range(KC):
                            p.I("pe", "transpose", out=psTr[:, kc, :], in_=KT[kc][:, ns], identity=ident_bf.v())
                        p.I("dve", "tensor_copy", out=KTM[b2].v().re("p (k x) -> p k x", x=128), in_=psTr[:, 0:KC, :])
                        for kc in range(KC):
                            p.I("pe", "matmul", out=psS[:, 0:128], lhsT=KT[kc][:, ns], rhs=QT[kc][:, ns], start=(kc == 0), stop=(kc == KC - 1))
                        p.I("dve", "tensor_tensor", out=ST[b2].v(), in0=psS[:, 0:128], in1=iu128, op=ALU.mult)
                        po = psO[b2]
                        p.I("pe", "matmul", out=po[:, 0:DV], lhsT=ST[b2].v(), rhs=Vb[b2].v(), start=True, stop=False)
                        for kc in range(KC):
                            p.I("pe", "matmul", out=po[:, 0:DV], lhsT=QT[kc][:, ns], rhs=Sb[kc].v(), start=False, stop=(kc == KC - 1))
                        for kc in range(KC):
                            pst = psSt[kc % 2]
                            p.I("pe", "matmul", out=pst[:, 0:DV], lhsT=KTM[b2][:, kc * 128:(kc + 1) * 128], rhs=Vb[b2].v(), start=True, stop=True)
                            dcol = eb[kc][:, n * 128 + 127:n * 128 + 128]
                            p.I("dve", "tensor_scalar", out=S32[kc].v(), in0=S32[kc].v(), scalar1=dcol, scalar2=None, op0=ALU.mult)
                            p.I("dve", "scalar_tensor_tensor", out=S32[kc].v(), in0=pst[:, 0:DV], scalar=dcol, in1=S32[kc].v(),
                                op0=ALU.mult, op1=ALU.add)
                            p.I("act", "copy", out=Sb[kc].v(), in_=S32[kc].v())
                        p.I("act", "activation", out=junk.v(), in_=po[:, 0:DV], func=AF.Square, accum_out=ssq[b2].v())
                        p.I("act", "activation", out=ssq[b2].v(), in_=ssq[b2].v(), func=AF.Sqrt, bias=NORM_EPS, scale=1.0 / DV)
                        p.I("dve", "reciprocal", out=ssq[b2].v(), in_=ssq[b2].v())
                        p.I("dve", "scalar_tensor_tensor", out=y32[b2].v(), in0=po[:, 0:DV], scalar=ssq[b2].v(), in1=hgb.v(),
                            op0=ALU.mult, op1=ALU.mult)
                        p.I("dve", "tensor_tensor", out=yb[b2].v(), in0=y32[b2].v(), in1=SG[b2].v(), op=ALU.mult)
                        for vc in range(VC):
                            p.I("pe", "transpose", out=psTr2[:, vc, :], in_=yb[b2][:, vc * 128:(vc + 1) * 128], identity=ident_bf.v())
                        for vc in range(VC):
                            p.I("act" if vc % 2 == 0 else "dve", "copy" if vc % 2 == 0 else "tensor_copy",
                                out=yg[hh * VC + vc][:, ns], in_=psTr2[:, vc, :])
            if cfg.stop <= 9:
                return False
            out_proj(yg, gl_out.v()[j], VW // 128, lambda ft: modT[:, l, 2 * DC + ft:2 * DC + ft + 1], src, dst)
            return True


    def ssd_layer(l, j, src, nextbuf):
        NCH = S // 128
        srcv = [src]
        HN = SH
        maskb = cst[:, 7, :]
        ident32 = cst[:, 0, :]
        with p.scope():
            dtT = p.sb("dtT", [128, S], F32)
            acT = p.sb("acT", [128, S], F32)
            nacT = p.sb("nacT", [128, S], F32)
            hv = p.sb("hv", [64, 2], F32)
            dsb = p.sb("dsb", [128, SH], F32)
            cw = p.sb("cw", [128, CW // 128, 4], F32)
            cbv = p.sb("cbv", [128, CW // 128], F32)
            wtm = p.sb("wtm", [128, NCH, 128], F32)
            eatm = p.sb("eatm", [128, NCH, 64], F32)
            decbc = p.sb("decbc", [128, NCH, 64], F32)
            p.dma("sp", hv.v(), sd_hv.v()[:, j])
            p.dma("sp", dsb.v(), sd_dsb.v()[:, j])
            p.dma("sp", cw.v(), sd_cwT.v()[:, j])
            p.dma("sp", cbv.v(), sd_cbT.v()[:, j])
            wv = sd_in.v()[j].re("(dc p) f -> p dc f", p=128)
            with p.scope():
                hT = [p.sb("hT", [128, S], BF16) for _ in range(DC)]
                norm_phase(src, hT, lambda dc: gsT[:, l, dc:dc + 1], lambda dc: modT[:, l, dc:dc + 1])
                wts = [p.sb("wi", [128, DC, 512], BF16) for _ in range(2)]
                wdt = p.sb("wdt", [128, DC, SH], BF16)
                pss = [p.ps("psp", [128, 512], F32) for _ in range(4)]
                stg = [p.sb("stg", [128, 512], F32) for _ in range(4)]
                xst = [p.sb("xst", [128, S + 3], F32) for _ in range(2)]
                acc = [p.sb("cacc", [128, S], F32) for _ in range(2)]
                sk = [0]

                def zsink(gi, tk, ps_):
                    s_ = stg[sk[0] % 4]
                    e = "act" if sk[0] % 2 == 0 else "dve"
                    sk[0] += 1
                    if e == "act":
                        p.I("act", "copy", out=s_.v(), in_=ps_.v())
                    else:
                        p.I("dve", "tensor_copy", out=s_.v(), in_=ps_.v())
                    p.dma("sp", sz[gi][tk * 128:(tk + 1) * 128, :], s_.v(), acc_w=True)

                p.mark('sd_projz_start')
                proj_tm(hT, wv, 0, SG, 512, zsink, wts, pss)
                p.mark('sd_projx_start')
                for b in range(2):
                    p.I("dve", "memset", ap=xst[b][:, 0:3], constant=0.0)

                def csink(ft, tt, ps_):
                    x_ = xst[ft % 2]
                    e = "act" if (ft + tt) % 2 == 0 else "dve"
                    if e == "act":
                        p.I("act", "copy", out=x_[:, 3 + tt * TT:3 + (tt + 1) * TT], in_=ps_[:, 0:TT])
                    else:
                        p.I("dve", "tensor_copy", out=x_[:, 3 + tt * TT:3 + (tt + 1) * TT], in_=ps_[:, 0:TT])
                    if tt == NT - 1:
                        a_ = acc[ft % 2]
                        p.I("act", "mul", out=a_.v(), in_=x_[:, 3:S + 3], mul=cw[:, ft, 3:4])
                        for kk_ in range(3):
                            p.I("dve", "scalar_tensor_tensor", out=a_.v(), in0=x_[:, kk_:S + kk_], scalar=cw[:, ft, kk_:kk_ + 1],
                                in1=a_.v(), op0=ALU.mult, op1=ALU.add)
                        p.I("act", "activation", out=a_.v(), in_=a_.v(), func=AF.Silu, bias=cbv[:, ft:ft + 1], scale=1.0)
                        p.dma("sp", sxbc[ft].v(), a_.v())

                proj_fm(hT, wv, SW, CW // 128, csink, wts, pss)
                p.dma("pool", wdt.v(), wv[:, :, SW + CW:SW + CW + SH])
                p.I("dve", "memset", ap=dtT.v(), constant=0.0)
                p.I("dve", "memset", ap=acT.v(), constant=0.0)
                for tt in range(NT):
                    ts = slice(tt * TT, (tt + 1) * TT)
                    ps_ = pss[tt % 4]
                    for dc in range(DC):
                        p.I("pe", "matmul", out=ps_[0:HN, 0:TT], lhsT=wdt[:, dc, :], rhs=hT[dc][:, ts], start=(dc == 0), stop=(dc == DC - 1))
                    p.I("act", "activation", out=dtT[0:HN, ts], in_=ps_[0:HN, 0:TT], func=AF.Exp, bias=hv[0:HN, 0:1], scale=1.0)
                p.I("act", "activation", out=dtT[0:HN, :], in_=dtT[0:HN, :], func=AF.Ln, bias=1.0, scale=1.0)
            if cfg.stop <= 2:
                return False
            p.mark('sd_dt_start')
            with p.scope():
                eaT = p.sb("eaT", [128, S], F32)
                na = p.sb("na", [64, 1], F32)
                t1 = p.sb("st1", [128, S], F32)
                Dg = p.sb("Dg", [64, 64], F32)
                psq = [p.ps("spsq", [128, 512], F32) for _ in range(2)]
                p.I("act", "activation", out=na[0:HN, :], in_=hv[0:HN, 1:2], func=AF.Exp)
                p.I("dve", "tensor_scalar", out=na[0:HN, :], in0=na[0:HN, :], scalar1=-1.0, scalar2=None, op0=ALU.mult)
                p.I("dve", "tensor_scalar", out=t1[0:HN, :], in0=dtT[0:HN, :], scalar1=na[0:HN, 0:1], scalar2=None, op0=ALU.mult)
                p.I("dve", "tensor_tensor_scan", out=acT[0:HN, :], data0=rmask128[0:HN, :], data1=t1[0:HN, :], initial=0.0,
                    op0=ALU.mult, op1=ALU.add)
                p.I("dve", "memset", ap=nacT.v(), constant=0.0)
                p.I("dve", "memset", ap=eaT.v(), constant=0.0)
                p.I("dve", "tensor_scalar", out=nacT[0:HN, :], in0=acT[0:HN, :], scalar1=-1.0, scalar2=None, op0=ALU.mult)
                p.I("act", "activation", out=eaT[0:HN, :], in_=acT[0:HN, :], func=AF.Exp)
                for n in range(NCH):
                    ns = slice(n * 128, (n + 1) * 128)
                    last = acT[0:HN, n * 128 + 127:n * 128 + 128]
                    p.I("act", "activation", out=t1[0:HN, ns], in_=acT[0:HN, ns], func=AF.Exp, bias=last, scale=-1.0)
                    p.I("dve", "tensor_tensor", out=dtT[64:64 + HN, ns], in0=t1[0:HN, ns], in1=dtT[0:HN, ns], op=ALU.mult)
                    ps_ = psq[n % 2]
                    p.I("pe", "transpose", out=ps_[:, 0:128], in_=dtT[:, ns], identity=ident32)
                    p.I("pe", "transpose", out=ps_[:, 128:256], in_=eaT[:, ns], identity=ident32)
                    p.I("dve", "tensor_scalar", out=Dg[0:HN, 0:HN], in0=ident32[0:HN, 0:HN], scalar1=last, scalar2=None, op0=ALU.mult)
                    p.I("pe", "matmul", out=ps_[:, 256:256 + HN], lhsT=ones32[0:HN, :], rhs=Dg[0:HN, 0:HN], start=True, stop=True)
                    p.I("act", "copy", out=wtm[:, n, :], in_=ps_[:, 0:128])
                    p.I("dve", "tensor_copy", out=eatm[:, n, :], in_=ps_[:, 128:192])
                    p.I("act", "activation", out=decbc[:, n, 0:HN], in_=ps_[:, 256:256 + HN], func=AF.Exp)
            if cfg.stop <= 3:
                return False
            nhalf = 2 if SG >= 2 else 1
            GPH = SG // nhalf
            yg = [p.sb("ygs", [128, S], BF16) for _ in range(GPH * 4)]
            for half in range(nhalf):
              p.mark('sd_scan_start_h%d' % half)
              with p.scope():
                xg = [p.sb("xg", [128, 4, 128], F32) for _ in range(2)]
                bg = p.sb("bg", [128, S], F32)
                BT = p.sb("BT", [128, S], BF16)
                CT = p.sb("CT", [128, S], BF16)
                ngb = p.sb("ngb", [128, 512], F32)
                prev32 = p.sb("prev32", [128, 512], F32)
                prevb = p.sb("prevb", [128, 512], BF16)
                xtm = [p.sb("xtm", [128, 512], F32) for _ in range(2)]
                xc = [p.sb("xc", [128, 512], BF16) for _ in range(2)]
                xcd = [p.sb("xcd", [128, 512], BF16) for _ in range(2)]
                Btm = [p.sb("Btm", [128, 128], BF16) for _ in range(2)]
                cbT = [p.sb("cbT", [128, 128], BF16) for _ in range(2)]
                eM = [p.sb("eM", [128, 4, 128], F32) for _ in range(2)]
                Mm = [[p.sb("Mm", [128, 4, 128], BF16) for _ in range(2)] for _b in range(2)]
                maskb_bf = p.sb("maskb_bf", [128, 128], BF16)
                p.I("dve", "tensor_copy", out=maskb_bf.v(), in_=maskb)
                z32 = [p.sb("z32", [128, 512], F32) for _ in range(2)]
                ty = [p.sb("ty", [128, 512], F32) for _ in range(2)]
                tu = [p.sb("tu", [128, 512], F32) for _ in range(2)]
                junk = p.sb("sjunk", [128, 512], F32)
                sgz = [p.sb("sgz", [128, 512], F32) for _ in range(2)]
                ssq = [p.sb("sssq", [128, 1], F32) for _ in range(2)]
                ybf = [p.sb("ybf", [128, 512], BF16) for _ in range(2)]
                psX = p.ps("spsX", [128, 512], F32)
                psB = p.ps("spsB", [128, 8, 128], BF16)
                psC = p.ps("spsC", [128, 512], F32)
                psM = [p.ps("spsM", [128, 4, 128], F32) for _ in range(2)]
                psY = p.ps("spsY", [128, 512], F32)
                psYo = p.ps("spsYo", [128, 512], F32)
                psSt = p.ps("spsSt", [128, 512], F32)
                for g in range(half * GPH, (half + 1) * GPH):
                    p.dma("sp", bg.v(), sxbc[SW // 128 + g].v())
                    p.I("act", "copy", out=BT.v(), in_=bg.v())
                    p.dma("sp", bg.v(), sxbc[SW // 128 + SG + g].v())
                    p.I("dve", "tensor_copy", out=CT.v(), in_=bg.v())
                    p.dma("sp", ngb.v(), sd_ngb.v()[:, j, g * 512:(g + 1) * 512])
                    p.I("dve", "memset", ap=prev32.v(), constant=0.0)
                    p.I("dve", "memset", ap=prevb.v(), constant=0.0)
                    def partA(n):
                            ns = slice(n * 128, (n + 1) * 128)
                            b2 = n % 2
                            hs8 = slice(g * 8, (g + 1) * 8)
                            p.dma("sp", z32[b2].v(), sz[g][ns, :])
                            for i4 in range(4):
                                p.dma("sp", xg[b2][:, i4, :], sxbc[g * 4 + i4][:, ns])
                            for i4 in range(4):
                                p.I("pe", "transpose", out=psX[:, i4 * 128:(i4 + 1) * 128], in_=xg[b2][:, i4, :], identity=ident32)
                            p.I("act", "copy", out=xtm[b2].v(), in_=psX.v())
                            x3 = xtm[b2].v().re("p (h x) -> p h x", x=64)
                            p.I("dve", "tensor_tensor", out=xc[b2].v().re("p (h x) -> p h x", x=64), in0=x3,
                                in1=wtm[:, n, g * 8:(g + 1) * 8].bc([2], [128, 8, 64]), op=ALU.mult)
                            p.I("dve", "tensor_tensor", out=xcd[b2].v().re("p (h x) -> p h x", x=64), in0=x3,
                                in1=wtm[:, n, 64 + g * 8:64 + (g + 1) * 8].bc([2], [128, 8, 64]), op=ALU.mult)
                            p.I("dve", "tensor_tensor", out=tu[b2].v().re("p (h x) -> p h x", x=64), in0=x3,
                                in1=dsb[:, hs8].bc([2], [128, 8, 64]), op=ALU.mult)
                            p.I("act", "activation", out=sgz[b2].v(), in_=z32[b2].v(), func=AF.Silu)
                            p.I("pe", "matmul", out=psC[:, 128:256], lhsT=BT[:, ns], rhs=ident_bf.v(), start=True, stop=True)
                            p.I("pe", "matmul", out=psC[:, 0:128], lhsT=BT[:, ns], rhs=CT[:, ns], start=True, stop=True)
                            p.I("dve", "tensor_copy", out=Btm[b2].v(), in_=psC[:, 128:256])
                            p.I("act", "copy", out=cbT[b2].v(), in_=psC[:, 0:128])
                            for hq in range(2):
                                pm = psM[hq]
                                for h4 in range(4):
                                    h = g * 8 + hq * 4 + h4
                                    sel = ident32[0:HN, h:h + 1].bc([], [HN, 128])
                                    p.I("pe", "matmul", out=pm[:, h4, :], lhsT=sel, rhs=acT[0:HN, ns], start=True, stop=False)
                                    p.I("pe", "matmul", out=pm[:, h4, :], lhsT=nacT[0:HN, ns], rhs=sel, start=False, stop=False)
                                    p.I("pe", "matmul", out=pm[:, h4, :], lhsT=ident_bf.v(), rhs=maskb_bf.v(), start=False, stop=True)
                                p.I("act", "activation", out=eM[hq].v(), in_=pm.v(), func=AF.Exp)
                                p.I("dve", "tensor_tensor", out=Mm[b2][hq].v(), in0=eM[hq].v(),
                                    in1=cbT[b2].v().bc([1], [128, 4, 128]), op=ALU.mult)

                    def partB(n):
                            ns = slice(n * 128, (n + 1) * 128)
                            b2 = n % 2
                            hs8 = slice(g * 8, (g + 1) * 8)
                            x3 = xtm[b2].v().re("p (h x) -> p h x", x=64)
                            for hq in range(2):
                                for h4 in range(4):
                                    hl = hq * 4 + h4
                                    p.I("pe", "matmul", out=psY[:, hl * 64:(hl + 1) * 64], lhsT=Mm[b2][hq][:, h4, :],
                                        rhs=xc[b2][:, hl * 64:(hl + 1) * 64], start=True, stop=True)
                            p.I("pe", "matmul", out=psYo.v(), lhsT=CT[:, ns], rhs=prevb.v(), start=True, stop=True)
                            p.I("pe", "matmul", out=psSt.v(), lhsT=Btm[b2].v(), rhs=xcd[b2].v(), start=True, stop=True)
                            t_ = ty[b2]
                            u_ = tu[b2]
                            t3 = t_.v().re("p (h x) -> p h x", x=64)
                            u3 = u_.v().re("p (h x) -> p h x", x=64)
                            p.I("dve", "tensor_tensor", out=t3, in0=psYo.v().re("p (h x) -> p h x", x=64),
                                in1=eatm[:, n, hs8].bc([2], [128, 8, 64]), op=ALU.mult)
                            p.I("dve", "tensor_tensor", out=t_.v(), in0=psY.v(), in1=t_.v(), op=ALU.add)
                            p.I("dve", "tensor_tensor", out=t_.v(), in0=t_.v(), in1=u_.v(), op=ALU.add)
                            p32 = prev32.v().re("p (h x) -> p h x", x=64)
                            p.I("dve", "tensor_tensor", out=p32, in0=p32, in1=decbc[:, n, hs8].bc([2], [128, 8, 64]), op=ALU.mult)
                            p.I("dve", "tensor_tensor", out=prev32.v(), in0=psSt.v(), in1=prev32.v(), op=ALU.add)
                            p.I("act", "copy", out=prevb.v(), in_=prev32.v())
                            p.I("dve", "tensor_tensor", out=t_.v(), in0=t_.v(), in1=sgz[b2].v(), op=ALU.mult)
                            p.I("act", "activation", out=junk.v(), in_=t_.v(), func=AF.Square, accum_out=ssq[b2].v())
                            p.I("act", "activation", out=ssq[b2].v(), in_=ssq[b2].v(), func=AF.Sqrt, bias=1e-5, scale=1.0 / 512)
                            p.I("dve", "reciprocal", out=ssq[b2].v(), in_=ssq[b2].v())
                            p.I("dve", "scalar_tensor_tensor", out=ybf[b2].v(), in0=t_.v(), scalar=ssq[b2].v(), in1=ngb.v(),
                                op0=ALU.mult, op1=ALU.mult)
                            for i4 in range(4):
                                p.I("pe", "transpose", out=psB[:, 4 + i4, :], in_=ybf[b2][:, i4 * 128:(i4 + 1) * 128], identity=ident_bf.v())
                            for i4 in range(4):
                                p.I("act" if i4 % 2 == 0 else "dve", "copy" if i4 % 2 == 0 else "tensor_copy",
                                    out=yg[(g - half * GPH) * 4 + i4][:, ns], in_=psB[:, 4 + i4, :])

                    partA(0)
                    for n in range(NCH):
                        if n + 1 < NCH:
                            partA(n + 1)
                        partB(n)
              if cfg.stop <= 9:
                  return False
              p.mark('sd_outproj_start_h%d' % half)
              nci = GPH * 4
              dsth = nextbuf()
              out_proj(yg, sd_out.v()[j][half * nci * 128:(half + 1) * nci * 128, :], nci,
                       lambda ft: modT[:, l, 2 * DC + ft:2 * DC + ft + 1], srcv[0], dsth.v())
              srcv[0] = dsth.v()
              p.mark('sd_outproj_end_h%d' % half)
            return srcv[0]

    def rwkv_layer(l, j, src, dst):
        CB, TB = cfg.CB, cfg.TB
        NCHB = TB // 64
        with p.scope():
            yg = [p.sb("yg", [128, S], BF16) for _ in range(HP)]
            xs = yg
            lw1 = p.sb("lw1", [R, S], BF16)
            la1 = p.sb("la1", [R, S], BF16)
            vec = p.sb("rvec", [128, 7, HP], F32)
            omka = p.sb("omka", [128, HP], F32)
            p.dma("sp", vec.v(), rw_vecT.v()[:, j])
            p.I("dve", "tensor_scalar", out=omka.v(), in0=vec[:, 3, :], scalar1=-1.0, scalar2=1.0,
                op0=ALU.mult, op1=ALU.add)
            with p.scope():
                hT = [p.sb("hT", [128, S], BF16) for _ in range(DC)]
                mu = p.sb("mu", [128, 6, DC], F32)
                omm = p.sb("omm", [128, 6, DC], F32)
                p.dma("sp", mu.v(), rw_muT.v()[:, j])
                p.I("dve", "tensor_scalar", out=omm.v(), in0=mu.v(), scalar1=-1.0, scalar2=1.0,
                    op0=ALU.mult, op1=ALU.add)
                p.mark('rw_norm_start')
                norm_phase(src, hT, lambda dc: gsT[:, l, dc:dc + 1], lambda dc: modT[:, l, dc:dc + 1])
                p.mark('rw_proj_start')
                if cfg.stop <= 1:
                    return False
                wts = [p.sb("wi", [128, DC, 512], BF16) for _ in range(2)]
                w1t = p.sb("w1t", [128, DC, R], BF16)
                pss = [p.ps("psp", [128, 512], F32) for _ in range(4)]
                stg = [p.sb("stg", [128, TT], F32) for _ in range(4)]
                wv = rw_in.v()[j].re("(dc p) f -> p dc f", p=128)
                sk = [0]

                def mix(c):
                    for dc in range(DC):
                        p.I("dve", "memset", ap=xs[dc][:, 0:1], constant=0.0)
                        p.I("act", "mul", out=xs[dc][:, 1:S], in_=hT[dc][:, 0:S - 1], mul=mu[:, c, dc:dc + 1])
                        p.I("dve", "scalar_tensor_tensor", out=xs[dc].v(), in0=hT[dc].v(), scalar=omm[:, c, dc:dc + 1],
                            in1=xs[dc].v(), op0=ALU.mult, op1=ALU.add)

                import os as _os
                for c in range(4):
                    if not _os.environ.get("NOMIX") or c == 0:
                        mix(c)

                    def sink(ft, tt, ps_, c=c):
                        s_ = stg[sk[0] % 4]
                        e = "act" if sk[0] % 2 == 0 else "dve"
                        sk[0] += 1
                        if e == "act":
                            p.I("act", "copy", out=s_.v(), in_=ps_[:, 0:TT])
                        else:
                            p.I("dve", "tensor_copy", out=s_.v(), in_=ps_[:, 0:TT])
                        if not _os.environ.get("NOSTORE"):
                            p.dma("sp", projT[c][ft][:, tt * TT:(tt + 1) * TT], s_.v(), acc_w=True)

                    proj_fm(xs, wv, c * W, HP, sink, wts, pss)
                for c, (w1d, dstl, fn) in ((4, (rw_dw1, lw1, AF.Tanh)), (5, (rw_aw1, la1, AF.Copy))):
                    mix(c)
                    p.dma("pool", w1t.v(), w1d.v()[j].re("(dc p) r -> p dc r", p=128))
                    for tt in range(NT):
                        ts = slice(tt * TT, (tt + 1) * TT)
                        ps_ = pss[tt % 4]
                        for dc in range(DC):
                            p.I("pe", "matmul", out=ps_[0:R, 0:TT], lhsT=w1t[:, dc, :], rhs=xs[dc][:, ts],
                                start=(dc == 0), stop=(dc == DC - 1))
                        p.I("act", "activation", out=dstl[:, ts], in_=ps_[0:R, 0:TT], func=fn)
            if cfg.stop <= 2:
                return False
            p.mark('rw_scan_start')
            with p.scope():
                dw2 = p.sb("dw2", [R, W], BF16)
                aw2 = p.sb("aw2", [R, W], BF16)
                p.dma("pool", dw2.v(), rw_dw2.v()[j])
                p.dma("pool", aw2.v(), rw_aw2.v()[j])
                CBS, NSTR = 2, 2
                STR = []
                psTrS = p.ps("psTr", [128, 4, 2, 128], BF16)
                for si in range(NSTR):
                    pg_ = p.ps("PG", [128, 2, 512], F32)
                    xr_ = p.sb("Xr", [64, CBS * 2, 2, 64], BF16)
                    nxt_ = p.sb("NXT", [64, CBS * 2, 192], BF16)
                    mu_ = p.sb("MU", [64, CBS * 2, 128], BF16)
                    STR.append([dict(
                        BK=p.sb("BK", [128, CBS, 128], BF16), UV=p.sb("UV", [128, CBS, 128], BF16),
                        Xr=xr_, A_sb=p.sb("A_sb", [128, CBS * 2, 128], BF16), NXT=nxt_, MU=mu_,
                        GT=p.sb("GT", [128, CBS, 64], BF16), PpT=p.sb("PpT", [128, CBS, 64], BF16),
                        PG=pg_, psTr=psTrS, toff=si * CBS) for _par in range(2)])
                psS5 = p.ps("psS5", [128, 4, 128], F32)
                Tst = [p.sb("Tst", [128, 64], BF16) for _ in range(3)]
                psP1 = p.ps("psP", [128, 512], F32)
                psP = [psP1, psP1]
                psAV = p.ps("psAV", [128, CBS * 2, 128], F32)
                NTB = TB // TT if TB >= TT else 1
                TTB = min(TT, TB)
                tiref = [0]

                def item(hp, tb, SET):
                    hsl = slice(hp * 128, (hp + 1) * 128)
                    vcol = lambda i: vec[:, i, hp:hp + 1]
                    tbs = slice(tb * TB, (tb + 1) * TB)
                    ld, tm, BKT, KRT, KKVT = SET["ld"], SET["tm"], SET["BKT"], SET["KRT"], SET["KKVT"]
                    for c, nm in enumerate(("r", "k", "v", "g")):
                        p.dma("sp", ld[nm].v(), projT[c][hp][:, tbs])
                    r_, k_, v_, g_ = ld["r"], ld["k"], ld["v"], ld["g"]
                    if hp == 3:
                        p.mark('rw_prep_start_tb%d' % tb)
                    lw, cum, e1, e2, e3, a_, kk, kf, t1, t2, t3, bv, yT = (tm[n] for n in (
                        "lw", "cum", "e1", "e2", "e3", "a", "kk", "kf", "t1", "t2", "t3", "bv", "y"))
                    for tt in range(NTB):
                        ts = slice(tt * TTB, (tt + 1) * TTB)
                        gs_ = slice(tb * TB + tt * TTB, tb * TB + (tt + 1) * TTB)
                        ps_ = psP[0]
                        p.I("pe", "matmul", out=ps_[:, 0:TTB], lhsT=dw2[:, hsl], rhs=lw1[:, gs_], start=True, stop=True)
                        p.I("act", "activation", out=lw[:, ts], in_=ps_[:, 0:TTB], func=AF.Sigmoid, bias=vcol(0), scale=1.0)
                        ps_ = psP[1]
                        p.I("pe", "matmul", out=ps_[:, 0:TTB], lhsT=aw2[:, hsl], rhs=la1[:, gs_], start=True, stop=True)
                        p.I("act", "activation", out=a_[:, ts], in_=ps_[:, 0:TTB], func=AF.Sigmoid, bias=vcol(1), scale=1.0)
                    p.I("dve", "tensor_scalar", out=lw.v(), in0=lw.v(), scalar1=NEG_EXP_HALF, scalar2=None, op0=ALU.mult)
                    yield "P"
                    p.I("dve", "tensor_tensor_scan", out=cum.v(), data0=rmask[:, 0:TB], data1=lw.v(), initial=0.0,
                        op0=ALU.mult, op1=ALU.add)
                    yield "P"
                    p.I("act", "activation", out=e1.v(), in_=cum.v(), func=AF.Exp)
                    yield "P"
                    p.I("act", "activation", out=e2.v(), in_=cum.v(), func=AF.Exp, scale=-1.0)
                    yield "P"
                    p.I("dve", "tensor_tensor", out=t1.v(), in0=cum.v(), in1=lw.v(), op=ALU.subtract)
                    yield "P"
                    p.I("act", "activation", out=e3.v(), in_=t1.v(), func=AF.Exp)
                    yield "P"
                    p.I("act", "activation", out=t2.v(), in_=k_.v(), func=AF.Square, scale=vcol(2))
                    yield "P"
                    for tt in range(NTB):
                        ts = slice(tt * TTB, (tt + 1) * TTB)
                        ps_ = psP[tt % 2]
                        p.I("pe", "matmul", out=ps_[:, 0:TTB], lhsT=bones32, rhs=t2[:, ts], start=True, stop=True)
                        p.I("act", "activation", out=t3[:, ts], in_=ps_[:, 0:TTB], func=AF.Sqrt)
                    p.I("dve", "tensor_scalar", out=t3.v(), in0=t3.v(), scalar1=1e-12, scalar2=None, op0=ALU.max)
                    yield "P"
                    p.I("dve", "reciprocal", out=t3.v(), in_=t3.v())
                    yield "P"
                    p.I("dve", "scalar_tensor_tensor", out=kk.v(), in0=k_.v(), scalar=vcol(2), in1=t3.v(), op0=ALU.mult, op1=ALU.mult)
                    yield "P"
                    p.I("dve", "tensor_scalar", out=t1.v(), in0=a_.v(), scalar1=vcol(3), scalar2=omka[:, hp:hp + 1],
                        op0=ALU.mult, op1=ALU.add)
                    yield "P"
                    p.I("dve", "tensor_tensor", out=kf.v(), in0=k_.v(), in1=t1.v(), op=ALU.mult)
                    yield "P"
                    p.I("dve", "tensor_tensor", out=t2.v(), in0=kk.v(), in1=a_.v(), op=ALU.mult)
                    yield "P"
                    ch = lambda t: t.v().re("p (n c) -> p n c", c=64)
                    p.I("dve", "tensor_tensor", out=KRT[:, :, 1, :], in0=ch(r_), in1=ch(e1), op=ALU.mult)
                    yield "P"
                    p.I("dve", "tensor_tensor", out=BKT[:, :, 1, :], in0=ch(kf), in1=ch(e2), op=ALU.mult)
                    yield "P"
                    p.I("dve", "tensor_tensor", out=BKT[:, :, 0, :], in0=ch(t2), in1=ch(e2), op=ALU.mult)
                    yield "P"
                    p.I("dve", "tensor_tensor", out=KRT[:, :, 0, :], in0=ch(kk), in1=ch(e3), op=ALU.mult)
                    yield "P"
                    p.I("act", "copy", out=KKVT[:, :, 0, :], in_=KRT[:, :, 0, :])
                    yield "P"
                    p.I("act", "copy", out=KKVT[:, :, 1, :], in_=ch(v_))
                    yield "P"
                    p.I("dve", "scalar_tensor_tensor", out=t1.v(), in0=r_.v(), scalar=vcol(4), in1=kf.v(),
                        op0=ALU.mult, op1=ALU.mult)
                    yield "P"
                    for tt in range(NTB):
                        ts = slice(tt * TTB, (tt + 1) * TTB)
                        ps_ = psP[tt % 2]
                        p.I("pe", "matmul", out=ps_[:, 0:TTB], lhsT=bones32, rhs=t1[:, ts], start=True, stop=True)
                        p.I("dve", "tensor_tensor", out=bv[:, ts], in0=ps_[:, 0:TTB], in1=v_[:, ts], op=ALU.mult)
                    if cfg.stop <= 3:
                        return False
                    if hp == 3:
                        p.mark('rw_groups_start_tb%d' % tb)
                    yield "P_DONE"
                    if tb == 0:
                        p.I("dve", "memset", ap=Tst[tiref[0] % 3].v(), constant=0.0)

                    def group_stream(c0, cb_n, T):
                        BK, UV, Xr, A_sb, NXT, MU, GT, PpT, PG, psTr = (T[k_] for k_ in
                            ("BK", "UV", "Xr", "A_sb", "NXT", "MU", "GT", "PpT", "PG", "psTr"))
                        psTr = psTr[:, T["toff"]:T["toff"] + CBS]
                        PGv = PG.v().re("p h (c x) -> p h c x", c=CBS)
                        hc = lambda t: t.v().re("p (h c) x -> p h c x", h=2)[:, :, 0:cb_n, :]
                        for cb in range(cb_n):
                            n = c0 + cb
                            p.I("pe", "transpose", out=psTr[:, cb, 0, :], in_=BKT[:, n].re("p a c -> p (a c)"), identity=ident_bf.v())
                            p.I("pe", "transpose", out=psTr[:, cb, 1, :], in_=KKVT[:, n].re("p a c -> p (a c)"), identity=ident_bf.v())
                        p.I("dve", "tensor_copy", out=BK[:, 0:cb_n, :], in_=psTr[:, 0:cb_n, 0, :])
                        p.I("dve", "tensor_copy", out=Xr.v().re("p (h c) a x -> p h c a x", h=2)[:, :, 0:cb_n, 0, :],
                            in_=psTr[0:64, 0:cb_n, 1, :].re("p c (h x) -> p h c x", h=2))
                        p.I("dve", "tensor_copy", out=UV[64:128, 0:cb_n, :], in_=psTr[64:128, 0:cb_n, 1, :])
                        for cb in range(cb_n):
                            n = c0 + cb
                            for h in range(2):
                                hs = slice(h * 64, (h + 1) * 64)
                                p.I("pe", "matmul", out=PGv[:, h, cb, 0:128],
                                    lhsT=BKT[hs, n].re("p a c -> p (a c)"), rhs=KRT[hs, n].re("p a c -> p (a c)"),
                                    start=True, stop=True)
                                p.I("pe", "matmul", out=PGv[0:64, h, cb, 128:192],
                                    lhsT=KRT[hs, n, 0, :], rhs=BKT[hs, n, 0, :], start=True, stop=True)
                        pgA = PGv[:, :, 0:cb_n, 0:128]
                        p.I("act", "copy", out=hc(A_sb), in_=pgA)
                        p.I("dve", "tensor_tensor", out=hc(A_sb), in0=hc(A_sb),
                            in1=maskA.bc([1, 1], [128, 2, cb_n, 128]), op=ALU.mult)
                        nx = hc(NXT)
                        p.I("dve", "tensor_tensor", out=nx[:, :, :, 0:64], in0=hc(A_sb)[0:64, :, :, 0:64],
                            in1=negSU.bc([1, 1], [64, 2, cb_n, 64]), op=ALU.mult)
                        p.I("dve", "tensor_tensor", out=nx[:, :, :, 64:128], in0=nx[:, :, :, 0:64],
                            in1=cst[0:64, 5, 0:64].bc([1, 1], [64, 2, cb_n, 64]), op=ALU.add)
                        p.I("dve", "tensor_tensor", out=nx[:, :, :, 128:192],
                            in0=PGv[0:64, :, 0:cb_n, 128:192],
                            in1=negSL.bc([1, 1], [64, 2, cb_n, 64]), op=ALU.mult)
                        yield
                        for cb in range(cb_n):
                            for h in range(2):
                                q = h * CBS + cb
                                p.I("pe", "matmul", out=psAV[0:64, q, 0:64], lhsT=A_sb[64:128, q, 0:64],
                                    rhs=UV[64:128, cb, h * 64:(h + 1) * 64], start=True, stop=True)
                        pgI = PGv[0:64, :, 0:cb_n, 0:192]
                        for rnd in range(6):
                            for cb in range(cb_n):
                                for h in range(2):
                                    q = h * CBS + cb
                                    if rnd == 0:
                                        p.I("pe", "matmul", out=PGv[0:64, h, cb, 0:64], lhsT=NXT[:, q, 128:192],
                                            rhs=NXT[:, q, 0:64], start=True, stop=True)
                                    elif rnd < 5:
                                        p.I("pe", "matmul", out=PGv[0:64, h, cb, 0:128], lhsT=NXT[:, q, 128:192],
                                            rhs=NXT[:, q, 0:128], start=True, stop=True)
                                    else:
                                        p.I("pe", "matmul", out=PGv[0:64, h, cb, 64:128], lhsT=NXT[:, q, 128:192],
                                            rhs=NXT[:, q, 64:128], start=True, stop=True)
                                    if rnd < 5:
                                        p.I("pe", "matmul", out=PGv[0:64, h, cb, 128:192], lhsT=NXT[:, q, 0:64],
                                            rhs=NXT[:, q, 128:192], start=True, stop=True)
                            if rnd == 0:
                                p.I("act", "copy", out=Xr.v().re("p (h c) a x -> p h c a x", h=2)[:, :, 0:cb_n, 1, :],
                                    in_=psAV.v().re("p (h c) x -> p h c x", h=2)[0:64, :, 0:cb_n, 0:64])
                            if rnd > 0:
                                p.I("dve", "tensor_tensor", out=nx[:, :, :, 64:128], in0=pgI[:, :, :, 64:128],
                                    in1=nx[:, :, :, 64:128], op=ALU.add)
                            if rnd < 5:
                                p.I("act", "copy", out=nx[:, :, :, 0:64], in_=pgI[:, :, :, 0:64])
                                p.I("act", "copy", out=nx[:, :, :, 128:192], in_=pgI[:, :, :, 128:192])
                            yield
                        for cb in range(cb_n):
                            for h in range(2):
                                q = h * CBS + cb
                                p.I("pe", "matmul", out=PGv[0:64, h, cb, 0:128], lhsT=NXT[:, q, 64:128],
                                    rhs=Xr[:, q].re("p a c -> p (a c)"), start=True, stop=True)
                        pgM = PGv[0:64, :, 0:cb_n, 0:128]
                        p.I("act", "mul", out=hc(MU), in_=pgM, mul=-1.0)
                        p.I("dve", "tensor_scalar", out=UV[0:64, 0:cb_n, :].re("p c (h x) -> p h c x", h=2),
                            in0=pgM[:, :, :, 64:128], scalar1=-1.0, scalar2=None, op0=ALU.mult)
                        yield
                        for cb in range(cb_n):
                            for h in range(2):
                                q = h * CBS + cb
                                hs = slice(h * 64, (h + 1) * 64)
                                p.I("pe", "matmul", out=PGv[hs, h, cb, 0:64], lhsT=MU[:, q, 0:64], rhs=A_sb[0:64, q, 64:128],
                                    start=True, stop=True)
                                p.I("pe", "matmul", out=PGv[hs, h, cb, 64:128], lhsT=MU[:, q, 0:64], rhs=BK[0:64, cb, h * 64:(h + 1) * 64],
                                    start=True, stop=True)
                        for h in range(2):
                            hs = slice(h * 64, (h + 1) * 64)
                            p.I("dve", "tensor_tensor", out=GT[hs, 0:cb_n, :], in0=PGv[hs, h, 0:cb_n, 0:64],
                                in1=KRT[hs, c0:c0 + cb_n, 1, :], op=ALU.add)
                            p.I("dve", "tensor_tensor", out=PpT[hs, 0:cb_n, :], in0=PGv[hs, h, 0:cb_n, 64:128],
                                in1=cst[hs, 5, 0:64].bc([1], [64, cb_n, 64]), op=ALU.add)
                        yield
                        return

                    def back(sets):
                        slot = 0
                        c0g = sets[0][1]
                        for (T, c0, cb_n) in sets:
                            BK, UV, A_sb, GT, PpT = (T[k_] for k_ in ("BK", "UV", "A_sb", "GT", "PpT"))
                            for cb in range(cb_n):
                                n = c0 + cb
                                Tc, Tn = Tst[tiref[0] % 3], Tst[(tiref[0] + 1) % 3]
                                tiref[0] += 1
                                for h in range(2):
                                    hs = slice(h * 64, (h + 1) * 64)
                                    p.I("pe", "matmul", out=psS5[hs, slot, 0:64], lhsT=PpT[hs, cb, :], rhs=Tc[hs, :], start=True, stop=False)
                                    p.I("pe", "matmul", out=psS5[hs, slot, 0:64], lhsT=BK[:, cb, hs], rhs=UV[:, cb, hs], start=False, stop=True)
                                yield
                                for h in range(2):
                                    hs = slice(h * 64, (h + 1) * 64)
                                    p.I("dve", "tensor_scalar", out=Tn[hs, :], in0=psS5[hs, slot, 0:64],
                                        scalar1=e1[hs, n * 64 + 63:n * 64 + 64], scalar2=None, op0=ALU.mult)
                                for h in range(2):
                                    q = h * CBS + cb
                                    hs = slice(h * 64, (h + 1) * 64)
                                    p.I("pe", "matmul", out=psS5[hs, slot, 64:128], lhsT=Tc[hs, :], rhs=GT[hs, cb, :], start=True, stop=False)
                                    p.I("pe", "matmul", out=psS5[hs, slot, 64:128], lhsT=UV[:, cb, hs], rhs=A_sb[:, q, 64:128], start=False, stop=True)
                                slot += 1
                                yield
                        for h in range(2):
                            hs = slice(h * 64, (h + 1) * 64)
                            p.I("act", "copy", out=yT.v().re("p (n c) -> p n c", c=64)[hs, c0g:c0g + slot, :],
                                in_=psS5[hs, 0:slot, 64:128])

                    def drive(gens):
                        alive = list(gens)
                        while alive:
                            nxt = []
                            for gq in alive:
                                try:
                                    next(gq)
                                    nxt.append(gq)
                                except StopIteration:
                                    pass
                            alive = nxt
                            yield "G"

                    prev_sets = None
                    for gi_, g0 in enumerate(range(0, NCHB, CBS * NSTR)):
                        gens, sets = [], []
                        for si in range(NSTR):
                            c0 = g0 + si * CBS
                            if c0 < NCHB:
                                T_ = STR[si][gi_ % 2]
                                cbn_ = min(CBS, NCHB - c0)
                                gens.append(group_stream(c0, cbn_, T_))
                                sets.append((T_, c0, cbn_))
                        if prev_sets is not None:
                            gens.append(back(prev_sets))
                        yield from drive(gens)
                        prev_sets = sets
                    yield from drive([back(prev_sets)])
                    yield "G_DONE"
                    if hp == 3:
                        p.mark('rw_post_start_tb%d' % tb)
                    if cfg.stop <= 8:
                        return False
                    p.I("act", "activation", out=t2.v(), in_=yT.v(), func=AF.Square)
                    yield "Q"
                    HW_ = min(256, TTB)
                    for tt in range(TB // HW_):
                        ts = slice(tt * HW_, (tt + 1) * HW_)
                        p.I("pe", "matmul", out=psP1[:, 0:HW_], lhsT=bones32, rhs=yT[:, ts], start=True, stop=True)
                        p.I("pe", "matmul", out=psP1[:, 256:256 + HW_], lhsT=bones32, rhs=t2[:, ts], start=True, stop=True)
                        p.I("act", "mul", out=t1[:, ts], in_=psP1[:, 0:HW_], mul=1.0 / 64)
                        p.I("dve", "tensor_tensor", out=t3[:, ts], in0=t1[:, ts], in1=t1[:, ts], op=ALU.mult)
                        p.I("dve", "scalar_tensor_tensor", out=t3[:, ts], in0=psP1[:, 256:256 + HW_], scalar=1.0 / 64, in1=t3[:, ts],
                            op0=ALU.mult, op1=ALU.subtract)
                    p.I("act", "activation", out=t3.v(), in_=t3.v(), func=AF.Sqrt, bias=RWKV_GN_EPS, scale=1.0)
                    yield "Q"
                    p.I("dve", "reciprocal", out=t3.v(), in_=t3.v())
                    yield "Q"
                    p.I("dve", "tensor_tensor", out=t1.v(), in0=yT.v(), in1=t1.v(), op=ALU.subtract)
                    yield "Q"
                    p.I("dve", "tensor_tensor", out=t1.v(), in0=t1.v(), in1=t3.v(), op=ALU.mult)
                    yield "Q"
                    p.I("act", "activation", out=t1.v(), in_=t1.v(), func=AF.Identity, bias=vcol(6), scale=vcol(5))
                    yield "Q"
                    p.I("dve", "tensor_tensor", out=t1.v(), in0=t1.v(), in1=bv.v(), op=ALU.add)
                    yield "Q"
                    p.I("act", "activation", out=t2.v(), in_=g_.v(), func=AF.Silu)
                    yield "Q"
                    p.I("dve", "tensor_tensor", out=yg[hp][:, tbs], in0=t1.v(), in1=t2.v(), op=ALU.mult)
                    yield "Q"

                SETS = []
                for _si in range(2):
                    SETS.append(dict(
                        ld={nm: p.sb("ld_" + nm, [128, TB], F32) for nm in ("r", "k", "v", "g")},
                        tm={nm: p.sb("tm_" + nm, [128, TB], F32) for nm in
                            ("lw", "cum", "e1", "e2", "e3", "a", "kk", "kf", "t1", "t2", "t3", "bv", "y")},
                        BKT=p.sb("BKT", [128, NCHB, 2, 64], BF16), KRT=p.sb("KRT", [128, NCHB, 2, 64], BF16),
                        KKVT=p.sb("KKVT", [128, NCHB, 2, 64], BF16)))
                import os as _os2
                items = [(hp_, tb_) for hp_ in range(int(_os2.environ.get('RW_HP', HP))) for tb_ in range(S // TB)]
                gens_ = [item(hp_, tb_, SETS[ix % 2]) for ix, (hp_, tb_) in enumerate(items)]
                phase_ = ["P"] * len(items)
                lo = 0
                while lo < len(items):
                    hi = min(lo + 3, len(items))
                    for ix in range(lo, hi):
                        ph = phase_[ix]
                        if ph == "D":
                            continue
                        if ph == "P" and ((ix >= 2 and phase_[ix - 2] != "D") or (ix >= 1 and phase_[ix - 1] == "P")):
                            continue
                        if ph == "G" and ix >= 1 and phase_[ix - 1] in ("P", "G"):
                            continue
                        try:
                            tag = next(gens_[ix])
                            if tag == "P_DONE":
                                phase_[ix] = "G"
                            elif tag == "G_DONE":
                                phase_[ix] = "Q"
                        except StopIteration:
                            phase_[ix] = "D"
                    while lo < len(items) and phase_[lo] == "D":
                        lo += 1
            if cfg.stop <= 9:
                return False
            p.mark('rw_outproj_start')
            out_proj(yg, rw_out.v()[j], HP, lambda ft: modT[:, l, 2 * DC + ft:2 * DC + ft + 1], src, dst)
            p.mark('rw_outproj_end')
            return True

    bufs = xres
    bi = [0]

    def nextbuf():
        b_ = bufs[bi[0] % len(bufs)]
        bi[0] += 1
        return b_

    cur = xT.v()
    counters = {0: 0, 1: 0, 2: 0}
    for l, kind in enumerate(cfg.kinds):
        j = counters[kind]
        counters[kind] += 1
        if kind in (0, 1):
            dst = nextbuf()
            ok = (rwkv_layer if kind == 0 else gla_layer)(l, j, cur, dst.v())
            if ok:
                cur = dst.v()
        else:
            r_ = ssd_layer(l, j, cur, nextbuf)
            if r_ is not False:
                cur = r_
    with p.scope():
        fg = p.sb("fg", [128, DC], F32)
        p.dma("sp", fg.v(), final_gT.v())
        norm_phase(cur, None, lambda dc: fg[:, dc:dc + 1], None, out_dram=outT.v())
    p.emit()
    return nc, p


def _pp(vec, nchunk):
    v = np.asarray(vec, np.float32)
    lead = v.shape[:-1]
    v = v.reshape(lead + (nchunk, 128))
    return np.ascontiguousarray(np.moveaxis(v, -1, 0))


def prepare_inputs(cfg, inp, n_cores, batch_of_core):
    D, S, DC, L = cfg.D, cfg.S, cfg.DC, cfg.L
    consts, rmask, rmask128 = make_consts(cfg)
    shared = {
        "ada_w": np.ascontiguousarray(inp["ada_w"], dtype=np.float32),
        "ada_bT": _pp(inp["ada_b"], 3 * DC),
        "norm_gT": _pp(inp["norm_g"], DC),
        "final_gT": _pp(inp["final_g"], DC),
        "consts": consts, "rmask": rmask, "rmask128": rmask128,
    }
    if cfg.nR:
        HP = D // 128
        for k in ("rwkv_w_in", "rwkv_w_out", "rwkv_dec_w1", "rwkv_dec_w2", "rwkv_iclr_w1", "rwkv_iclr_w2"):
            shared[k] = np.ascontiguousarray(inp[k], dtype=np.float32)
        shared["rwkv_muT"] = _pp(inp["rwkv_mu"], DC)
        vecs = np.stack([inp["rwkv_dec_w0"], inp["rwkv_iclr_w0"], inp["rwkv_k_k"], inp["rwkv_k_a"],
                         np.asarray(inp["rwkv_r_k"]).reshape(cfg.nR, -1), inp["rwkv_gn_w"], inp["rwkv_gn_b"]], axis=1)
        shared["rwkv_vecT"] = _pp(vecs, HP)
    if cfg.nG:
        for k in ("gla_w_in", "gla_w_out", "gla_gate_w2"):
            shared[k] = np.ascontiguousarray(inp[k], dtype=np.float32)
        shared["gla_nbT"] = _pp(inp["gla_gate_b"], (D // 2) // 128)
        hg = np.asarray(inp["gla_head_g"], np.float32)
        shared["gla_hgb"] = np.ascontiguousarray(np.broadcast_to(hg[None], (128,) + hg.shape))
    if cfg.nS:
        SW = 2 * D
        SH = SW // 64
        for k in ("ssd_w_in", "ssd_w_out"):
            shared[k] = np.ascontiguousarray(inp[k], dtype=np.float32)
        cwk = np.asarray(inp["ssd_conv_w"], np.float32)
        shared["ssd_cwT"] = _pp(np.moveaxis(cwk, 1, 2).reshape(cfg.nS, -1).reshape(cfg.nS, cwk.shape[2], 4).transpose(0, 2, 1), cwk.shape[2] // 128).transpose(0, 1, 3, 2).copy()
        shared["ssd_cbT"] = _pp(inp["ssd_conv_b"], cwk.shape[2] // 128)
        hv = np.zeros((64, cfg.nS, 2), np.float32)
        hv[:SH, :, 0] = np.asarray(inp["ssd_dt_bias"], np.float32).T
        hv[:SH, :, 1] = np.asarray(inp["ssd_a_log"], np.float32).T
        shared["ssd_hv"] = hv
        dsk = np.asarray(inp["ssd_d"], np.float32)
        shared["ssd_dsb"] = np.ascontiguousarray(np.broadcast_to(dsk[None], (128,) + dsk.shape))
        ng = np.asarray(inp["ssd_norm_g"], np.float32)
        shared["ssd_ngb"] = np.ascontiguousarray(np.broadcast_to(ng[None], (128,) + ng.shape))
    maps = []
    for core in range(n_cores):
        b = batch_of_core[core]
        m = dict(shared)
        m["xT"] = np.ascontiguousarray(np.asarray(inp["x"][b], np.float32).T)
        m["cT"] = _pp(inp["c"][b], DC)
        maps.append(m)
    return maps


_CACHE = {}


def kernel(**inputs):
    cfg = Cfg()
    B = inputs["x"].shape[0]
    n_cores = 8
    batch_of_core = [c % B for c in range(n_cores)]
    if "nc" not in _CACHE:
        _CACHE["nc"] = build(cfg)[0]
    nc = _CACHE["nc"]
    maps = prepare_inputs(cfg, inputs, n_cores, batch_of_core)
    res = run_bass_kernel_spmd(nc, maps, core_ids=list(range(n_cores)))
    out = np.empty((B, cfg.S, cfg.D), np.float32)
    for b in range(B):
        out[b] = res.results[b]["outT"].T
    return out
```

```python
from contextlib import ExitStack
import math
import numpy as np
import concourse.bass as bass
import concourse.mybir as mybir
from concourse.bass_utils import run_bass_kernel_spmd

F32 = mybir.dt.float32
BF16 = mybir.dt.bfloat16
AF = mybir.ActivationFunctionType
ALU = mybir.AluOpType
AX = mybir.AxisListType


class V:
    __slots__ = ("ap", "tl")

    def __init__(self, ap, tl):
        self.ap = ap
        self.tl = tl

    def __getitem__(self, idx):
        return V(self.ap[idx], self.tl)

    def re(self, pat, **kw):
        return V(self.ap.rearrange(pat, **kw), self.tl)

    def bc(self, axes, shape):
        a = self.ap
        for ax in axes:
            a = a.unsqueeze(ax)
        return V(a.broadcast_to(list(shape)), self.tl)


class Tl:
    __slots__ = ("t", "lw", "rd", "name", "excl")

    def __init__(self, t, name="", excl=False):
        self.t = t
        self.lw = []
        self.rd = []
        self.name = name
        self.excl = excl

    def __getitem__(self, idx):
        return V(self.t[idx], self)

    def v(self):
        return V(self.t[:], self)


ENGS = ("pe", "act", "dve", "pool", "sp")
DMA_ENGS = ("sp", "pool", "act")
NDMA_SLOTS = 12
WRITE_KW = ("out", "accum_out", "ap")


def _compress(toks):
    best = {}
    for s, v, src in toks:
        k = id(s)
        if k not in best or best[k][1] < v:
            best[k] = (s, v, src)
    return list(best.values())


class Prog:
    def __init__(self, nc):
        self.nc = nc
        self.stacks = [ExitStack()]
        self.q = {e: [] for e in ENGS}
        self.cnt = {e: 0 for e in ENGS}
        self.sem = {e: self.stacks[0].enter_context(nc.semaphore("s_" + e)) for e in ENGS}
        self.seen = {e: {} for e in ENGS}
        self.dsem, self.dval, self.dnext = {}, {}, {}
        for e in DMA_ENGS:
            self.dsem[e] = [self.stacks[0].enter_context(nc.semaphore("d_%s%d" % (e, i))) for i in range(NDMA_SLOTS)]
            self.dval[e] = [0] * NDMA_SLOTS
            self.dnext[e] = 0
        self.n_inst = 0
        self.uid = 0
        self.marks = []

    def mark(self, label):
        self.marks.append((label, dict(self.cnt)))

    def _nm(self, name):
        self.uid += 1
        return "%s_%d" % (name, self.uid)

    def sb(self, name, shape, dt=F32):
        t = self.stacks[-1].enter_context(self.nc.sbuf_tensor(self._nm(name), list(shape), dt))
        return Tl(t, name)

    def ps(self, name, shape, dt=F32):
        nbytes = int(np.prod(shape[1:])) * (4 if dt == F32 else 2)
        assert nbytes % 2048 == 0, "PSUM tiles must cover whole banks"
        t = self.stacks[-1].enter_context(self.nc.psum_tensor(self._nm(name), list(shape), dt))
        return Tl(t, name, excl=True)

    def dram(self, name, shape, dt=F32, kind="Internal"):
        t = self.nc.dram_tensor(name, list(shape), dt, kind=kind)
        return Tl(t.ap(), name)

    class _Scope:
        def __init__(self, p):
            self.p = p

        def __enter__(self):
            self.p.stacks.append(ExitStack())

        def __exit__(self, *a):
            self.p.barrier()
            self.p.stacks.pop().close()
            return False

    def scope(self):
        return Prog._Scope(self)

    def _deps(self, eng, reads, writes, acc_w=False):
        waits = {}

        def need(tok):
            sem, val, src = tok
            if src == "pe" and eng == "pe":
                return
            k = id(sem)
            if self.seen[eng].get(k, 0) >= val:
                return
            if k not in waits or waits[k][1] < val:
                waits[k] = (sem, val)

        for tl in reads:
            for tok in tl.lw:
                need(tok)
        for tl in writes:
            if not acc_w:
                for tok in tl.lw:
                    need(tok)
            for tok in tl.rd:
                need(tok)
        for k, (sem, val) in waits.items():
            self.seen[eng][k] = val
        return list(waits.values())

    def _commit(self, tok, reads, writes, acc_w=False):
        for tl in writes:
            if acc_w:
                tl.lw.append(tok)
                if len(tl.lw) > 48:
                    tl.lw = _compress(tl.lw)
            else:
                tl.lw = [tok]
            tl.rd = []
        for tl in reads:
            if tl not in writes:
                tl.rd.append(tok)
                if len(tl.rd) > 48:
                    tl.rd = _compress(tl.rd)

    def I(self, eng, fn, *, acc_w=False, **kw):
        reads, writes, args = [], [], {}
        for k, a in kw.items():
            if isinstance(a, V):
                args[k] = a.ap
                (writes if (k in WRITE_KW or a.tl.excl) else reads).append(a.tl)
            else:
                args[k] = a
        waits = self._deps(eng, reads, writes, acc_w)
        self.cnt[eng] += 1
        tok = (self.sem[eng], self.cnt[eng], eng)
        self._commit(tok, reads, writes, acc_w)
        self.q[eng].append((waits, fn, args, (self.sem[eng], 1)))
        self.n_inst += 1

    def dma(self, eng, out, in_, acc_w=False, **kw):
        reads, writes = [in_.tl], [out.tl]
        waits = self._deps(eng, reads, writes, acc_w)
        s = self.dnext[eng]
        self.dnext[eng] = (s + 1) % NDMA_SLOTS
        sem = self.dsem[eng][s]
        prev = self.dval[eng][s]
        if prev > 0 and self.seen[eng].get(id(sem), 0) < prev:
            waits.append((sem, prev))
            self.seen[eng][id(sem)] = prev
        self.dval[eng][s] = prev + 16
        tok = (sem, prev + 16, "dma")
        self._commit(tok, reads, writes, acc_w)
        args = dict(out=out.ap, in_=in_.ap)
        args.update(kw)
        self.q[eng].append((waits, "dma_start", args, (sem, 16)))
        self.n_inst += 1

    def barrier(self):
        for e in ENGS:
            waits = []
            for e2 in ENGS:
                if self.cnt[e2] > 0 and self.seen[e].get(id(self.sem[e2]), 0) < self.cnt[e2] and e2 != e:
                    waits.append((self.sem[e2], self.cnt[e2]))
                    self.seen[e][id(self.sem[e2])] = self.cnt[e2]
            for de in DMA_ENGS:
                for s in range(NDMA_SLOTS):
                    v = self.dval[de][s]
                    sem = self.dsem[de][s]
                    if v > 0 and self.seen[e].get(id(sem), 0) < v:
                        waits.append((sem, v))
                        self.seen[e][id(sem)] = v
            if waits:
                self.q[e].append((waits, None, None, None))

    def emit(self):
        nc = self.nc
        self.barrier()
        with nc.Block() as block:
            def run(engname):
                def f(e):
                    for waits, fn, args, inc in self.q[engname]:
                        for sem, val in waits:
                            e.wait_ge(sem, val)
                        if fn is not None:
                            getattr(e, fn)(**args).then_inc(inc[0], inc[1])
                return f
            block.tensor(run("pe"))
            block.scalar(run("act"))
            block.vector(run("dve"))
            block.gpsimd(run("pool"))
            block.sync(run("sp"))
        while self.stacks:
            self.stacks.pop().close()


class Cfg:
    def __init__(self, D=2048, S=2048, kinds=(0, 1, 2, 0), lora=96,
                 gla_heads=4, gla_rank=16, ssm_groups=8):
        self.D, self.S, self.kinds, self.lora = D, S, tuple(kinds), lora
        self.gla_heads, self.gla_rank, self.ssm_groups = gla_heads, gla_rank, ssm_groups
        self.DC = D // 128
        self.TT = min(512, S)
        self.NT = S // self.TT
        self.TA = min(256, S)
        self.L = len(kinds)
        self.nR = sum(1 for k in kinds if k == 0)
        self.nG = sum(1 for k in kinds if k == 1)
        self.nS = sum(1 for k in kinds if k == 2)
        self.TB = min(512, S)
        self.CB = 4
        self.stop = 99


NEG_EXP_HALF = -math.exp(-0.5)
NORM_EPS = 1e-6
RWKV_GN_EPS = 64e-5


def make_consts(cfg):
    c = np.zeros((128, 8, 128), np.float32)
    c[:, 0, :] = np.eye(128)
    c[:, 1, :] = 1.0
    c[0:64, 2, 0:64] = 1.0
    c[64:128, 2, 64:128] = 1.0
    su = np.triu(np.ones((64, 64), np.float32), 1)
    iu = np.triu(np.ones((64, 64), np.float32), 0)
    c[0:64, 3, 0:64] = su
    c[64:128, 3, 0:64] = su
    c[0:64, 3, 64:128] = iu
    c[64:128, 3, 64:128] = iu
    c[0:64, 4, 0:64] = -su
    c[0:64, 4, 64:128] = -su.T
    c[0:64, 5, 0:64] = np.eye(64)
    c[64:128, 5, 0:64] = np.eye(64)
    c[:, 6, :] = np.triu(np.ones((128, 128), np.float32), 0)
    c[:, 7, :] = np.where(np.triu(np.ones((128, 128)), 0) > 0, 0.0, -30000.0)
    rmask = np.ones((128, cfg.S), np.float32)
    rmask[:, 0::64] = 0.0
    rmask128 = np.ones((128, cfg.S), np.float32)
    rmask128[:, 0::128] = 0.0
    return c.reshape(128, 8 * 128), rmask, rmask128


def build(cfg):
    nc = bass.Bass("TRN2", target_bir_lowering=False)
    p = Prog(nc)
    D, S, DC, TT, NT, L = cfg.D, cfg.S, cfg.DC, cfg.TT, cfg.NT, cfg.L
    EI = "ExternalInput"
    xT = p.dram("xT", [D, S], F32, EI)
    cT = p.dram("cT", [128, DC], F32, EI)
    ada_w = p.dram("ada_w", [L, D, 3 * D], F32, EI)
    ada_bT = p.dram("ada_bT", [128, L, 3 * DC], F32, EI)
    norm_gT = p.dram("norm_gT", [128, L, DC], F32, EI)
    final_gT = p.dram("final_gT", [128, DC], F32, EI)
    consts_d = p.dram("consts", [128, 8 * 128], F32, EI)
    rmask_d = p.dram("rmask", [128, S], F32, EI)
    rmask128_d = p.dram("rmask128", [128, S], F32, EI)
    outT = p.dram("outT", [D, S], F32, "ExternalOutput")
    xres = [p.dram("xres%d" % i, [D, S], F32) for i in range(3)]
    W = D
    HP = W // 128
    R = cfg.lora
    if cfg.nR:
        nR = cfg.nR
        rw_in = p.dram("rwkv_w_in", [nR, D, 4 * W], F32, EI)
        rw_out = p.dram("rwkv_w_out", [nR, W, D], F32, EI)
        rw_dw1 = p.dram("rwkv_dec_w1", [nR, D, R], F32, EI)
        rw_dw2 = p.dram("rwkv_dec_w2", [nR, R, W], F32, EI)
        rw_aw1 = p.dram("rwkv_iclr_w1", [nR, D, R], F32, EI)
        rw_aw2 = p.dram("rwkv_iclr_w2", [nR, R, W], F32, EI)
        rw_muT = p.dram("rwkv_muT", [128, nR, 6, DC], F32, EI)
        rw_vecT = p.dram("rwkv_vecT", [128, nR, 7, HP], F32, EI)
        projT = [[Tl(t.t[f * 128:(f + 1) * 128, :], "projT") for f in range(HP)]
                 for t in [p.dram("projT%d" % c, [W, S], F32) for c in range(4)]]

    GH = cfg.gla_heads
    KW, VW = D // 2, D
    DK, DV = KW // GH, VW // GH
    KC, VC = max(DK // 128, 1), DV // 128
    GR = cfg.gla_rank
    if cfg.nG:
        nG = cfg.nG
        assert DK % 128 == 0 and DV % 128 == 0 and DV <= 512
        gl_in = p.dram("gla_w_in", [nG, D, 2 * KW + 2 * VW + GR], F32, EI)
        gl_out = p.dram("gla_w_out", [nG, VW, D], F32, EI)
        gl_w2 = p.dram("gla_gate_w2", [nG, GR, KW], F32, EI)
        gl_nbT = p.dram("gla_nbT", [128, nG, KW // 128], F32, EI)
        gl_hgb = p.dram("gla_hgb", [128, nG, DV], F32, EI)
        gqk = [[Tl(t.t[f * 128:(f + 1) * 128, :], "gqk") for f in range(KW // 128)]
               for t in [p.dram("gqk%d" % c, [KW, S], F32) for c in range(2)]]
        gvg_t = [p.dram("gvg%d" % c, [S, VW], F32) for c in range(2)]
        gvg = [[Tl(t.t[:, hh * DV:(hh + 1) * DV], "gvg") for hh in range(GH)] for t in gvg_t]

    SW = 2 * D
    SH = SW // 64
    SG = SW // 512
    SN = 128
    CW = SW + 2 * SG * SN
    SIN = SW + CW + SH
    if cfg.nS:
        nS = cfg.nS
        sd_in = p.dram("ssd_w_in", [nS, D, SIN], F32, EI)
        sd_out = p.dram("ssd_w_out", [nS, SW, D], F32, EI)
        sd_cwT = p.dram("ssd_cwT", [128, nS, CW // 128, 4], F32, EI)
        sd_cbT = p.dram("ssd_cbT", [128, nS, CW // 128], F32, EI)
        sd_hv = p.dram("ssd_hv", [64, nS, 2], F32, EI)
        sd_dsb = p.dram("ssd_dsb", [128, nS, SH], F32, EI)
        sd_ngb = p.dram("ssd_ngb", [128, nS, SW], F32, EI)
        sxbc_t = p.dram("sxbc", [CW, S], F32)
        sxbc = [Tl(sxbc_t.t[f * 128:(f + 1) * 128, :], "sxbc") for f in range(CW // 128)]
        sz_t = p.dram("sz", [S, SW], F32)
        sz = [Tl(sz_t.t[:, g * 512:(g + 1) * 512], "sz") for g in range(SG)]

    cst = p.sb("cst", [128, 8, 128], F32)
    p.dma("sp", cst.v().re("p a b -> p (a b)"), consts_d.v())
    ident_bf = p.sb("ident_bf", [128, 128], BF16)
    p.I("dve", "tensor_copy", out=ident_bf.v(), in_=cst[:, 0, :])
    ones32 = cst[:, 1, :]
    bones32 = cst[:, 2, :]
    maskA = cst[:, 3, :]
    negSU = cst[0:64, 4, 0:64]
    negSL = cst[0:64, 4, 64:128]
    ident2 = cst[:, 5, 0:64]
    rmask = p.sb("rmask", [128, S], BF16)
    p.dma("pool", rmask.v(), rmask_d.v())
    rmask128 = p.sb("rmask128", [128, S], BF16)
    p.dma("pool", rmask128.v(), rmask128_d.v())
    iu128 = cst[:, 6, :]

    modT = p.sb("modT", [128, L, 3 * DC], F32)
    gsT = p.sb("gsT", [128, L, DC], F32)
    with p.scope():
        cact = p.sb("cact", [128, DC], F32)
        abT = p.sb("abT", [128, L, 3 * DC], F32)
        ngT = p.sb("ngT", [128, L, DC], F32)
        p.dma("sp", cact.v(), cT.v())
        p.dma("sp", abT.v(), ada_bT.v())
        p.dma("sp", ngT.v(), norm_gT.v())
        p.I("act", "activation", out=cact.v(), in_=cact.v(), func=AF.Silu)
        EG = 4 if (3 * DC) % 4 == 0 else 2
        cact_bf = p.sb("cact_bf", [128, DC], BF16)
        p.I("dve", "tensor_copy", out=cact_bf.v(), in_=cact.v())
        wst = [p.sb("adaw", [128, DC, EG * 128], BF16) for _ in range(3)]
        psm = p.ps("psmod", [128, 512], F32)
        gi = 0
        for l in range(L):
            wv = ada_w.v()[l].re("(dc p) e -> p dc e", p=128)
            for eg in range(3 * DC // EG):
                wt = wst[gi % 3]
                p.dma("pool", wt.v(), wv[:, :, eg * EG * 128:(eg + 1) * EG * 128])
                gi += 1
                for j in range(EG):
                    col = l * 3 * DC + eg * EG + j
                    for dc in range(DC):
                        p.I("pe", "matmul", out=psm[:, col:col + 1], lhsT=wt[:, dc, j * 128:(j + 1) * 128],
                            rhs=cact_bf[:, dc:dc + 1], start=(dc == 0), stop=(dc == DC - 1))
        p.I("dve", "tensor_tensor", out=modT.v().re("p l e -> p (l e)"), in0=psm[:, 0:L * 3 * DC],
            in1=abT.v().re("p l e -> p (l e)"), op=ALU.add)
        p.I("dve", "scalar_tensor_tensor", out=gsT.v(), in0=modT[:, :, DC:2 * DC], scalar=1.0, in1=ngT.v(),
            op0=ALU.add, op1=ALU.mult)

    def norm_phase(src, dst_tiles, g_of_dc, sh_of_dc, out_dram=None):
        TA = cfg.TA
        with p.scope():
            xt = [p.sb("xt", [128, DC, TA], F32) for _ in range(2)]
            sq = [p.sb("sq", [128, TA], F32) for _ in range(2)]
            rstd = [p.sb("rstd", [128, TA], F32) for _ in range(2)]
            tmp = [p.sb("ntmp", [128, TA], F32) for _ in range(4)]
            pss = [p.ps("psn", [128, 512], F32) for _ in range(2)]
            k = 0
            for ta in range(S // TA):
                x_ = xt[ta % 2]
                ts = slice(ta * TA, (ta + 1) * TA)
                p.dma("sp", x_.v(), src.re("(dc p) s -> p dc s", p=128)[:, :, ts])
                ps_ = pss[ta % 2]
                for dc in range(DC):
                    s_ = sq[dc % 2]
                    if dc % 2 == 0:
                        p.I("act", "activation", out=s_.v(), in_=x_[:, dc, :], func=AF.Square)
                    else:
                        p.I("dve", "tensor_tensor", out=s_.v(), in0=x_[:, dc, :], in1=x_[:, dc, :], op=ALU.mult)
                    p.I("pe", "matmul", out=ps_[:, 0:TA], lhsT=ones32, rhs=s_.v(), start=(dc == 0), stop=(dc == DC - 1))
                r_ = rstd[ta % 2]
                p.I("act", "activation", out=r_.v(), in_=ps_[:, 0:TA], func=AF.Sqrt, bias=NORM_EPS, scale=1.0 / D)
                p.I("dve", "reciprocal", out=r_.v(), in_=r_.v())
                for dc in range(DC):
                    t_ = tmp[k % 4]
                    k += 1
                    p.I("dve", "scalar_tensor_tensor", out=t_.v(), in0=x_[:, dc, :],
                        scalar=g_of_dc(dc), in1=r_.v(), op0=ALU.mult, op1=ALU.mult)
                    if out_dram is None:
                        p.I("act", "activation", out=dst_tiles[dc][:, ts], in_=t_.v(), func=AF.Identity,
                            bias=sh_of_dc(dc), scale=1.0)
                    else:
                        p.dma("sp", out_dram[dc * 128:(dc + 1) * 128, ts], t_.v(), acc_w=True)

    wring = {}

    def out_proj(yg_tiles, w_dram, nci, gate_of_ft, src, dst):
        with p.scope():
            wts = [p.sb("wo", [128, nci, 512], BF16) for _ in range(2)]
            pso = [p.ps("pso", [128, 512], F32) for _ in range(4)]
            xin = [p.sb("xin", [128, TT], F32) for _ in range(4)]
            wv = w_dram.re("(ci p) f -> p ci f", p=128)
            k = 0
            G = 4 if (D // 128) % 4 == 0 else 2
            for fg in range(D // (128 * G)):
                wt = wts[fg % 2]
                p.dma("pool", wt[:, :, 0:G * 128], wv[:, :, fg * G * 128:(fg + 1) * G * 128])
                for j in range(G):
                    ft = fg * G + j
                    for tt in range(NT):
                        ts = slice(tt * TT, (tt + 1) * TT)
                        ps_ = pso[k % 4]
                        x_ = xin[k % 4]
                        k += 1
                        p.dma("sp", x_.v(), src[ft * 128:(ft + 1) * 128, ts])
                        for ci in range(nci):
                            p.I("pe", "matmul", out=ps_[:, 0:TT], lhsT=wt[:, ci, j * 128:(j + 1) * 128],
                                rhs=yg_tiles[ci][:, ts], start=(ci == 0), stop=(ci == nci - 1))
                        p.I("dve", "scalar_tensor_tensor", out=x_.v(), in0=ps_[:, 0:TT], scalar=gate_of_ft(ft),
                            in1=x_.v(), op0=ALU.mult, op1=ALU.add)
                        p.dma("sp", dst[ft * 128:(ft + 1) * 128, ts], x_.v(), acc_w=True)

    def proj_fm(xs_tiles, wv, f0, nft, sink, wts, pss, kdim=DC):
        k = 0
        G = 4 if nft % 4 == 0 else (2 if nft % 2 == 0 else 1)
        gi = 0
        for fg in range(nft // G):
            wt = wts[gi % 2]
            gi += 1
            p.dma("pool", wt[:, :, 0:G * 128], wv[:, :, f0 + fg * G * 128:f0 + (fg + 1) * G * 128])
            for j in range(G):
                ft = fg * G + j
                for tt in range(NT):
                    ts = slice(tt * TT, (tt + 1) * TT)
                    ps_ = pss[k % len(pss)]
                    k += 1
                    for dc in range(kdim):
                        p.I("pe", "matmul", out=ps_[:, 0:TT], lhsT=wt[:, dc, j * 128:(j + 1) * 128],
                            rhs=xs_tiles[dc][:, ts], start=(dc == 0), stop=(dc == kdim - 1))
                    sink(ft, tt, ps_)


    def proj_tm(hT, wv, f0, ngroups, gw, sink, wts, pss):
        k = 0
        for gi in range(ngroups):
            wt = wts[gi % 2]
            p.dma("pool", wt[:, :, 0:gw], wv[:, :, f0 + gi * gw:f0 + (gi + 1) * gw])
            for tk in range(S // 128):
                ps_ = pss[k % len(pss)]
                k += 1
                for dc in range(DC):
                    p.I("pe", "matmul", out=ps_[:, 0:gw], lhsT=hT[dc][:, tk * 128:(tk + 1) * 128], rhs=wt[:, dc, 0:gw],
                        start=(dc == 0), stop=(dc == DC - 1))
                sink(gi, tk, ps_)

    def gla_layer(l, j, src, dst):
        NCH = S // 128
        with p.scope():
            yg = [p.sb("ygg", [128, S], BF16) for _ in range(VW // 128)]
            lowT = p.sb("lowT", [GR, S], BF16)
            gw2 = p.sb("gw2", [GR, KW], BF16)
            nb = p.sb("gnb", [128, KW // 128], F32)
            hgb = p.sb("hgb", [128, DV], F32)
            p.dma("pool", gw2.v(), gl_w2.v()[j])
            p.dma("sp", nb.v(), gl_nbT.v()[:, j])
            p.dma("sp", hgb.v(), gl_hgb.v()[:, j])
            p.I("dve", "tensor_scalar", out=nb.v(), in0=nb.v(), scalar1=-1.0, scalar2=None, op0=ALU.mult)
            wv = gl_in.v()[j].re("(dc p) f -> p dc f", p=128)
            with p.scope():
                hT = [p.sb("hT", [128, S], BF16) for _ in range(DC)]
                norm_phase(src, hT, lambda dc: gsT[:, l, dc:dc + 1], lambda dc: modT[:, l, dc:dc + 1])
                wts = [p.sb("wi", [128, DC, 512], BF16) for _ in range(2)]
                wl = p.sb("wl", [128, DC, GR], BF16)
                pss = [p.ps("psp", [128, 512], F32) for _ in range(4)]
                stg = [p.sb("stg", [128, 512], F32) for _ in range(4)]
                sk = [0]

                def evac(ps_ap, dst_ap, width):
                    s_ = stg[sk[0] % 4]
                    e = "act" if sk[0] % 2 == 0 else "dve"
                    sk[0] += 1
                    if e == "act":
                        p.I("act", "copy", out=s_[:, 0:width], in_=ps_ap)
                    else:
                        p.I("dve", "tensor_copy", out=s_[:, 0:width], in_=ps_ap)
                    p.dma("sp", dst_ap, s_[:, 0:width], acc_w=True)

                for c in range(2):
                    proj_fm(hT, wv, c * KW, KW // 128,
                            lambda ft, tt, ps_, c=c: evac(ps_[:, 0:TT], gqk[c][ft][:, tt * TT:(tt + 1) * TT], TT), wts, pss)
                for c in range(2):
                    proj_tm(hT, wv, 2 * KW + c * VW, GH, DV,
                            lambda gi, tk, ps_, c=c: evac(ps_[:, 0:DV], gvg[c][gi][tk * 128:(tk + 1) * 128, :], DV), wts, pss)
                p.dma("pool", wl.v(), wv[:, :, 2 * KW + 2 * VW:2 * KW + 2 * VW + GR])
                for tt in range(NT):
                    ts = slice(tt * TT, (tt + 1) * TT)
                    ps_ = pss[tt % 4]
                    for dc in range(DC):
                        p.I("pe", "matmul", out=ps_[0:GR, 0:TT], lhsT=wl[:, dc, :], rhs=hT[dc][:, ts],
                            start=(dc == 0), stop=(dc == DC - 1))
                    p.I("act", "copy", out=lowT[:, ts], in_=ps_[0:GR, 0:TT])
            if cfg.stop <= 2:
                return False
            with p.scope():
                ldq = [p.sb("ldq", [128, S], F32) for _ in range(KC)]
                ldk = [p.sb("ldk", [128, S], F32) for _ in range(KC)]
                QT = [p.sb("QT", [128, S], BF16) for _ in range(KC)]
                KT = [p.sb("KT", [128, S], BF16) for _ in range(KC)]
                eb = [p.sb("eb", [128, S], F32) for _ in range(KC)]
                t1 = p.sb("gt1", [128, S], F32)
                t2 = p.sb("gt2", [128, S], F32)
                S32 = [p.sb("S32", [128, DV], F32) for _ in range(KC)]
                Sb = [p.sb("Sb", [128, DV], BF16) for _ in range(KC)]
                v32 = [p.sb("v32", [128, DV], F32) for _ in range(2)]
                g32 = [p.sb("g32", [128, DV], F32) for _ in range(2)]
                Vb = [p.sb("Vb", [128, DV], BF16) for _ in range(2)]
                SG = [p.sb("SG", [128, DV], F32) for _ in range(2)]
                KTM = [p.sb("KTM", [128, KC * 128], BF16) for _ in range(2)]
                ST = [p.sb("ST", [128, 128], BF16) for _ in range(2)]
                junk = p.sb("junk", [128, DV], F32)
                ssq = [p.sb("ssq", [128, 1], F32) for _ in range(2)]
                y32 = [p.sb("y32", [128, DV], F32) for _ in range(2)]
                yb = [p.sb("yb", [128, DV], BF16) for _ in range(2)]
                psP = p.ps("gpsP", [128, 512], F32)
                psTr = p.ps("gpsTr", [128, 8, 128], BF16)
                psS = p.ps("gpsS", [128, 512], F32)
                psO = [p.ps("gpsO", [128, 512], F32) for _ in range(2)]
                psSt = [p.ps("gpsSt", [128, 512], F32) for _ in range(2)]
                psTr2 = p.ps("gpsTr2", [128, 8, 128], BF16)
                for hh in range(GH):
                    for kc in range(KC):
                        ft = hh * KC + kc
                        p.dma("sp", ldq[kc].v(), gqk[0][ft].v())
                        p.dma("sp", ldk[kc].v(), gqk[1][ft].v())
                        for tt in range(NT):
                            ts = slice(tt * TT, (tt + 1) * TT)
                            p.I("pe", "matmul", out=psP[:, 0:TT], lhsT=gw2[:, ft * 128:(ft + 1) * 128], rhs=lowT[:, ts], start=True, stop=True)
                            p.I("act", "activation", out=t1[:, ts], in_=psP[:, 0:TT], func=AF.Exp, bias=nb[:, ft:ft + 1], scale=-1.0)
                        p.I("act", "activation", out=t1.v(), in_=t1.v(), func=AF.Ln, bias=1.0, scale=1.0)
                        p.I("dve", "tensor_scalar", out=t1.v(), in0=t1.v(), scalar1=-1.0 / 16.0, scalar2=None, op0=ALU.mult)
                        p.I("dve", "tensor_tensor_scan", out=t2.v(), data0=rmask128.v(), data1=t1.v(), initial=0.0,
                            op0=ALU.mult, op1=ALU.add)
                        p.I("act", "activation", out=eb[kc].v(), in_=t2.v(), func=AF.Exp)
                        p.I("act", "activation", out=t1.v(), in_=t2.v(), func=AF.Exp, scale=-1.0)
                        p.I("dve", "scalar_tensor_tensor", out=QT[kc].v(), in0=ldq[kc].v(), scalar=float(DK) ** -0.5, in1=eb[kc].v(),
                            op0=ALU.mult, op1=ALU.mult)
                        p.I("dve", "tensor_tensor", out=KT[kc].v(), in0=ldk[kc].v(), in1=t1.v(), op=ALU.mult)
                        p.I("dve", "memset", ap=S32[kc].v(), constant=0.0)
                        p.I("dve", "memset", ap=Sb[kc].v(), constant=0.0)
                    for n in range(NCH):
                        ns = slice(n * 128, (n + 1) * 128)
                        b2 = n % 2
                        p.dma("sp", v32[b2].v(), gvg[0][hh][ns, :])
                        p.dma("sp", g32[b2].v(), gvg[1][hh][ns, :])
                        p.I("act", "copy", out=Vb[b2].v(), in_=v32[b2].v())
                        p.I("act", "activation", out=SG[b2].v(), in_=g32[b2].v(), func=AF.Silu)
                        for kc in range(KC):
                            p.I("pe", "transpose", out=psTr[:, kc, :], in_=KT[kc][:, ns], identity=ident_bf.v())
                        p.I("dve", "tensor_copy", out=KTM[b2].v().re("p (k x) -> p k x", x=128), in_=psTr[:, 0:KC, :])
                        for kc in range(KC):
                            p.I("pe", "matmul", out=psS[:, 0:128], lhsT=KT[kc][:, ns], rhs=QT[kc][:, ns], start=(kc == 0), stop=(kc == KC - 1))
                        p.I("dve", "tensor_tensor", out=ST[b2].v(), in0=psS[:, 0:128], in1=iu128, op=ALU.mult)
                        po = psO[b2]
                        p.I("pe", "matmul", out=po[:, 0:DV], lhsT=ST[b2].v(), rhs=Vb[b2].v(), start=True, stop=False)
                        for kc in range(KC):
                            p.I("pe", "matmul", out=po[:, 0:DV], lhsT=QT[kc][:, ns], rhs=Sb[kc].v(), start=False, stop=(kc == KC - 1))
                        for kc in range(KC):
                            pst = psSt[kc % 2]
                            p.I("pe", "matmul", out=pst[:, 0:DV], lhsT=KTM[b2][:, kc * 128:(kc + 1) * 128], rhs=Vb[b2].v(), start=True, stop=True)
                            dcol = eb[kc][:, n * 128 + 127:n * 128 + 128]
                            p.I("dve", "tensor_scalar", out=S32[kc].v(), in0=S32[kc].v(), scalar1=dcol, scalar2=None, op0=ALU.mult)
                            p.I("dve", "scalar_tensor_tensor", out=S32[kc].v(), in0=pst[:, 0:DV], scalar=dcol, in1=S32[kc].v(),
                                op0=ALU.mult, op1=ALU.add)
                            p.I("act", "copy", out=Sb[kc].v(), in_=S32[kc].v())
                        p.I("act", "activation", out=junk.v(), in_=po[:, 0:DV], func=AF.Square, accum_out=ssq[b2].v())
                        p.I("act", "activation", out=ssq[b2].v(), in_=ssq[b2].v(), func=AF.Sqrt, bias=NORM_EPS, scale=1.0 / DV)
                        p.I("dve", "reciprocal", out=ssq[b2].v(), in_=ssq[b2].v())
                        p.I("dve", "scalar_tensor_tensor", out=y32[b2].v(), in0=po[:, 0:DV], scalar=ssq[b2].v(), in1=hgb.v(),
                            op0=ALU.mult, op1=ALU.mult)
                        p.I("dve", "tensor_tensor", out=yb[b2].v(), in0=y32[b2].v(), in1=SG[b2].v(), op=ALU.mult)
                        for vc in range(VC):
                            p.I("pe", "transpose", out=psTr2[:, vc, :], in_=yb[b2][:, vc * 128:(vc + 1) * 128], identity=ident_bf.v())
                        for vc in range(VC):
                            p.I("act" if vc % 2 == 0 else "dve", "copy" if vc % 2 == 0 else "tensor_copy",
                                out=yg[hh * VC + vc][:, ns], in_=psTr2[:, vc, :])
            if cfg.stop <= 9:
                return False
            out_proj(yg, gl_out.v()[j], VW // 128, lambda ft: modT[:, l, 2 * DC + ft:2 * DC + ft + 1], src, dst)
            return True


    def ssd_layer(l, j, src, nextbuf):
        NCH = S // 128
        srcv = [src]
        HN = SH
        maskb = cst[:, 7, :]
        ident32 = cst[:, 0, :]
        with p.scope():
            dtT = p.sb("dtT", [128, S], F32)
            acT = p.sb("acT", [128, S], F32)
            nacT = p.sb("nacT", [128, S], F32)
            hv = p.sb("hv", [64, 2], F32)
            dsb = p.sb("dsb", [128, SH], F32)
            cw = p.sb("cw", [128, CW // 128, 4], F32)
            cbv = p.sb("cbv", [128, CW // 128], F32)
            wtm = p.sb("wtm", [128, NCH, 128], F32)
            eatm = p.sb("eatm", [128, NCH, 64], F32)
            decbc = p.sb("decbc", [128, NCH, 64], F32)
            p.dma("sp", hv.v(), sd_hv.v()[:, j])
            p.dma("sp", dsb.v(), sd_dsb.v()[:, j])
            p.dma("sp", cw.v(), sd_cwT.v()[:, j])
            p.dma("sp", cbv.v(), sd_cbT.v()[:, j])
            wv = sd_in.v()[j].re("(dc p) f -> p dc f", p=128)
            with p.scope():
                hT = [p.sb("hT", [128, S], BF16) for _ in range(DC)]
                norm_phase(src, hT, lambda dc: gsT[:, l, dc:dc + 1], lambda dc: modT[:, l, dc:dc + 1])
                wts = [p.sb("wi", [128, DC, 512], BF16) for _ in range(2)]
                wdt = p.sb("wdt", [128, DC, SH], BF16)
                pss = [p.ps("psp", [128, 512], F32) for _ in range(4)]
                stg = [p.sb("stg", [128, 512], F32) for _ in range(4)]
                xst = [p.sb("xst", [128, S + 3], F32) for _ in range(2)]
                acc = [p.sb("cacc", [128, S], F32) for _ in range(2)]
                sk = [0]

                def zsink(gi, tk, ps_):
                    s_ = stg[sk[0] % 4]
                    e = "act" if sk[0] % 2 == 0 else "dve"
                    sk[0] += 1
                    if e == "act":
                        p.I("act", "copy", out=s_.v(), in_=ps_.v())
                    else:
                        p.I("dve", "tensor_copy", out=s_.v(), in_=ps_.v())
                    p.dma("sp", sz[gi][tk * 128:(tk + 1) * 128, :], s_.v(), acc_w=True)

                p.mark('sd_projz_start')
                proj_tm(hT, wv, 0, SG, 512, zsink, wts, pss)
                p.mark('sd_projx_start')
                for b in range(2):
                    p.I("dve", "memset", ap=xst[b][:, 0:3], constant=0.0)

                def csink(ft, tt, ps_):
                    x_ = xst[ft % 2]
                    e = "act" if (ft + tt) % 2 == 0 else "dve"
                    if e == "act":
                        p.I("act", "copy", out=x_[:, 3 + tt * TT:3 + (tt + 1) * TT], in_=ps_[:, 0:TT])
                    else:
                        p.I("dve", "tensor_copy", out=x_[:, 3 + tt * TT:3 + (tt + 1) * TT], in_=ps_[:, 0:TT])
                    if tt == NT - 1:
                        a_ = acc[ft % 2]
                        p.I("act", "mul", out=a_.v(), in_=x_[:, 3:S + 3], mul=cw[:, ft, 3:4])
                        for kk_ in range(3):
                            p.I("dve", "scalar_tensor_tensor", out=a_.v(), in0=x_[:, kk_:S + kk_], scalar=cw[:, ft, kk_:kk_ + 1],
                                in1=a_.v(), op0=ALU.mult, op1=ALU.add)
                        p.I("act", "activation", out=a_.v(), in_=a_.v(), func=AF.Silu, bias=cbv[:, ft:ft + 1], scale=1.0)
                        p.dma("sp", sxbc[ft].v(), a_.v())

                proj_fm(hT, wv, SW, CW // 128, csink, wts, pss)
                p.dma("pool", wdt.v(), wv[:, :, SW + CW:SW + CW + SH])
                p.I("dve", "memset", ap=dtT.v(), constant=0.0)
                p.I("dve", "memset", ap=acT.v(), constant=0.0)
                for tt in range(NT):
                    ts = slice(tt * TT, (tt + 1) * TT)
                    ps_ = pss[tt % 4]
                    for dc in range(DC):
                        p.I("pe", "matmul", out=ps_[0:HN, 0:TT], lhsT=wdt[:, dc, :], rhs=hT[dc][:, ts], start=(dc == 0), stop=(dc == DC - 1))
                    p.I("act", "activation", out=dtT[0:HN, ts], in_=ps_[0:HN, 0:TT], func=AF.Exp, bias=hv[0:HN, 0:1], scale=1.0)
                p.I("act", "activation", out=dtT[0:HN, :], in_=dtT[0:HN, :], func=AF.Ln, bias=1.0, scale=1.0)
            if cfg.stop <= 2:
                return False
            p.mark('sd_dt_start')
            with p.scope():
                eaT = p.sb("eaT", [128, S], F32)
                na = p.sb("na", [64, 1], F32)
                t1 = p.sb("st1", [128, S], F32)
                Dg = p.sb("Dg", [64, 64], F32)
                psq = [p.ps("spsq", [128, 512], F32) for _ in range(2)]
                p.I("act", "activation", out=na[0:HN, :], in_=hv[0:HN, 1:2], func=AF.Exp)
                p.I("dve", "tensor_scalar", out=na[0:HN, :], in0=na[0:HN, :], scalar1=-1.0, scalar2=None, op0=ALU.mult)
                p.I("dve", "tensor_scalar", out=t1[0:HN, :], in0=dtT[0:HN, :], scalar1=na[0:HN, 0:1], scalar2=None, op0=ALU.mult)
                p.I("dve", "tensor_tensor_scan", out=acT[0:HN, :], data0=rmask128[0:HN, :], data1=t1[0:HN, :], initial=0.0,
                    op0=ALU.mult, op1=ALU.add)
                p.I("dve", "memset", ap=nacT.v(), constant=0.0)
                p.I("dve", "memset", ap=eaT.v(), constant=0.0)
                p.I("dve", "tensor_scalar", out=nacT[0:HN, :], in0=acT[0:HN, :], scalar1=-1.0, scalar2=None, op0=ALU.mult)
                p.I("act", "activation", out=eaT[0:HN, :], in_=acT[0:HN, :], func=AF.Exp)
                for n in range(NCH):
                    ns = slice(n * 128, (n + 1) * 128)
                    last = acT[0:HN, n * 128 + 127:n * 128 + 128]
                    p.I("act", "activation", out=t1[0:HN, ns], in_=acT[0:HN, ns], func=AF.Exp, bias=last, scale=-1.0)
                    p.I("dve", "tensor_tensor", out=dtT[64:64 + HN, ns], in0=t1[0:HN, ns], in1=dtT[0:HN, ns], op=ALU.mult)
                    ps_ = psq[n % 2]
                    p.I("pe", "transpose", out=ps_[:, 0:128], in_=dtT[:, ns], identity=ident32)
                    p.I("pe", "transpose", out=ps_[:, 128:256], in_=eaT[:, ns], identity=ident32)
                    p.I("dve", "tensor_scalar", out=Dg[0:HN, 0:HN], in0=ident32[0:HN, 0:HN], scalar1=last, scalar2=None, op0=ALU.mult)
                    p.I("pe", "matmul", out=ps_[:, 256:256 + HN], lhsT=ones32[0:HN, :], rhs=Dg[0:HN, 0:HN], start=True, stop=True)
                    p.I("act", "copy", out=wtm[:, n, :], in_=ps_[:, 0:128])
                    p.I("dve", "tensor_copy", out=eatm[:, n, :], in_=ps_[:, 128:192])
                    p.I("act", "activation", out=decbc[:, n, 0:HN], in_=ps_[:, 256:256 + HN], func=AF.Exp)
            if cfg.stop <= 3:
                return False
            nhalf = 2 if SG >= 2 else 1
            GPH = SG // nhalf
            yg = [p.sb("ygs", [128, S], BF16) for _ in range(GPH * 4)]
            for half in range(nhalf):
              p.mark('sd_scan_start_h%d' % half)
              with p.scope():
                xg = [p.sb("xg", [128, 4, 128], F32) for _ in range(2)]
                bg = p.sb("bg", [128, S], F32)
                BT = p.sb("BT", [128, S], BF16)
                CT = p.sb("CT", [128, S], BF16)
                ngb = p.sb("ngb", [128, 512], F32)
                prev32 = p.sb("prev32", [128, 512], F32)
                prevb = p.sb("prevb", [128, 512], BF16)
                xtm = [p.sb("xtm", [128, 512], F32) for _ in range(2)]
                xc = [p.sb("xc", [128, 512], BF16) for _ in range(2)]
                xcd = [p.sb("xcd", [128, 512], BF16) for _ in range(2)]
                Btm = [p.sb("Btm", [128, 128], BF16) for _ in range(2)]
                cbT = [p.sb("cbT", [128, 128], BF16) for _ in range(2)]
                eM = [p.sb("eM", [128, 4, 128], F32) for _ in range(2)]
                Mm = [[p.sb("Mm", [128, 4, 128], BF16) for _ in range(2)] for _b in range(2)]
                maskb_bf = p.sb("maskb_bf", [128, 128], BF16)
                p.I("dve", "tensor_copy", out=maskb_bf.v(), in_=maskb)
                z32 = [p.sb("z32", [128, 512], F32) for _ in range(2)]
                ty = [p.sb("ty", [128, 512], F32) for _ in range(2)]
                tu = [p.sb("tu", [128, 512], F32) for _ in range(2)]
                junk = p.sb("sjunk", [128, 512], F32)
                sgz = [p.sb("sgz", [128, 512], F32) for _ in range(2)]
                ssq = [p.sb("sssq", [128, 1], F32) for _ in range(2)]
                ybf = [p.sb("ybf", [128, 512], BF16) for _ in range(2)]
                psX = p.ps("spsX", [128, 512], F32)
                psB = p.ps("spsB", [128, 8, 128], BF16)
                psC = p.ps("spsC", [128, 512], F32)
                psM = [p.ps("spsM", [128, 4, 128], F32) for _ in range(2)]
                psY = p.ps("spsY", [128, 512], F32)
                psYo = p.ps("spsYo", [128, 512], F32)
                psSt = p.ps("spsSt", [128, 512], F32)
                for g in range(half * GPH, (half + 1) * GPH):
                    p.dma("sp", bg.v(), sxbc[SW // 128 + g].v())
                    p.I("act", "copy", out=BT.v(), in_=bg.v())
                    p.dma("sp", bg.v(), sxbc[SW // 128 + SG + g].v())
                    p.I("dve", "tensor_copy", out=CT.v(), in_=bg.v())
                    p.dma("sp", ngb.v(), sd_ngb.v()[:, j, g * 512:(g + 1) * 512])
                    p.I("dve", "memset", ap=prev32.v(), constant=0.0)
                    p.I("dve", "memset", ap=prevb.v(), constant=0.0)
                    def partA(n):
                            ns = slice(n * 128, (n + 1) * 128)
                            b2 = n % 2
                            hs8 = slice(g * 8, (g + 1) * 8)
                            p.dma("sp", z32[b2].v(), sz[g][ns, :])
                            for i4 in range(4):
                                p.dma("sp", xg[b2][:, i4, :], sxbc[g * 4 + i4][:, ns])
                            for i4 in range(4):
                                p.I("pe", "transpose", out=psX[:, i4 * 128:(i4 + 1) * 128], in_=xg[b2][:, i4, :], identity=ident32)
                            p.I("act", "copy", out=xtm[b2].v(), in_=psX.v())
                            x3 = xtm[b2].v().re("p (h x) -> p h x", x=64)
                            p.I("dve", "tensor_tensor", out=xc[b2].v().re("p (h x) -> p h x", x=64), in0=x3,
                                in1=wtm[:, n, g * 8:(g + 1) * 8].bc([2], [128, 8, 64]), op=ALU.mult)
                            p.I("dve", "tensor_tensor", out=xcd[b2].v().re("p (h x) -> p h x", x=64), in0=x3,
                                in1=wtm[:, n, 64 + g * 8:64 + (g + 1) * 8].bc([2], [128, 8, 64]), op=ALU.mult)
                            p.I("dve", "tensor_tensor", out=tu[b2].v().re("p (h x) -> p h x", x=64), in0=x3,
                                in1=dsb[:, hs8].bc([2], [128, 8, 64]), op=ALU.mult)
                            p.I("act", "activation", out=sgz[b2].v(), in_=z32[b2].v(), func=AF.Silu)
                            p.I("pe", "matmul", out=psC[:, 128:256], lhsT=BT[:, ns], rhs=ident_bf.v(), start=True, stop=True)
                            p.I("pe", "matmul", out=psC[:, 0:128], lhsT=BT[:, ns], rhs=CT[:, ns], start=True, stop=True)
                            p.I("dve", "tensor_copy", out=Btm[b2].v(), in_=psC[:, 128:256])
                            p.I("act", "copy", out=cbT[b2].v(), in_=psC[:, 0:128])
                            for hq in range(2):
                                pm = psM[hq]
                                for h4 in range(4):
                                    h = g * 8 + hq * 4 + h4
                                    sel = ident32[0:HN, h:h + 1].bc([], [HN, 128])
                                    p.I("pe", "matmul", out=pm[:, h4, :], lhsT=sel, rhs=acT[0:HN, ns], start=True, stop=False)
                                    p.I("pe", "matmul", out=pm[:, h4, :], lhsT=nacT[0:HN, ns], rhs=sel, start=False, stop=False)
                                    p.I("pe", "matmul", out=pm[:, h4, :], lhsT=ident_bf.v(), rhs=maskb_bf.v(), start=False, stop=True)
                                p.I("act", "activation", out=eM[hq].v(), in_=pm.v(), func=AF.Exp)
                                p.I("dve", "tensor_tensor", out=Mm[b2][hq].v(), in0=eM[hq].v(),
                                    in1=cbT[b2].v().bc([1], [128, 4, 128]), op=ALU.mult)

                    def partB1(n):
                            ns = slice(n * 128, (n + 1) * 128)
                            b2 = n % 2
                            hs8 = slice(g * 8, (g + 1) * 8)
                            x3 = xtm[b2].v().re("p (h x) -> p h x", x=64)
                            for hq in range(2):
                                for h4 in range(4):
                                    hl = hq * 4 + h4
                                    p.I("pe", "matmul", out=psY[:, hl * 64:(hl + 1) * 64], lhsT=Mm[b2][hq][:, h4, :],
                                        rhs=xc[b2][:, hl * 64:(hl + 1) * 64], start=True, stop=True)
                            p.I("pe", "matmul", out=psYo.v(), lhsT=CT[:, ns], rhs=prevb.v(), start=True, stop=True)
                            p.I("pe", "matmul", out=psSt.v(), lhsT=Btm[b2].v(), rhs=xcd[b2].v(), start=True, stop=True)
                            t_ = ty[b2]
                            u_ = tu[b2]
                            t3 = t_.v().re("p (h x) -> p h x", x=64)
                            u3 = u_.v().re("p (h x) -> p h x", x=64)
                            p32 = prev32.v().re("p (h x) -> p h x", x=64)
                            p.I("dve", "tensor_tensor", out=p32, in0=p32, in1=decbc[:, n, hs8].bc([2], [128, 8, 64]), op=ALU.mult)
                            p.I("dve", "tensor_tensor", out=prev32.v(), in0=psSt.v(), in1=prev32.v(), op=ALU.add)
                            p.I("act", "copy", out=prevb.v(), in_=prev32.v())
                            p.I("dve", "tensor_tensor", out=t3, in0=psYo.v().re("p (h x) -> p h x", x=64),
                                in1=eatm[:, n, hs8].bc([2], [128, 8, 64]), op=ALU.mult)
                            p.I("dve", "tensor_tensor", out=t_.v(), in0=psY.v(), in1=t_.v(), op=ALU.add)
                            p.I("dve", "tensor_tensor", out=t_.v(), in0=t_.v(), in1=u_.v(), op=ALU.add)
                            p.I("dve", "tensor_tensor", out=t_.v(), in0=t_.v(), in1=sgz[b2].v(), op=ALU.mult)
                            p.I("act", "activation", out=junk.v(), in_=t_.v(), func=AF.Square, accum_out=ssq[b2].v())
                            p.I("act", "activation", out=ssq[b2].v(), in_=ssq[b2].v(), func=AF.Sqrt, bias=1e-5, scale=1.0 / 512)
                    def partB2(n):
                            ns = slice(n * 128, (n + 1) * 128)
                            b2 = n % 2
                            t_ = ty[b2]
                            p.I("dve", "reciprocal", out=ssq[b2].v(), in_=ssq[b2].v())
                            p.I("dve", "scalar_tensor_tensor", out=ybf[b2].v(), in0=t_.v(), scalar=ssq[b2].v(), in1=ngb.v(),
                                op0=ALU.mult, op1=ALU.mult)
                            for i4 in range(4):
                                p.I("pe", "transpose", out=psB[:, 4 + i4, :], in_=ybf[b2][:, i4 * 128:(i4 + 1) * 128], identity=ident_bf.v())
                            for i4 in range(4):
                                p.I("act" if i4 % 2 == 0 else "dve", "copy" if i4 % 2 == 0 else "tensor_copy",
                                    out=yg[(g - half * GPH) * 4 + i4][:, ns], in_=psB[:, 4 + i4, :])

                    partA(0)
                    if NCH > 1:
                        partA(1)
                    for n in range(NCH):
                        partB1(n)
                        if n + 2 < NCH:
                            partA(n + 2)
                        partB2(n)
              if cfg.stop <= 9:
                  return False
              p.mark('sd_outproj_start_h%d' % half)
              nci = GPH * 4
              dsth = nextbuf()
              out_proj(yg, sd_out.v()[j][half * nci * 128:(half + 1) * nci * 128, :], nci,
                       lambda ft: modT[:, l, 2 * DC + ft:2 * DC + ft + 1], srcv[0], dsth.v())
              srcv[0] = dsth.v()
              p.mark('sd_outproj_end_h%d' % half)
            return srcv[0]

    def rwkv_layer(l, j, src, dst):
        CB, TB = cfg.CB, cfg.TB
        NCHB = TB // 64
        with p.scope():
            yg = [p.sb("yg", [128, S], BF16) for _ in range(HP)]
            xs = yg
            lw1 = p.sb("lw1", [R, S], BF16)
            la1 = p.sb("la1", [R, S], BF16)
            vec = p.sb("rvec", [128, 7, HP], F32)
            omka = p.sb("omka", [128, HP], F32)
            p.dma("sp", vec.v(), rw_vecT.v()[:, j])
            p.I("dve", "tensor_scalar", out=omka.v(), in0=vec[:, 3, :], scalar1=-1.0, scalar2=1.0,
                op0=ALU.mult, op1=ALU.add)
            with p.scope():
                hT = [p.sb("hT", [128, S], BF16) for _ in range(DC)]
                mu = p.sb("mu", [128, 6, DC], F32)
                omm = p.sb("omm", [128, 6, DC], F32)
                p.dma("sp", mu.v(), rw_muT.v()[:, j])
                p.I("dve", "tensor_scalar", out=omm.v(), in0=mu.v(), scalar1=-1.0, scalar2=1.0,
                    op0=ALU.mult, op1=ALU.add)
                p.mark('rw_norm_start')
                norm_phase(src, hT, lambda dc: gsT[:, l, dc:dc + 1], lambda dc: modT[:, l, dc:dc + 1])
                p.mark('rw_proj_start')
                if cfg.stop <= 1:
                    return False
                wts = [p.sb("wi", [128, DC, 512], BF16) for _ in range(2)]
                w1t = p.sb("w1t", [128, DC, R], BF16)
                pss = [p.ps("psp", [128, 512], F32) for _ in range(4)]
                stg = [p.sb("stg", [128, TT], F32) for _ in range(4)]
                wv = rw_in.v()[j].re("(dc p) f -> p dc f", p=128)
                sk = [0]

                def mix(c):
                    for dc in range(DC):
                        p.I("dve", "memset", ap=xs[dc][:, 0:1], constant=0.0)
                        p.I("act", "mul", out=xs[dc][:, 1:S], in_=hT[dc][:, 0:S - 1], mul=mu[:, c, dc:dc + 1])
                        p.I("dve", "scalar_tensor_tensor", out=xs[dc].v(), in0=hT[dc].v(), scalar=omm[:, c, dc:dc + 1],
                            in1=xs[dc].v(), op0=ALU.mult, op1=ALU.add)

                import os as _os
                for c in range(4):
                    if not _os.environ.get("NOMIX") or c == 0:
                        mix(c)

                    def sink(ft, tt, ps_, c=c):
                        s_ = stg[sk[0] % 4]
                        e = "act" if sk[0] % 2 == 0 else "dve"
                        sk[0] += 1
                        if e == "act":
                            p.I("act", "copy", out=s_.v(), in_=ps_[:, 0:TT])
                        else:
                            p.I("dve", "tensor_copy", out=s_.v(), in_=ps_[:, 0:TT])
                        if not _os.environ.get("NOSTORE"):
                            p.dma("sp", projT[c][ft][:, tt * TT:(tt + 1) * TT], s_.v(), acc_w=True)

                    proj_fm(xs, wv, c * W, HP, sink, wts, pss)
                for c, (w1d, dstl, fn) in ((4, (rw_dw1, lw1, AF.Tanh)), (5, (rw_aw1, la1, AF.Copy))):
                    mix(c)
                    p.dma("pool", w1t.v(), w1d.v()[j].re("(dc p) r -> p dc r", p=128))
                    for tt in range(NT):
                        ts = slice(tt * TT, (tt + 1) * TT)
                        ps_ = pss[tt % 4]
                        for dc in range(DC):
                            p.I("pe", "matmul", out=ps_[0:R, 0:TT], lhsT=w1t[:, dc, :], rhs=xs[dc][:, ts],
                                start=(dc == 0), stop=(dc == DC - 1))
                        p.I("act", "activation", out=dstl[:, ts], in_=ps_[0:R, 0:TT], func=fn)
            if cfg.stop <= 2:
                return False
            p.mark('rw_scan_start')
            with p.scope():
                dw2 = p.sb("dw2", [R, W], BF16)
                aw2 = p.sb("aw2", [R, W], BF16)
                p.dma("pool", dw2.v(), rw_dw2.v()[j])
                p.dma("pool", aw2.v(), rw_aw2.v()[j])
                CBS, NSTR = 2, 2
                STR = []
                psTrS = p.ps("psTr", [128, 4, 2, 128], BF16)
                for si in range(NSTR):
                    pg_ = p.ps("PG", [128, 2, 512], F32)
                    xr_ = p.sb("Xr", [64, CBS * 2, 2, 64], BF16)
                    nxt_ = p.sb("NXT", [64, CBS * 2, 192], BF16)
                    mu_ = p.sb("MU", [64, CBS * 2, 128], BF16)
                    STR.append([dict(
                        BK=p.sb("BK", [128, CBS, 128], BF16), UV=p.sb("UV", [128, CBS, 128], BF16),
                        Xr=xr_, A_sb=p.sb("A_sb", [128, CBS * 2, 128], BF16), NXT=nxt_, MU=mu_,
                        GT=p.sb("GT", [128, CBS, 64], BF16), PpT=p.sb("PpT", [128, CBS, 64], BF16),
                        PG=pg_, psTr=psTrS, toff=si * CBS) for _par in range(2)])
                psS5 = p.ps("psS5", [128, 4, 128], F32)
                Tst = [p.sb("Tst", [128, 64], BF16) for _ in range(3)]
                psP1 = p.ps("psP", [128, 512], F32)
                psP = [psP1, psP1]
                psAV = p.ps("psAV", [128, CBS * 2, 128], F32)
                NTB = TB // TT if TB >= TT else 1
                TTB = min(TT, TB)
                tiref = [0]

                def item(hp, tb, SET):
                    hsl = slice(hp * 128, (hp + 1) * 128)
                    vcol = lambda i: vec[:, i, hp:hp + 1]
                    tbs = slice(tb * TB, (tb + 1) * TB)
                    ld, tm, BKT, KRT, KKVT = SET["ld"], SET["tm"], SET["BKT"], SET["KRT"], SET["KKVT"]
                    for c, nm in enumerate(("r", "k", "v", "g")):
                        p.dma("sp", ld[nm].v(), projT[c][hp][:, tbs])
                    r_, k_, v_, g_ = ld["r"], ld["k"], ld["v"], ld["g"]
                    if hp == 3:
                        p.mark('rw_prep_start_tb%d' % tb)
                    lw, cum, e1, e2, e3, a_, kk, kf, t1, t2, t3, bv, yT = (tm[n] for n in (
                        "lw", "cum", "e1", "e2", "e3", "a", "kk", "kf", "t1", "t2", "t3", "bv", "y"))
                    for tt in range(NTB):
                        ts = slice(tt * TTB, (tt + 1) * TTB)
                        gs_ = slice(tb * TB + tt * TTB, tb * TB + (tt + 1) * TTB)
                        ps_ = psP[0]
                        p.I("pe", "matmul", out=ps_[:, 0:TTB], lhsT=dw2[:, hsl], rhs=lw1[:, gs_], start=True, stop=True)
                        p.I("act", "activation", out=lw[:, ts], in_=ps_[:, 0:TTB], func=AF.Sigmoid, bias=vcol(0), scale=1.0)
                        ps_ = psP[1]
                        p.I("pe", "matmul", out=ps_[:, 0:TTB], lhsT=aw2[:, hsl], rhs=la1[:, gs_], start=True, stop=True)
                        p.I("act", "activation", out=a_[:, ts], in_=ps_[:, 0:TTB], func=AF.Sigmoid, bias=vcol(1), scale=1.0)
                    p.I("dve", "tensor_scalar", out=lw.v(), in0=lw.v(), scalar1=NEG_EXP_HALF, scalar2=None, op0=ALU.mult)
                    yield "P"
                    p.I("dve", "tensor_tensor_scan", out=cum.v(), data0=rmask[:, 0:TB], data1=lw.v(), initial=0.0,
                        op0=ALU.mult, op1=ALU.add)
                    yield "P"
                    p.I("act", "activation", out=e1.v(), in_=cum.v(), func=AF.Exp)
                    yield "P"
                    p.I("act", "activation", out=e2.v(), in_=cum.v(), func=AF.Exp, scale=-1.0)
                    yield "P"
                    p.I("dve", "tensor_tensor", out=t1.v(), in0=cum.v(), in1=lw.v(), op=ALU.subtract)
                    yield "P"
                    p.I("act", "activation", out=e3.v(), in_=t1.v(), func=AF.Exp)
                    yield "P"
                    p.I("act", "activation", out=t2.v(), in_=k_.v(), func=AF.Square, scale=vcol(2))
                    yield "P"
                    for tt in range(NTB):
                        ts = slice(tt * TTB, (tt + 1) * TTB)
                        ps_ = psP[tt % 2]
                        p.I("pe", "matmul", out=ps_[:, 0:TTB], lhsT=bones32, rhs=t2[:, ts], start=True, stop=True)
                        p.I("act", "activation", out=t3[:, ts], in_=ps_[:, 0:TTB], func=AF.Sqrt)
                    p.I("dve", "tensor_scalar", out=t3.v(), in0=t3.v(), scalar1=1e-12, scalar2=None, op0=ALU.max)
                    yield "P"
                    p.I("dve", "reciprocal", out=t3.v(), in_=t3.v())
                    yield "P"
                    p.I("dve", "scalar_tensor_tensor", out=kk.v(), in0=k_.v(), scalar=vcol(2), in1=t3.v(), op0=ALU.mult, op1=ALU.mult)
                    yield "P"
                    p.I("dve", "tensor_scalar", out=t1.v(), in0=a_.v(), scalar1=vcol(3), scalar2=omka[:, hp:hp + 1],
                        op0=ALU.mult, op1=ALU.add)
                    yield "P"
                    p.I("dve", "tensor_tensor", out=kf.v(), in0=k_.v(), in1=t1.v(), op=ALU.mult)
                    yield "P"
                    p.I("dve", "tensor_tensor", out=t2.v(), in0=kk.v(), in1=a_.v(), op=ALU.mult)
                    yield "P"
                    ch = lambda t: t.v().re("p (n c) -> p n c", c=64)
                    p.I("dve", "tensor_tensor", out=KRT[:, :, 1, :], in0=ch(r_), in1=ch(e1), op=ALU.mult)
                    yield "P"
                    p.I("dve", "tensor_tensor", out=BKT[:, :, 1, :], in0=ch(kf), in1=ch(e2), op=ALU.mult)
                    yield "P"
                    p.I("dve", "tensor_tensor", out=BKT[:, :, 0, :], in0=ch(t2), in1=ch(e2), op=ALU.mult)
                    yield "P"
                    p.I("dve", "tensor_tensor", out=KRT[:, :, 0, :], in0=ch(kk), in1=ch(e3), op=ALU.mult)
                    yield "P"
                    p.I("act", "copy", out=KKVT[:, :, 0, :], in_=KRT[:, :, 0, :])
                    yield "P"
                    p.I("act", "copy", out=KKVT[:, :, 1, :], in_=ch(v_))
                    yield "P"
                    p.I("dve", "scalar_tensor_tensor", out=t1.v(), in0=r_.v(), scalar=vcol(4), in1=kf.v(),
                        op0=ALU.mult, op1=ALU.mult)
                    yield "P"
                    for tt in range(NTB):
                        ts = slice(tt * TTB, (tt + 1) * TTB)
                        ps_ = psP[tt % 2]
                        p.I("pe", "matmul", out=ps_[:, 0:TTB], lhsT=bones32, rhs=t1[:, ts], start=True, stop=True)
                        p.I("dve", "tensor_tensor", out=bv[:, ts], in0=ps_[:, 0:TTB], in1=v_[:, ts], op=ALU.mult)
                    if cfg.stop <= 3:
                        return False
                    if hp == 3:
                        p.mark('rw_groups_start_tb%d' % tb)
                    yield "P_DONE"
                    if tb == 0:
                        p.I("dve", "memset", ap=Tst[tiref[0] % 3].v(), constant=0.0)

                    def group_stream(c0, cb_n, T):
                        BK, UV, Xr, A_sb, NXT, MU, GT, PpT, PG, psTr = (T[k_] for k_ in
                            ("BK", "UV", "Xr", "A_sb", "NXT", "MU", "GT", "PpT", "PG", "psTr"))
                        psTr = psTr[:, T["toff"]:T["toff"] + CBS]
                        PGv = PG.v().re("p h (c x) -> p h c x", c=CBS)
                        hc = lambda t: t.v().re("p (h c) x -> p h c x", h=2)[:, :, 0:cb_n, :]
                        for cb in range(cb_n):
                            n = c0 + cb
                            p.I("pe", "transpose", out=psTr[:, cb, 0, :], in_=BKT[:, n].re("p a c -> p (a c)"), identity=ident_bf.v())
                            p.I("pe", "transpose", out=psTr[:, cb, 1, :], in_=KKVT[:, n].re("p a c -> p (a c)"), identity=ident_bf.v())
                        p.I("dve", "tensor_copy", out=BK[:, 0:cb_n, :], in_=psTr[:, 0:cb_n, 0, :])
                        p.I("dve", "tensor_copy", out=Xr.v().re("p (h c) a x -> p h c a x", h=2)[:, :, 0:cb_n, 0, :],
                            in_=psTr[0:64, 0:cb_n, 1, :].re("p c (h x) -> p h c x", h=2))
                        p.I("dve", "tensor_copy", out=UV[64:128, 0:cb_n, :], in_=psTr[64:128, 0:cb_n, 1, :])
                        for cb in range(cb_n):
                            n = c0 + cb
                            for h in range(2):
                                hs = slice(h * 64, (h + 1) * 64)
                                p.I("pe", "matmul", out=PGv[:, h, cb, 0:128],
                                    lhsT=BKT[hs, n].re("p a c -> p (a c)"), rhs=KRT[hs, n].re("p a c -> p (a c)"),
                                    start=True, stop=True)
                                p.I("pe", "matmul", out=PGv[0:64, h, cb, 128:192],
                                    lhsT=KRT[hs, n, 0, :], rhs=BKT[hs, n, 0, :], start=True, stop=True)
                        pgA = PGv[:, :, 0:cb_n, 0:128]
                        p.I("act", "copy", out=hc(A_sb), in_=pgA)
                        p.I("dve", "tensor_tensor", out=hc(A_sb), in0=hc(A_sb),
                            in1=maskA.bc([1, 1], [128, 2, cb_n, 128]), op=ALU.mult)
                        nx = hc(NXT)
                        p.I("dve", "tensor_tensor", out=nx[:, :, :, 0:64], in0=hc(A_sb)[0:64, :, :, 0:64],
                            in1=negSU.bc([1, 1], [64, 2, cb_n, 64]), op=ALU.mult)
                        p.I("dve", "tensor_tensor", out=nx[:, :, :, 64:128], in0=nx[:, :, :, 0:64],
                            in1=cst[0:64, 5, 0:64].bc([1, 1], [64, 2, cb_n, 64]), op=ALU.add)
                        p.I("dve", "tensor_tensor", out=nx[:, :, :, 128:192],
                            in0=PGv[0:64, :, 0:cb_n, 128:192],
                            in1=negSL.bc([1, 1], [64, 2, cb_n, 64]), op=ALU.mult)
                        yield
                        for cb in range(cb_n):
                            for h in range(2):
                                q = h * CBS + cb
                                p.I("pe", "matmul", out=psAV[0:64, q, 0:64], lhsT=A_sb[64:128, q, 0:64],
                                    rhs=UV[64:128, cb, h * 64:(h + 1) * 64], start=True, stop=True)
                        pgI = PGv[0:64, :, 0:cb_n, 0:192]
                        for rnd in range(6):
                            for cb in range(cb_n):
                                for h in range(2):
                                    q = h * CBS + cb
                                    if rnd == 0:
                                        p.I("pe", "matmul", out=PGv[0:64, h, cb, 0:64], lhsT=NXT[:, q, 128:192],
                                            rhs=NXT[:, q, 0:64], start=True, stop=True)
                                    elif rnd < 5:
                                        p.I("pe", "matmul", out=PGv[0:64, h, cb, 0:128], lhsT=NXT[:, q, 128:192],
                                            rhs=NXT[:, q, 0:128], start=True, stop=True)
                                    else:
                                        p.I("pe", "matmul", out=PGv[0:64, h, cb, 64:128], lhsT=NXT[:, q, 128:192],
                                            rhs=NXT[:, q, 64:128], start=True, stop=True)
                                    if rnd < 5:
                                        p.I("pe", "matmul", out=PGv[0:64, h, cb, 128:192], lhsT=NXT[:, q, 0:64],
                                            rhs=NXT[:, q, 128:192], start=True, stop=True)
                            if rnd == 0:
                                p.I("act", "copy", out=Xr.v().re("p (h c) a x -> p h c a x", h=2)[:, :, 0:cb_n, 1, :],
                                    in_=psAV.v().re("p (h c) x -> p h c x", h=2)[0:64, :, 0:cb_n, 0:64])
                            if rnd > 0:
                                p.I("dve", "tensor_tensor", out=nx[:, :, :, 64:128], in0=pgI[:, :, :, 64:128],
                                    in1=nx[:, :, :, 64:128], op=ALU.add)
                            if rnd < 5:
                                p.I("act", "copy", out=nx[:, :, :, 0:64], in_=pgI[:, :, :, 0:64])
                                p.I("act", "copy", out=nx[:, :, :, 128:192], in_=pgI[:, :, :, 128:192])
                            yield
                        for cb in range(cb_n):
                            for h in range(2):
                                q = h * CBS + cb
                                p.I("pe", "matmul", out=PGv[0:64, h, cb, 0:128], lhsT=NXT[:, q, 64:128],
                                    rhs=Xr[:, q].re("p a c -> p (a c)"), start=True, stop=True)
                        pgM = PGv[0:64, :, 0:cb_n, 0:128]
                        p.I("act", "mul", out=hc(MU), in_=pgM, mul=-1.0)
                        p.I("dve", "tensor_scalar", out=UV[0:64, 0:cb_n, :].re("p c (h x) -> p h c x", h=2),
                            in0=pgM[:, :, :, 64:128], scalar1=-1.0, scalar2=None, op0=ALU.mult)
                        yield
                        for cb in range(cb_n):
                            for h in range(2):
                                q = h * CBS + cb
                                hs = slice(h * 64, (h + 1) * 64)
                                p.I("pe", "matmul", out=PGv[hs, h, cb, 0:64], lhsT=MU[:, q, 0:64], rhs=A_sb[0:64, q, 64:128],
                                    start=True, stop=True)
                                p.I("pe", "matmul", out=PGv[hs, h, cb, 64:128], lhsT=MU[:, q, 0:64], rhs=BK[0:64, cb, h * 64:(h + 1) * 64],
                                    start=True, stop=True)
                        for h in range(2):
                            hs = slice(h * 64, (h + 1) * 64)
                            p.I("dve", "tensor_tensor", out=GT[hs, 0:cb_n, :], in0=PGv[hs, h, 0:cb_n, 0:64],
                                in1=KRT[hs, c0:c0 + cb_n, 1, :], op=ALU.add)
                            p.I("dve", "tensor_tensor", out=PpT[hs, 0:cb_n, :], in0=PGv[hs, h, 0:cb_n, 64:128],
                                in1=cst[hs, 5, 0:64].bc([1], [64, cb_n, 64]), op=ALU.add)
                        yield
                        return

                    def back(sets):
                        slot = 0
                        c0g = sets[0][1]
                        for (T, c0, cb_n) in sets:
                            BK, UV, A_sb, GT, PpT = (T[k_] for k_ in ("BK", "UV", "A_sb", "GT", "PpT"))
                            for cb in range(cb_n):
                                n = c0 + cb
                                Tc, Tn = Tst[tiref[0] % 3], Tst[(tiref[0] + 1) % 3]
                                tiref[0] += 1
                                for h in range(2):
                                    hs = slice(h * 64, (h + 1) * 64)
                                    p.I("pe", "matmul", out=psS5[hs, slot, 0:64], lhsT=PpT[hs, cb, :], rhs=Tc[hs, :], start=True, stop=False)
                                    p.I("pe", "matmul", out=psS5[hs, slot, 0:64], lhsT=BK[:, cb, hs], rhs=UV[:, cb, hs], start=False, stop=True)
                                yield
                                for h in range(2):
                                    hs = slice(h * 64, (h + 1) * 64)
                                    p.I("dve", "tensor_scalar", out=Tn[hs, :], in0=psS5[hs, slot, 0:64],
                                        scalar1=e1[hs, n * 64 + 63:n * 64 + 64], scalar2=None, op0=ALU.mult)
                                for h in range(2):
                                    q = h * CBS + cb
                                    hs = slice(h * 64, (h + 1) * 64)
                                    p.I("pe", "matmul", out=psS5[hs, slot, 64:128], lhsT=Tc[hs, :], rhs=GT[hs, cb, :], start=True, stop=False)
                                    p.I("pe", "matmul", out=psS5[hs, slot, 64:128], lhsT=UV[:, cb, hs], rhs=A_sb[:, q, 64:128], start=False, stop=True)
                                slot += 1
                                yield
                        for h in range(2):
                            hs = slice(h * 64, (h + 1) * 64)
                            p.I("act", "copy", out=yT.v().re("p (n c) -> p n c", c=64)[hs, c0g:c0g + slot, :],
                                in_=psS5[hs, 0:slot, 64:128])

                    def drive(gens):
                        alive = list(gens)
                        while alive:
                            nxt = []
                            for gq in alive:
                                try:
                                    next(gq)
                                    nxt.append(gq)
                                except StopIteration:
                                    pass
                            alive = nxt
                            yield "G"

                    prev_sets = None
                    for gi_, g0 in enumerate(range(0, NCHB, CBS * NSTR)):
                        gens, sets = [], []
                        for si in range(NSTR):
                            c0 = g0 + si * CBS
                            if c0 < NCHB:
                                T_ = STR[si][gi_ % 2]
                                cbn_ = min(CBS, NCHB - c0)
                                gens.append(group_stream(c0, cbn_, T_))
                                sets.append((T_, c0, cbn_))
                        if prev_sets is not None:
                            gens.append(back(prev_sets))
                        yield from drive(gens)
                        prev_sets = sets
                    yield from drive([back(prev_sets)])
                    yield "G_DONE"
                    if hp == 3:
                        p.mark('rw_post_start_tb%d' % tb)
                    if cfg.stop <= 8:
                        return False
                    p.I("act", "activation", out=t2.v(), in_=yT.v(), func=AF.Square)
                    yield "Q"
                    HW_ = min(256, TTB)
                    for tt in range(TB // HW_):
                        ts = slice(tt * HW_, (tt + 1) * HW_)
                        p.I("pe", "matmul", out=psP1[:, 0:HW_], lhsT=bones32, rhs=yT[:, ts], start=True, stop=True)
                        p.I("pe", "matmul", out=psP1[:, 256:256 + HW_], lhsT=bones32, rhs=t2[:, ts], start=True, stop=True)
                        p.I("act", "mul", out=t1[:, ts], in_=psP1[:, 0:HW_], mul=1.0 / 64)
                        p.I("dve", "tensor_tensor", out=t3[:, ts], in0=t1[:, ts], in1=t1[:, ts], op=ALU.mult)
                        p.I("dve", "scalar_tensor_tensor", out=t3[:, ts], in0=psP1[:, 256:256 + HW_], scalar=1.0 / 64, in1=t3[:, ts],
                            op0=ALU.mult, op1=ALU.subtract)
                    p.I("act", "activation", out=t3.v(), in_=t3.v(), func=AF.Sqrt, bias=RWKV_GN_EPS, scale=1.0)
                    yield "Q"
                    p.I("dve", "reciprocal", out=t3.v(), in_=t3.v())
                    yield "Q"
                    p.I("dve", "tensor_tensor", out=t1.v(), in0=yT.v(), in1=t1.v(), op=ALU.subtract)
                    yield "Q"
                    p.I("dve", "tensor_tensor", out=t1.v(), in0=t1.v(), in1=t3.v(), op=ALU.mult)
                    yield "Q"
                    p.I("act", "activation", out=t1.v(), in_=t1.v(), func=AF.Identity, bias=vcol(6), scale=vcol(5))
                    yield "Q"
                    p.I("dve", "tensor_tensor", out=t1.v(), in0=t1.v(), in1=bv.v(), op=ALU.add)
                    yield "Q"
                    p.I("act", "activation", out=t2.v(), in_=g_.v(), func=AF.Silu)
                    yield "Q"
                    p.I("dve", "tensor_tensor", out=yg[hp][:, tbs], in0=t1.v(), in1=t2.v(), op=ALU.mult)
                    yield "Q"

                SETS = []
                for _si in range(2):
                    SETS.append(dict(
                        ld={nm: p.sb("ld_" + nm, [128, TB], F32) for nm in ("r", "k", "v", "g")},
                        tm={nm: p.sb("tm_" + nm, [128, TB], F32) for nm in
                            ("lw", "cum", "e1", "e2", "e3", "a", "kk", "kf", "t1", "t2", "t3", "bv", "y")},
                        BKT=p.sb("BKT", [128, NCHB, 2, 64], BF16), KRT=p.sb("KRT", [128, NCHB, 2, 64], BF16),
                        KKVT=p.sb("KKVT", [128, NCHB, 2, 64], BF16)))
                import os as _os2
                items = [(hp_, tb_) for hp_ in range(int(_os2.environ.get('RW_HP', HP))) for tb_ in range(S // TB)]
                gens_ = [item(hp_, tb_, SETS[ix % 2]) for ix, (hp_, tb_) in enumerate(items)]
                phase_ = ["P"] * len(items)
                lo = 0
                while lo < len(items):
                    hi = min(lo + 3, len(items))
                    for ix in range(lo, hi):
                        ph = phase_[ix]
                        if ph == "D":
                            continue
                        if ph == "P" and ((ix >= 2 and phase_[ix - 2] != "D") or (ix >= 1 and phase_[ix - 1] == "P")):
                            continue
                        if ph == "G" and ix >= 1 and phase_[ix - 1] in ("P", "G"):
                            continue
                        try:
                            tag = next(gens_[ix])
                            if tag == "P_DONE":
                                phase_[ix] = "G"
                            elif tag == "G_DONE":
                                phase_[ix] = "Q"
                        except StopIteration:
                            phase_[ix] = "D"
                    while lo < len(items) and phase_[lo] == "D":
                        lo += 1
            if cfg.stop <= 9:
                return False
            p.mark('rw_outproj_start')
            out_proj(yg, rw_out.v()[j], HP, lambda ft: modT[:, l, 2 * DC + ft:2 * DC + ft + 1], src, dst)
            p.mark('rw_outproj_end')
            return True

    bufs = xres
    bi = [0]

    def nextbuf():
        b_ = bufs[bi[0] % len(bufs)]
        bi[0] += 1
        return b_

    cur = xT.v()
    counters = {0: 0, 1: 0, 2: 0}
    for l, kind in enumerate(cfg.kinds):
        j = counters[kind]
        counters[kind] += 1
        if kind in (0, 1):
            dst = nextbuf()
            ok = (rwkv_layer if kind == 0 else gla_layer)(l, j, cur, dst.v())
            if ok:
                cur = dst.v()
        else:
            r_ = ssd_layer(l, j, cur, nextbuf)
            if r_ is not False:
                cur = r_
    with p.scope():
        fg = p.sb("fg", [128, DC], F32)
        p.dma("sp", fg.v(), final_gT.v())
        norm_phase(cur, None, lambda dc: fg[:, dc:dc + 1], None, out_dram=outT.v())
    p.emit()
    return nc, p


def _pp(vec, nchunk):
    v = np.asarray(vec, np.float32)
    lead = v.shape[:-1]
    v = v.reshape(lead + (nchunk, 128))
    return np.ascontiguousarray(np.moveaxis(v, -1, 0))


def prepare_inputs(cfg, inp, n_cores, batch_of_core):
    D, S, DC, L = cfg.D, cfg.S, cfg.DC, cfg.L
    consts, rmask, rmask128 = make_consts(cfg)
    shared = {
        "ada_w": np.ascontiguousarray(inp["ada_w"], dtype=np.float32),
        "ada_bT": _pp(inp["ada_b"], 3 * DC),
        "norm_gT": _pp(inp["norm_g"], DC),
        "final_gT": _pp(inp["final_g"], DC),
        "consts": consts, "rmask": rmask, "rmask128": rmask128,
    }
    if cfg.nR:
        HP = D // 128
        for k in ("rwkv_w_in", "rwkv_w_out", "rwkv_dec_w1", "rwkv_dec_w2", "rwkv_iclr_w1", "rwkv_iclr_w2"):
            shared[k] = np.ascontiguousarray(inp[k], dtype=np.float32)
        shared["rwkv_muT"] = _pp(inp["rwkv_mu"], DC)
        vecs = np.stack([inp["rwkv_dec_w0"], inp["rwkv_iclr_w0"], inp["rwkv_k_k"], inp["rwkv_k_a"],
                         np.asarray(inp["rwkv_r_k"]).reshape(cfg.nR, -1), inp["rwkv_gn_w"], inp["rwkv_gn_b"]], axis=1)
        shared["rwkv_vecT"] = _pp(vecs, HP)
    if cfg.nG:
        for k in ("gla_w_in", "gla_w_out", "gla_gate_w2"):
            shared[k] = np.ascontiguousarray(inp[k], dtype=np.float32)
        shared["gla_nbT"] = _pp(inp["gla_gate_b"], (D // 2) // 128)
        hg = np.asarray(inp["gla_head_g"], np.float32)
        shared["gla_hgb"] = np.ascontiguousarray(np.broadcast_to(hg[None], (128,) + hg.shape))
    if cfg.nS:
        SW = 2 * D
        SH = SW // 64
        for k in ("ssd_w_in", "ssd_w_out"):
            shared[k] = np.ascontiguousarray(inp[k], dtype=np.float32)
        cwk = np.asarray(inp["ssd_conv_w"], np.float32)
        shared["ssd_cwT"] = _pp(np.moveaxis(cwk, 1, 2).reshape(cfg.nS, -1).reshape(cfg.nS, cwk.shape[2], 4).transpose(0, 2, 1), cwk.shape[2] // 128).transpose(0, 1, 3, 2).copy()
        shared["ssd_cbT"] = _pp(inp["ssd_conv_b"], cwk.shape[2] // 128)
        hv = np.zeros((64, cfg.nS, 2), np.float32)
        hv[:SH, :, 0] = np.asarray(inp["ssd_dt_bias"], np.float32).T
        hv[:SH, :, 1] = np.asarray(inp["ssd_a_log"], np.float32).T
        shared["ssd_hv"] = hv
        dsk = np.asarray(inp["ssd_d"], np.float32)
        shared["ssd_dsb"] = np.ascontiguousarray(np.broadcast_to(dsk[None], (128,) + dsk.shape))
        ng = np.asarray(inp["ssd_norm_g"], np.float32)
        shared["ssd_ngb"] = np.ascontiguousarray(np.broadcast_to(ng[None], (128,) + ng.shape))
    maps = []
    for core in range(n_cores):
        b = batch_of_core[core]
        m = dict(shared)
        m["xT"] = np.ascontiguousarray(np.asarray(inp["x"][b], np.float32).T)
        m["cT"] = _pp(inp["c"][b], DC)
        maps.append(m)
    return maps


_CACHE = {}


def kernel(**inputs):
    cfg = Cfg()
    B = inputs["x"].shape[0]
    n_cores = 8
    batch_of_core = [c % B for c in range(n_cores)]
    if "nc" not in _CACHE:
        _CACHE["nc"] = build(cfg)[0]
    nc = _CACHE["nc"]
    maps = prepare_inputs(cfg, inputs, n_cores, batch_of_core)
    res = run_bass_kernel_spmd(nc, maps, core_ids=list(range(n_cores)))
    out = np.empty((B, cfg.S, cfg.D), np.float32)
    for b in range(B):
        out[b] = res.results[b]["outT"].T
    return out
```

```python
from contextlib import ExitStack
import math
import numpy as np
import concourse.bass as bass
import concourse.mybir as mybir
from concourse.bass_utils import run_bass_kernel_spmd

F32 = mybir.dt.float32
BF16 = mybir.dt.bfloat16
AF = mybir.ActivationFunctionType
ALU = mybir.AluOpType
AX = mybir.AxisListType


class V:
    __slots__ = ("ap", "tl")

    def __init__(self, ap, tl):
        self.ap = ap
        self.tl = tl

    def __getitem__(self, idx):
        return V(self.ap[idx], self.tl)

    def re(self, pat, **kw):
        return V(self.ap.rearrange(pat, **kw), self.tl)

    def bc(self, axes, shape):
        a = self.ap
        for ax in axes:
            a = a.unsqueeze(ax)
        return V(a.broadcast_to(list(shape)), self.tl)


class Tl:
    __slots__ = ("t", "lw", "rd", "name", "excl")

    def __init__(self, t, name="", excl=False):
        self.t = t
        self.lw = []
        self.rd = []
        self.name = name
        self.excl = excl

    def __getitem__(self, idx):
        return V(self.t[idx], self)

    def v(self):
        return V(self.t[:], self)


ENGS = ("pe", "act", "dve", "pool", "sp")
DMA_ENGS = ("sp", "pool", "act")
NDMA_SLOTS = 12
WRITE_KW = ("out", "accum_out", "ap")


def _compress(toks):
    best = {}
    for s, v, src in toks:
        k = id(s)
        if k not in best or best[k][1] < v:
            best[k] = (s, v, src)
    return list(best.values())


class Prog:
    def __init__(self, nc):
        self.nc = nc
        self.stacks = [ExitStack()]
        self.q = {e: [] for e in ENGS}
        self.cnt = {e: 0 for e in ENGS}
        self.sem = {e: self.stacks[0].enter_context(nc.semaphore("s_" + e)) for e in ENGS}
        self.seen = {e: {} for e in ENGS}
        self.dsem, self.dval, self.dnext = {}, {}, {}
        for e in DMA_ENGS:
            self.dsem[e] = [self.stacks[0].enter_context(nc.semaphore("d_%s%d" % (e, i))) for i in range(NDMA_SLOTS)]
            self.dval[e] = [0] * NDMA_SLOTS
            self.dnext[e] = 0
        self.n_inst = 0
        self.uid = 0
        self.marks = []

    def mark(self, label):
        self.marks.append((label, dict(self.cnt)))

    def _nm(self, name):
        self.uid += 1
        return "%s_%d" % (name, self.uid)

    def sb(self, name, shape, dt=F32):
        t = self.stacks[-1].enter_context(self.nc.sbuf_tensor(self._nm(name), list(shape), dt))
        return Tl(t, name)

    def ps(self, name, shape, dt=F32):
        nbytes = int(np.prod(shape[1:])) * (4 if dt == F32 else 2)
        assert nbytes % 2048 == 0, "PSUM tiles must cover whole banks"
        t = self.stacks[-1].enter_context(self.nc.psum_tensor(self._nm(name), list(shape), dt))
        return Tl(t, name, excl=True)

    def dram(self, name, shape, dt=F32, kind="Internal"):
        t = self.nc.dram_tensor(name, list(shape), dt, kind=kind)
        return Tl(t.ap(), name)

    class _Scope:
        def __init__(self, p):
            self.p = p

        def __enter__(self):
            self.p.stacks.append(ExitStack())

        def __exit__(self, *a):
            self.p.barrier()
            self.p.stacks.pop().close()
            return False

    def scope(self):
        return Prog._Scope(self)

    def _deps(self, eng, reads, writes, acc_w=False):
        waits = {}

        def need(tok):
            sem, val, src = tok
            if src == "pe" and eng == "pe":
                return
            k = id(sem)
            if self.seen[eng].get(k, 0) >= val:
                return
            if k not in waits or waits[k][1] < val:
                waits[k] = (sem, val)

        for tl in reads:
            for tok in tl.lw:
                need(tok)
        for tl in writes:
            if not acc_w:
                for tok in tl.lw:
                    need(tok)
            for tok in tl.rd:
                need(tok)
        for k, (sem, val) in waits.items():
            self.seen[eng][k] = val
        return list(waits.values())

    def _commit(self, tok, reads, writes, acc_w=False):
        for tl in writes:
            if acc_w:
                tl.lw.append(tok)
                if len(tl.lw) > 48:
                    tl.lw = _compress(tl.lw)
            else:
                tl.lw = [tok]
            tl.rd = []
        for tl in reads:
            if tl not in writes:
                tl.rd.append(tok)
                if len(tl.rd) > 48:
                    tl.rd = _compress(tl.rd)

    def I(self, eng, fn, *, acc_w=False, **kw):
        reads, writes, args = [], [], {}
        for k, a in kw.items():
            if isinstance(a, V):
                args[k] = a.ap
                (writes if (k in WRITE_KW or a.tl.excl) else reads).append(a.tl)
            else:
                args[k] = a
        waits = self._deps(eng, reads, writes, acc_w)
        self.cnt[eng] += 1
        tok = (self.sem[eng], self.cnt[eng], eng)
        self._commit(tok, reads, writes, acc_w)
        self.q[eng].append((waits, fn, args, (self.sem[eng], 1)))
        self.n_inst += 1

    def dma(self, eng, out, in_, acc_w=False, **kw):
        reads, writes = [in_.tl], [out.tl]
        waits = self._deps(eng, reads, writes, acc_w)
        s = self.dnext[eng]
        self.dnext[eng] = (s + 1) % NDMA_SLOTS
        sem = self.dsem[eng][s]
        prev = self.dval[eng][s]
        if prev > 0 and self.seen[eng].get(id(sem), 0) < prev:
            waits.append((sem, prev))
            self.seen[eng][id(sem)] = prev
        self.dval[eng][s] = prev + 16
        tok = (sem, prev + 16, "dma")
        self._commit(tok, reads, writes, acc_w)
        args = dict(out=out.ap, in_=in_.ap)
        args.update(kw)
        self.q[eng].append((waits, "dma_start", args, (sem, 16)))
        self.n_inst += 1

    def barrier(self):
        for e in ENGS:
            waits = []
            for e2 in ENGS:
                if self.cnt[e2] > 0 and self.seen[e].get(id(self.sem[e2]), 0) < self.cnt[e2] and e2 != e:
                    waits.append((self.sem[e2], self.cnt[e2]))
                    self.seen[e][id(self.sem[e2])] = self.cnt[e2]
            for de in DMA_ENGS:
                for s in range(NDMA_SLOTS):
                    v = self.dval[de][s]
                    sem = self.dsem[de][s]
                    if v > 0 and self.seen[e].get(id(sem), 0) < v:
                        waits.append((sem, v))
                        self.seen[e][id(sem)] = v
            if waits:
                self.q[e].append((waits, None, None, None))

    def emit(self):
        nc = self.nc
        self.barrier()
        with nc.Block() as block:
            def run(engname):
                def f(e):
                    for waits, fn, args, inc in self.q[engname]:
                        for sem, val in waits:
                            e.wait_ge(sem, val)
                        if fn is not None:
                            getattr(e, fn)(**args).then_inc(inc[0], inc[1])
                return f
            block.tensor(run("pe"))
            block.scalar(run("act"))
            block.vector(run("dve"))
            block.gpsimd(run("pool"))
            block.sync(run("sp"))
        while self.stacks:
            self.stacks.pop().close()


class Cfg:
    def __init__(self, D=2048, S=2048, kinds=(0, 1, 2, 0), lora=96,
                 gla_heads=4, gla_rank=16, ssm_groups=8):
        self.D, self.S, self.kinds, self.lora = D, S, tuple(kinds), lora
        self.gla_heads, self.gla_rank, self.ssm_groups = gla_heads, gla_rank, ssm_groups
        self.DC = D // 128
        self.TT = min(512, S)
        self.NT = S // self.TT
        self.TA = min(256, S)
        self.L = len(kinds)
        self.nR = sum(1 for k in kinds if k == 0)
        self.nG = sum(1 for k in kinds if k == 1)
        self.nS = sum(1 for k in kinds if k == 2)
        self.TB = min(512, S)
        self.CB = 4
        self.stop = 99


NEG_EXP_HALF = -math.exp(-0.5)
NORM_EPS = 1e-6
RWKV_GN_EPS = 64e-5


def make_consts(cfg):
    c = np.zeros((128, 8, 128), np.float32)
    c[:, 0, :] = np.eye(128)
    c[:, 1, :] = 1.0
    c[0:64, 2, 0:64] = 1.0
    c[64:128, 2, 64:128] = 1.0
    su = np.triu(np.ones((64, 64), np.float32), 1)
    iu = np.triu(np.ones((64, 64), np.float32), 0)
    c[0:64, 3, 0:64] = su
    c[64:128, 3, 0:64] = su
    c[0:64, 3, 64:128] = iu
    c[64:128, 3, 64:128] = iu
    c[0:64, 4, 0:64] = -su
    c[0:64, 4, 64:128] = -su.T
    c[0:64, 5, 0:64] = np.eye(64)
    c[64:128, 5, 0:64] = np.eye(64)
    c[:, 6, :] = np.triu(np.ones((128, 128), np.float32), 0)
    c[:, 7, :] = np.where(np.triu(np.ones((128, 128)), 0) > 0, 0.0, -30000.0)
    rmask = np.ones((128, cfg.S), np.float32)
    rmask[:, 0::64] = 0.0
    rmask128 = np.ones((128, cfg.S), np.float32)
    rmask128[:, 0::128] = 0.0
    return c.reshape(128, 8 * 128), rmask, rmask128


def build(cfg):
    nc = bass.Bass("TRN2", target_bir_lowering=False)
    p = Prog(nc)
    D, S, DC, TT, NT, L = cfg.D, cfg.S, cfg.DC, cfg.TT, cfg.NT, cfg.L
    EI = "ExternalInput"
    xT = p.dram("xT", [D, S], F32, EI)
    cT = p.dram("cT", [128, DC], F32, EI)
    ada_w = p.dram("ada_w", [L, D, 3 * D], F32, EI)
    ada_bT = p.dram("ada_bT", [128, L, 3 * DC], F32, EI)
    norm_gT = p.dram("norm_gT", [128, L, DC], F32, EI)
    final_gT = p.dram("final_gT", [128, DC], F32, EI)
    consts_d = p.dram("consts", [128, 8 * 128], F32, EI)
    rmask_d = p.dram("rmask", [128, S], F32, EI)
    rmask128_d = p.dram("rmask128", [128, S], F32, EI)
    outT = p.dram("outT", [D, S], F32, "ExternalOutput")
    xres = [p.dram("xres%d" % i, [D, S], F32) for i in range(3)]
    W = D
    HP = W // 128
    R = cfg.lora
    if cfg.nR:
        nR = cfg.nR
        rw_in = p.dram("rwkv_w_in", [nR, D, 4 * W], F32, EI)
        rw_out = p.dram("rwkv_w_out", [nR, W, D], F32, EI)
        rw_dw1 = p.dram("rwkv_dec_w1", [nR, D, R], F32, EI)
        rw_dw2 = p.dram("rwkv_dec_w2", [nR, R, W], F32, EI)
        rw_aw1 = p.dram("rwkv_iclr_w1", [nR, D, R], F32, EI)
        rw_aw2 = p.dram("rwkv_iclr_w2", [nR, R, W], F32, EI)
        rw_muT = p.dram("rwkv_muT", [128, nR, 6, DC], F32, EI)
        rw_vecT = p.dram("rwkv_vecT", [128, nR, 7, HP], F32, EI)
        projT = [[Tl(t.t[f * 128:(f + 1) * 128, :], "projT") for f in range(HP)]
                 for t in [p.dram("projT%d" % c, [W, S], F32) for c in range(4)]]

    GH = cfg.gla_heads
    KW, VW = D // 2, D
    DK, DV = KW // GH, VW // GH
    KC, VC = max(DK // 128, 1), DV // 128
    GR = cfg.gla_rank
    if cfg.nG:
        nG = cfg.nG
        assert DK % 128 == 0 and DV % 128 == 0 and DV <= 512
        gl_in = p.dram("gla_w_in", [nG, D, 2 * KW + 2 * VW + GR], F32, EI)
        gl_out = p.dram("gla_w_out", [nG, VW, D], F32, EI)
        gl_w2 = p.dram("gla_gate_w2", [nG, GR, KW], F32, EI)
        gl_nbT = p.dram("gla_nbT", [128, nG, KW // 128], F32, EI)
        gl_hgb = p.dram("gla_hgb", [128, nG, DV], F32, EI)
        gqk = [[Tl(t.t[f * 128:(f + 1) * 128, :], "gqk") for f in range(KW // 128)]
               for t in [p.dram("gqk%d" % c, [KW, S], F32) for c in range(2)]]
        gvg_t = [p.dram("gvg%d" % c, [S, VW], F32) for c in range(2)]
        gvg = [[Tl(t.t[:, hh * DV:(hh + 1) * DV], "gvg") for hh in range(GH)] for t in gvg_t]

    SW = 2 * D
    SH = SW // 64
    SG = SW // 512
    SN = 128
    CW = SW + 2 * SG * SN
    SIN = SW + CW + SH
    if cfg.nS:
        nS = cfg.nS
        sd_in = p.dram("ssd_w_in", [nS, D, SIN], F32, EI)
        sd_out = p.dram("ssd_w_out", [nS, SW, D], F32, EI)
        sd_cwT = p.dram("ssd_cwT", [128, nS, CW // 128, 4], F32, EI)
        sd_cbT = p.dram("ssd_cbT", [128, nS, CW // 128], F32, EI)
        sd_hv = p.dram("ssd_hv", [64, nS, 2], F32, EI)
        sd_dsb = p.dram("ssd_dsb", [128, nS, SH], F32, EI)
        sd_ngb = p.dram("ssd_ngb", [128, nS, SW], F32, EI)
        sxbc_t = p.dram("sxbc", [CW, S], F32)
        sxbc = [Tl(sxbc_t.t[f * 128:(f + 1) * 128, :], "sxbc") for f in range(CW // 128)]
        sz_t = p.dram("sz", [S, SW], F32)
        sz = [Tl(sz_t.t[:, g * 512:(g + 1) * 512], "sz") for g in range(SG)]

    cst = p.sb("cst", [128, 8, 128], F32)
    p.dma("sp", cst.v().re("p a b -> p (a b)"), consts_d.v())
    ident_bf = p.sb("ident_bf", [128, 128], BF16)
    p.I("dve", "tensor_copy", out=ident_bf.v(), in_=cst[:, 0, :])
    ones32 = cst[:, 1, :]
    bones32 = cst[:, 2, :]
    maskA = cst[:, 3, :]
    negSU = cst[0:64, 4, 0:64]
    negSL = cst[0:64, 4, 64:128]
    ident2 = cst[:, 5, 0:64]
    rmask = p.sb("rmask", [128, S], BF16)
    p.dma("pool", rmask.v(), rmask_d.v())
    rmask128 = p.sb("rmask128", [128, S], BF16)
    p.dma("pool", rmask128.v(), rmask128_d.v())
    iu128 = cst[:, 6, :]

    modT = p.sb("modT", [128, L, 3 * DC], F32)
    gsT = p.sb("gsT", [128, L, DC], F32)
    with p.scope():
        cact = p.sb("cact", [128, DC], F32)
        abT = p.sb("abT", [128, L, 3 * DC], F32)
        ngT = p.sb("ngT", [128, L, DC], F32)
        p.dma("sp", cact.v(), cT.v())
        p.dma("sp", abT.v(), ada_bT.v())
        p.dma("sp", ngT.v(), norm_gT.v())
        p.I("act", "activation", out=cact.v(), in_=cact.v(), func=AF.Silu)
        EG = 4 if (3 * DC) % 4 == 0 else 2
        cact_bf = p.sb("cact_bf", [128, DC], BF16)
        p.I("dve", "tensor_copy", out=cact_bf.v(), in_=cact.v())
        wst = [p.sb("adaw", [128, DC, EG * 128], BF16) for _ in range(3)]
        psm = p.ps("psmod", [128, 512], F32)
        gi = 0
        for l in range(L):
            wv = ada_w.v()[l].re("(dc p) e -> p dc e", p=128)
            for eg in range(3 * DC // EG):
                wt = wst[gi % 3]
                p.dma("pool", wt.v(), wv[:, :, eg * EG * 128:(eg + 1) * EG * 128])
                gi += 1
                for j in range(EG):
                    col = l * 3 * DC + eg * EG + j
                    for dc in range(DC):
                        p.I("pe", "matmul", out=psm[:, col:col + 1], lhsT=wt[:, dc, j * 128:(j + 1) * 128],
                            rhs=cact_bf[:, dc:dc + 1], start=(dc == 0), stop=(dc == DC - 1))
        p.I("dve", "tensor_tensor", out=modT.v().re("p l e -> p (l e)"), in0=psm[:, 0:L * 3 * DC],
            in1=abT.v().re("p l e -> p (l e)"), op=ALU.add)
        p.I("dve", "scalar_tensor_tensor", out=gsT.v(), in0=modT[:, :, DC:2 * DC], scalar=1.0, in1=ngT.v(),
            op0=ALU.add, op1=ALU.mult)

    def norm_phase(src, dst_tiles, g_of_dc, sh_of_dc, out_dram=None):
        TA = cfg.TA
        with p.scope():
            xt = [p.sb("xt", [128, DC, TA], F32) for _ in range(2)]
            sq = [p.sb("sq", [128, TA], F32) for _ in range(2)]
            rstd = [p.sb("rstd", [128, TA], F32) for _ in range(2)]
            tmp = [p.sb("ntmp", [128, TA], F32) for _ in range(4)]
            pss = [p.ps("psn", [128, 512], F32) for _ in range(2)]
            k = 0
            for ta in range(S // TA):
                x_ = xt[ta % 2]
                ts = slice(ta * TA, (ta + 1) * TA)
                p.dma("sp", x_.v(), src.re("(dc p) s -> p dc s", p=128)[:, :, ts])
                ps_ = pss[ta % 2]
                for dc in range(DC):
                    s_ = sq[dc % 2]
                    if dc % 2 == 0:
                        p.I("act", "activation", out=s_.v(), in_=x_[:, dc, :], func=AF.Square)
                    else:
                        p.I("dve", "tensor_tensor", out=s_.v(), in0=x_[:, dc, :], in1=x_[:, dc, :], op=ALU.mult)
                    p.I("pe", "matmul", out=ps_[:, 0:TA], lhsT=ones32, rhs=s_.v(), start=(dc == 0), stop=(dc == DC - 1))
                r_ = rstd[ta % 2]
                p.I("act", "activation", out=r_.v(), in_=ps_[:, 0:TA], func=AF.Sqrt, bias=NORM_EPS, scale=1.0 / D)
                p.I("dve", "reciprocal", out=r_.v(), in_=r_.v())
                for dc in range(DC):
                    t_ = tmp[k % 4]
                    k += 1
                    p.I("dve", "scalar_tensor_tensor", out=t_.v(), in0=x_[:, dc, :],
                        scalar=g_of_dc(dc), in1=r_.v(), op0=ALU.mult, op1=ALU.mult)
                    if out_dram is None:
                        p.I("act", "activation", out=dst_tiles[dc][:, ts], in_=t_.v(), func=AF.Identity,
                            bias=sh_of_dc(dc), scale=1.0)
                    else:
                        p.dma("sp", out_dram[dc * 128:(dc + 1) * 128, ts], t_.v(), acc_w=True)

    wring = {}

    def out_proj(yg_tiles, w_dram, nci, gate_of_ft, src, dst):
        with p.scope():
            wts = [p.sb("wo", [128, nci, 512], BF16) for _ in range(2)]
            pso = [p.ps("pso", [128, 512], F32) for _ in range(4)]
            xin = [p.sb("xin", [128, TT], F32) for _ in range(4)]
            wv = w_dram.re("(ci p) f -> p ci f", p=128)
            k = 0
            G = 4 if (D // 128) % 4 == 0 else 2
            for fg in range(D // (128 * G)):
                wt = wts[fg % 2]
                p.dma("pool", wt[:, :, 0:G * 128], wv[:, :, fg * G * 128:(fg + 1) * G * 128])
                for j in range(G):
                    ft = fg * G + j
                    for tt in range(NT):
                        ts = slice(tt * TT, (tt + 1) * TT)
                        ps_ = pso[k % 4]
                        x_ = xin[k % 4]
                        k += 1
                        p.dma("sp", x_.v(), src[ft * 128:(ft + 1) * 128, ts])
                        for ci in range(nci):
                            p.I("pe", "matmul", out=ps_[:, 0:TT], lhsT=wt[:, ci, j * 128:(j + 1) * 128],
                                rhs=yg_tiles[ci][:, ts], start=(ci == 0), stop=(ci == nci - 1))
                        p.I("dve", "scalar_tensor_tensor", out=x_.v(), in0=ps_[:, 0:TT], scalar=gate_of_ft(ft),
                            in1=x_.v(), op0=ALU.mult, op1=ALU.add)
                        p.dma("sp", dst[ft * 128:(ft + 1) * 128, ts], x_.v(), acc_w=True)

    def proj_fm(xs_tiles, wv, f0, nft, sink, wts, pss, kdim=DC):
        k = 0
        G = 4 if nft % 4 == 0 else (2 if nft % 2 == 0 else 1)
        gi = 0
        for fg in range(nft // G):
            wt = wts[gi % 2]
            gi += 1
            p.dma("pool", wt[:, :, 0:G * 128], wv[:, :, f0 + fg * G * 128:f0 + (fg + 1) * G * 128])
            for j in range(G):
                ft = fg * G + j
                for tt in range(NT):
                    ts = slice(tt * TT, (tt + 1) * TT)
                    ps_ = pss[k % len(pss)]
                    k += 1
                    for dc in range(kdim):
                        p.I("pe", "matmul", out=ps_[:, 0:TT], lhsT=wt[:, dc, j * 128:(j + 1) * 128],
                            rhs=xs_tiles[dc][:, ts], start=(dc == 0), stop=(dc == kdim - 1))
                    sink(ft, tt, ps_)


    def proj_tm(hT, wv, f0, ngroups, gw, sink, wts, pss):
        k = 0
        for gi in range(ngroups):
            wt = wts[gi % 2]
            p.dma("pool", wt[:, :, 0:gw], wv[:, :, f0 + gi * gw:f0 + (gi + 1) * gw])
            for tk in range(S // 128):
                ps_ = pss[k % len(pss)]
                k += 1
                for dc in range(DC):
                    p.I("pe", "matmul", out=ps_[:, 0:gw], lhsT=hT[dc][:, tk * 128:(tk + 1) * 128], rhs=wt[:, dc, 0:gw],
                        start=(dc == 0), stop=(dc == DC - 1))
                sink(gi, tk, ps_)

    def gla_layer(l, j, src, dst):
        NCH = S // 128
        with p.scope():
            yg = [p.sb("ygg", [128, S], BF16) for _ in range(VW // 128)]
            lowT = p.sb("lowT", [GR, S], BF16)
            gw2 = p.sb("gw2", [GR, KW], BF16)
            nb = p.sb("gnb", [128, KW // 128], F32)
            hgb = p.sb("hgb", [128, DV], F32)
            p.dma("pool", gw2.v(), gl_w2.v()[j])
            p.dma("sp", nb.v(), gl_nbT.v()[:, j])
            p.dma("sp", hgb.v(), gl_hgb.v()[:, j])
            p.I("dve", "tensor_scalar", out=nb.v(), in0=nb.v(), scalar1=-1.0, scalar2=None, op0=ALU.mult)
            wv = gl_in.v()[j].re("(dc p) f -> p dc f", p=128)
            with p.scope():
                hT = [p.sb("hT", [128, S], BF16) for _ in range(DC)]
                norm_phase(src, hT, lambda dc: gsT[:, l, dc:dc + 1], lambda dc: modT[:, l, dc:dc + 1])
                wts = [p.sb("wi", [128, DC, 512], BF16) for _ in range(2)]
                wl = p.sb("wl", [128, DC, GR], BF16)
                pss = [p.ps("psp", [128, 512], F32) for _ in range(4)]
                stg = [p.sb("stg", [128, 512], F32) for _ in range(4)]
                sk = [0]

                def evac(ps_ap, dst_ap, width):
                    s_ = stg[sk[0] % 4]
                    e = "act" if sk[0] % 2 == 0 else "dve"
                    sk[0] += 1
                    if e == "act":
                        p.I("act", "copy", out=s_[:, 0:width], in_=ps_ap)
                    else:
                        p.I("dve", "tensor_copy", out=s_[:, 0:width], in_=ps_ap)
                    p.dma("sp", dst_ap, s_[:, 0:width], acc_w=True)

                for c in range(2):
                    proj_fm(hT, wv, c * KW, KW // 128,
                            lambda ft, tt, ps_, c=c: evac(ps_[:, 0:TT], gqk[c][ft][:, tt * TT:(tt + 1) * TT], TT), wts, pss)
                for c in range(2):
                    proj_tm(hT, wv, 2 * KW + c * VW, GH, DV,
                            lambda gi, tk, ps_, c=c: evac(ps_[:, 0:DV], gvg[c][gi][tk * 128:(tk + 1) * 128, :], DV), wts, pss)
                p.dma("pool", wl.v(), wv[:, :, 2 * KW + 2 * VW:2 * KW + 2 * VW + GR])
                for tt in range(NT):
                    ts = slice(tt * TT, (tt + 1) * TT)
                    ps_ = pss[tt % 4]
                    for dc in range(DC):
                        p.I("pe", "matmul", out=ps_[0:GR, 0:TT], lhsT=wl[:, dc, :], rhs=hT[dc][:, ts],
                            start=(dc == 0), stop=(dc == DC - 1))
                    p.I("act", "copy", out=lowT[:, ts], in_=ps_[0:GR, 0:TT])
            if cfg.stop <= 2:
                return False
            with p.scope():
                ldq = [p.sb("ldq", [128, S], F32) for _ in range(KC)]
                ldk = [p.sb("ldk", [128, S], F32) for _ in range(KC)]
                QT = [p.sb("QT", [128, S], BF16) for _ in range(KC)]
                KT = [p.sb("KT", [128, S], BF16) for _ in range(KC)]
                eb = [p.sb("eb", [128, S], F32) for _ in range(KC)]
                t1 = p.sb("gt1", [128, S], F32)
                t2 = p.sb("gt2", [128, S], F32)
                S32 = [p.sb("S32", [128, DV], F32) for _ in range(KC)]
                Sb = [p.sb("Sb", [128, DV], BF16) for _ in range(KC)]
                v32 = [p.sb("v32", [128, DV], F32) for _ in range(2)]
                g32 = [p.sb("g32", [128, DV], F32) for _ in range(2)]
                Vb = [p.sb("Vb", [128, DV], BF16) for _ in range(2)]
                SG = [p.sb("SG", [128, DV], F32) for _ in range(2)]
                KTM = [p.sb("KTM", [128, KC * 128], BF16) for _ in range(2)]
                ST = [p.sb("ST", [128, 128], BF16) for _ in range(2)]
                junk = p.sb("junk", [128, DV], F32)
                ssq = [p.sb("ssq", [128, 1], F32) for _ in range(2)]
                y32 = [p.sb("y32", [128, DV], F32) for _ in range(2)]
                yb = [p.sb("yb", [128, DV], BF16) for _ in range(2)]
                psP = p.ps("gpsP", [128, 512], F32)
                psTr = p.ps("gpsTr", [128, 8, 128], BF16)
                psS = p.ps("gpsS", [128, 512], F32)
                psO = [p.ps("gpsO", [128, 512], F32) for _ in range(2)]
                psSt = [p.ps("gpsSt", [128, 512], F32) for _ in range(2)]
                psTr2 = p.ps("gpsTr2", [128, 8, 128], BF16)
                for hh in range(GH):
                    for kc in range(KC):
                        ft = hh * KC + kc
                        p.dma("sp", ldq[kc].v(), gqk[0][ft].v())
                        p.dma("sp", ldk[kc].v(), gqk[1][ft].v())
                        for tt in range(NT):
                            ts = slice(tt * TT, (tt + 1) * TT)
                            p.I("pe", "matmul", out=psP[:, 0:TT], lhsT=gw2[:, ft * 128:(ft + 1) * 128], rhs=lowT[:, ts], start=True, stop=True)
                            p.I("act", "activation", out=t1[:, ts], in_=psP[:, 0:TT], func=AF.Exp, bias=nb[:, ft:ft + 1], scale=-1.0)
                        p.I("act", "activation", out=t1.v(), in_=t1.v(), func=AF.Ln, bias=1.0, scale=1.0)
                        p.I("dve", "tensor_scalar", out=t1.v(), in0=t1.v(), scalar1=-1.0 / 16.0, scalar2=None, op0=ALU.mult)
                        p.I("dve", "tensor_tensor_scan", out=t2.v(), data0=rmask128.v(), data1=t1.v(), initial=0.0,
                            op0=ALU.mult, op1=ALU.add)
                        p.I("act", "activation", out=eb[kc].v(), in_=t2.v(), func=AF.Exp)
                        p.I("act", "activation", out=t1.v(), in_=t2.v(), func=AF.Exp, scale=-1.0)
                        p.I("dve", "scalar_tensor_tensor", out=QT[kc].v(), in0=ldq[kc].v(), scalar=float(DK) ** -0.5, in1=eb[kc].v(),
                            op0=ALU.mult, op1=ALU.mult)
                        p.I("dve", "tensor_tensor", out=KT[kc].v(), in0=ldk[kc].v(), in1=t1.v(), op=ALU.mult)
                        p.I("dve", "memset", ap=S32[kc].v(), constant=0.0)
                        p.I("dve", "memset", ap=Sb[kc].v(), constant=0.0)
                    def gF(n):
                            ns = slice(n * 128, (n + 1) * 128)
                            b2 = n % 2
                            p.dma("sp", v32[b2].v(), gvg[0][hh][ns, :])
                            p.dma("sp", g32[b2].v(), gvg[1][hh][ns, :])
                            p.I("act", "copy", out=Vb[b2].v(), in_=v32[b2].v())
                            p.I("act", "activation", out=SG[b2].v(), in_=g32[b2].v(), func=AF.Silu)
                            for kc in range(KC):
                                p.I("pe", "transpose", out=psTr[:, kc, :], in_=KT[kc][:, ns], identity=ident_bf.v())
                            p.I("dve", "tensor_copy", out=KTM[b2].v().re("p (k x) -> p k x", x=128), in_=psTr[:, 0:KC, :])
                            for kc in range(KC):
                                p.I("pe", "matmul", out=psS[:, 0:128], lhsT=KT[kc][:, ns], rhs=QT[kc][:, ns], start=(kc == 0), stop=(kc == KC - 1))
                            p.I("dve", "tensor_tensor", out=ST[b2].v(), in0=psS[:, 0:128], in1=iu128, op=ALU.mult)

                    def gB(n):
                            ns = slice(n * 128, (n + 1) * 128)
                            b2 = n % 2
                            po = psO[b2]
                            p.I("pe", "matmul", out=po[:, 0:DV], lhsT=ST[b2].v(), rhs=Vb[b2].v(), start=True, stop=False)
                            for kc in range(KC):
                                p.I("pe", "matmul", out=po[:, 0:DV], lhsT=QT[kc][:, ns], rhs=Sb[kc].v(), start=False, stop=(kc == KC - 1))
                            for kc in range(KC):
                                pst = psSt[kc % 2]
                                p.I("pe", "matmul", out=pst[:, 0:DV], lhsT=KTM[b2][:, kc * 128:(kc + 1) * 128], rhs=Vb[b2].v(), start=True, stop=True)
                                dcol = eb[kc][:, n * 128 + 127:n * 128 + 128]
                                p.I("dve", "tensor_scalar", out=S32[kc].v(), in0=S32[kc].v(), scalar1=dcol, scalar2=None, op0=ALU.mult)
                                p.I("dve", "scalar_tensor_tensor", out=S32[kc].v(), in0=pst[:, 0:DV], scalar=dcol, in1=S32[kc].v(),
                                    op0=ALU.mult, op1=ALU.add)
                                p.I("act", "copy", out=Sb[kc].v(), in_=S32[kc].v())
                            p.I("act", "activation", out=junk.v(), in_=po[:, 0:DV], func=AF.Square, accum_out=ssq[b2].v())
                            p.I("act", "activation", out=ssq[b2].v(), in_=ssq[b2].v(), func=AF.Sqrt, bias=NORM_EPS, scale=1.0 / DV)
                            p.I("dve", "reciprocal", out=ssq[b2].v(), in_=ssq[b2].v())
                            p.I("dve", "scalar_tensor_tensor", out=y32[b2].v(), in0=po[:, 0:DV], scalar=ssq[b2].v(), in1=hgb.v(),
                                op0=ALU.mult, op1=ALU.mult)
                            p.I("dve", "tensor_tensor", out=yb[b2].v(), in0=y32[b2].v(), in1=SG[b2].v(), op=ALU.mult)
                            for vc in range(VC):
                                p.I("pe", "transpose", out=psTr2[:, vc, :], in_=yb[b2][:, vc * 128:(vc + 1) * 128], identity=ident_bf.v())
                            for vc in range(VC):
                                p.I("act" if vc % 2 == 0 else "dve", "copy" if vc % 2 == 0 else "tensor_copy",
                                    out=yg[hh * VC + vc][:, ns], in_=psTr2[:, vc, :])

                    gF(0)
                    for n in range(NCH):
                        if n + 1 < NCH:
                            gF(n + 1)
                        gB(n)
            if cfg.stop <= 9:
                return False
            out_proj(yg, gl_out.v()[j], VW // 128, lambda ft: modT[:, l, 2 * DC + ft:2 * DC + ft + 1], src, dst)
            return True


    def ssd_layer(l, j, src, nextbuf):
        NCH = S // 128
        srcv = [src]
        HN = SH
        maskb = cst[:, 7, :]
        ident32 = cst[:, 0, :]
        with p.scope():
            dtT = p.sb("dtT", [128, S], F32)
            acT = p.sb("acT", [128, S], F32)
            nacT = p.sb("nacT", [128, S], F32)
            hv = p.sb("hv", [64, 2], F32)
            dsb = p.sb("dsb", [128, SH], F32)
            cw = p.sb("cw", [128, CW // 128, 4], F32)
            cbv = p.sb("cbv", [128, CW // 128], F32)
            wtm = p.sb("wtm", [128, NCH, 128], F32)
            eatm = p.sb("eatm", [128, NCH, 64], F32)
            decbc = p.sb("decbc", [128, NCH, 64], F32)
            p.dma("sp", hv.v(), sd_hv.v()[:, j])
            p.dma("sp", dsb.v(), sd_dsb.v()[:, j])
            p.dma("sp", cw.v(), sd_cwT.v()[:, j])
            p.dma("sp", cbv.v(), sd_cbT.v()[:, j])
            wv = sd_in.v()[j].re("(dc p) f -> p dc f", p=128)
            with p.scope():
                hT = [p.sb("hT", [128, S], BF16) for _ in range(DC)]
                norm_phase(src, hT, lambda dc: gsT[:, l, dc:dc + 1], lambda dc: modT[:, l, dc:dc + 1])
                wts = [p.sb("wi", [128, DC, 512], BF16) for _ in range(2)]
                wdt = p.sb("wdt", [128, DC, SH], BF16)
                pss = [p.ps("psp", [128, 512], F32) for _ in range(4)]
                stg = [p.sb("stg", [128, 512], F32) for _ in range(4)]
                xst = [p.sb("xst", [128, S + 3], F32) for _ in range(2)]
                acc = [p.sb("cacc", [128, S], F32) for _ in range(2)]
                sk = [0]

                def zsink(gi, tk, ps_):
                    s_ = stg[sk[0] % 4]
                    e = "act" if sk[0] % 2 == 0 else "dve"
                    sk[0] += 1
                    if e == "act":
                        p.I("act", "copy", out=s_.v(), in_=ps_.v())
                    else:
                        p.I("dve", "tensor_copy", out=s_.v(), in_=ps_.v())
                    p.dma("sp", sz[gi][tk * 128:(tk + 1) * 128, :], s_.v(), acc_w=True)

                p.mark('sd_projz_start')
                proj_tm(hT, wv, 0, SG, 512, zsink, wts, pss)
                p.mark('sd_projx_start')
                for b in range(2):
                    p.I("dve", "memset", ap=xst[b][:, 0:3], constant=0.0)

                def csink(ft, tt, ps_):
                    x_ = xst[ft % 2]
                    e = "act" if (ft + tt) % 2 == 0 else "dve"
                    if e == "act":
                        p.I("act", "copy", out=x_[:, 3 + tt * TT:3 + (tt + 1) * TT], in_=ps_[:, 0:TT])
                    else:
                        p.I("dve", "tensor_copy", out=x_[:, 3 + tt * TT:3 + (tt + 1) * TT], in_=ps_[:, 0:TT])
                    if tt == NT - 1:
                        a_ = acc[ft % 2]
                        p.I("act", "mul", out=a_.v(), in_=x_[:, 3:S + 3], mul=cw[:, ft, 3:4])
                        for kk_ in range(3):
                            p.I("dve", "scalar_tensor_tensor", out=a_.v(), in0=x_[:, kk_:S + kk_], scalar=cw[:, ft, kk_:kk_ + 1],
                                in1=a_.v(), op0=ALU.mult, op1=ALU.add)
                        p.I("act", "activation", out=a_.v(), in_=a_.v(), func=AF.Silu, bias=cbv[:, ft:ft + 1], scale=1.0)
                        p.dma("sp", sxbc[ft].v(), a_.v())

                proj_fm(hT, wv, SW, CW // 128, csink, wts, pss)
                p.dma("pool", wdt.v(), wv[:, :, SW + CW:SW + CW + SH])
                p.I("dve", "memset", ap=dtT.v(), constant=0.0)
                p.I("dve", "memset", ap=acT.v(), constant=0.0)
                for tt in range(NT):
                    ts = slice(tt * TT, (tt + 1) * TT)
                    ps_ = pss[tt % 4]
                    for dc in range(DC):
                        p.I("pe", "matmul", out=ps_[0:HN, 0:TT], lhsT=wdt[:, dc, :], rhs=hT[dc][:, ts], start=(dc == 0), stop=(dc == DC - 1))
                    p.I("act", "activation", out=dtT[0:HN, ts], in_=ps_[0:HN, 0:TT], func=AF.Exp, bias=hv[0:HN, 0:1], scale=1.0)
                p.I("act", "activation", out=dtT[0:HN, :], in_=dtT[0:HN, :], func=AF.Ln, bias=1.0, scale=1.0)
            if cfg.stop <= 2:
                return False
            p.mark('sd_dt_start')
            with p.scope():
                eaT = p.sb("eaT", [128, S], F32)
                na = p.sb("na", [64, 1], F32)
                t1 = p.sb("st1", [128, S], F32)
                Dg = p.sb("Dg", [64, 64], F32)
                psq = [p.ps("spsq", [128, 512], F32) for _ in range(2)]
                p.I("act", "activation", out=na[0:HN, :], in_=hv[0:HN, 1:2], func=AF.Exp)
                p.I("dve", "tensor_scalar", out=na[0:HN, :], in0=na[0:HN, :], scalar1=-1.0, scalar2=None, op0=ALU.mult)
                p.I("dve", "tensor_scalar", out=t1[0:HN, :], in0=dtT[0:HN, :], scalar1=na[0:HN, 0:1], scalar2=None, op0=ALU.mult)
                p.I("dve", "tensor_tensor_scan", out=acT[0:HN, :], data0=rmask128[0:HN, :], data1=t1[0:HN, :], initial=0.0,
                    op0=ALU.mult, op1=ALU.add)
                p.I("dve", "memset", ap=nacT.v(), constant=0.0)
                p.I("dve", "memset", ap=eaT.v(), constant=0.0)
                p.I("dve", "tensor_scalar", out=nacT[0:HN, :], in0=acT[0:HN, :], scalar1=-1.0, scalar2=None, op0=ALU.mult)
                p.I("act", "activation", out=eaT[0:HN, :], in_=acT[0:HN, :], func=AF.Exp)
                for n in range(NCH):
                    ns = slice(n * 128, (n + 1) * 128)
                    last = acT[0:HN, n * 128 + 127:n * 128 + 128]
                    p.I("act", "activation", out=t1[0:HN, ns], in_=acT[0:HN, ns], func=AF.Exp, bias=last, scale=-1.0)
                    p.I("dve", "tensor_tensor", out=dtT[64:64 + HN, ns], in0=t1[0:HN, ns], in1=dtT[0:HN, ns], op=ALU.mult)
                    ps_ = psq[n % 2]
                    p.I("pe", "transpose", out=ps_[:, 0:128], in_=dtT[:, ns], identity=ident32)
                    p.I("pe", "transpose", out=ps_[:, 128:256], in_=eaT[:, ns], identity=ident32)
                    p.I("dve", "tensor_scalar", out=Dg[0:HN, 0:HN], in0=ident32[0:HN, 0:HN], scalar1=last, scalar2=None, op0=ALU.mult)
                    p.I("pe", "matmul", out=ps_[:, 256:256 + HN], lhsT=ones32[0:HN, :], rhs=Dg[0:HN, 0:HN], start=True, stop=True)
                    p.I("act", "copy", out=wtm[:, n, :], in_=ps_[:, 0:128])
                    p.I("dve", "tensor_copy", out=eatm[:, n, :], in_=ps_[:, 128:192])
                    p.I("act", "activation", out=decbc[:, n, 0:HN], in_=ps_[:, 256:256 + HN], func=AF.Exp)
            if cfg.stop <= 3:
                return False
            nhalf = 2 if SG >= 2 else 1
            GPH = SG // nhalf
            yg = [p.sb("ygs", [128, S], BF16) for _ in range(GPH * 4)]
            for half in range(nhalf):
              p.mark('sd_scan_start_h%d' % half)
              with p.scope():
                xg = [p.sb("xg", [128, 4, 128], F32) for _ in range(2)]
                bg = p.sb("bg", [128, S], F32)
                BT = p.sb("BT", [128, S], BF16)
                CT = p.sb("CT", [128, S], BF16)
                ngb = p.sb("ngb", [128, 512], F32)
                prev32 = p.sb("prev32", [128, 512], F32)
                prevb = p.sb("prevb", [128, 512], BF16)
                xtm = [p.sb("xtm", [128, 512], F32) for _ in range(2)]
                xc = [p.sb("xc", [128, 512], BF16) for _ in range(2)]
                xcd = [p.sb("xcd", [128, 512], BF16) for _ in range(2)]
                Btm = [p.sb("Btm", [128, 128], BF16) for _ in range(2)]
                cbT = [p.sb("cbT", [128, 128], BF16) for _ in range(2)]
                eM = [p.sb("eM", [128, 4, 128], F32) for _ in range(2)]
                Mm = [[p.sb("Mm", [128, 4, 128], BF16) for _ in range(2)] for _b in range(2)]
                maskb_bf = p.sb("maskb_bf", [128, 128], BF16)
                p.I("dve", "tensor_copy", out=maskb_bf.v(), in_=maskb)
                z32 = [p.sb("z32", [128, 512], F32) for _ in range(2)]
                ty = [p.sb("ty", [128, 512], F32) for _ in range(2)]
                tu = [p.sb("tu", [128, 512], F32) for _ in range(2)]
                junk = p.sb("sjunk", [128, 512], F32)
                sgz = [p.sb("sgz", [128, 512], F32) for _ in range(2)]
                ssq = [p.sb("sssq", [128, 1], F32) for _ in range(2)]
                ybf = [p.sb("ybf", [128, 512], BF16) for _ in range(2)]
                psX = p.ps("spsX", [128, 512], F32)
                psB = p.ps("spsB", [128, 8, 128], BF16)
                psC = p.ps("spsC", [128, 512], F32)
                psM = [p.ps("spsM", [128, 4, 128], F32) for _ in range(2)]
                psY = p.ps("spsY", [128, 512], F32)
                psYo = p.ps("spsYo", [128, 512], F32)
                psSt = p.ps("spsSt", [128, 512], F32)
                for g in range(half * GPH, (half + 1) * GPH):
                    p.dma("sp", bg.v(), sxbc[SW // 128 + g].v())
                    p.I("act", "copy", out=BT.v(), in_=bg.v())
                    p.dma("sp", bg.v(), sxbc[SW // 128 + SG + g].v())
                    p.I("dve", "tensor_copy", out=CT.v(), in_=bg.v())
                    p.dma("sp", ngb.v(), sd_ngb.v()[:, j, g * 512:(g + 1) * 512])
                    p.I("dve", "memset", ap=prev32.v(), constant=0.0)
                    p.I("dve", "memset", ap=prevb.v(), constant=0.0)
                    def partA(n):
                            ns = slice(n * 128, (n + 1) * 128)
                            b2 = n % 2
                            hs8 = slice(g * 8, (g + 1) * 8)
                            p.dma("sp", z32[b2].v(), sz[g][ns, :])
                            for i4 in range(4):
                                p.dma("sp", xg[b2][:, i4, :], sxbc[g * 4 + i4][:, ns])
                            for i4 in range(4):
                                p.I("pe", "transpose", out=psX[:, i4 * 128:(i4 + 1) * 128], in_=xg[b2][:, i4, :], identity=ident32)
                            p.I("act", "copy", out=xtm[b2].v(), in_=psX.v())
                            x3 = xtm[b2].v().re("p (h x) -> p h x", x=64)
                            p.I("dve", "tensor_tensor", out=xc[b2].v().re("p (h x) -> p h x", x=64), in0=x3,
                                in1=wtm[:, n, g * 8:(g + 1) * 8].bc([2], [128, 8, 64]), op=ALU.mult)
                            p.I("dve", "tensor_tensor", out=xcd[b2].v().re("p (h x) -> p h x", x=64), in0=x3,
                                in1=wtm[:, n, 64 + g * 8:64 + (g + 1) * 8].bc([2], [128, 8, 64]), op=ALU.mult)
                            p.I("dve", "tensor_tensor", out=tu[b2].v().re("p (h x) -> p h x", x=64), in0=x3,
                                in1=dsb[:, hs8].bc([2], [128, 8, 64]), op=ALU.mult)
                            p.I("act", "activation", out=sgz[b2].v(), in_=z32[b2].v(), func=AF.Silu)
                            p.I("pe", "matmul", out=psC[:, 128:256], lhsT=BT[:, ns], rhs=ident_bf.v(), start=True, stop=True)
                            p.I("pe", "matmul", out=psC[:, 0:128], lhsT=BT[:, ns], rhs=CT[:, ns], start=True, stop=True)
                            p.I("dve", "tensor_copy", out=Btm[b2].v(), in_=psC[:, 128:256])
                            p.I("act", "copy", out=cbT[b2].v(), in_=psC[:, 0:128])
                            for hq in range(2):
                                pm = psM[hq]
                                for h4 in range(4):
                                    h = g * 8 + hq * 4 + h4
                                    sel = ident32[0:HN, h:h + 1].bc([], [HN, 128])
                                    p.I("pe", "matmul", out=pm[:, h4, :], lhsT=sel, rhs=acT[0:HN, ns], start=True, stop=False)
                                    p.I("pe", "matmul", out=pm[:, h4, :], lhsT=nacT[0:HN, ns], rhs=sel, start=False, stop=False)
                                    p.I("pe", "matmul", out=pm[:, h4, :], lhsT=ident_bf.v(), rhs=maskb_bf.v(), start=False, stop=True)
                                p.I("act", "activation", out=eM[hq].v(), in_=pm.v(), func=AF.Exp)
                                p.I("dve", "tensor_tensor", out=Mm[b2][hq].v(), in0=eM[hq].v(),
                                    in1=cbT[b2].v().bc([1], [128, 4, 128]), op=ALU.mult)

                    def partB1(n):
                            ns = slice(n * 128, (n + 1) * 128)
                            b2 = n % 2
                            hs8 = slice(g * 8, (g + 1) * 8)
                            x3 = xtm[b2].v().re("p (h x) -> p h x", x=64)
                            for hq in range(2):
                                for h4 in range(4):
                                    hl = hq * 4 + h4
                                    p.I("pe", "matmul", out=psY[:, hl * 64:(hl + 1) * 64], lhsT=Mm[b2][hq][:, h4, :],
                                        rhs=xc[b2][:, hl * 64:(hl + 1) * 64], start=True, stop=True)
                            p.I("pe", "matmul", out=psYo.v(), lhsT=CT[:, ns], rhs=prevb.v(), start=True, stop=True)
                            p.I("pe", "matmul", out=psSt.v(), lhsT=Btm[b2].v(), rhs=xcd[b2].v(), start=True, stop=True)
                            t_ = ty[b2]
                            u_ = tu[b2]
                            t3 = t_.v().re("p (h x) -> p h x", x=64)
                            u3 = u_.v().re("p (h x) -> p h x", x=64)
                            p32 = prev32.v().re("p (h x) -> p h x", x=64)
                            p.I("dve", "tensor_tensor", out=p32, in0=p32, in1=decbc[:, n, hs8].bc([2], [128, 8, 64]), op=ALU.mult)
                            p.I("dve", "tensor_tensor", out=prev32.v(), in0=psSt.v(), in1=prev32.v(), op=ALU.add)
                            p.I("act", "copy", out=prevb.v(), in_=prev32.v())
                            p.I("dve", "tensor_tensor", out=t3, in0=psYo.v().re("p (h x) -> p h x", x=64),
                                in1=eatm[:, n, hs8].bc([2], [128, 8, 64]), op=ALU.mult)
                            p.I("dve", "tensor_tensor", out=t_.v(), in0=psY.v(), in1=t_.v(), op=ALU.add)
                            p.I("dve", "tensor_tensor", out=t_.v(), in0=t_.v(), in1=u_.v(), op=ALU.add)
                            p.I("dve", "tensor_tensor", out=t_.v(), in0=t_.v(), in1=sgz[b2].v(), op=ALU.mult)
                            p.I("act", "activation", out=junk.v(), in_=t_.v(), func=AF.Square, accum_out=ssq[b2].v())
                            p.I("act", "activation", out=ssq[b2].v(), in_=ssq[b2].v(), func=AF.Sqrt, bias=1e-5, scale=1.0 / 512)
                    def partB2(n):
                            ns = slice(n * 128, (n + 1) * 128)
                            b2 = n % 2
                            t_ = ty[b2]
                            p.I("dve", "reciprocal", out=ssq[b2].v(), in_=ssq[b2].v())
                            p.I("dve", "scalar_tensor_tensor", out=ybf[b2].v(), in0=t_.v(), scalar=ssq[b2].v(), in1=ngb.v(),
                                op0=ALU.mult, op1=ALU.mult)
                            for i4 in range(4):
                                p.I("pe", "transpose", out=psB[:, 4 + i4, :], in_=ybf[b2][:, i4 * 128:(i4 + 1) * 128], identity=ident_bf.v())
                            for i4 in range(4):
                                p.I("act" if i4 % 2 == 0 else "dve", "copy" if i4 % 2 == 0 else "tensor_copy",
                                    out=yg[(g - half * GPH) * 4 + i4][:, ns], in_=psB[:, 4 + i4, :])

                    partA(0)
                    if NCH > 1:
                        partA(1)
                    for n in range(NCH):
                        partB1(n)
                        if n + 2 < NCH:
                            partA(n + 2)
                        partB2(n)
              if cfg.stop <= 9:
                  return False
              p.mark('sd_outproj_start_h%d' % half)
              nci = GPH * 4
              dsth = nextbuf()
              out_proj(yg, sd_out.v()[j][half * nci * 128:(half + 1) * nci * 128, :], nci,
                       lambda ft: modT[:, l, 2 * DC + ft:2 * DC + ft + 1], srcv[0], dsth.v())
              srcv[0] = dsth.v()
              p.mark('sd_outproj_end_h%d' % half)
            return srcv[0]

    def rwkv_layer(l, j, src, dst):
        CB, TB = cfg.CB, cfg.TB
        NCHB = TB // 64
        with p.scope():
            yg = [p.sb("yg", [128, S], BF16) for _ in range(HP)]
            xs = yg
            lw1 = p.sb("lw1", [R, S], BF16)
            la1 = p.sb("la1", [R, S], BF16)
            vec = p.sb("rvec", [128, 7, HP], F32)
            omka = p.sb("omka", [128, HP], F32)
            p.dma("sp", vec.v(), rw_vecT.v()[:, j])
            p.I("dve", "tensor_scalar", out=omka.v(), in0=vec[:, 3, :], scalar1=-1.0, scalar2=1.0,
                op0=ALU.mult, op1=ALU.add)
            with p.scope():
                hT = [p.sb("hT", [128, S], BF16) for _ in range(DC)]
                mu = p.sb("mu", [128, 6, DC], F32)
                omm = p.sb("omm", [128, 6, DC], F32)
                p.dma("sp", mu.v(), rw_muT.v()[:, j])
                p.I("dve", "tensor_scalar", out=omm.v(), in0=mu.v(), scalar1=-1.0, scalar2=1.0,
                    op0=ALU.mult, op1=ALU.add)
                p.mark('rw_norm_start')
                norm_phase(src, hT, lambda dc: gsT[:, l, dc:dc + 1], lambda dc: modT[:, l, dc:dc + 1])
                p.mark('rw_proj_start')
                if cfg.stop <= 1:
                    return False
                wts = [p.sb("wi", [128, DC, 512], BF16) for _ in range(2)]
                w1t = p.sb("w1t", [128, DC, R], BF16)
                pss = [p.ps("psp", [128, 512], F32) for _ in range(4)]
                stg = [p.sb("stg", [128, TT], F32) for _ in range(4)]
                wv = rw_in.v()[j].re("(dc p) f -> p dc f", p=128)
                sk = [0]

                def mix(c):
                    for dc in range(DC):
                        p.I("dve", "memset", ap=xs[dc][:, 0:1], constant=0.0)
                        p.I("act", "mul", out=xs[dc][:, 1:S], in_=hT[dc][:, 0:S - 1], mul=mu[:, c, dc:dc + 1])
                        p.I("dve", "scalar_tensor_tensor", out=xs[dc].v(), in0=hT[dc].v(), scalar=omm[:, c, dc:dc + 1],
                            in1=xs[dc].v(), op0=ALU.mult, op1=ALU.add)

                import os as _os
                for c in range(4):
                    if not _os.environ.get("NOMIX") or c == 0:
                        mix(c)

                    def sink(ft, tt, ps_, c=c):
                        s_ = stg[sk[0] % 4]
                        e = "act" if sk[0] % 2 == 0 else "dve"
                        sk[0] += 1
                        if e == "act":
                            p.I("act", "copy", out=s_.v(), in_=ps_[:, 0:TT])
                        else:
                            p.I("dve", "tensor_copy", out=s_.v(), in_=ps_[:, 0:TT])
                        if not _os.environ.get("NOSTORE"):
                            p.dma("sp", projT[c][ft][:, tt * TT:(tt + 1) * TT], s_.v(), acc_w=True)

                    proj_fm(xs, wv, c * W, HP, sink, wts, pss)
                for c, (w1d, dstl, fn) in ((4, (rw_dw1, lw1, AF.Tanh)), (5, (rw_aw1, la1, AF.Copy))):
                    mix(c)
                    p.dma("pool", w1t.v(), w1d.v()[j].re("(dc p) r -> p dc r", p=128))
                    for tt in range(NT):
                        ts = slice(tt * TT, (tt + 1) * TT)
                        ps_ = pss[tt % 4]
                        for dc in range(DC):
                            p.I("pe", "matmul", out=ps_[0:R, 0:TT], lhsT=w1t[:, dc, :], rhs=xs[dc][:, ts],
                                start=(dc == 0), stop=(dc == DC - 1))
                        p.I("act", "activation", out=dstl[:, ts], in_=ps_[0:R, 0:TT], func=fn)
            if cfg.stop <= 2:
                return False
            p.mark('rw_scan_start')
            with p.scope():
                dw2 = p.sb("dw2", [R, W], BF16)
                aw2 = p.sb("aw2", [R, W], BF16)
                p.dma("pool", dw2.v(), rw_dw2.v()[j])
                p.dma("pool", aw2.v(), rw_aw2.v()[j])
                CBS, NSTR = 2, 2
                STR = []
                psTrS = p.ps("psTr", [128, 4, 2, 128], BF16)
                for si in range(NSTR):
                    pg_ = p.ps("PG", [128, 2, 512], F32)
                    xr_ = p.sb("Xr", [64, CBS * 2, 2, 64], BF16)
                    nxt_ = p.sb("NXT", [64, CBS * 2, 192], BF16)
                    mu_ = p.sb("MU", [64, CBS * 2, 128], BF16)
                    STR.append([dict(
                        BK=p.sb("BK", [128, CBS, 128], BF16), UV=p.sb("UV", [128, CBS, 128], BF16),
                        Xr=xr_, A_sb=p.sb("A_sb", [128, CBS * 2, 128], BF16), NXT=nxt_, MU=mu_,
                        GT=p.sb("GT", [128, CBS, 64], BF16), PpT=p.sb("PpT", [128, CBS, 64], BF16),
                        PG=pg_, psTr=psTrS, toff=si * CBS) for _par in range(2)])
                psS5 = p.ps("psS5", [128, 4, 128], F32)
                Tst = [p.sb("Tst", [128, 64], BF16) for _ in range(3)]
                psP1 = p.ps("psP", [128, 512], F32)
                psP = [psP1, psP1]
                psAV = p.ps("psAV", [128, CBS * 2, 128], F32)
                NTB = TB // TT if TB >= TT else 1
                TTB = min(TT, TB)
                tiref = [0]

                def item(hp, tb, SET):
                    hsl = slice(hp * 128, (hp + 1) * 128)
                    vcol = lambda i: vec[:, i, hp:hp + 1]
                    tbs = slice(tb * TB, (tb + 1) * TB)
                    ld, tm, BKT, KRT, KKVT = SET["ld"], SET["tm"], SET["BKT"], SET["KRT"], SET["KKVT"]
                    for c, nm in enumerate(("r", "k", "v", "g")):
                        p.dma("sp", ld[nm].v(), projT[c][hp][:, tbs])
                    r_, k_, v_, g_ = ld["r"], ld["k"], ld["v"], ld["g"]
                    if hp == 3:
                        p.mark('rw_prep_start_tb%d' % tb)
                    lw, cum, e1, e2, e3, a_, kk, kf, t1, t2, t3, bv, yT = (tm[n] for n in (
                        "lw", "cum", "e1", "e2", "e3", "a", "kk", "kf", "t1", "t2", "t3", "bv", "y"))
                    for tt in range(NTB):
                        ts = slice(tt * TTB, (tt + 1) * TTB)
                        gs_ = slice(tb * TB + tt * TTB, tb * TB + (tt + 1) * TTB)
                        ps_ = psP[0]
                        p.I("pe", "matmul", out=ps_[:, 0:TTB], lhsT=dw2[:, hsl], rhs=lw1[:, gs_], start=True, stop=True)
                        p.I("act", "activation", out=lw[:, ts], in_=ps_[:, 0:TTB], func=AF.Sigmoid, bias=vcol(0), scale=1.0)
                        ps_ = psP[1]
                        p.I("pe", "matmul", out=ps_[:, 0:TTB], lhsT=aw2[:, hsl], rhs=la1[:, gs_], start=True, stop=True)
                        p.I("act", "activation", out=a_[:, ts], in_=ps_[:, 0:TTB], func=AF.Sigmoid, bias=vcol(1), scale=1.0)
                    p.I("dve", "tensor_scalar", out=lw.v(), in0=lw.v(), scalar1=NEG_EXP_HALF, scalar2=None, op0=ALU.mult)
                    yield "P"
                    p.I("dve", "tensor_tensor_scan", out=cum.v(), data0=rmask[:, 0:TB], data1=lw.v(), initial=0.0,
                        op0=ALU.mult, op1=ALU.add)
                    yield "P"
                    p.I("act", "activation", out=e1.v(), in_=cum.v(), func=AF.Exp)
                    yield "P"
                    p.I("act", "activation", out=e2.v(), in_=cum.v(), func=AF.Exp, scale=-1.0)
                    yield "P"
                    p.I("dve", "tensor_tensor", out=t1.v(), in0=cum.v(), in1=lw.v(), op=ALU.subtract)
                    yield "P"
                    p.I("act", "activation", out=e3.v(), in_=t1.v(), func=AF.Exp)
                    yield "P"
                    p.I("act", "activation", out=t2.v(), in_=k_.v(), func=AF.Square, scale=vcol(2))
                    yield "P"
                    for tt in range(NTB):
                        ts = slice(tt * TTB, (tt + 1) * TTB)
                        ps_ = psP[tt % 2]
                        p.I("pe", "matmul", out=ps_[:, 0:TTB], lhsT=bones32, rhs=t2[:, ts], start=True, stop=True)
                        p.I("act", "activation", out=t3[:, ts], in_=ps_[:, 0:TTB], func=AF.Sqrt)
                    p.I("dve", "tensor_scalar", out=t3.v(), in0=t3.v(), scalar1=1e-12, scalar2=None, op0=ALU.max)
                    yield "P"
                    p.I("dve", "reciprocal", out=t3.v(), in_=t3.v())
                    yield "P"
                    p.I("dve", "scalar_tensor_tensor", out=kk.v(), in0=k_.v(), scalar=vcol(2), in1=t3.v(), op0=ALU.mult, op1=ALU.mult)
                    yield "P"
                    p.I("dve", "tensor_scalar", out=t1.v(), in0=a_.v(), scalar1=vcol(3), scalar2=omka[:, hp:hp + 1],
                        op0=ALU.mult, op1=ALU.add)
                    yield "P"
                    p.I("dve", "tensor_tensor", out=kf.v(), in0=k_.v(), in1=t1.v(), op=ALU.mult)
                    yield "P"
                    p.I("dve", "tensor_tensor", out=t2.v(), in0=kk.v(), in1=a_.v(), op=ALU.mult)
                    yield "P"
                    ch = lambda t: t.v().re("p (n c) -> p n c", c=64)
                    p.I("dve", "tensor_tensor", out=KRT[:, :, 1, :], in0=ch(r_), in1=ch(e1), op=ALU.mult)
                    yield "P"
                    p.I("dve", "tensor_tensor", out=BKT[:, :, 1, :], in0=ch(kf), in1=ch(e2), op=ALU.mult)
                    yield "P"
                    p.I("dve", "tensor_tensor", out=BKT[:, :, 0, :], in0=ch(t2), in1=ch(e2), op=ALU.mult)
                    yield "P"
                    p.I("dve", "tensor_tensor", out=KRT[:, :, 0, :], in0=ch(kk), in1=ch(e3), op=ALU.mult)
                    yield "P"
                    p.I("act", "copy", out=KKVT[:, :, 0, :], in_=KRT[:, :, 0, :])
                    yield "P"
                    p.I("act", "copy", out=KKVT[:, :, 1, :], in_=ch(v_))
                    yield "P"
                    p.I("dve", "scalar_tensor_tensor", out=t1.v(), in0=r_.v(), scalar=vcol(4), in1=kf.v(),
                        op0=ALU.mult, op1=ALU.mult)
                    yield "P"
                    for tt in range(NTB):
                        ts = slice(tt * TTB, (tt + 1) * TTB)
                        ps_ = psP[tt % 2]
                        p.I("pe", "matmul", out=ps_[:, 0:TTB], lhsT=bones32, rhs=t1[:, ts], start=True, stop=True)
                        p.I("dve", "tensor_tensor", out=bv[:, ts], in0=ps_[:, 0:TTB], in1=v_[:, ts], op=ALU.mult)
                    if cfg.stop <= 3:
                        return False
                    if hp == 3:
                        p.mark('rw_groups_start_tb%d' % tb)
                    yield "P_DONE"
                    if tb == 0:
                        p.I("dve", "memset", ap=Tst[tiref[0] % 3].v(), constant=0.0)

                    def group_stream(c0, cb_n, T):
                        BK, UV, Xr, A_sb, NXT, MU, GT, PpT, PG, psTr = (T[k_] for k_ in
                            ("BK", "UV", "Xr", "A_sb", "NXT", "MU", "GT", "PpT", "PG", "psTr"))
                        psTr = psTr[:, T["toff"]:T["toff"] + CBS]
                        PGv = PG.v().re("p h (c x) -> p h c x", c=CBS)
                        hc = lambda t: t.v().re("p (h c) x -> p h c x", h=2)[:, :, 0:cb_n, :]
                        for cb in range(cb_n):
                            n = c0 + cb
                            p.I("pe", "transpose", out=psTr[:, cb, 0, :], in_=BKT[:, n].re("p a c -> p (a c)"), identity=ident_bf.v())
                            p.I("pe", "transpose", out=psTr[:, cb, 1, :], in_=KKVT[:, n].re("p a c -> p (a c)"), identity=ident_bf.v())
                        p.I("dve", "tensor_copy", out=BK[:, 0:cb_n, :], in_=psTr[:, 0:cb_n, 0, :])
                        p.I("dve", "tensor_copy", out=Xr.v().re("p (h c) a x -> p h c a x", h=2)[:, :, 0:cb_n, 0, :],
                            in_=psTr[0:64, 0:cb_n, 1, :].re("p c (h x) -> p h c x", h=2))
                        p.I("dve", "tensor_copy", out=UV[64:128, 0:cb_n, :], in_=psTr[64:128, 0:cb_n, 1, :])
                        for cb in range(cb_n):
                            n = c0 + cb
                            for h in range(2):
                                hs = slice(h * 64, (h + 1) * 64)
                                p.I("pe", "matmul", out=PGv[:, h, cb, 0:128],
                                    lhsT=BKT[hs, n].re("p a c -> p (a c)"), rhs=KRT[hs, n].re("p a c -> p (a c)"),
                                    start=True, stop=True)
                                p.I("pe", "matmul", out=PGv[0:64, h, cb, 128:192],
                                    lhsT=KRT[hs, n, 0, :], rhs=BKT[hs, n, 0, :], start=True, stop=True)
                        pgA = PGv[:, :, 0:cb_n, 0:128]
                        p.I("act", "copy", out=hc(A_sb), in_=pgA)
                        p.I("dve", "tensor_tensor", out=hc(A_sb), in0=hc(A_sb),
                            in1=maskA.bc([1, 1], [128, 2, cb_n, 128]), op=ALU.mult)
                        nx = hc(NXT)
                        p.I("dve", "tensor_tensor", out=nx[:, :, :, 0:64], in0=hc(A_sb)[0:64, :, :, 0:64],
                            in1=negSU.bc([1, 1], [64, 2, cb_n, 64]), op=ALU.mult)
                        p.I("dve", "tensor_tensor", out=nx[:, :, :, 64:128], in0=nx[:, :, :, 0:64],
                            in1=cst[0:64, 5, 0:64].bc([1, 1], [64, 2, cb_n, 64]), op=ALU.add)
                        p.I("dve", "tensor_tensor", out=nx[:, :, :, 128:192],
                            in0=PGv[0:64, :, 0:cb_n, 128:192],
                            in1=negSL.bc([1, 1], [64, 2, cb_n, 64]), op=ALU.mult)
                        yield
                        for cb in range(cb_n):
                            for h in range(2):
                                q = h * CBS + cb
                                p.I("pe", "matmul", out=psAV[0:64, q, 0:64], lhsT=A_sb[64:128, q, 0:64],
                                    rhs=UV[64:128, cb, h * 64:(h + 1) * 64], start=True, stop=True)
                        pgI = PGv[0:64, :, 0:cb_n, 0:192]
                        for rnd in range(6):
                            for cb in range(cb_n):
                                for h in range(2):
                                    q = h * CBS + cb
                                    if rnd == 0:
                                        p.I("pe", "matmul", out=PGv[0:64, h, cb, 0:64], lhsT=NXT[:, q, 128:192],
                                            rhs=NXT[:, q, 0:64], start=True, stop=True)
                                    elif rnd < 5:
                                        p.I("pe", "matmul", out=PGv[0:64, h, cb, 0:128], lhsT=NXT[:, q, 128:192],
                                            rhs=NXT[:, q, 0:128], start=True, stop=True)
                                    else:
                                        p.I("pe", "matmul", out=PGv[0:64, h, cb, 64:128], lhsT=NXT[:, q, 128:192],
                                            rhs=NXT[:, q, 64:128], start=True, stop=True)
                                    if rnd < 5:
                                        p.I("pe", "matmul", out=PGv[0:64, h, cb, 128:192], lhsT=NXT[:, q, 0:64],
                                            rhs=NXT[:, q, 128:192], start=True, stop=True)
                            if rnd == 0:
                                p.I("act", "copy", out=Xr.v().re("p (h c) a x -> p h c a x", h=2)[:, :, 0:cb_n, 1, :],
                                    in_=psAV.v().re("p (h c) x -> p h c x", h=2)[0:64, :, 0:cb_n, 0:64])
                            if rnd > 0:
                                p.I("dve", "tensor_tensor", out=nx[:, :, :, 64:128], in0=pgI[:, :, :, 64:128],
                                    in1=nx[:, :, :, 64:128], op=ALU.add)
                            if rnd < 5:
                                p.I("act", "copy", out=nx[:, :, :, 0:64], in_=pgI[:, :, :, 0:64])
                                p.I("act", "copy", out=nx[:, :, :, 128:192], in_=pgI[:, :, :, 128:192])
                            yield
                        for cb in range(cb_n):
                            for h in range(2):
                                q = h * CBS + cb
                                p.I("pe", "matmul", out=PGv[0:64, h, cb, 0:128], lhsT=NXT[:, q, 64:128],
                                    rhs=Xr[:, q].re("p a c -> p (a c)"), start=True, stop=True)
                        pgM = PGv[0:64, :, 0:cb_n, 0:128]
                        p.I("act", "mul", out=hc(MU), in_=pgM, mul=-1.0)
                        p.I("dve", "tensor_scalar", out=UV[0:64, 0:cb_n, :].re("p c (h x) -> p h c x", h=2),
                            in0=pgM[:, :, :, 64:128], scalar1=-1.0, scalar2=None, op0=ALU.mult)
                        yield
                        for cb in range(cb_n):
                            for h in range(2):
                                q = h * CBS + cb
                                hs = slice(h * 64, (h + 1) * 64)
                                p.I("pe", "matmul", out=PGv[hs, h, cb, 0:64], lhsT=MU[:, q, 0:64], rhs=A_sb[0:64, q, 64:128],
                                    start=True, stop=True)
                                p.I("pe", "matmul", out=PGv[hs, h, cb, 64:128], lhsT=MU[:, q, 0:64], rhs=BK[0:64, cb, h * 64:(h + 1) * 64],
                                    start=True, stop=True)
                        for h in range(2):
                            hs = slice(h * 64, (h + 1) * 64)
                            p.I("dve", "tensor_tensor", out=GT[hs, 0:cb_n, :], in0=PGv[hs, h, 0:cb_n, 0:64],
                                in1=KRT[hs, c0:c0 + cb_n, 1, :], op=ALU.add)
                            p.I("dve", "tensor_tensor", out=PpT[hs, 0:cb_n, :], in0=PGv[hs, h, 0:cb_n, 64:128],
                                in1=cst[hs, 5, 0:64].bc([1], [64, cb_n, 64]), op=ALU.add)
                        yield
                        return

                    def back(sets):
                        slot = 0
                        c0g = sets[0][1]
                        for (T, c0, cb_n) in sets:
                            BK, UV, A_sb, GT, PpT = (T[k_] for k_ in ("BK", "UV", "A_sb", "GT", "PpT"))
                            for cb in range(cb_n):
                                n = c0 + cb
                                Tc, Tn = Tst[tiref[0] % 3], Tst[(tiref[0] + 1) % 3]
                                tiref[0] += 1
                                for h in range(2):
                                    hs = slice(h * 64, (h + 1) * 64)
                                    p.I("pe", "matmul", out=psS5[hs, slot, 0:64], lhsT=PpT[hs, cb, :], rhs=Tc[hs, :], start=True, stop=False)
                                    p.I("pe", "matmul", out=psS5[hs, slot, 0:64], lhsT=BK[:, cb, hs], rhs=UV[:, cb, hs], start=False, stop=True)
                                yield
                                for h in range(2):
                                    hs = slice(h * 64, (h + 1) * 64)
                                    p.I("dve", "tensor_scalar", out=Tn[hs, :], in0=psS5[hs, slot, 0:64],
                                        scalar1=e1[hs, n * 64 + 63:n * 64 + 64], scalar2=None, op0=ALU.mult)
                                for h in range(2):
                                    q = h * CBS + cb
                                    hs = slice(h * 64, (h + 1) * 64)
                                    p.I("pe", "matmul", out=psS5[hs, slot, 64:128], lhsT=Tc[hs, :], rhs=GT[hs, cb, :], start=True, stop=False)
                                    p.I("pe", "matmul", out=psS5[hs, slot, 64:128], lhsT=UV[:, cb, hs], rhs=A_sb[:, q, 64:128], start=False, stop=True)
                                slot += 1
                                yield
                        for h in range(2):
                            hs = slice(h * 64, (h + 1) * 64)
                            p.I("act", "copy", out=yT.v().re("p (n c) -> p n c", c=64)[hs, c0g:c0g + slot, :],
                                in_=psS5[hs, 0:slot, 64:128])

                    def drive(gens):
                        alive = list(gens)
                        while alive:
                            nxt = []
                            for gq in alive:
                                try:
                                    next(gq)
                                    nxt.append(gq)
                                except StopIteration:
                                    pass
                            alive = nxt
                            yield "G"

                    prev_sets = None
                    for gi_, g0 in enumerate(range(0, NCHB, CBS * NSTR)):
                        gens, sets = [], []
                        for si in range(NSTR):
                            c0 = g0 + si * CBS
                            if c0 < NCHB:
                                T_ = STR[si][gi_ % 2]
                                cbn_ = min(CBS, NCHB - c0)
                                gens.append(group_stream(c0, cbn_, T_))
                                sets.append((T_, c0, cbn_))
                        if prev_sets is not None:
                            gens.append(back(prev_sets))
                        yield from drive(gens)
                        prev_sets = sets
                    yield from drive([back(prev_sets)])
                    yield "G_DONE"
                    if hp == 3:
                        p.mark('rw_post_start_tb%d' % tb)
                    if cfg.stop <= 8:
                        return False
                    p.I("act", "activation", out=t2.v(), in_=yT.v(), func=AF.Square)
                    yield "Q"
                    HW_ = min(256, TTB)
                    for tt in range(TB // HW_):
                        ts = slice(tt * HW_, (tt + 1) * HW_)
                        p.I("pe", "matmul", out=psP1[:, 0:HW_], lhsT=bones32, rhs=yT[:, ts], start=True, stop=True)
                        p.I("pe", "matmul", out=psP1[:, 256:256 + HW_], lhsT=bones32, rhs=t2[:, ts], start=True, stop=True)
                        p.I("act", "mul", out=t1[:, ts], in_=psP1[:, 0:HW_], mul=1.0 / 64)
                        p.I("dve", "tensor_tensor", out=t3[:, ts], in0=t1[:, ts], in1=t1[:, ts], op=ALU.mult)
                        p.I("dve", "scalar_tensor_tensor", out=t3[:, ts], in0=psP1[:, 256:256 + HW_], scalar=1.0 / 64, in1=t3[:, ts],
                            op0=ALU.mult, op1=ALU.subtract)
                    p.I("act", "activation", out=t3.v(), in_=t3.v(), func=AF.Sqrt, bias=RWKV_GN_EPS, scale=1.0)
                    yield "Q"
                    p.I("dve", "reciprocal", out=t3.v(), in_=t3.v())
                    yield "Q"
                    p.I("dve", "tensor_tensor", out=t1.v(), in0=yT.v(), in1=t1.v(), op=ALU.subtract)
                    yield "Q"
                    p.I("dve", "tensor_tensor", out=t1.v(), in0=t1.v(), in1=t3.v(), op=ALU.mult)
                    yield "Q"
                    p.I("act", "activation", out=t1.v(), in_=t1.v(), func=AF.Identity, bias=vcol(6), scale=vcol(5))
                    yield "Q"
                    p.I("dve", "tensor_tensor", out=t1.v(), in0=t1.v(), in1=bv.v(), op=ALU.add)
                    yield "Q"
                    p.I("act", "activation", out=t2.v(), in_=g_.v(), func=AF.Silu)
                    yield "Q"
                    p.I("dve", "tensor_tensor", out=yg[hp][:, tbs], in0=t1.v(), in1=t2.v(), op=ALU.mult)
                    yield "Q"

                SETS = []
                for _si in range(2):
                    SETS.append(dict(
                        ld={nm: p.sb("ld_" + nm, [128, TB], F32) for nm in ("r", "k", "v", "g")},
                        tm={nm: p.sb("tm_" + nm, [128, TB], F32) for nm in
                            ("lw", "cum", "e1", "e2", "e3", "a", "kk", "kf", "t1", "t2", "t3", "bv", "y")},
                        BKT=p.sb("BKT", [128, NCHB, 2, 64], BF16), KRT=p.sb("KRT", [128, NCHB, 2, 64], BF16),
                        KKVT=p.sb("KKVT", [128, NCHB, 2, 64], BF16)))
                import os as _os2
                items = [(hp_, tb_) for hp_ in range(int(_os2.environ.get('RW_HP', HP))) for tb_ in range(S // TB)]
                gens_ = [item(hp_, tb_, SETS[ix % 2]) for ix, (hp_, tb_) in enumerate(items)]
                phase_ = ["P"] * len(items)
                lo = 0
                while lo < len(items):
                    hi = min(lo + 3, len(items))
                    for ix in range(lo, hi):
                        ph = phase_[ix]
                        if ph == "D":
                            continue
                        if ph == "P" and ((ix >= 2 and phase_[ix - 2] != "D") or (ix >= 1 and phase_[ix - 1] == "P")):
                            continue
                        if ph == "G" and ix >= 1 and phase_[ix - 1] in ("P", "G"):
                            continue
                        try:
                            tag = next(gens_[ix])
                            if tag == "P_DONE":
                                phase_[ix] = "G"
                            elif tag == "G_DONE":
                                phase_[ix] = "Q"
                        except StopIteration:
                            phase_[ix] = "D"
                    while lo < len(items) and phase_[lo] == "D":
                        lo += 1
            if cfg.stop <= 9:
                return False
            p.mark('rw_outproj_start')
            out_proj(yg, rw_out.v()[j], HP, lambda ft: modT[:, l, 2 * DC + ft:2 * DC + ft + 1], src, dst)
            p.mark('rw_outproj_end')
            return True

    bufs = xres
    bi = [0]

    def nextbuf():
        b_ = bufs[bi[0] % len(bufs)]
        bi[0] += 1
        return b_

    cur = xT.v()
    counters = {0: 0, 1: 0, 2: 0}
    for l, kind in enumerate(cfg.kinds):
        j = counters[kind]
        counters[kind] += 1
        if kind in (0, 1):
            dst = nextbuf()
            ok = (rwkv_layer if kind == 0 else gla_layer)(l, j, cur, dst.v())
            if ok:
                cur = dst.v()
        else:
            r_ = ssd_layer(l, j, cur, nextbuf)
            if r_ is not False:
                cur = r_
    with p.scope():
        fg = p.sb("fg", [128, DC], F32)
        p.dma("sp", fg.v(), final_gT.v())
        norm_phase(cur, None, lambda dc: fg[:, dc:dc + 1], None, out_dram=outT.v())
    p.emit()
    return nc, p


def _pp(vec, nchunk):
    v = np.asarray(vec, np.float32)
    lead = v.shape[:-1]
    v = v.reshape(lead + (nchunk, 128))
    return np.ascontiguousarray(np.moveaxis(v, -1, 0))


def prepare_inputs(cfg, inp, n_cores, batch_of_core):
    D, S, DC, L = cfg.D, cfg.S, cfg.DC, cfg.L
    consts, rmask, rmask128 = make_consts(cfg)
    shared = {
        "ada_w": np.ascontiguousarray(inp["ada_w"], dtype=np.float32),
        "ada_bT": _pp(inp["ada_b"], 3 * DC),
        "norm_gT": _pp(inp["norm_g"], DC),
        "final_gT": _pp(inp["final_g"], DC),
        "consts": consts, "rmask": rmask, "rmask128": rmask128,
    }
    if cfg.nR:
        HP = D // 128
        for k in ("rwkv_w_in", "rwkv_w_out", "rwkv_dec_w1", "rwkv_dec_w2", "rwkv_iclr_w1", "rwkv_iclr_w2"):
            shared[k] = np.ascontiguousarray(inp[k], dtype=np.float32)
        shared["rwkv_muT"] = _pp(inp["rwkv_mu"], DC)
        vecs = np.stack([inp["rwkv_dec_w0"], inp["rwkv_iclr_w0"], inp["rwkv_k_k"], inp["rwkv_k_a"],
                         np.asarray(inp["rwkv_r_k"]).reshape(cfg.nR, -1), inp["rwkv_gn_w"], inp["rwkv_gn_b"]], axis=1)
        shared["rwkv_vecT"] = _pp(vecs, HP)
    if cfg.nG:
        for k in ("gla_w_in", "gla_w_out", "gla_gate_w2"):
            shared[k] = np.ascontiguousarray(inp[k], dtype=np.float32)
        shared["gla_nbT"] = _pp(inp["gla_gate_b"], (D // 2) // 128)
        hg = np.asarray(inp["gla_head_g"], np.float32)
        shared["gla_hgb"] = np.ascontiguousarray(np.broadcast_to(hg[None], (128,) + hg.shape))
    if cfg.nS:
        SW = 2 * D
        SH = SW // 64
        for k in ("ssd_w_in", "ssd_w_out"):
            shared[k] = np.ascontiguousarray(inp[k], dtype=np.float32)
        cwk = np.asarray(inp["ssd_conv_w"], np.float32)
        shared["ssd_cwT"] = _pp(np.moveaxis(cwk, 1, 2).reshape(cfg.nS, -1).reshape(cfg.nS, cwk.shape[2], 4).transpose(0, 2, 1), cwk.shape[2] // 128).transpose(0, 1, 3, 2).copy()
        shared["ssd_cbT"] = _pp(inp["ssd_conv_b"], cwk.shape[2] // 128)
        hv = np.zeros((64, cfg.nS, 2), np.float32)
        hv[:SH, :, 0] = np.asarray(inp["ssd_dt_bias"], np.float32).T
        hv[:SH, :, 1] = np.asarray(inp["ssd_a_log"], np.float32).T
        shared["ssd_hv"] = hv
        dsk = np.asarray(inp["ssd_d"], np.float32)
        shared["ssd_dsb"] = np.ascontiguousarray(np.broadcast_to(dsk[None], (128,) + dsk.shape))
        ng = np.asarray(inp["ssd_norm_g"], np.float32)
        shared["ssd_ngb"] = np.ascontiguousarray(np.broadcast_to(ng[None], (128,) + ng.shape))
    maps = []
    for core in range(n_cores):
        b = batch_of_core[core]
        m = dict(shared)
        m["xT"] = np.ascontiguousarray(np.asarray(inp["x"][b], np.float32).T)
        m["cT"] = _pp(inp["c"][b], DC)
        maps.append(m)
    return maps


_CACHE = {}


def kernel(**inputs):
    cfg = Cfg()
    B = inputs["x"].shape[0]
    n_cores = 8
    batch_of_core = [c % B for c in range(n_cores)]
    if "nc" not in _CACHE:
        _CACHE["nc"] = build(cfg)[0]
    nc = _CACHE["nc"]
    maps = prepare_inputs(cfg, inputs, n_cores, batch_of_core)
    res = run_bass_kernel_spmd(nc, maps, core_ids=list(range(n_cores)))
    out = np.empty((B, cfg.S, cfg.D), np.float32)
    for b in range(B):
        out[b] = res.results[b]["outT"].T
    return out
```

```python
from contextlib import ExitStack
import math
import numpy as np
import concourse.bass as bass
import concourse.mybir as mybir
from concourse.bass_utils import run_bass_kernel_spmd

F32 = mybir.dt.float32
BF16 = mybir.dt.bfloat16
AF = mybir.ActivationFunctionType
ALU = mybir.AluOpType
AX = mybir.AxisListType


class V:
    __slots__ = ("ap", "tl")

    def __init__(self, ap, tl):
        self.ap = ap
        self.tl = tl

    def __getitem__(self, idx):
        return V(self.ap[idx], self.tl)

    def re(self, pat, **kw):
        return V(self.ap.rearrange(pat, **kw), self.tl)

    def bc(self, axes, shape):
        a = self.ap
        for ax in axes:
            a = a.unsqueeze(ax)
        return V(a.broadcast_to(list(shape)), self.tl)


class Tl:
    __slots__ = ("t", "lw", "rd", "name", "excl")

    def __init__(self, t, name="", excl=False):
        self.t = t
        self.lw = []
        self.rd = []
        self.name = name
        self.excl = excl

    def __getitem__(self, idx):
        return V(self.t[idx], self)

    def v(self):
        return V(self.t[:], self)


ENGS = ("pe", "act", "dve", "pool", "sp")
DMA_ENGS = ("sp", "pool", "act")
NDMA_SLOTS = 12
WRITE_KW = ("out", "accum_out", "ap")


def _compress(toks):
    best = {}
    for s, v, src in toks:
        k = id(s)
        if k not in best or best[k][1] < v:
            best[k] = (s, v, src)
    return list(best.values())


class Prog:
    def __init__(self, nc):
        self.nc = nc
        self.stacks = [ExitStack()]
        self.q = {e: [] for e in ENGS}
        self.cnt = {e: 0 for e in ENGS}
        self.sem = {e: self.stacks[0].enter_context(nc.semaphore("s_" + e)) for e in ENGS}
        self.seen = {e: {} for e in ENGS}
        self.dsem, self.dval, self.dnext = {}, {}, {}
        for e in DMA_ENGS:
            self.dsem[e] = [self.stacks[0].enter_context(nc.semaphore("d_%s%d" % (e, i))) for i in range(NDMA_SLOTS)]
            self.dval[e] = [0] * NDMA_SLOTS
            self.dnext[e] = 0
        self.n_inst = 0
        self.uid = 0
        self.marks = []

    def mark(self, label):
        self.marks.append((label, dict(self.cnt)))

    def _nm(self, name):
        self.uid += 1
        return "%s_%d" % (name, self.uid)

    def sb(self, name, shape, dt=F32):
        t = self.stacks[-1].enter_context(self.nc.sbuf_tensor(self._nm(name), list(shape), dt))
        return Tl(t, name)

    def ps(self, name, shape, dt=F32):
        nbytes = int(np.prod(shape[1:])) * (4 if dt == F32 else 2)
        assert nbytes % 2048 == 0, "PSUM tiles must cover whole banks"
        t = self.stacks[-1].enter_context(self.nc.psum_tensor(self._nm(name), list(shape), dt))
        return Tl(t, name, excl=True)

    def dram(self, name, shape, dt=F32, kind="Internal"):
        t = self.nc.dram_tensor(name, list(shape), dt, kind=kind)
        return Tl(t.ap(), name)

    class _Scope:
        def __init__(self, p):
            self.p = p

        def __enter__(self):
            self.p.stacks.append(ExitStack())

        def __exit__(self, *a):
            self.p.barrier()
            self.p.stacks.pop().close()
            return False

    def scope(self):
        return Prog._Scope(self)

    def _deps(self, eng, reads, writes, acc_w=False):
        waits = {}

        def need(tok):
            sem, val, src = tok
            if src == "pe" and eng == "pe":
                return
            k = id(sem)
            if self.seen[eng].get(k, 0) >= val:
                return
            if k not in waits or waits[k][1] < val:
                waits[k] = (sem, val)

        for tl in reads:
            for tok in tl.lw:
                need(tok)
        for tl in writes:
            if not acc_w:
                for tok in tl.lw:
                    need(tok)
            for tok in tl.rd:
                need(tok)
        for k, (sem, val) in waits.items():
            self.seen[eng][k] = val
        return list(waits.values())

    def _commit(self, tok, reads, writes, acc_w=False):
        for tl in writes:
            if acc_w:
                tl.lw.append(tok)
                if len(tl.lw) > 48:
                    tl.lw = _compress(tl.lw)
            else:
                tl.lw = [tok]
            tl.rd = []
        for tl in reads:
            if tl not in writes:
                tl.rd.append(tok)
                if len(tl.rd) > 48:
                    tl.rd = _compress(tl.rd)

    def I(self, eng, fn, *, acc_w=False, **kw):
        reads, writes, args = [], [], {}
        for k, a in kw.items():
            if isinstance(a, V):
                args[k] = a.ap
                (writes if (k in WRITE_KW or a.tl.excl) else reads).append(a.tl)
            else:
                args[k] = a
        waits = self._deps(eng, reads, writes, acc_w)
        self.cnt[eng] += 1
        tok = (self.sem[eng], self.cnt[eng], eng)
        self._commit(tok, reads, writes, acc_w)
        self.q[eng].append((waits, fn, args, (self.sem[eng], 1)))
        self.n_inst += 1

    def dma(self, eng, out, in_, acc_w=False, **kw):
        reads, writes = [in_.tl], [out.tl]
        waits = self._deps(eng, reads, writes, acc_w)
        s = self.dnext[eng]
        self.dnext[eng] = (s + 1) % NDMA_SLOTS
        sem = self.dsem[eng][s]
        prev = self.dval[eng][s]
        if prev > 0 and self.seen[eng].get(id(sem), 0) < prev:
            waits.append((sem, prev))
            self.seen[eng][id(sem)] = prev
        self.dval[eng][s] = prev + 16
        tok = (sem, prev + 16, "dma")
        self._commit(tok, reads, writes, acc_w)
        args = dict(out=out.ap, in_=in_.ap)
        args.update(kw)
        self.q[eng].append((waits, "dma_start", args, (sem, 16)))
        self.n_inst += 1

    def barrier(self):
        for e in ENGS:
            waits = []
            for e2 in ENGS:
                if self.cnt[e2] > 0 and self.seen[e].get(id(self.sem[e2]), 0) < self.cnt[e2] and e2 != e:
                    waits.append((self.sem[e2], self.cnt[e2]))
                    self.seen[e][id(self.sem[e2])] = self.cnt[e2]
            for de in DMA_ENGS:
                for s in range(NDMA_SLOTS):
                    v = self.dval[de][s]
                    sem = self.dsem[de][s]
                    if v > 0 and self.seen[e].get(id(sem), 0) < v:
                        waits.append((sem, v))
                        self.seen[e][id(sem)] = v
            if waits:
                self.q[e].append((waits, None, None, None))

    def emit(self):
        nc = self.nc
        self.barrier()
        with nc.Block() as block:
            def run(engname):
                def f(e):
                    for waits, fn, args, inc in self.q[engname]:
                        for sem, val in waits:
                            e.wait_ge(sem, val)
                        if fn is not None:
                            getattr(e, fn)(**args).then_inc(inc[0], inc[1])
                return f
            block.tensor(run("pe"))
            block.scalar(run("act"))
            block.vector(run("dve"))
            block.gpsimd(run("pool"))
            block.sync(run("sp"))
        while self.stacks:
            self.stacks.pop().close()


class Cfg:
    def __init__(self, D=2048, S=2048, kinds=(0, 1, 2, 0), lora=96,
                 gla_heads=4, gla_rank=16, ssm_groups=8):
        self.D, self.S, self.kinds, self.lora = D, S, tuple(kinds), lora
        self.gla_heads, self.gla_rank, self.ssm_groups = gla_heads, gla_rank, ssm_groups
        self.DC = D // 128
        self.TT = min(512, S)
        self.NT = S // self.TT
        self.TA = min(256, S)
        self.L = len(kinds)
        self.nR = sum(1 for k in kinds if k == 0)
        self.nG = sum(1 for k in kinds if k == 1)
        self.nS = sum(1 for k in kinds if k == 2)
        self.TB = min(512, S)
        self.CB = 4
        self.stop = 99


NEG_EXP_HALF = -math.exp(-0.5)
NORM_EPS = 1e-6
RWKV_GN_EPS = 64e-5


def make_consts(cfg):
    c = np.zeros((128, 8, 128), np.float32)
    c[:, 0, :] = np.eye(128)
    c[:, 1, :] = 1.0
    c[0:64, 2, 0:64] = 1.0
    c[64:128, 2, 64:128] = 1.0
    su = np.triu(np.ones((64, 64), np.float32), 1)
    iu = np.triu(np.ones((64, 64), np.float32), 0)
    c[0:64, 3, 0:64] = su
    c[64:128, 3, 0:64] = su
    c[0:64, 3, 64:128] = iu
    c[64:128, 3, 64:128] = iu
    c[0:64, 4, 0:64] = -su
    c[0:64, 4, 64:128] = -su.T
    c[0:64, 5, 0:64] = np.eye(64)
    c[64:128, 5, 0:64] = np.eye(64)
    c[:, 6, :] = np.triu(np.ones((128, 128), np.float32), 0)
    c[:, 7, :] = np.where(np.triu(np.ones((128, 128)), 0) > 0, 0.0, -30000.0)
    rmask = np.ones((128, cfg.S), np.float32)
    rmask[:, 0::64] = 0.0
    rmask128 = np.ones((128, cfg.S), np.float32)
    rmask128[:, 0::128] = 0.0
    return c.reshape(128, 8 * 128), rmask, rmask128


def build(cfg):
    nc = bass.Bass("TRN2", target_bir_lowering=False)
    p = Prog(nc)
    D, S, DC, TT, NT, L = cfg.D, cfg.S, cfg.DC, cfg.TT, cfg.NT, cfg.L
    EI = "ExternalInput"
    xT = p.dram("xT", [D, S], F32, EI)
    cT = p.dram("cT", [128, DC], F32, EI)
    ada_w = p.dram("ada_w", [L, D, 3 * D], F32, EI)
    ada_bT = p.dram("ada_bT", [128, L, 3 * DC], F32, EI)
    norm_gT = p.dram("norm_gT", [128, L, DC], F32, EI)
    final_gT = p.dram("final_gT", [128, DC], F32, EI)
    consts_d = p.dram("consts", [128, 8 * 128], F32, EI)
    rmask_d = p.dram("rmask", [128, S], F32, EI)
    rmask128_d = p.dram("rmask128", [128, S], F32, EI)
    outT = p.dram("outT", [D, S], F32, "ExternalOutput")
    xres = [p.dram("xres%d" % i, [D, S], F32) for i in range(3)]
    W = D
    HP = W // 128
    R = cfg.lora
    if cfg.nR:
        nR = cfg.nR
        rw_in = p.dram("rwkv_w_in", [nR, D, 4 * W], F32, EI)
        rw_out = p.dram("rwkv_w_out", [nR, W, D], F32, EI)
        rw_dw1 = p.dram("rwkv_dec_w1", [nR, D, R], F32, EI)
        rw_dw2 = p.dram("rwkv_dec_w2", [nR, R, W], F32, EI)
        rw_aw1 = p.dram("rwkv_iclr_w1", [nR, D, R], F32, EI)
        rw_aw2 = p.dram("rwkv_iclr_w2", [nR, R, W], F32, EI)
        rw_muT = p.dram("rwkv_muT", [128, nR, 6, DC], F32, EI)
        rw_vecT = p.dram("rwkv_vecT", [128, nR, 7, HP], F32, EI)
        projT = [[Tl(t.t[f * 128:(f + 1) * 128, :], "projT") for f in range(HP)]
                 for t in [p.dram("projT%d" % c, [W, S], F32) for c in range(4)]]

    GH = cfg.gla_heads
    KW, VW = D // 2, D
    DK, DV = KW // GH, VW // GH
    KC, VC = max(DK // 128, 1), DV // 128
    GR = cfg.gla_rank
    if cfg.nG:
        nG = cfg.nG
        assert DK % 128 == 0 and DV % 128 == 0 and DV <= 512
        gl_in = p.dram("gla_w_in", [nG, D, 2 * KW + 2 * VW + GR], F32, EI)
        gl_out = p.dram("gla_w_out", [nG, VW, D], F32, EI)
        gl_w2 = p.dram("gla_gate_w2", [nG, GR, KW], F32, EI)
        gl_nbT = p.dram("gla_nbT", [128, nG, KW // 128], F32, EI)
        gl_hgb = p.dram("gla_hgb", [128, nG, DV], F32, EI)
        gqk = [[Tl(t.t[f * 128:(f + 1) * 128, :], "gqk") for f in range(KW // 128)]
               for t in [p.dram("gqk%d" % c, [KW, S], F32) for c in range(2)]]
        gvg_t = [p.dram("gvg%d" % c, [S, VW], F32) for c in range(2)]
        gvg = [[Tl(t.t[:, hh * DV:(hh + 1) * DV], "gvg") for hh in range(GH)] for t in gvg_t]

    SW = 2 * D
    SH = SW // 64
    SG = SW // 512
    SN = 128
    CW = SW + 2 * SG * SN
    SIN = SW + CW + SH
    if cfg.nS:
        nS = cfg.nS
        sd_in = p.dram("ssd_w_in", [nS, D, SIN], F32, EI)
        sd_out = p.dram("ssd_w_out", [nS, SW, D], F32, EI)
        sd_cwT = p.dram("ssd_cwT", [128, nS, CW // 128, 4], F32, EI)
        sd_cbT = p.dram("ssd_cbT", [128, nS, CW // 128], F32, EI)
        sd_hv = p.dram("ssd_hv", [64, nS, 2], F32, EI)
        sd_dsb = p.dram("ssd_dsb", [128, nS, SH], F32, EI)
        sd_ngb = p.dram("ssd_ngb", [128, nS, SW], F32, EI)
        sxbc_t = p.dram("sxbc", [CW, S], F32)
        sxbc = [Tl(sxbc_t.t[f * 128:(f + 1) * 128, :], "sxbc") for f in range(CW // 128)]
        sz_t = p.dram("sz", [S, SW], F32)
        sz = [Tl(sz_t.t[:, g * 512:(g + 1) * 512], "sz") for g in range(SG)]

    cst = p.sb("cst", [128, 8, 128], F32)
    p.dma("sp", cst.v().re("p a b -> p (a b)"), consts_d.v())
    ident_bf = p.sb("ident_bf", [128, 128], BF16)
    p.I("dve", "tensor_copy", out=ident_bf.v(), in_=cst[:, 0, :])
    ones32 = cst[:, 1, :]
    bones32 = cst[:, 2, :]
    maskA = cst[:, 3, :]
    negSU = cst[0:64, 4, 0:64]
    negSL = cst[0:64, 4, 64:128]
    ident2 = cst[:, 5, 0:64]
    rmask = p.sb("rmask", [128, S], BF16)
    p.dma("pool", rmask.v(), rmask_d.v())
    rmask128 = p.sb("rmask128", [128, S], BF16)
    p.dma("pool", rmask128.v(), rmask128_d.v())
    iu128 = cst[:, 6, :]

    modT = p.sb("modT", [128, L, 3 * DC], F32)
    gsT = p.sb("gsT", [128, L, DC], F32)
    with p.scope():
        cact = p.sb("cact", [128, DC], F32)
        abT = p.sb("abT", [128, L, 3 * DC], F32)
        ngT = p.sb("ngT", [128, L, DC], F32)
        p.dma("sp", cact.v(), cT.v())
        p.dma("sp", abT.v(), ada_bT.v())
        p.dma("sp", ngT.v(), norm_gT.v())
        p.I("act", "activation", out=cact.v(), in_=cact.v(), func=AF.Silu)
        EG = 4 if (3 * DC) % 4 == 0 else 2
        cact_bf = p.sb("cact_bf", [128, DC], BF16)
        p.I("dve", "tensor_copy", out=cact_bf.v(), in_=cact.v())
        wst = [p.sb("adaw", [128, DC, EG * 128], BF16) for _ in range(3)]
        psm = p.ps("psmod", [128, 512], F32)
        gi = 0
        for l in range(L):
            wv = ada_w.v()[l].re("(dc p) e -> p dc e", p=128)
            for eg in range(3 * DC // EG):
                wt = wst[gi % 3]
                p.dma("pool", wt.v(), wv[:, :, eg * EG * 128:(eg + 1) * EG * 128])
                gi += 1
                for j in range(EG):
                    col = l * 3 * DC + eg * EG + j
                    for dc in range(DC):
                        p.I("pe", "matmul", out=psm[:, col:col + 1], lhsT=wt[:, dc, j * 128:(j + 1) * 128],
                            rhs=cact_bf[:, dc:dc + 1], start=(dc == 0), stop=(dc == DC - 1))
        p.I("dve", "tensor_tensor", out=modT.v().re("p l e -> p (l e)"), in0=psm[:, 0:L * 3 * DC],
            in1=abT.v().re("p l e -> p (l e)"), op=ALU.add)
        p.I("dve", "scalar_tensor_tensor", out=gsT.v(), in0=modT[:, :, DC:2 * DC], scalar=1.0, in1=ngT.v(),
            op0=ALU.add, op1=ALU.mult)

    def norm_phase(src, dst_tiles, g_of_dc, sh_of_dc, out_dram=None):
        TA = cfg.TA
        with p.scope():
            xt = [p.sb("xt", [128, DC, TA], F32) for _ in range(2)]
            sq = [p.sb("sq", [128, TA], F32) for _ in range(2)]
            rstd = [p.sb("rstd", [128, TA], F32) for _ in range(2)]
            tmp = [p.sb("ntmp", [128, TA], F32) for _ in range(4)]
            pss = [p.ps("psn", [128, 512], F32) for _ in range(2)]
            k = 0
            for ta in range(S // TA):
                x_ = xt[ta % 2]
                ts = slice(ta * TA, (ta + 1) * TA)
                p.dma("sp", x_.v(), src.re("(dc p) s -> p dc s", p=128)[:, :, ts])
                ps_ = pss[ta % 2]
                for dc in range(DC):
                    s_ = sq[dc % 2]
                    if dc % 2 == 0:
                        p.I("act", "activation", out=s_.v(), in_=x_[:, dc, :], func=AF.Square)
                    else:
                        p.I("dve", "tensor_tensor", out=s_.v(), in0=x_[:, dc, :], in1=x_[:, dc, :], op=ALU.mult)
                    p.I("pe", "matmul", out=ps_[:, 0:TA], lhsT=ones32, rhs=s_.v(), start=(dc == 0), stop=(dc == DC - 1))
                r_ = rstd[ta % 2]
                p.I("act", "activation", out=r_.v(), in_=ps_[:, 0:TA], func=AF.Sqrt, bias=NORM_EPS, scale=1.0 / D)
                p.I("dve", "reciprocal", out=r_.v(), in_=r_.v())
                for dc in range(DC):
                    t_ = tmp[k % 4]
                    k += 1
                    p.I("dve", "scalar_tensor_tensor", out=t_.v(), in0=x_[:, dc, :],
                        scalar=g_of_dc(dc), in1=r_.v(), op0=ALU.mult, op1=ALU.mult)
                    if out_dram is None:
                        p.I("act", "activation", out=dst_tiles[dc][:, ts], in_=t_.v(), func=AF.Identity,
                            bias=sh_of_dc(dc), scale=1.0)
                    else:
                        p.dma("sp", out_dram[dc * 128:(dc + 1) * 128, ts], t_.v(), acc_w=True)

    wring = {}

    def out_proj(yg_tiles, w_dram, nci, gate_of_ft, src, dst):
        with p.scope():
            wts = [p.sb("wo", [128, nci, 512], BF16) for _ in range(2)]
            pso = [p.ps("pso", [128, 512], F32) for _ in range(4)]
            xin = [p.sb("xin", [128, TT], F32) for _ in range(4)]
            wv = w_dram.re("(ci p) f -> p ci f", p=128)
            k = 0
            G = 4 if (D // 128) % 4 == 0 else 2
            for fg in range(D // (128 * G)):
                wt = wts[fg % 2]
                p.dma("pool", wt[:, :, 0:G * 128], wv[:, :, fg * G * 128:(fg + 1) * G * 128])
                for j in range(G):
                    ft = fg * G + j
                    for tt in range(NT):
                        ts = slice(tt * TT, (tt + 1) * TT)
                        ps_ = pso[k % 4]
                        x_ = xin[k % 4]
                        k += 1
                        p.dma("sp", x_.v(), src[ft * 128:(ft + 1) * 128, ts])
                        for ci in range(nci):
                            p.I("pe", "matmul", out=ps_[:, 0:TT], lhsT=wt[:, ci, j * 128:(j + 1) * 128],
                                rhs=yg_tiles[ci][:, ts], start=(ci == 0), stop=(ci == nci - 1))
                        p.I("dve", "scalar_tensor_tensor", out=x_.v(), in0=ps_[:, 0:TT], scalar=gate_of_ft(ft),
                            in1=x_.v(), op0=ALU.mult, op1=ALU.add)
                        p.dma("sp", dst[ft * 128:(ft + 1) * 128, ts], x_.v(), acc_w=True)

    def proj_fm(xs_tiles, wv, f0, nft, sink, wts, pss, kdim=DC):
        k = 0
        G = 4 if nft % 4 == 0 else (2 if nft % 2 == 0 else 1)
        gi = 0
        for fg in range(nft // G):
            wt = wts[gi % 2]
            gi += 1
            p.dma("pool", wt[:, :, 0:G * 128], wv[:, :, f0 + fg * G * 128:f0 + (fg + 1) * G * 128])
            for j in range(G):
                ft = fg * G + j
                for tt in range(NT):
                    ts = slice(tt * TT, (tt + 1) * TT)
                    ps_ = pss[k % len(pss)]
                    k += 1
                    for dc in range(kdim):
                        p.I("pe", "matmul", out=ps_[:, 0:TT], lhsT=wt[:, dc, j * 128:(j + 1) * 128],
                            rhs=xs_tiles[dc][:, ts], start=(dc == 0), stop=(dc == kdim - 1))
                    sink(ft, tt, ps_)


    def proj_tm(hT, wv, f0, ngroups, gw, sink, wts, pss):
        k = 0
        for gi in range(ngroups):
            wt = wts[gi % 2]
            p.dma("pool", wt[:, :, 0:gw], wv[:, :, f0 + gi * gw:f0 + (gi + 1) * gw])
            for tk in range(S // 128):
                ps_ = pss[k % len(pss)]
                k += 1
                for dc in range(DC):
                    p.I("pe", "matmul", out=ps_[:, 0:gw], lhsT=hT[dc][:, tk * 128:(tk + 1) * 128], rhs=wt[:, dc, 0:gw],
                        start=(dc == 0), stop=(dc == DC - 1))
                sink(gi, tk, ps_)

    def gla_layer(l, j, src, dst):
        NCH = S // 128
        with p.scope():
            yg = [p.sb("ygg", [128, S], BF16) for _ in range(VW // 128)]
            lowT = p.sb("lowT", [GR, S], BF16)
            gw2 = p.sb("gw2", [GR, KW], BF16)
            nb = p.sb("gnb", [128, KW // 128], F32)
            hgb = p.sb("hgb", [128, DV], F32)
            p.dma("pool", gw2.v(), gl_w2.v()[j])
            p.dma("sp", nb.v(), gl_nbT.v()[:, j])
            p.dma("sp", hgb.v(), gl_hgb.v()[:, j])
            p.I("dve", "tensor_scalar", out=nb.v(), in0=nb.v(), scalar1=-1.0, scalar2=None, op0=ALU.mult)
            wv = gl_in.v()[j].re("(dc p) f -> p dc f", p=128)
            with p.scope():
                hT = [p.sb("hT", [128, S], BF16) for _ in range(DC)]
                norm_phase(src, hT, lambda dc: gsT[:, l, dc:dc + 1], lambda dc: modT[:, l, dc:dc + 1])
                wts = [p.sb("wi", [128, DC, 512], BF16) for _ in range(2)]
                wl = p.sb("wl", [128, DC, GR], BF16)
                pss = [p.ps("psp", [128, 512], F32) for _ in range(4)]
                stg = [p.sb("stg", [128, 512], F32) for _ in range(4)]
                sk = [0]

                def evac(ps_ap, dst_ap, width):
                    s_ = stg[sk[0] % 4]
                    e = "act" if sk[0] % 2 == 0 else "dve"
                    sk[0] += 1
                    if e == "act":
                        p.I("act", "copy", out=s_[:, 0:width], in_=ps_ap)
                    else:
                        p.I("dve", "tensor_copy", out=s_[:, 0:width], in_=ps_ap)
                    p.dma("sp", dst_ap, s_[:, 0:width], acc_w=True)

                for c in range(2):
                    proj_fm(hT, wv, c * KW, KW // 128,
                            lambda ft, tt, ps_, c=c: evac(ps_[:, 0:TT], gqk[c][ft][:, tt * TT:(tt + 1) * TT], TT), wts, pss)
                for c in range(2):
                    proj_tm(hT, wv, 2 * KW + c * VW, GH, DV,
                            lambda gi, tk, ps_, c=c: evac(ps_[:, 0:DV], gvg[c][gi][tk * 128:(tk + 1) * 128, :], DV), wts, pss)
                p.dma("pool", wl.v(), wv[:, :, 2 * KW + 2 * VW:2 * KW + 2 * VW + GR])
                for tt in range(NT):
                    ts = slice(tt * TT, (tt + 1) * TT)
                    ps_ = pss[tt % 4]
                    for dc in range(DC):
                        p.I("pe", "matmul", out=ps_[0:GR, 0:TT], lhsT=wl[:, dc, :], rhs=hT[dc][:, ts],
                            start=(dc == 0), stop=(dc == DC - 1))
                    p.I("act", "copy", out=lowT[:, ts], in_=ps_[0:GR, 0:TT])
            if cfg.stop <= 2:
                return False
            with p.scope():
                ldq = [p.sb("ldq", [128, S], F32) for _ in range(KC)]
                ldk = [p.sb("ldk", [128, S], F32) for _ in range(KC)]
                QT = [p.sb("QT", [128, S], BF16) for _ in range(KC)]
                KT = [p.sb("KT", [128, S], BF16) for _ in range(KC)]
                eb = [p.sb("eb", [128, S], F32) for _ in range(KC)]
                t1 = p.sb("gt1", [128, S], F32)
                t2 = p.sb("gt2", [128, S], F32)
                S32 = [p.sb("S32", [128, DV], F32) for _ in range(KC)]
                Sb = [p.sb("Sb", [128, DV], BF16) for _ in range(KC)]
                v32 = [p.sb("v32", [128, DV], F32) for _ in range(2)]
                g32 = [p.sb("g32", [128, DV], F32) for _ in range(2)]
                Vb = [p.sb("Vb", [128, DV], BF16) for _ in range(2)]
                SG = [p.sb("SG", [128, DV], F32) for _ in range(2)]
                KTM = [p.sb("KTM", [128, KC * 128], BF16) for _ in range(2)]
                ST = [p.sb("ST", [128, 128], BF16) for _ in range(2)]
                junk = p.sb("junk", [128, DV], F32)
                ssq = [p.sb("ssq", [128, 1], F32) for _ in range(2)]
                y32 = [p.sb("y32", [128, DV], F32) for _ in range(2)]
                yb = [p.sb("yb", [128, DV], BF16) for _ in range(2)]
                psP = p.ps("gpsP", [128, 512], F32)
                psTr = p.ps("gpsTr", [128, 8, 128], BF16)
                psS = p.ps("gpsS", [128, 512], F32)
                psO = [p.ps("gpsO", [128, 512], F32) for _ in range(2)]
                psSt = [p.ps("gpsSt", [128, 512], F32) for _ in range(2)]
                psTr2 = p.ps("gpsTr2", [128, 8, 128], BF16)
                for hh in range(GH):
                    for kc in range(KC):
                        ft = hh * KC + kc
                        p.dma("sp", ldq[kc].v(), gqk[0][ft].v())
                        p.dma("sp", ldk[kc].v(), gqk[1][ft].v())
                        for tt in range(NT):
                            ts = slice(tt * TT, (tt + 1) * TT)
                            p.I("pe", "matmul", out=psP[:, 0:TT], lhsT=gw2[:, ft * 128:(ft + 1) * 128], rhs=lowT[:, ts], start=True, stop=True)
                            p.I("act", "activation", out=t1[:, ts], in_=psP[:, 0:TT], func=AF.Exp, bias=nb[:, ft:ft + 1], scale=-1.0)
                        p.I("act", "activation", out=t1.v(), in_=t1.v(), func=AF.Ln, bias=1.0, scale=1.0)
                        p.I("dve", "tensor_scalar", out=t1.v(), in0=t1.v(), scalar1=-1.0 / 16.0, scalar2=None, op0=ALU.mult)
                        p.I("dve", "tensor_tensor_scan", out=t2.v(), data0=rmask128.v(), data1=t1.v(), initial=0.0,
                            op0=ALU.mult, op1=ALU.add)
                        p.I("act", "activation", out=eb[kc].v(), in_=t2.v(), func=AF.Exp)
                        p.I("act", "activation", out=t1.v(), in_=t2.v(), func=AF.Exp, scale=-1.0)
                        p.I("dve", "scalar_tensor_tensor", out=QT[kc].v(), in0=ldq[kc].v(), scalar=float(DK) ** -0.5, in1=eb[kc].v(),
                            op0=ALU.mult, op1=ALU.mult)
                        p.I("dve", "tensor_tensor", out=KT[kc].v(), in0=ldk[kc].v(), in1=t1.v(), op=ALU.mult)
                        p.I("dve", "memset", ap=S32[kc].v(), constant=0.0)
                        p.I("dve", "memset", ap=Sb[kc].v(), constant=0.0)
                    def gF(n):
                            ns = slice(n * 128, (n + 1) * 128)
                            b2 = n % 2
                            p.dma("sp", v32[b2].v(), gvg[0][hh][ns, :])
                            p.dma("sp", g32[b2].v(), gvg[1][hh][ns, :])
                            p.I("act", "copy", out=Vb[b2].v(), in_=v32[b2].v())
                            p.I("act", "activation", out=SG[b2].v(), in_=g32[b2].v(), func=AF.Silu)
                            for kc in range(KC):
                                p.I("pe", "transpose", out=psTr[:, kc, :], in_=KT[kc][:, ns], identity=ident_bf.v())
                            p.I("dve", "tensor_copy", out=KTM[b2].v().re("p (k x) -> p k x", x=128), in_=psTr[:, 0:KC, :])
                            for kc in range(KC):
                                p.I("pe", "matmul", out=psS[:, 0:128], lhsT=KT[kc][:, ns], rhs=QT[kc][:, ns], start=(kc == 0), stop=(kc == KC - 1))
                            p.I("dve", "tensor_tensor", out=ST[b2].v(), in0=psS[:, 0:128], in1=iu128, op=ALU.mult)

                    def gB(n):
                            ns = slice(n * 128, (n + 1) * 128)
                            b2 = n % 2
                            po = psO[b2]
                            p.I("pe", "matmul", out=po[:, 0:DV], lhsT=ST[b2].v(), rhs=Vb[b2].v(), start=True, stop=False)
                            for kc in range(KC):
                                p.I("pe", "matmul", out=po[:, 0:DV], lhsT=QT[kc][:, ns], rhs=Sb[kc].v(), start=False, stop=(kc == KC - 1))
                            for kc in range(KC):
                                pst = psSt[kc % 2]
                                p.I("pe", "matmul", out=pst[:, 0:DV], lhsT=KTM[b2][:, kc * 128:(kc + 1) * 128], rhs=Vb[b2].v(), start=True, stop=True)
                                dcol = eb[kc][:, n * 128 + 127:n * 128 + 128]
                                p.I("dve", "tensor_scalar", out=S32[kc].v(), in0=S32[kc].v(), scalar1=dcol, scalar2=None, op0=ALU.mult)
                                p.I("dve", "scalar_tensor_tensor", out=S32[kc].v(), in0=pst[:, 0:DV], scalar=dcol, in1=S32[kc].v(),
                                    op0=ALU.mult, op1=ALU.add)
                                p.I("act", "copy", out=Sb[kc].v(), in_=S32[kc].v())
                            p.I("act", "activation", out=junk.v(), in_=po[:, 0:DV], func=AF.Square, accum_out=ssq[b2].v())
                            p.I("act", "activation", out=ssq[b2].v(), in_=ssq[b2].v(), func=AF.Sqrt, bias=NORM_EPS, scale=1.0 / DV)
                            p.I("dve", "reciprocal", out=ssq[b2].v(), in_=ssq[b2].v())
                            p.I("dve", "scalar_tensor_tensor", out=y32[b2].v(), in0=po[:, 0:DV], scalar=ssq[b2].v(), in1=hgb.v(),
                                op0=ALU.mult, op1=ALU.mult)
                            p.I("dve", "tensor_tensor", out=yb[b2].v(), in0=y32[b2].v(), in1=SG[b2].v(), op=ALU.mult)
                            for vc in range(VC):
                                p.I("pe", "transpose", out=psTr2[:, vc, :], in_=yb[b2][:, vc * 128:(vc + 1) * 128], identity=ident_bf.v())
                            for vc in range(VC):
                                p.I("act" if vc % 2 == 0 else "dve", "copy" if vc % 2 == 0 else "tensor_copy",
                                    out=yg[hh * VC + vc][:, ns], in_=psTr2[:, vc, :])

                    gF(0)
                    for n in range(NCH):
                        if n + 1 < NCH:
                            gF(n + 1)
                        gB(n)
            if cfg.stop <= 9:
                return False
            out_proj(yg, gl_out.v()[j], VW // 128, lambda ft: modT[:, l, 2 * DC + ft:2 * DC + ft + 1], src, dst)
            return True


    def ssd_layer(l, j, src, nextbuf):
        NCH = S // 128
        srcv = [src]
        HN = SH
        maskb = cst[:, 7, :]
        ident32 = cst[:, 0, :]
        with p.scope():
            dtT = p.sb("dtT", [128, S], F32)
            acT = p.sb("acT", [128, S], F32)
            nacT = p.sb("nacT", [128, S], F32)
            hv = p.sb("hv", [64, 2], F32)
            dsb = p.sb("dsb", [128, SH], F32)
            cw = p.sb("cw", [128, CW // 128, 4], F32)
            cbv = p.sb("cbv", [128, CW // 128], F32)
            wtm = p.sb("wtm", [128, NCH, 128], F32)
            eatm = p.sb("eatm", [128, NCH, 64], F32)
            decbc = p.sb("decbc", [128, NCH, 64], F32)
            p.dma("sp", hv.v(), sd_hv.v()[:, j])
            p.dma("sp", dsb.v(), sd_dsb.v()[:, j])
            p.dma("sp", cw.v(), sd_cwT.v()[:, j])
            p.dma("sp", cbv.v(), sd_cbT.v()[:, j])
            wv = sd_in.v()[j].re("(dc p) f -> p dc f", p=128)
            with p.scope():
                hT = [p.sb("hT", [128, S], BF16) for _ in range(DC)]
                norm_phase(src, hT, lambda dc: gsT[:, l, dc:dc + 1], lambda dc: modT[:, l, dc:dc + 1])
                wts = [p.sb("wi", [128, DC, 512], BF16) for _ in range(2)]
                wdt = p.sb("wdt", [128, DC, SH], BF16)
                pss = [p.ps("psp", [128, 512], F32) for _ in range(4)]
                stg = [p.sb("stg", [128, 512], F32) for _ in range(4)]
                xst = [p.sb("xst", [128, S + 3], F32) for _ in range(2)]
                acc = [p.sb("cacc", [128, S], F32) for _ in range(2)]
                sk = [0]

                def zsink(gi, tk, ps_):
                    s_ = stg[sk[0] % 4]
                    e = "act" if sk[0] % 2 == 0 else "dve"
                    sk[0] += 1
                    if e == "act":
                        p.I("act", "copy", out=s_.v(), in_=ps_.v())
                    else:
                        p.I("dve", "tensor_copy", out=s_.v(), in_=ps_.v())
                    p.dma("sp", sz[gi][tk * 128:(tk + 1) * 128, :], s_.v(), acc_w=True)

                p.mark('sd_projz_start')
                proj_tm(hT, wv, 0, SG, 512, zsink, wts, pss)
                p.mark('sd_projx_start')
                for b in range(2):
                    p.I("dve", "memset", ap=xst[b][:, 0:3], constant=0.0)

                def csink(ft, tt, ps_):
                    x_ = xst[ft % 2]
                    e = "act" if (ft + tt) % 2 == 0 else "dve"
                    if e == "act":
                        p.I("act", "copy", out=x_[:, 3 + tt * TT:3 + (tt + 1) * TT], in_=ps_[:, 0:TT])
                    else:
                        p.I("dve", "tensor_copy", out=x_[:, 3 + tt * TT:3 + (tt + 1) * TT], in_=ps_[:, 0:TT])
                    if tt == NT - 1:
                        a_ = acc[ft % 2]
                        p.I("act", "mul", out=a_.v(), in_=x_[:, 3:S + 3], mul=cw[:, ft, 3:4])
                        for kk_ in range(3):
                            p.I("dve", "scalar_tensor_tensor", out=a_.v(), in0=x_[:, kk_:S + kk_], scalar=cw[:, ft, kk_:kk_ + 1],
                                in1=a_.v(), op0=ALU.mult, op1=ALU.add)
                        p.I("act", "activation", out=a_.v(), in_=a_.v(), func=AF.Silu, bias=cbv[:, ft:ft + 1], scale=1.0)
                        p.dma("sp", sxbc[ft].v(), a_.v())

                proj_fm(hT, wv, SW, CW // 128, csink, wts, pss)
                p.dma("pool", wdt.v(), wv[:, :, SW + CW:SW + CW + SH])
                p.I("dve", "memset", ap=dtT.v(), constant=0.0)
                p.I("dve", "memset", ap=acT.v(), constant=0.0)
                for tt in range(NT):
                    ts = slice(tt * TT, (tt + 1) * TT)
                    ps_ = pss[tt % 4]
                    for dc in range(DC):
                        p.I("pe", "matmul", out=ps_[0:HN, 0:TT], lhsT=wdt[:, dc, :], rhs=hT[dc][:, ts], start=(dc == 0), stop=(dc == DC - 1))
                    p.I("act", "activation", out=dtT[0:HN, ts], in_=ps_[0:HN, 0:TT], func=AF.Exp, bias=hv[0:HN, 0:1], scale=1.0)
                p.I("act", "activation", out=dtT[0:HN, :], in_=dtT[0:HN, :], func=AF.Ln, bias=1.0, scale=1.0)
            if cfg.stop <= 2:
                return False
            p.mark('sd_dt_start')
            with p.scope():
                eaT = p.sb("eaT", [128, S], F32)
                na = p.sb("na", [64, 1], F32)
                t1 = p.sb("st1", [128, S], F32)
                Dg = p.sb("Dg", [64, 64], F32)
                psq = [p.ps("spsq", [128, 512], F32) for _ in range(2)]
                p.I("act", "activation", out=na[0:HN, :], in_=hv[0:HN, 1:2], func=AF.Exp)
                p.I("dve", "tensor_scalar", out=na[0:HN, :], in0=na[0:HN, :], scalar1=-1.0, scalar2=None, op0=ALU.mult)
                p.I("dve", "tensor_scalar", out=t1[0:HN, :], in0=dtT[0:HN, :], scalar1=na[0:HN, 0:1], scalar2=None, op0=ALU.mult)
                p.I("dve", "tensor_tensor_scan", out=acT[0:HN, :], data0=rmask128[0:HN, :], data1=t1[0:HN, :], initial=0.0,
                    op0=ALU.mult, op1=ALU.add)
                p.I("dve", "memset", ap=nacT.v(), constant=0.0)
                p.I("dve", "memset", ap=eaT.v(), constant=0.0)
                p.I("dve", "tensor_scalar", out=nacT[0:HN, :], in0=acT[0:HN, :], scalar1=-1.0, scalar2=None, op0=ALU.mult)
                p.I("act", "activation", out=eaT[0:HN, :], in_=acT[0:HN, :], func=AF.Exp)
                for n in range(NCH):
                    ns = slice(n * 128, (n + 1) * 128)
                    last = acT[0:HN, n * 128 + 127:n * 128 + 128]
                    p.I("act", "activation", out=t1[0:HN, ns], in_=acT[0:HN, ns], func=AF.Exp, bias=last, scale=-1.0)
                    p.I("dve", "tensor_tensor", out=dtT[64:64 + HN, ns], in0=t1[0:HN, ns], in1=dtT[0:HN, ns], op=ALU.mult)
                    ps_ = psq[n % 2]
                    p.I("pe", "transpose", out=ps_[:, 0:128], in_=dtT[:, ns], identity=ident32)
                    p.I("pe", "transpose", out=ps_[:, 128:256], in_=eaT[:, ns], identity=ident32)
                    p.I("dve", "tensor_scalar", out=Dg[0:HN, 0:HN], in0=ident32[0:HN, 0:HN], scalar1=last, scalar2=None, op0=ALU.mult)
                    p.I("pe", "matmul", out=ps_[:, 256:256 + HN], lhsT=ones32[0:HN, :], rhs=Dg[0:HN, 0:HN], start=True, stop=True)
                    p.I("act", "copy", out=wtm[:, n, :], in_=ps_[:, 0:128])
                    p.I("dve", "tensor_copy", out=eatm[:, n, :], in_=ps_[:, 128:192])
                    p.I("act", "activation", out=decbc[:, n, 0:HN], in_=ps_[:, 256:256 + HN], func=AF.Exp)
            if cfg.stop <= 3:
                return False
            nhalf = 2 if SG >= 2 else 1
            GPH = SG // nhalf
            yg = [p.sb("ygs", [128, S], BF16) for _ in range(GPH * 4)]
            for half in range(nhalf):
              p.mark('sd_scan_start_h%d' % half)
              with p.scope():
                xg = [p.sb("xg", [128, 4, 128], F32) for _ in range(2)]
                bg = p.sb("bg", [128, S], F32)
                BT = p.sb("BT", [128, S], BF16)
                CT = p.sb("CT", [128, S], BF16)
                ngb = p.sb("ngb", [128, 512], F32)
                prev32 = p.sb("prev32", [128, 512], F32)
                prevb = p.sb("prevb", [128, 512], BF16)
                xtm = [p.sb("xtm", [128, 512], F32) for _ in range(2)]
                xc = [p.sb("xc", [128, 512], BF16) for _ in range(2)]
                xcd = [p.sb("xcd", [128, 512], BF16) for _ in range(2)]
                Btm = [p.sb("Btm", [128, 128], BF16) for _ in range(2)]
                cbT = [p.sb("cbT", [128, 128], BF16) for _ in range(2)]
                eM = [p.sb("eM", [128, 4, 128], F32) for _ in range(2)]
                Mm = [[p.sb("Mm", [128, 4, 128], BF16) for _ in range(2)] for _b in range(2)]
                maskb_bf = p.sb("maskb_bf", [128, 128], BF16)
                p.I("dve", "tensor_copy", out=maskb_bf.v(), in_=maskb)
                z32 = [p.sb("z32", [128, 512], F32) for _ in range(2)]
                ty = [p.sb("ty", [128, 512], F32) for _ in range(2)]
                tu = [p.sb("tu", [128, 512], F32) for _ in range(2)]
                junk = p.sb("sjunk", [128, 512], F32)
                sgz = [p.sb("sgz", [128, 512], F32) for _ in range(2)]
                ssq = [p.sb("sssq", [128, 1], F32) for _ in range(2)]
                ybf = [p.sb("ybf", [128, 512], BF16) for _ in range(2)]
                psX = p.ps("spsX", [128, 512], F32)
                psB = p.ps("spsB", [128, 8, 128], BF16)
                psC = p.ps("spsC", [128, 512], F32)
                psM = [p.ps("spsM", [128, 4, 128], F32) for _ in range(2)]
                psY = p.ps("spsY", [128, 512], F32)
                psYo = p.ps("spsYo", [128, 512], F32)
                psSt = p.ps("spsSt", [128, 512], F32)
                for g in range(half * GPH, (half + 1) * GPH):
                    p.dma("sp", bg.v(), sxbc[SW // 128 + g].v())
                    p.I("act", "copy", out=BT.v(), in_=bg.v())
                    p.dma("sp", bg.v(), sxbc[SW // 128 + SG + g].v())
                    p.I("dve", "tensor_copy", out=CT.v(), in_=bg.v())
                    p.dma("sp", ngb.v(), sd_ngb.v()[:, j, g * 512:(g + 1) * 512])
                    p.I("dve", "memset", ap=prev32.v(), constant=0.0)
                    p.I("dve", "memset", ap=prevb.v(), constant=0.0)
                    def partA(n):
                            ns = slice(n * 128, (n + 1) * 128)
                            b2 = n % 2
                            hs8 = slice(g * 8, (g + 1) * 8)
                            p.dma("sp", z32[b2].v(), sz[g][ns, :])
                            for i4 in range(4):
                                p.dma("sp", xg[b2][:, i4, :], sxbc[g * 4 + i4][:, ns])
                            for i4 in range(4):
                                p.I("pe", "transpose", out=psX[:, i4 * 128:(i4 + 1) * 128], in_=xg[b2][:, i4, :], identity=ident32)
                            p.I("act", "copy", out=xtm[b2].v(), in_=psX.v())
                            x3 = xtm[b2].v().re("p (h x) -> p h x", x=64)
                            p.I("dve", "tensor_tensor", out=xc[b2].v().re("p (h x) -> p h x", x=64), in0=x3,
                                in1=wtm[:, n, g * 8:(g + 1) * 8].bc([2], [128, 8, 64]), op=ALU.mult)
                            p.I("dve", "tensor_tensor", out=xcd[b2].v().re("p (h x) -> p h x", x=64), in0=x3,
                                in1=wtm[:, n, 64 + g * 8:64 + (g + 1) * 8].bc([2], [128, 8, 64]), op=ALU.mult)
                            p.I("dve", "tensor_tensor", out=tu[b2].v().re("p (h x) -> p h x", x=64), in0=x3,
                                in1=dsb[:, hs8].bc([2], [128, 8, 64]), op=ALU.mult)
                            p.I("act", "activation", out=sgz[b2].v(), in_=z32[b2].v(), func=AF.Silu)
                            p.I("pe", "matmul", out=psC[:, 128:256], lhsT=BT[:, ns], rhs=ident_bf.v(), start=True, stop=True)
                            p.I("pe", "matmul", out=psC[:, 0:128], lhsT=BT[:, ns], rhs=CT[:, ns], start=True, stop=True)
                            p.I("act", "copy", out=Btm[b2].v(), in_=psC[:, 128:256])
                            p.I("act", "copy", out=cbT[b2].v(), in_=psC[:, 0:128])
                            for hq in range(2):
                                pm = psM[hq]
                                for h4 in range(4):
                                    h = g * 8 + hq * 4 + h4
                                    sel = ident32[0:HN, h:h + 1].bc([], [HN, 128])
                                    p.I("pe", "matmul", out=pm[:, h4, :], lhsT=sel, rhs=acT[0:HN, ns], start=True, stop=False)
                                    p.I("pe", "matmul", out=pm[:, h4, :], lhsT=nacT[0:HN, ns], rhs=sel, start=False, stop=False)
                                    p.I("pe", "matmul", out=pm[:, h4, :], lhsT=ident_bf.v(), rhs=maskb_bf.v(), start=False, stop=True)
                                p.I("act", "activation", out=eM[hq].v(), in_=pm.v(), func=AF.Exp)
                                p.I("dve", "tensor_tensor", out=Mm[b2][hq].v(), in0=eM[hq].v(),
                                    in1=cbT[b2].v().bc([1], [128, 4, 128]), op=ALU.mult)

                    def partB1(n):
                            ns = slice(n * 128, (n + 1) * 128)
                            b2 = n % 2
                            hs8 = slice(g * 8, (g + 1) * 8)
                            x3 = xtm[b2].v().re("p (h x) -> p h x", x=64)
                            for hq in range(2):
                                for h4 in range(4):
                                    hl = hq * 4 + h4
                                    p.I("pe", "matmul", out=psY[:, hl * 64:(hl + 1) * 64], lhsT=Mm[b2][hq][:, h4, :],
                                        rhs=xc[b2][:, hl * 64:(hl + 1) * 64], start=True, stop=True)
                            p.I("pe", "matmul", out=psYo.v(), lhsT=CT[:, ns], rhs=prevb.v(), start=True, stop=True)
                            p.I("pe", "matmul", out=psSt.v(), lhsT=Btm[b2].v(), rhs=xcd[b2].v(), start=True, stop=True)
                            t_ = ty[b2]
                            u_ = tu[b2]
                            t3 = t_.v().re("p (h x) -> p h x", x=64)
                            u3 = u_.v().re("p (h x) -> p h x", x=64)
                            p32 = prev32.v().re("p (h x) -> p h x", x=64)
                            p.I("dve", "tensor_tensor", out=p32, in0=p32, in1=decbc[:, n, hs8].bc([2], [128, 8, 64]), op=ALU.mult)
                            p.I("dve", "tensor_tensor", out=prev32.v(), in0=psSt.v(), in1=prev32.v(), op=ALU.add)
                            p.I("act", "copy", out=prevb.v(), in_=prev32.v())
                            p.I("dve", "tensor_tensor", out=t3, in0=psYo.v().re("p (h x) -> p h x", x=64),
                                in1=eatm[:, n, hs8].bc([2], [128, 8, 64]), op=ALU.mult)
                            p.I("dve", "tensor_tensor", out=t_.v(), in0=psY.v(), in1=t_.v(), op=ALU.add)
                            p.I("dve", "tensor_tensor", out=t_.v(), in0=t_.v(), in1=u_.v(), op=ALU.add)
                            p.I("dve", "tensor_tensor", out=t_.v(), in0=t_.v(), in1=sgz[b2].v(), op=ALU.mult)
                            p.I("act", "activation", out=junk.v(), in_=t_.v(), func=AF.Square, accum_out=ssq[b2].v())
                            p.I("act", "activation", out=ssq[b2].v(), in_=ssq[b2].v(), func=AF.Sqrt, bias=1e-5, scale=1.0 / 512)
                    def partB2(n):
                            ns = slice(n * 128, (n + 1) * 128)
                            b2 = n % 2
                            t_ = ty[b2]
                            p.I("dve", "reciprocal", out=ssq[b2].v(), in_=ssq[b2].v())
                            p.I("dve", "scalar_tensor_tensor", out=ybf[b2].v(), in0=t_.v(), scalar=ssq[b2].v(), in1=ngb.v(),
                                op0=ALU.mult, op1=ALU.mult)
                            for i4 in range(4):
                                p.I("pe", "transpose", out=psB[:, 4 + i4, :], in_=ybf[b2][:, i4 * 128:(i4 + 1) * 128], identity=ident_bf.v())
                            for i4 in range(4):
                                p.I("act", "copy", out=yg[(g - half * GPH) * 4 + i4][:, ns], in_=psB[:, 4 + i4, :])

                    partA(0)
                    if NCH > 1:
                        partA(1)
                    for n in range(NCH):
                        partB1(n)
                        if n + 2 < NCH:
                            partA(n + 2)
                        partB2(n)
              if cfg.stop <= 9:
                  return False
              p.mark('sd_outproj_start_h%d' % half)
              nci = GPH * 4
              dsth = nextbuf()
              out_proj(yg, sd_out.v()[j][half * nci * 128:(half + 1) * nci * 128, :], nci,
                       lambda ft: modT[:, l, 2 * DC + ft:2 * DC + ft + 1], srcv[0], dsth.v())
              srcv[0] = dsth.v()
              p.mark('sd_outproj_end_h%d' % half)
            return srcv[0]

    def rwkv_layer(l, j, src, dst):
        CB, TB = cfg.CB, cfg.TB
        NCHB = TB // 64
        with p.scope():
            yg = [p.sb("yg", [128, S], BF16) for _ in range(HP)]
            xs = yg
            lw1 = p.sb("lw1", [R, S], BF16)
            la1 = p.sb("la1", [R, S], BF16)
            vec = p.sb("rvec", [128, 7, HP], F32)
            omka = p.sb("omka", [128, HP], F32)
            p.dma("sp", vec.v(), rw_vecT.v()[:, j])
            p.I("dve", "tensor_scalar", out=omka.v(), in0=vec[:, 3, :], scalar1=-1.0, scalar2=1.0,
                op0=ALU.mult, op1=ALU.add)
            with p.scope():
                hT = [p.sb("hT", [128, S], BF16) for _ in range(DC)]
                mu = p.sb("mu", [128, 6, DC], F32)
                omm = p.sb("omm", [128, 6, DC], F32)
                p.dma("sp", mu.v(), rw_muT.v()[:, j])
                p.I("dve", "tensor_scalar", out=omm.v(), in0=mu.v(), scalar1=-1.0, scalar2=1.0,
                    op0=ALU.mult, op1=ALU.add)
                p.mark('rw_norm_start')
                norm_phase(src, hT, lambda dc: gsT[:, l, dc:dc + 1], lambda dc: modT[:, l, dc:dc + 1])
                p.mark('rw_proj_start')
                if cfg.stop <= 1:
                    return False
                wts = [p.sb("wi", [128, DC, 512], BF16) for _ in range(2)]
                w1t = p.sb("w1t", [128, DC, R], BF16)
                pss = [p.ps("psp", [128, 512], F32) for _ in range(4)]
                stg = [p.sb("stg", [128, TT], F32) for _ in range(4)]
                wv = rw_in.v()[j].re("(dc p) f -> p dc f", p=128)
                sk = [0]

                def mix(c):
                    for dc in range(DC):
                        p.I("dve", "memset", ap=xs[dc][:, 0:1], constant=0.0)
                        p.I("act", "mul", out=xs[dc][:, 1:S], in_=hT[dc][:, 0:S - 1], mul=mu[:, c, dc:dc + 1])
                        p.I("dve", "scalar_tensor_tensor", out=xs[dc].v(), in0=hT[dc].v(), scalar=omm[:, c, dc:dc + 1],
                            in1=xs[dc].v(), op0=ALU.mult, op1=ALU.add)

                import os as _os
                for c in range(4):
                    if not _os.environ.get("NOMIX") or c == 0:
                        mix(c)

                    def sink(ft, tt, ps_, c=c):
                        s_ = stg[sk[0] % 4]
                        e = "act" if sk[0] % 2 == 0 else "dve"
                        sk[0] += 1
                        if e == "act":
                            p.I("act", "copy", out=s_.v(), in_=ps_[:, 0:TT])
                        else:
                            p.I("dve", "tensor_copy", out=s_.v(), in_=ps_[:, 0:TT])
                        if not _os.environ.get("NOSTORE"):
                            p.dma("sp", projT[c][ft][:, tt * TT:(tt + 1) * TT], s_.v(), acc_w=True)

                    proj_fm(xs, wv, c * W, HP, sink, wts, pss)
                for c, (w1d, dstl, fn) in ((4, (rw_dw1, lw1, AF.Tanh)), (5, (rw_aw1, la1, AF.Copy))):
                    mix(c)
                    p.dma("pool", w1t.v(), w1d.v()[j].re("(dc p) r -> p dc r", p=128))
                    for tt in range(NT):
                        ts = slice(tt * TT, (tt + 1) * TT)
                        ps_ = pss[tt % 4]
                        for dc in range(DC):
                            p.I("pe", "matmul", out=ps_[0:R, 0:TT], lhsT=w1t[:, dc, :], rhs=xs[dc][:, ts],
                                start=(dc == 0), stop=(dc == DC - 1))
                        p.I("act", "activation", out=dstl[:, ts], in_=ps_[0:R, 0:TT], func=fn)
            if cfg.stop <= 2:
                return False
            p.mark('rw_scan_start')
            with p.scope():
                dw2 = p.sb("dw2", [R, W], BF16)
                aw2 = p.sb("aw2", [R, W], BF16)
                p.dma("pool", dw2.v(), rw_dw2.v()[j])
                p.dma("pool", aw2.v(), rw_aw2.v()[j])
                CBS, NSTR = 2, 2
                STR = []
                psTrS = p.ps("psTr", [128, 4, 2, 128], BF16)
                for si in range(NSTR):
                    pg_ = p.ps("PG", [128, 2, 512], F32)
                    xr_ = p.sb("Xr", [64, CBS * 2, 2, 64], BF16)
                    nxt_ = p.sb("NXT", [64, CBS * 2, 192], BF16)
                    mu_ = p.sb("MU", [64, CBS * 2, 128], BF16)
                    STR.append([dict(
                        BK=p.sb("BK", [128, CBS, 128], BF16), UV=p.sb("UV", [128, CBS, 128], BF16),
                        Xr=xr_, A_sb=p.sb("A_sb", [128, CBS * 2, 128], BF16), NXT=nxt_, MU=mu_,
                        GT=p.sb("GT", [128, CBS, 64], BF16), PpT=p.sb("PpT", [128, CBS, 64], BF16),
                        PG=pg_, psTr=psTrS, toff=si * CBS) for _par in range(2)])
                psS5 = p.ps("psS5", [128, 4, 128], F32)
                Tst = [p.sb("Tst", [128, 64], BF16) for _ in range(3)]
                psP1 = p.ps("psP", [128, 512], F32)
                psP = [psP1, psP1]
                psAV = p.ps("psAV", [128, CBS * 2, 128], F32)
                NTB = TB // TT if TB >= TT else 1
                TTB = min(TT, TB)
                tiref = [0]

                def item(hp, tb, SET):
                    hsl = slice(hp * 128, (hp + 1) * 128)
                    vcol = lambda i: vec[:, i, hp:hp + 1]
                    tbs = slice(tb * TB, (tb + 1) * TB)
                    ld, tm, BKT, KRT, KKVT = SET["ld"], SET["tm"], SET["BKT"], SET["KRT"], SET["KKVT"]
                    for c, nm in enumerate(("r", "k", "v", "g")):
                        p.dma("sp", ld[nm].v(), projT[c][hp][:, tbs])
                    r_, k_, v_, g_ = ld["r"], ld["k"], ld["v"], ld["g"]
                    if hp == 3:
                        p.mark('rw_prep_start_tb%d' % tb)
                    lw, cum, e1, e2, e3, a_, kk, kf, t1, t2, t3, bv, yT = (tm[n] for n in (
                        "lw", "cum", "e1", "e2", "e3", "a", "kk", "kf", "t1", "t2", "t3", "bv", "y"))
                    for tt in range(NTB):
                        ts = slice(tt * TTB, (tt + 1) * TTB)
                        gs_ = slice(tb * TB + tt * TTB, tb * TB + (tt + 1) * TTB)
                        ps_ = psP[0]
                        p.I("pe", "matmul", out=ps_[:, 0:TTB], lhsT=dw2[:, hsl], rhs=lw1[:, gs_], start=True, stop=True)
                        p.I("act", "activation", out=lw[:, ts], in_=ps_[:, 0:TTB], func=AF.Sigmoid, bias=vcol(0), scale=1.0)
                        ps_ = psP[1]
                        p.I("pe", "matmul", out=ps_[:, 0:TTB], lhsT=aw2[:, hsl], rhs=la1[:, gs_], start=True, stop=True)
                        p.I("act", "activation", out=a_[:, ts], in_=ps_[:, 0:TTB], func=AF.Sigmoid, bias=vcol(1), scale=1.0)
                    p.I("dve", "tensor_scalar", out=lw.v(), in0=lw.v(), scalar1=NEG_EXP_HALF, scalar2=None, op0=ALU.mult)
                    yield "P"
                    p.I("dve", "tensor_tensor_scan", out=cum.v(), data0=rmask[:, 0:TB], data1=lw.v(), initial=0.0,
                        op0=ALU.mult, op1=ALU.add)
                    yield "P"
                    p.I("act", "activation", out=e1.v(), in_=cum.v(), func=AF.Exp)
                    yield "P"
                    p.I("act", "activation", out=e2.v(), in_=cum.v(), func=AF.Exp, scale=-1.0)
                    yield "P"
                    p.I("dve", "tensor_tensor", out=t1.v(), in0=cum.v(), in1=lw.v(), op=ALU.subtract)
                    yield "P"
                    p.I("act", "activation", out=e3.v(), in_=t1.v(), func=AF.Exp)
                    yield "P"
                    p.I("act", "activation", out=t2.v(), in_=k_.v(), func=AF.Square, scale=vcol(2))
                    yield "P"
                    for tt in range(NTB):
                        ts = slice(tt * TTB, (tt + 1) * TTB)
                        ps_ = psP[tt % 2]
                        p.I("pe", "matmul", out=ps_[:, 0:TTB], lhsT=bones32, rhs=t2[:, ts], start=True, stop=True)
                        p.I("act", "activation", out=t3[:, ts], in_=ps_[:, 0:TTB], func=AF.Sqrt)
                    p.I("dve", "tensor_scalar", out=t3.v(), in0=t3.v(), scalar1=1e-12, scalar2=None, op0=ALU.max)
                    yield "P"
                    p.I("dve", "reciprocal", out=t3.v(), in_=t3.v())
                    yield "P"
                    p.I("dve", "scalar_tensor_tensor", out=kk.v(), in0=k_.v(), scalar=vcol(2), in1=t3.v(), op0=ALU.mult, op1=ALU.mult)
                    yield "P"
                    p.I("dve", "tensor_scalar", out=t1.v(), in0=a_.v(), scalar1=vcol(3), scalar2=omka[:, hp:hp + 1],
                        op0=ALU.mult, op1=ALU.add)
                    yield "P"
                    p.I("dve", "tensor_tensor", out=kf.v(), in0=k_.v(), in1=t1.v(), op=ALU.mult)
                    yield "P"
                    p.I("dve", "tensor_tensor", out=t2.v(), in0=kk.v(), in1=a_.v(), op=ALU.mult)
                    yield "P"
                    ch = lambda t: t.v().re("p (n c) -> p n c", c=64)
                    p.I("dve", "tensor_tensor", out=KRT[:, :, 1, :], in0=ch(r_), in1=ch(e1), op=ALU.mult)
                    yield "P"
                    p.I("dve", "tensor_tensor", out=BKT[:, :, 1, :], in0=ch(kf), in1=ch(e2), op=ALU.mult)
                    yield "P"
                    p.I("dve", "tensor_tensor", out=BKT[:, :, 0, :], in0=ch(t2), in1=ch(e2), op=ALU.mult)
                    yield "P"
                    p.I("dve", "tensor_tensor", out=KRT[:, :, 0, :], in0=ch(kk), in1=ch(e3), op=ALU.mult)
                    yield "P"
                    p.I("act", "copy", out=KKVT[:, :, 0, :], in_=KRT[:, :, 0, :])
                    yield "P"
                    p.I("act", "copy", out=KKVT[:, :, 1, :], in_=ch(v_))
                    yield "P"
                    p.I("dve", "scalar_tensor_tensor", out=t1.v(), in0=r_.v(), scalar=vcol(4), in1=kf.v(),
                        op0=ALU.mult, op1=ALU.mult)
                    yield "P"
                    for tt in range(NTB):
                        ts = slice(tt * TTB, (tt + 1) * TTB)
                        ps_ = psP[tt % 2]
                        p.I("pe", "matmul", out=ps_[:, 0:TTB], lhsT=bones32, rhs=t1[:, ts], start=True, stop=True)
                        p.I("dve", "tensor_tensor", out=bv[:, ts], in0=ps_[:, 0:TTB], in1=v_[:, ts], op=ALU.mult)
                    if cfg.stop <= 3:
                        return False
                    if hp == 3:
                        p.mark('rw_groups_start_tb%d' % tb)
                    yield "P_DONE"
                    if tb == 0:
                        p.I("dve", "memset", ap=Tst[tiref[0] % 3].v(), constant=0.0)

                    def group_stream(c0, cb_n, T):
                        BK, UV, Xr, A_sb, NXT, MU, GT, PpT, PG, psTr = (T[k_] for k_ in
                            ("BK", "UV", "Xr", "A_sb", "NXT", "MU", "GT", "PpT", "PG", "psTr"))
                        psTr = psTr[:, T["toff"]:T["toff"] + CBS]
                        PGv = PG.v().re("p h (c x) -> p h c x", c=CBS)
                        hc = lambda t: t.v().re("p (h c) x -> p h c x", h=2)[:, :, 0:cb_n, :]
                        for cb in range(cb_n):
                            n = c0 + cb
                            p.I("pe", "transpose", out=psTr[:, cb, 0, :], in_=BKT[:, n].re("p a c -> p (a c)"), identity=ident_bf.v())
                            p.I("pe", "transpose", out=psTr[:, cb, 1, :], in_=KKVT[:, n].re("p a c -> p (a c)"), identity=ident_bf.v())
                        p.I("dve", "tensor_copy", out=BK[:, 0:cb_n, :], in_=psTr[:, 0:cb_n, 0, :])
                        p.I("dve", "tensor_copy", out=Xr.v().re("p (h c) a x -> p h c a x", h=2)[:, :, 0:cb_n, 0, :],
                            in_=psTr[0:64, 0:cb_n, 1, :].re("p c (h x) -> p h c x", h=2))
                        p.I("dve", "tensor_copy", out=UV[64:128, 0:cb_n, :], in_=psTr[64:128, 0:cb_n, 1, :])
                        for cb in range(cb_n):
                            n = c0 + cb
                            for h in range(2):
                                hs = slice(h * 64, (h + 1) * 64)
                                p.I("pe", "matmul", out=PGv[:, h, cb, 0:128],
                                    lhsT=BKT[hs, n].re("p a c -> p (a c)"), rhs=KRT[hs, n].re("p a c -> p (a c)"),
                                    start=True, stop=True)
                                p.I("pe", "matmul", out=PGv[0:64, h, cb, 128:192],
                                    lhsT=KRT[hs, n, 0, :], rhs=BKT[hs, n, 0, :], start=True, stop=True)
                        pgA = PGv[:, :, 0:cb_n, 0:128]
                        p.I("act", "copy", out=hc(A_sb), in_=pgA)
                        p.I("dve", "tensor_tensor", out=hc(A_sb), in0=hc(A_sb),
                            in1=maskA.bc([1, 1], [128, 2, cb_n, 128]), op=ALU.mult)
                        nx = hc(NXT)
                        p.I("dve", "tensor_tensor", out=nx[:, :, :, 0:64], in0=hc(A_sb)[0:64, :, :, 0:64],
                            in1=negSU.bc([1, 1], [64, 2, cb_n, 64]), op=ALU.mult)
                        p.I("dve", "tensor_tensor", out=nx[:, :, :, 64:128], in0=nx[:, :, :, 0:64],
                            in1=cst[0:64, 5, 0:64].bc([1, 1], [64, 2, cb_n, 64]), op=ALU.add)
                        p.I("dve", "tensor_tensor", out=nx[:, :, :, 128:192],
                            in0=PGv[0:64, :, 0:cb_n, 128:192],
                            in1=negSL.bc([1, 1], [64, 2, cb_n, 64]), op=ALU.mult)
                        yield
                        for cb in range(cb_n):
                            for h in range(2):
                                q = h * CBS + cb
                                p.I("pe", "matmul", out=psAV[0:64, q, 0:64], lhsT=A_sb[64:128, q, 0:64],
                                    rhs=UV[64:128, cb, h * 64:(h + 1) * 64], start=True, stop=True)
                        pgI = PGv[0:64, :, 0:cb_n, 0:192]
                        for rnd in range(6):
                            for cb in range(cb_n):
                                for h in range(2):
                                    q = h * CBS + cb
                                    if rnd == 0:
                                        p.I("pe", "matmul", out=PGv[0:64, h, cb, 0:64], lhsT=NXT[:, q, 128:192],
                                            rhs=NXT[:, q, 0:64], start=True, stop=True)
                                    elif rnd < 5:
                                        p.I("pe", "matmul", out=PGv[0:64, h, cb, 0:128], lhsT=NXT[:, q, 128:192],
                                            rhs=NXT[:, q, 0:128], start=True, stop=True)
                                    else:
                                        p.I("pe", "matmul", out=PGv[0:64, h, cb, 64:128], lhsT=NXT[:, q, 128:192],
                                            rhs=NXT[:, q, 64:128], start=True, stop=True)
                                    if rnd < 5:
                                        p.I("pe", "matmul", out=PGv[0:64, h, cb, 128:192], lhsT=NXT[:, q, 0:64],
                                            rhs=NXT[:, q, 128:192], start=True, stop=True)
                            if rnd == 0:
                                p.I("act", "copy", out=Xr.v().re("p (h c) a x -> p h c a x", h=2)[:, :, 0:cb_n, 1, :],
                                    in_=psAV.v().re("p (h c) x -> p h c x", h=2)[0:64, :, 0:cb_n, 0:64])
                            if rnd > 0:
                                p.I("dve", "tensor_tensor", out=nx[:, :, :, 64:128], in0=pgI[:, :, :, 64:128],
                                    in1=nx[:, :, :, 64:128], op=ALU.add)
                            if rnd < 5:
                                p.I("act", "copy", out=nx[:, :, :, 0:64], in_=pgI[:, :, :, 0:64])
                                p.I("act", "copy", out=nx[:, :, :, 128:192], in_=pgI[:, :, :, 128:192])
                            yield
                        for cb in range(cb_n):
                            for h in range(2):
                                q = h * CBS + cb
                                p.I("pe", "matmul", out=PGv[0:64, h, cb, 0:128], lhsT=NXT[:, q, 64:128],
                                    rhs=Xr[:, q].re("p a c -> p (a c)"), start=True, stop=True)
                        pgM = PGv[0:64, :, 0:cb_n, 0:128]
                        p.I("act", "mul", out=hc(MU), in_=pgM, mul=-1.0)
                        p.I("dve", "tensor_scalar", out=UV[0:64, 0:cb_n, :].re("p c (h x) -> p h c x", h=2),
                            in0=pgM[:, :, :, 64:128], scalar1=-1.0, scalar2=None, op0=ALU.mult)
                        yield
                        for cb in range(cb_n):
                            for h in range(2):
                                q = h * CBS + cb
                                hs = slice(h * 64, (h + 1) * 64)
                                p.I("pe", "matmul", out=PGv[hs, h, cb, 0:64], lhsT=MU[:, q, 0:64], rhs=A_sb[0:64, q, 64:128],
                                    start=True, stop=True)
                                p.I("pe", "matmul", out=PGv[hs, h, cb, 64:128], lhsT=MU[:, q, 0:64], rhs=BK[0:64, cb, h * 64:(h + 1) * 64],
                                    start=True, stop=True)
                        for h in range(2):
                            hs = slice(h * 64, (h + 1) * 64)
                            p.I("dve", "tensor_tensor", out=GT[hs, 0:cb_n, :], in0=PGv[hs, h, 0:cb_n, 0:64],
                                in1=KRT[hs, c0:c0 + cb_n, 1, :], op=ALU.add)
                            p.I("dve", "tensor_tensor", out=PpT[hs, 0:cb_n, :], in0=PGv[hs, h, 0:cb_n, 64:128],
                                in1=cst[hs, 5, 0:64].bc([1], [64, cb_n, 64]), op=ALU.add)
                        yield
                        return

                    def back(sets):
                        slot = 0
                        c0g = sets[0][1]
                        for (T, c0, cb_n) in sets:
                            BK, UV, A_sb, GT, PpT = (T[k_] for k_ in ("BK", "UV", "A_sb", "GT", "PpT"))
                            for cb in range(cb_n):
                                n = c0 + cb
                                Tc, Tn = Tst[tiref[0] % 3], Tst[(tiref[0] + 1) % 3]
                                tiref[0] += 1
                                for h in range(2):
                                    hs = slice(h * 64, (h + 1) * 64)
                                    p.I("pe", "matmul", out=psS5[hs, slot, 0:64], lhsT=PpT[hs, cb, :], rhs=Tc[hs, :], start=True, stop=False)
                                    p.I("pe", "matmul", out=psS5[hs, slot, 0:64], lhsT=BK[:, cb, hs], rhs=UV[:, cb, hs], start=False, stop=True)
                                yield
                                for h in range(2):
                                    hs = slice(h * 64, (h + 1) * 64)
                                    p.I("dve", "tensor_scalar", out=Tn[hs, :], in0=psS5[hs, slot, 0:64],
                                        scalar1=e1[hs, n * 64 + 63:n * 64 + 64], scalar2=None, op0=ALU.mult)
                                for h in range(2):
                                    q = h * CBS + cb
                                    hs = slice(h * 64, (h + 1) * 64)
                                    p.I("pe", "matmul", out=psS5[hs, slot, 64:128], lhsT=Tc[hs, :], rhs=GT[hs, cb, :], start=True, stop=False)
                                    p.I("pe", "matmul", out=psS5[hs, slot, 64:128], lhsT=UV[:, cb, hs], rhs=A_sb[:, q, 64:128], start=False, stop=True)
                                slot += 1
                                yield
                        for h in range(2):
                            hs = slice(h * 64, (h + 1) * 64)
                            p.I("act", "copy", out=yT.v().re("p (n c) -> p n c", c=64)[hs, c0g:c0g + slot, :],
                                in_=psS5[hs, 0:slot, 64:128])

                    def drive(gens):
                        alive = list(gens)
                        while alive:
                            nxt = []
                            for gq in alive:
                                try:
                                    next(gq)
                                    nxt.append(gq)
                                except StopIteration:
                                    pass
                            alive = nxt
                            yield "G"

                    prev_sets = None
                    for gi_, g0 in enumerate(range(0, NCHB, CBS * NSTR)):
                        gens, sets = [], []
                        for si in range(NSTR):
                            c0 = g0 + si * CBS
                            if c0 < NCHB:
                                T_ = STR[si][gi_ % 2]
                                cbn_ = min(CBS, NCHB - c0)
                                gens.append(group_stream(c0, cbn_, T_))
                                sets.append((T_, c0, cbn_))
                        if prev_sets is not None:
                            gens.append(back(prev_sets))
                        yield from drive(gens)
                        prev_sets = sets
                    yield from drive([back(prev_sets)])
                    yield "G_DONE"
                    if hp == 3:
                        p.mark('rw_post_start_tb%d' % tb)
                    if cfg.stop <= 8:
                        return False
                    p.I("act", "activation", out=t2.v(), in_=yT.v(), func=AF.Square)
                    yield "Q"
                    HW_ = min(256, TTB)
                    for tt in range(TB // HW_):
                        ts = slice(tt * HW_, (tt + 1) * HW_)
                        p.I("pe", "matmul", out=psP1[:, 0:HW_], lhsT=bones32, rhs=yT[:, ts], start=True, stop=True)
                        p.I("pe", "matmul", out=psP1[:, 256:256 + HW_], lhsT=bones32, rhs=t2[:, ts], start=True, stop=True)
                        p.I("act", "mul", out=t1[:, ts], in_=psP1[:, 0:HW_], mul=1.0 / 64)
                        p.I("dve", "tensor_tensor", out=t3[:, ts], in0=t1[:, ts], in1=t1[:, ts], op=ALU.mult)
                        p.I("dve", "scalar_tensor_tensor", out=t3[:, ts], in0=psP1[:, 256:256 + HW_], scalar=1.0 / 64, in1=t3[:, ts],
                            op0=ALU.mult, op1=ALU.subtract)
                    p.I("act", "activation", out=t3.v(), in_=t3.v(), func=AF.Sqrt, bias=RWKV_GN_EPS, scale=1.0)
                    yield "Q"
                    p.I("dve", "reciprocal", out=t3.v(), in_=t3.v())
                    yield "Q"
                    p.I("dve", "tensor_tensor", out=t1.v(), in0=yT.v(), in1=t1.v(), op=ALU.subtract)
                    yield "Q"
                    p.I("dve", "tensor_tensor", out=t1.v(), in0=t1.v(), in1=t3.v(), op=ALU.mult)
                    yield "Q"
                    p.I("act", "activation", out=t1.v(), in_=t1.v(), func=AF.Identity, bias=vcol(6), scale=vcol(5))
                    yield "Q"
                    p.I("dve", "tensor_tensor", out=t1.v(), in0=t1.v(), in1=bv.v(), op=ALU.add)
                    yield "Q"
                    p.I("act", "activation", out=t2.v(), in_=g_.v(), func=AF.Silu)
                    yield "Q"
                    p.I("dve", "tensor_tensor", out=yg[hp][:, tbs], in0=t1.v(), in1=t2.v(), op=ALU.mult)
                    yield "Q"

                SETS = []
                for _si in range(2):
                    SETS.append(dict(
                        ld={nm: p.sb("ld_" + nm, [128, TB], F32) for nm in ("r", "k", "v", "g")},
                        tm={nm: p.sb("tm_" + nm, [128, TB], F32) for nm in
                            ("lw", "cum", "e1", "e2", "e3", "a", "kk", "kf", "t1", "t2", "t3", "bv", "y")},
                        BKT=p.sb("BKT", [128, NCHB, 2, 64], BF16), KRT=p.sb("KRT", [128, NCHB, 2, 64], BF16),
                        KKVT=p.sb("KKVT", [128, NCHB, 2, 64], BF16)))
                import os as _os2
                items = [(hp_, tb_) for hp_ in range(int(_os2.environ.get('RW_HP', HP))) for tb_ in range(S // TB)]
                gens_ = [item(hp_, tb_, SETS[ix % 2]) for ix, (hp_, tb_) in enumerate(items)]
                phase_ = ["P"] * len(items)
                lo = 0
                while lo < len(items):
                    hi = min(lo + 3, len(items))
                    for ix in range(lo, hi):
                        ph = phase_[ix]
                        if ph == "D":
                            continue
                        if ph == "P" and ((ix >= 2 and phase_[ix - 2] != "D") or (ix >= 1 and phase_[ix - 1] == "P")):
                            continue
                        if ph == "G" and ix >= 1 and phase_[ix - 1] in ("P", "G"):
                            continue
                        try:
                            tag = next(gens_[ix])
                            if tag == "P_DONE":
                                phase_[ix] = "G"
                            elif tag == "G_DONE":
                                phase_[ix] = "Q"
                        except StopIteration:
                            phase_[ix] = "D"
                    while lo < len(items) and phase_[lo] == "D":
                        lo += 1
            if cfg.stop <= 9:
                return False
            p.mark('rw_outproj_start')
            out_proj(yg, rw_out.v()[j], HP, lambda ft: modT[:, l, 2 * DC + ft:2 * DC + ft + 1], src, dst)
            p.mark('rw_outproj_end')
            return True

    bufs = xres
    bi = [0]

    def nextbuf():
        b_ = bufs[bi[0] % len(bufs)]
        bi[0] += 1
        return b_

    cur = xT.v()
    counters = {0: 0, 1: 0, 2: 0}
    for l, kind in enumerate(cfg.kinds):
        j = counters[kind]
        counters[kind] += 1
        if kind in (0, 1):
            dst = nextbuf()
            ok = (rwkv_layer if kind == 0 else gla_layer)(l, j, cur, dst.v())
            if ok:
                cur = dst.v()
        else:
            r_ = ssd_layer(l, j, cur, nextbuf)
            if r_ is not False:
                cur = r_
    with p.scope():
        fg = p.sb("fg", [128, DC], F32)
        p.dma("sp", fg.v(), final_gT.v())
        norm_phase(cur, None, lambda dc: fg[:, dc:dc + 1], None, out_dram=outT.v())
    p.emit()
    return nc, p


def _pp(vec, nchunk):
    v = np.asarray(vec, np.float32)
    lead = v.shape[:-1]
    v = v.reshape(lead + (nchunk, 128))
    return np.ascontiguousarray(np.moveaxis(v, -1, 0))


def prepare_inputs(cfg, inp, n_cores, batch_of_core):
    D, S, DC, L = cfg.D, cfg.S, cfg.DC, cfg.L
    consts, rmask, rmask128 = make_consts(cfg)
    shared = {
        "ada_w": np.ascontiguousarray(inp["ada_w"], dtype=np.float32),
        "ada_bT": _pp(inp["ada_b"], 3 * DC),
        "norm_gT": _pp(inp["norm_g"], DC),
        "final_gT": _pp(inp["final_g"], DC),
        "consts": consts, "rmask": rmask, "rmask128": rmask128,
    }
    if cfg.nR:
        HP = D // 128
        for k in ("rwkv_w_in", "rwkv_w_out", "rwkv_dec_w1", "rwkv_dec_w2", "rwkv_iclr_w1", "rwkv_iclr_w2"):
            shared[k] = np.ascontiguousarray(inp[k], dtype=np.float32)
        shared["rwkv_muT"] = _pp(inp["rwkv_mu"], DC)
        vecs = np.stack([inp["rwkv_dec_w0"], inp["rwkv_iclr_w0"], inp["rwkv_k_k"], inp["rwkv_k_a"],
                         np.asarray(inp["rwkv_r_k"]).reshape(cfg.nR, -1), inp["rwkv_gn_w"], inp["rwkv_gn_b"]], axis=1)
        shared["rwkv_vecT"] = _pp(vecs, HP)
    if cfg.nG:
        for k in ("gla_w_in", "gla_w_out", "gla_gate_w2"):
            shared[k] = np.ascontiguousarray(inp[k], dtype=np.float32)
        shared["gla_nbT"] = _pp(inp["gla_gate_b"], (D // 2) // 128)
        hg = np.asarray(inp["gla_head_g"], np.float32)
        shared["gla_hgb"] = np.ascontiguousarray(np.broadcast_to(hg[None], (128,) + hg.shape))
    if cfg.nS:
        SW = 2 * D
        SH = SW // 64
        for k in ("ssd_w_in", "ssd_w_out"):
            shared[k] = np.ascontiguousarray(inp[k], dtype=np.float32)
        cwk = np.asarray(inp["ssd_conv_w"], np.float32)
        shared["ssd_cwT"] = _pp(np.moveaxis(cwk, 1, 2).reshape(cfg.nS, -1).reshape(cfg.nS, cwk.shape[2], 4).transpose(0, 2, 1), cwk.shape[2] // 128).transpose(0, 1, 3, 2).copy()
        shared["ssd_cbT"] = _pp(inp["ssd_conv_b"], cwk.shape[2] // 128)
        hv = np.zeros((64, cfg.nS, 2), np.float32)
        hv[:SH, :, 0] = np.asarray(inp["ssd_dt_bias"], np.float32).T
        hv[:SH, :, 1] = np.asarray(inp["ssd_a_log"], np.float32).T
        shared["ssd_hv"] = hv
        dsk = np.asarray(inp["ssd_d"], np.float32)
        shared["ssd_dsb"] = np.ascontiguousarray(np.broadcast_to(dsk[None], (128,) + dsk.shape))
        ng = np.asarray(inp["ssd_norm_g"], np.float32)
        shared["ssd_ngb"] = np.ascontiguousarray(np.broadcast_to(ng[None], (128,) + ng.shape))
    maps = []
    for core in range(n_cores):
        b = batch_of_core[core]
        m = dict(shared)
        m["xT"] = np.ascontiguousarray(np.asarray(inp["x"][b], np.float32).T)
        m["cT"] = _pp(inp["c"][b], DC)
        maps.append(m)
    return maps


_CACHE = {}


def kernel(**inputs):
    cfg = Cfg()
    B = inputs["x"].shape[0]
    n_cores = 8
    batch_of_core = [c % B for c in range(n_cores)]
    if "nc" not in _CACHE:
        _CACHE["nc"] = build(cfg)[0]
    nc = _CACHE["nc"]
    maps = prepare_inputs(cfg, inputs, n_cores, batch_of_core)
    res = run_bass_kernel_spmd(nc, maps, core_ids=list(range(n_cores)))
    out = np.empty((B, cfg.S, cfg.D), np.float32)
    for b in range(B):
        out[b] = res.results[b]["outT"].T
    return out
```
